# Optimizing a Trainium2 kernel written in Bass

```python
import jax, jax.numpy as jnp
from jax import lax
import numpy as np

D_MODEL = 2048
BATCH = 4
SEQ = 2048
DEPTH = 1
DEC_BATCH = 128
DEC_SEQ = 4
PAST_LEN = 16384
PAGE_SIZE = 128

HEAD_DIM = 64
N_HEADS = D_MODEL // HEAD_DIM
D_A = N_HEADS * HEAD_DIM
D_CONV = D_MODEL
CONV_W = 31
FFN_CONV_W = 3
D_FF = ((8 * D_MODEL // 3 + 127) // 128) * 128
DECAY_LORA = max(32, round(1.8 * D_MODEL ** 0.5 / 32) * 32)
AAA_LORA = max(32, round(1.8 * D_MODEL ** 0.5 / 32) * 32)
GATE_LORA = max(32, round(0.6 * D_MODEL ** 0.8 / 32) * 32)
GN_EPS = HEAD_DIM * 1e-5
N_MOD = 6
IN_COLS = 3 * D_A + 2 * D_CONV + D_A + D_CONV

kernel_name = 'rwkv7_conformer_gated_hybrid_step'


def _rmsnorm(x, g, eps=1e-6):
    xf = x.astype(jnp.float32)
    y = xf * lax.rsqrt(jnp.mean(xf * xf, axis=-1, keepdims=True) + eps)
    return (y * g.astype(jnp.float32)).astype(x.dtype)


def _layernorm(x, g, b, eps=1e-5):
    xf = x.astype(jnp.float32)
    m = jnp.mean(xf, axis=-1, keepdims=True)
    var = jnp.mean(jnp.square(xf - m), axis=-1, keepdims=True)
    y = (xf - m) * lax.rsqrt(var + eps)
    return (y * g.astype(jnp.float32) + b.astype(jnp.float32)).astype(x.dtype)


def _causal_dwconv(buf, u, k, b):
    z = jnp.concatenate([buf.astype(u.dtype), u], axis=1)
    y = lax.conv_general_dilated(z, k[:, None, :].astype(u.dtype), (1,), 'VALID',
                                 dimension_numbers=('NWC', 'WIO', 'NWC'),
                                 feature_group_count=u.shape[-1])
    return y + b.astype(u.dtype), z[:, z.shape[1] - (k.shape[0] - 1):]


def _wkv7(S0, r, decay, k, v, kk, b):
    def step(S, inp):
        r_t, w_t, k_t, v_t, kk_t, b_t = inp
        sa = jnp.einsum('bhvk,bhk->bhv', S, kk_t)
        S = (S * w_t[:, :, None, :] - sa[..., None] * b_t[:, :, None, :]
             + v_t[..., None] * k_t[:, :, None, :])
        return S, jnp.einsum('bhvk,bhk->bhv', S, r_t)
    seq = tuple(jnp.swapaxes(t, 0, 1) for t in (r, decay, k, v, kk, b))
    S, ys = lax.scan(step, S0.astype(jnp.float32), seq)
    return jnp.swapaxes(ys, 0, 1), S


def _token_mixers(h, shift0, wkv0, conv0, p):
    bt, t, _ = h.shape
    f32 = jnp.float32
    heads = lambda z: z.astype(f32).reshape(bt, t, N_HEADS, HEAD_DIM)
    h_prev = jnp.concatenate([shift0[:, None, :].astype(h.dtype), h[:, :-1]], axis=1)
    dx = h_prev - h
    mu = p['mu']
    xr, xw, xk, xv, xa, xg = (h + dx * mu[i] for i in range(6))
    w_in = p['w_in']
    r = xr @ w_in[:, :D_A]
    k = xk @ w_in[:, D_A:2 * D_A]
    v = xv @ w_in[:, 2 * D_A:3 * D_A]
    rest = h @ w_in[:, 3 * D_A:]
    u_glu = rest[..., :2 * D_CONV] + p['b_glu']
    gate_a = rest[..., 2 * D_CONV:2 * D_CONV + D_A]
    gate_b = rest[..., 2 * D_CONV + D_A:]
    w_pre = (p['w0'] + jnp.tanh(xw @ p['w1']) @ p['w2']).astype(f32)
    decay = jnp.exp(-jnp.exp(-jax.nn.softplus(-w_pre) - 0.5))
    a = jax.nn.sigmoid(p['a0'] + (xa @ p['a1']) @ p['a2'])
    g = jax.nn.sigmoid(xg @ p['g1']) @ p['g2']
    kk = heads(k * p['k_k'])
    kk = kk / jnp.maximum(jnp.sqrt(jnp.sum(kk * kk, axis=-1, keepdims=True)), 1e-12)
    k = k * (1 + (a - 1) * p['k_a'])
    r_h, k_h, v_h, a_h = heads(r), heads(k), heads(v), heads(a)
    y, wkv1 = _wkv7(wkv0, r_h, heads(decay), k_h, v_h, kk, kk * a_h)
    ym = jnp.mean(y, axis=-1, keepdims=True)
    yv = jnp.mean(jnp.square(y - ym), axis=-1, keepdims=True)
    y = ((y - ym) * lax.rsqrt(yv + GN_EPS)).reshape(bt, t, D_A)
    y = y * p['lnx_g'].astype(f32) + p['lnx_b'].astype(f32)
    bonus = jnp.sum(r_h * k_h * p['r_k'].astype(f32), axis=-1, keepdims=True) * v_h
    y_a = ((y + bonus.reshape(bt, t, D_A)) * g.astype(f32)).astype(h.dtype)
    glu = u_glu[..., :D_CONV] * jax.nn.sigmoid(u_glu[..., D_CONV:])
    zc, conv1 = _causal_dwconv(conv0, glu, p['dw_k'], p['dw_b'])
    y_b = jax.nn.silu(_layernorm(zc, p['ln_conv_g'], p['ln_conv_b']))
    merged = jax.nn.sigmoid(gate_a) * y_a + jax.nn.sigmoid(gate_b) * y_b
    return merged @ p['w_out'], h[:, -1], wkv1.astype(wkv0.dtype), conv1


def _conv_ffn(h, buf0, p):
    u = h @ p['w_up']
    u, buf1 = _causal_dwconv(buf0, u, p['ffn_dw_k'], p['ffn_dw_b'])
    ug, uv = u[..., :D_FF], u[..., D_FF:]
    return (jax.nn.silu(ug) * uv) @ p['w_down'], buf1


def _layer(x, c, shift0, wkv0, conv0, ffn0, p):
    mod = jnp.einsum('bd,de->be', jax.nn.silu(c), p['w_ada']) + p['b_ada']
    mod = mod.reshape(c.shape[0], 1, N_MOD, D_MODEL)
    sh1, sc1, gt1, sh2, sc2, gt2 = (mod[:, :, i] for i in range(N_MOD))
    h = _rmsnorm(x, p['norm1_g']) * (1 + sc1) + sh1
    mix, shift1, wkv1, conv1 = _token_mixers(h, shift0, wkv0, conv0, p)
    x = x + gt1 * mix
    h2 = _rmsnorm(x, p['norm2_g']) * (1 + sc2) + sh2
    f, ffn1 = _conv_ffn(h2, ffn0, p)
    x = x + gt2 * f
    return x, shift1, wkv1, conv1, ffn1


def setup_inputs(seed: int = 0) -> dict:
    key = jax.random.key(seed)
    ks = iter(jax.random.split(key, 48))
    nrm = lambda shape, s: jax.random.normal(next(ks), shape, jnp.float32) * s
    D = D_MODEL
    F2 = 2 * D_FF
    return {
        'x_prompt': nrm((BATCH, SEQ, D), 1.0),
        'x_sample': nrm((DEC_BATCH, DEC_SEQ, D), 1.0),
        'state_shift': nrm((DEC_BATCH, D), 1.0),
        'state_wkv': nrm((DEC_BATCH, N_HEADS, HEAD_DIM, HEAD_DIM), 0.3),
        'state_conv': nrm((DEC_BATCH, CONV_W - 1, D_CONV), 0.5),
        'state_ffn': nrm((DEC_BATCH, FFN_CONV_W - 1, F2), 1.0),
        'c_prompt': nrm((BATCH, D), 1.0),
        'c_sample': nrm((DEC_BATCH, D), 1.0),
        'norm1_g': 1.0 + nrm((D,), 0.02),
        'norm2_g': 1.0 + nrm((D,), 0.02),
        'normf_g': 1.0 + nrm((D,), 0.02),
        'w_ada': nrm((D, N_MOD * D), 0.5 * D ** -0.5),
        'b_ada': nrm((N_MOD * D,), 0.02),
        'mu': jax.random.uniform(next(ks), (6, D), jnp.float32),
        'w_in': nrm((D, IN_COLS), D ** -0.5),
        'b_glu': nrm((2 * D_CONV,), 0.02),
        'w0': nrm((D,), 0.5),
        'w1': nrm((D, DECAY_LORA), D ** -0.5),
        'w2': nrm((DECAY_LORA, D), DECAY_LORA ** -0.5),
        'a0': nrm((D,), 0.5),
        'a1': nrm((D, AAA_LORA), D ** -0.5),
        'a2': nrm((AAA_LORA, D), AAA_LORA ** -0.5),
        'g1': nrm((D, GATE_LORA), D ** -0.5),
        'g2': nrm((GATE_LORA, D), GATE_LORA ** -0.5),
        'k_k': 0.85 + nrm((D,), 0.02),
        'k_a': 1.0 + nrm((D,), 0.02),
        'r_k': nrm((N_HEADS, HEAD_DIM), 0.1),
        'lnx_g': 1.0 + nrm((D_A,), 0.02),
        'lnx_b': nrm((D_A,), 0.02),
        'dw_k': nrm((CONV_W, D_CONV), CONV_W ** -0.5),
        'dw_b': nrm((D_CONV,), 0.02),
        'ln_conv_g': 1.0 + nrm((D_CONV,), 0.02),
        'ln_conv_b': nrm((D_CONV,), 0.02),
        'w_out': nrm((D, D), D ** -0.5),
        'w_up': nrm((D, F2), D ** -0.5),
        'ffn_dw_k': nrm((FFN_CONV_W, F2), FFN_CONV_W ** -0.5),
        'ffn_dw_b': nrm((F2,), 0.02),
        'w_down': nrm((D_FF, D), D_FF ** -0.5),
    }


def reference(x_prompt, x_sample, state_shift, state_wkv, state_conv, state_ffn,
              c_prompt, c_sample, norm1_g, norm2_g, normf_g, w_ada, b_ada, mu, w_in, b_glu,
              w0, w1, w2, a0, a1, a2, g1, g2, k_k, k_a, r_k, lnx_g, lnx_b,
              dw_k, dw_b, ln_conv_g, ln_conv_b, w_out, w_up, ffn_dw_k, ffn_dw_b, w_down):
    p = dict(norm1_g=norm1_g, norm2_g=norm2_g, w_ada=w_ada, b_ada=b_ada, mu=mu, w_in=w_in,
             b_glu=b_glu, w0=w0, w1=w1, w2=w2, a0=a0, a1=a1, a2=a2, g1=g1, g2=g2,
             k_k=k_k, k_a=k_a, r_k=r_k, lnx_g=lnx_g, lnx_b=lnx_b, dw_k=dw_k, dw_b=dw_b,
             ln_conv_g=ln_conv_g, ln_conv_b=ln_conv_b, w_out=w_out, w_up=w_up,
             ffn_dw_k=ffn_dw_k, ffn_dw_b=ffn_dw_b, w_down=w_down)
    bp = x_prompt.shape[0]
    dt = x_prompt.dtype
    z_shift = jnp.zeros((bp, D_MODEL), dt)
    z_wkv = jnp.zeros((bp, N_HEADS, HEAD_DIM, HEAD_DIM), dt)
    z_conv = jnp.zeros((bp, CONV_W - 1, D_CONV), dt)
    z_ffn = jnp.zeros((bp, FFN_CONV_W - 1, 2 * D_FF), dt)
    y_p, y_s = x_prompt, x_sample
    for _ in range(DEPTH):
        y_p, shift_p, wkv_p, conv_p, ffn_p = _layer(y_p, c_prompt, z_shift, z_wkv, z_conv, z_ffn, p)
        y_s, shift_s, wkv_s, conv_s, ffn_s = _layer(y_s, c_sample, state_shift, state_wkv,
                                                    state_conv, state_ffn, p)
    y_p = _rmsnorm(y_p, normf_g)
    y_s = _rmsnorm(y_s, normf_g)
    return (y_p, y_s, shift_p, wkv_p, conv_p, ffn_p, shift_s, wkv_s, conv_s, ffn_s)
```

```python
import numpy as np
from contextlib import ExitStack
import concourse.bass as bass
import concourse.mybir as mybir
from concourse.bass_utils import run_bass_kernel_spmd

F32 = mybir.dt.float32
BF16 = mybir.dt.bfloat16
TB = 256
AF = mybir.ActivationFunctionType
ALU = mybir.AluOpType

D = 2048
NK = 16
SEQ = 2048
NS = 16
DFF = 5504
NFC = 43
F2 = 2 * DFF
NHP = 16
C0 = float(np.exp(-0.5))
GN_EPS = 64 * 1e-5

_PSPEC = [("norm1_g", 16), ("norm2_g", 16), ("normf_g", 16), ("b_ada", 96), ("mu", 96), ("b_glu", 32),
          ("w0", 16), ("a0", 16), ("k_k", 16), ("k_a", 16), ("r_k", 16), ("lnx_g", 16), ("lnx_b", 16),
          ("dw_b", 16), ("ln_conv_g", 16), ("ln_conv_b", 16), ("dw_k", 496), ("ffn_dw_k", 258), ("ffn_dw_b", 86)]
POFF = {}
_o = 0
for _n, _c in _PSPEC:
    POFF[_n] = _o
    _o += _c
NPROW = 1280


STOP = [None]


class StopEmit(Exception):
    pass


def stage(n):
    if STOP[0] is not None and STOP[0] == n:
        raise StopEmit()


class Sched:
    EPOCH = 20000

    def __init__(self, nc, stack, n_dma_sems=16):
        self.nc = nc
        self.stack = stack
        self.engs = {}
        for name, h in (("pe", nc.tensor), ("act", nc.scalar), ("dve", nc.vector),
                        ("pool", nc.gpsimd), ("sp", nc.sync)):
            self.engs[name] = dict(h=h, sems=[], count=0, seen={})
        self.dma_sems = [stack.enter_context(nc.semaphore(f"dq{i}")) for i in range(n_dma_sems)]
        self.dma_uses = [0] * n_dma_sems
        self.dma_next = 0
        self.dma_next_pool = 0
        self.last_w = {}
        self.readers = {}
        self.ninstr = 0

    def _sem_for(self, ename, idx):
        e = self.engs[ename]
        ep = idx // self.EPOCH
        while len(e["sems"]) <= ep:
            e["sems"].append(self.stack.enter_context(self.nc.semaphore(f"s_{ename}{len(e['sems'])}")))
        return e["sems"][ep], idx % self.EPOCH + 1

    def _wait(self, ename, ev):
        e = self.engs[ename]
        if ev[0] == "eng":
            _, src, idx = ev
            if src == ename and ename == "pe":
                return
            if e["seen"].get(src, -1) >= idx:
                return
            sem, val = self._sem_for(src, idx)
            e["h"].wait_ge(sem, val)
            e["seen"][src] = idx
        else:
            _, si, val = ev
            key = ("dma", si)
            if e["seen"].get(key, 0) >= val:
                return
            e["h"].wait_ge(self.dma_sems[si], val)
            e["seen"][key] = val

    def _deps(self, ename, reads, writes):
        evs = []
        for k in reads:
            if k in self.last_w:
                evs.append(self.last_w[k])
        for k in writes:
            if k in self.last_w:
                evs.append(self.last_w[k])
            for r in self.readers.get(k, ()):
                evs.append(r)
        for ev in evs:
            self._wait(ename, ev)

    def _record(self, ev, reads, writes):
        for k in reads:
            self.readers.setdefault(k, []).append(ev)
        for k in writes:
            self.last_w[k] = ev
            self.readers[k] = []

    def op(self, ename, fn, reads=(), writes=()):
        e = self.engs[ename]
        self._deps(ename, reads, writes)
        idx = e["count"]
        sem, val = self._sem_for(ename, idx)
        ins = fn(e["h"])
        ins.then_inc(sem, 1)
        e["count"] += 1
        self.ninstr += 1
        self._record(("eng", ename, idx), reads, writes)

    def dma(self, out, in_, reads=(), writes=(), q="sp", **kw):
        qname = q
        e = self.engs[qname]
        self._deps(qname, reads, writes)
        half = len(self.dma_sems) // 2
        if qname == "pool":
            si = half + self.dma_next_pool % half
            self.dma_next_pool += 1
        else:
            si = self.dma_next % half
            self.dma_next += 1
        prev = self.dma_uses[si]
        if prev > 0:
            self._wait(qname, ("dma", si, 16 * prev))
        self.dma_uses[si] = prev + 1
        e["h"].dma_start(out=out, in_=in_, **kw).then_inc(self.dma_sems[si], 16)
        self.ninstr += 1
        ev = ("dma", si, 16 * (prev + 1))
        self._record(ev, reads, writes)

    def finish(self):
        for si, uses in enumerate(self.dma_uses):
            if uses:
                self._wait("sp", ("dma", si, 16 * uses))
        for name, e in self.engs.items():
            if name != "sp" and e["count"]:
                self._wait("sp", ("eng", name, e["count"] - 1))


def build_nc(n_pblocks=SEQ // TB, do_sample=True):
    nc = bass.Bass("TRN2", target_bir_lowering=False)
    din = lambda n, s: nc.dram_tensor(n, s, F32, kind="ExternalInput").ap()
    dout = lambda n, s: nc.dram_tensor(n, s, F32, kind="ExternalOutput").ap()
    xp = din("xp", [SEQ, D]); xs = din("xs", [64, D]); st_shift = din("st_shift", [NS, D])
    st_wkv = din("st_wkv", [NS, 32, 64, 64]); st_conv = din("st_conv", [NS, 30, D]); st_ffn = din("st_ffn", [NS, 2, F2])
    cvec = din("cvec", [17, D]); params = din("params", [NPROW, 128])
    w_ada = din("w_ada", [D, 6 * D]); w_in = din("w_in", [D, 7 * D])
    w1 = din("w1", [D, 96]); w2 = din("w2", [96, D]); a1 = din("a1", [D, 96]); a2 = din("a2", [96, D])
    g1 = din("g1", [D, 256]); g2 = din("g2", [256, D]); w_out = din("w_out", [D, D])
    w_up = din("w_up", [D, F2]); w_down = din("w_down", [DFF, D])
    yp = dout("yp", [SEQ, D]); ys = dout("ys", [64, D]); shift_p = dout("shift_p", [16, 128])
    wkv_p = dout("wkv_p", [32, 64, 64]); conv_p = dout("conv_p", [30, D]); ffn_p = dout("ffn_p", [2, F2])
    shift_s = dout("shift_s", [NS, D]); wkv_s = dout("wkv_s", [NS, 32, 64, 64])
    conv_s = dout("conv_s", [NS, 30, D]); ffn_s = dout("ffn_s", [NS, 2, F2])

    kview = lambda W: W.rearrange("(kc ki) n -> ki kc n", ki=128)

    with ExitStack() as st:
        cnt = [0]

        def sb(shape, name=None, dt=F32):
            cnt[0] += 1
            return st.enter_context(nc.sbuf_tensor(name or f"t{cnt[0]}", shape, dt))

        ident = sb([128, 128], "ident"); ones = sb([128, TB], "ones"); bones = sb([128, 128], "bones")
        ones_bf = sb([128, 128], "ones_bf", BF16)
        m_su = sb([64, 64]); m_u = sb([64, 64]); m_sl = sb([64, 64]); blk = sb([64, 64])
        ms_su = sb([64, 64]); ms_u = sb([64, 64]); ms_sl = sb([64, 64])
        rowsel = sb([64, 16])
        PT = sb([128, NPROW], "PT")
        TP = sb([128, 6, 16], "TP"); TS = sb([128, 6, 16, 16], "TS")
        omu = sb([128, 96]); omka = sb([128, 16])
        shiftst = sb([128, 16]); Mst = sb([128, 16, 64]); convhalo = sb([128, 16, 30], None, BF16); ffnhalo = sb([128, 86, 2])
        xtile = sb([128, D], "xtile"); xT = sb([128, 16, TB], "xT"); h = sb([128, 16, TB], "h", BF16)
        dx = sb([128, 16, TB], "dx", BF16); mixact = sb([128, 6144], "mixact")
        mixbf = mixact[:, :].bitcast(BF16)
        hf = mixact[:, 0:16 * TB].rearrange("p (k t) -> p k t", t=TB)
        MT = mixact[:, 0:96 * 17].rearrange("p (m c) -> p m c", c=17)
        zc = sb([128, 16, TB], "zc", BF16); merged = sb([128, 16, TB], "merged", BF16)
        rstd = sb([128, TB]); rstd2 = sb([128, TB])
        ring = [sb([128, 16, 128], f"ring{i}", BF16) for i in range(4)]
        PV = {n: sb([128, TB], "pv_" + n) for n in
              ["r", "k0", "v", "lw", "a", "g", "sga", "glu", "glb", "kk", "kkn", "t1", "k", "b", "rk", "bonus",
               "cs", "cx", "eW", "eWi", "eWp", "y", "t2", "t3", "t4", "gb"]}
        for n in ["tw", "xa1", "sg0", "sg1"]:
            PV[n] = sb([128, TB], "pv_" + n, BF16)
        zext = sb([128, 32 + TB], "zext2", BF16); uext = sb([128, 8 + TB], "uext"); zsbf = sb([128, 16, 34], "zsbf", BF16)
        smpreg = sb([128, 9216], "smpreg")
        CTN = ["NRB", "LKT", "RKT", "X", "U", "ZTa", "ZTb"]

        def mkset(c):
            B = {}
            if c == 0:
                B["TL"] = sb([128, 2, 64])[:]; B["TR"] = sb([128, 2, 64])[:]
                B["TLz"] = sb([128, 2, 2, 64])[:]; B["TRz"] = sb([128, 2, 2, 64])[:]
                for n in CTN:
                    B[n] = sb([64, 2, 64], "ct_" + n)[:]
                B["PZ"] = [sb([64, 2, 2, 64], f"pz{i}")[:] for i in range(2)]
                for n in ["Vtok", "nbtok", "ktok", "Ytok"]:
                    B[n] = sb([64, 128])[:]
            else:
                o = [2688 * (c - 1)]

                def take(nparts, words):
                    v = smpreg[0:nparts, o[0]:o[0] + words]
                    o[0] += words
                    return v
                B["TL"] = take(128, 128).rearrange("p (a t) -> p a t", a=2)
                B["TR"] = take(128, 128).rearrange("p (a t) -> p a t", a=2)
                B["TLz"] = take(128, 256).rearrange("p (h a t) -> p h a t", h=2, a=2)
                B["TRz"] = take(128, 256).rearrange("p (h a t) -> p h a t", h=2, a=2)
                for n in CTN:
                    B[n] = take(64, 128).rearrange("p (h t) -> p h t", h=2)
                B["PZ"] = [take(64, 256).rearrange("p (h a t) -> p h a t", h=2, a=2) for i in range(2)]
                for n in ["Vtok", "nbtok", "ktok", "Ytok"]:
                    B[n] = take(64, 128)
            return B

        CS = [mkset(c) for c in range(4)]
        Ytok = sb([64, 128], "Ytok_st")
        NBX = smpreg[0:64, 0:2048].rearrange("p (s c) -> p s c", s=16)
        KX = smpreg[0:64, 2048:4096].rearrange("p (s c) -> p s c", s=16)
        KKX = smpreg[:, 4096:5120].rearrange("p (s c) -> p s c", s=16)
        RX = smpreg[:, 5120:6144].rearrange("p (s c) -> p s c", s=16)
        Msz = smpreg[:, 6144:8192].rearrange("p (h s c) -> p h s c", h=2, s=16)
        selm = smpreg[:, 8192:9216].rearrange("p (s c) -> p s c", s=16)
        Stmp = sb([64, 128]); Sout = sb([64, 128])
        SGA = [PV["sga"], smpreg[:, 8064:8064 + TB]]
        GG = [PV["g"], smpreg[:, 8064 + TB:8064 + 2 * TB]]
        otile = xtile
        cvt = xtile[0:17, :]; cT = sb([128, 16, 17], None, BF16); ptile = sb([128, 128])
        diag = sb([128, 31, 128], "diag", BF16)
        pss = [st.enter_context(nc.psum_tensor(f"ps{i}", [128, 512], F32)) for i in range(8)]
        block = st.enter_context(nc.Block())
        S = Sched(nc, st)

        psi = [0]

        psl_restrict = [False]

        def PSL():
            if psl_restrict[0]:
                i = 6 + psi[0] % 2
            else:
                i = psi[0] % 8
            psi[0] += 1
            return pss[i][:, :], ("ps", i)

        def MM(out, lhsT, rhs, start, stop, r, w):
            S.op("pe", lambda e: e.matmul(out, lhsT=lhsT, rhs=rhs, start=start, stop=stop), reads=r, writes=w)

        def TRP(out, in_, r, w):
            k = in_.shape[0]
            S.op("pe", lambda e: e.transpose(out=out, in_=in_, identity=ident[0:k, 0:k]), reads=list(r) + ["ident"], writes=w)

        def TT(out, a, b, op, r, w, eng="dve"):
            S.op(eng, lambda e: e.tensor_tensor(out=out, in0=a, in1=b, op=op), reads=r, writes=w)

        def TSC(out, a, s1, s2, op0, op1, r, w, eng="dve"):
            if s2 is None:
                S.op(eng, lambda e: e.tensor_scalar(out=out, in0=a, scalar1=s1, scalar2=None, op0=op0), reads=r, writes=w)
            else:
                S.op(eng, lambda e: e.tensor_scalar(out=out, in0=a, scalar1=s1, scalar2=s2, op0=op0, op1=op1), reads=r, writes=w)

        def STT(out, a, sc, b, op0, op1, r, w):
            S.op("dve", lambda e: e.scalar_tensor_tensor(out=out, in0=a, scalar=sc, in1=b, op0=op0, op1=op1), reads=r, writes=w)

        def CP(out, a, r, w, eng="dve"):
            if eng == "act":
                S.op("act", lambda e: e.copy(out=out, in_=a), reads=r, writes=w)
            else:
                S.op(eng, lambda e: e.tensor_copy(out=out, in_=a), reads=r, writes=w)

        def ACT(out, a, func, r, w, bias=0.0, scale=1.0):
            S.op("act", lambda e: e.activation(out=out, in_=a, func=func, bias=bias, scale=scale), reads=r, writes=w)

        def RSQRT(out, a, r, w, scale=1.0, eps=0.0):
            ACT(out, a, AF.Sqrt, r, w, bias=eps, scale=scale)
            S.op("dve", lambda e: e.reciprocal(out=out, in_=out), reads=w, writes=w)

        ringi = [0]

        def WLOAD(src, shape3):
            i = ringi[0] % 4
            ringi[0] += 1
            kp, nk, n = shape3
            dst = ring[i][0:kp, 0:nk, 0:n]
            S.dma(dst, src, writes=[("ring", i)], q="pool")
            return dst, ("ring", i)

        MULT, ADD, SUB = ALU.mult, ALU.add, ALU.subtract
        pt = lambda name, c: PT[:, POFF[name] + c:POFF[name] + c + 1]
        ptr = lambda name, c0, n: PT[:, POFF[name] + c0:POFF[name] + c0 + n]

        S.op("pool", lambda e: e.memset(ident[:], 0.0), writes=["ident"])
        S.op("pool", lambda e: e.affine_select(out=ident[:], in_=ident[:], pattern=[[-1, 128]], compare_op=ALU.not_equal,
                                               fill=1.0, base=0, channel_multiplier=1), reads=["ident"], writes=["ident"])
        S.op("pool", lambda e: e.memset(ones[:], 1.0), writes=["ones"])
        S.op("pool", lambda e: e.memset(ones_bf[:], 1.0), writes=["ones"])
        S.op("pool", lambda e: e.memset(bones[:], 0.0), writes=["bones"])
        S.op("pool", lambda e: e.memset(bones[0:64, 0:64], 1.0), reads=["bones"], writes=["bones"])
        S.op("pool", lambda e: e.memset(bones[64:128, 64:128], 1.0), reads=["bones"], writes=["bones"])
        for m, cm, pat, op in ((m_su, -1, 1, ALU.is_gt), (m_u, -1, 1, ALU.is_ge), (m_sl, 1, -1, ALU.is_gt)):
            S.op("pool", lambda e: e.memset(m[:], 1.0), writes=["masks"])
            S.op("pool", lambda e: e.affine_select(out=m[:], in_=m[:], pattern=[[pat, 64]], compare_op=op, fill=0.0,
                                                   base=0, channel_multiplier=cm), reads=["masks"], writes=["masks"])
        S.op("pool", lambda e: e.memset(blk[:], 1.0), writes=["masks"])
        blk3 = blk[:].rearrange("p (a b) -> p a b", b=4)
        S.op("pool", lambda e: e.affine_select(out=blk3, in_=blk3, pattern=[[-4, 16], [0, 4]], compare_op=ALU.is_ge, fill=0.0,
                                               base=0, channel_multiplier=1), reads=["masks"], writes=["masks"])
        S.op("pool", lambda e: e.affine_select(out=blk3, in_=blk3, pattern=[[4, 16], [0, 4]], compare_op=ALU.is_ge, fill=0.0,
                                               base=3, channel_multiplier=-1), reads=["masks"], writes=["masks"])
        for ms, m in ((ms_su, m_su), (ms_u, m_u), (ms_sl, m_sl)):
            TT(ms[:], m[:], blk[:], MULT, ["masks"], ["masks"], eng="pool")
        S.op("pool", lambda e: e.memset(rowsel[:], 1.0), writes=["masks"])
        S.op("pool", lambda e: e.affine_select(out=rowsel[:], in_=rowsel[:], pattern=[[-4, 16]], compare_op=ALU.is_ge, fill=0.0,
                                               base=0, channel_multiplier=1), reads=["masks"], writes=["masks"])
        S.op("pool", lambda e: e.affine_select(out=rowsel[:], in_=rowsel[:], pattern=[[4, 16]], compare_op=ALU.is_ge, fill=0.0,
                                               base=3, channel_multiplier=-1), reads=["masks"], writes=["masks"])
        for t_ in (shiftst, Mst, convhalo, ffnhalo):
            S.op("pool", lambda e: e.memset(t_[:], 0.0), writes=["state"])
        S.op("pool", lambda e: e.memset(smpreg[:], 0.0), writes=["smpreg"])
        for c in range(4):
            S.op("pool", lambda e: e.memset(CS[c]["TLz"], 0.0), reads=["smpreg"], writes=[f"TLz{c}"])
            S.op("pool", lambda e: e.memset(CS[c]["TRz"], 0.0), reads=["smpreg"], writes=[f"TRz{c}"])

        for i in range(NPROW // 128):
            S.dma(ptile[:], params[128 * i:128 * i + 128, :], writes=["ptile"])
            ps_, pk = PSL()
            TRP(ps_[:, 0:128], ptile[:], ["ptile"], [pk])
            CP(PT[:, 128 * i:128 * i + 128], ps_[:, 0:128], [pk], ["PT"])
        TSC(omu[:], ptr("mu", 0, 96), -1.0, 1.0, MULT, ADD, ["PT"], ["PT2"])
        TSC(omka[:], ptr("k_a", 0, 16), -1.0, 1.0, MULT, ADD, ["PT"], ["PT2"])

        S.dma(cvt, cvec, writes=["xtile"])
        ACT(cvt, cvt, AF.Silu, ["xtile"], ["xtile"])
        for kc in range(16):
            ps_, pk = PSL()
            TRP(ps_[:, 0:17], cvt[:, 128 * kc:128 * kc + 128], ["xtile"], [pk])
            CP(cT[:, kc, :], ps_[:, 0:17], [pk], ["cT"])
        for m in range(96):
            wv, wk = WLOAD(kview(w_ada)[:, :, 128 * m:128 * m + 128], (128, 16, 128))
            ps_, pk = PSL()
            for kc in range(16):
                MM(ps_[:, 0:17], wv[:, kc, :], cT[:, kc, :], kc == 0, kc == 15, [wk, "cT"], [pk])
            TSC(MT[:, m, :], ps_[:, 0:17], pt("b_ada", m), None, ADD, None, [pk, "PT"], ["mixact"])
        for (ti, mi, kind) in ((0, 1, "gs1"), (1, 0, "sh"), (2, 2, "gt"), (3, 4, "gs2"), (4, 3, "sh"), (5, 5, "gt")):
            src_p = MT[:, 16 * mi:16 * mi + 16, 0]
            src_s = MT[:, 16 * mi:16 * mi + 16, 1:17]
            if kind.startswith("gs"):
                g = ptr("norm1_g" if kind == "gs1" else "norm2_g", 0, 16)
                STT(TP[:, ti, :], src_p, 1.0, g, ADD, MULT, ["mixact", "PT"], ["TP"])
                STT(TS[:, ti, :, :], src_s, 1.0, g.unsqueeze(2).to_broadcast([128, 16, 16]), ADD, MULT, ["mixact", "PT"], ["TS"])
            else:
                CP(TP[:, ti, :], src_p, ["mixact"], ["TP"])
                CP(TS[:, ti, :, :], src_s, ["mixact"], ["TS"])

        def emit_block(kind, bi):
            smp = kind == "s"
            T = 64 if smp else TB
            nch = T // 64
            bc = lambda tab16: tab16.unsqueeze(2).to_broadcast([128, 16, T])
            v4 = lambda ap: ap.rearrange("p k (s t) -> p k s t", t=4)
            last = (not smp) and bi == n_pblocks - 1

            xsrc_rows = 64 if smp else 128
            for ti_ in range(max(1, T // 128)):
                xsrc = xs if smp else xp[TB * bi + 128 * ti_:TB * bi + 128 * ti_ + 128, :]
                S.dma(xtile[0:xsrc_rows, :], xsrc, writes=["xtile"])
                R = xsrc_rows
                for q in range(4):
                    ps_, pk0 = PSL()
                    for j in range(4):
                        kc = 4 * q + j
                        TRP(ps_[:, j * R:(j + 1) * R], xtile[0:R, 128 * kc:128 * kc + 128], ["xtile"], [pk0])
                    CP(xT[:, 4 * q:4 * q + 4, 128 * ti_:128 * ti_ + R], ps_[:, 0:4 * R].rearrange("p (a t) -> p a t", a=4),
                       [pk0], ["xT"], eng="act")
            stage(1)

            def rms(src, key, out_rstd):
                TT(dx[:, :, 0:T], src[:, :, 0:T], src[:, :, 0:T], MULT, [key], ["dx"])
                ps_, pk = PSL()
                for kc in range(16):
                    MM(ps_[:, 0:T], ones_bf[:], dx[:, kc, 0:T], kc == 0, kc == 15, ["ones", "dx"], [pk])
                RSQRT(out_rstd[:, 0:T], ps_[:, 0:T], [pk], ["rstd"], scale=1.0 / D, eps=1e-6)

            def modnorm(src, skey, rs, tg, tsft):
                d = hf[:, :, 0:T]
                TT(d, src[:, :, 0:T], rs[:, 0:T].unsqueeze(1).to_broadcast([128, 16, T]), MULT, [skey, "rstd"], ["mixact"])
                if smp:
                    for (tix, op) in ((tg, MULT), (tsft, ADD)):
                        TT(v4(d), v4(d), TS[:, tix, :, :].unsqueeze(3).to_broadcast([128, 16, 16, 4]), op, ["mixact", "TS"], ["mixact"])
                else:
                    TT(d, d, bc(TP[:, tg, :]), MULT, ["mixact", "TP"], ["mixact"])
                    TT(d, d, bc(TP[:, tsft, :]), ADD, ["mixact", "TP"], ["mixact"])

            rms(xT, "xT", rstd)
            modnorm(xT, "xT", rstd, 0, 1)
            CP(h[:, :, 0:T], hf[:, :, 0:T], ["mixact"], ["h"], eng="act")
            stage(2)
            if smp:
                S.dma(otile[0:16, :], st_shift, writes=["xtile"])
                h4 = v4(hf[:, :, 0:64])
                d4 = v4(dx[:, :, 0:64])
                for q in range(4):
                    ps_, pk = PSL()
                    for j in range(4):
                        TRP(ps_[:, 16 * j:16 * j + 16], otile[0:16, 128 * (4 * q + j):128 * (4 * q + j) + 128], ["xtile"], [pk])
                    TT(d4[:, 4 * q:4 * q + 4, :, 0], ps_[:, 0:64].rearrange("p (a s) -> p a s", a=4), h4[:, 4 * q:4 * q + 4, :, 0],
                       SUB, [pk, "mixact"], ["dx"])
                TT(d4[:, :, :, 1:4], h4[:, :, :, 0:3], h4[:, :, :, 1:4], SUB, ["mixact"], ["dx"])
                for kc in range(16):
                    ps_, pk = PSL()
                    TRP(ps_[0:16, 0:128], h4[:, kc, :, 3], ["mixact"], [pk])
                    CP(otile[0:16, 128 * kc:128 * kc + 128], ps_[0:16, 0:128], [pk], ["xtile"])
                S.dma(shift_s, otile[0:16, :], reads=["xtile"])
            else:
                TT(dx[:, :, 0], shiftst[:], hf[:, :, 0], SUB, ["state", "mixact"], ["dx"])
                TT(dx[:, :, 1:T], hf[:, :, 0:T - 1], hf[:, :, 1:T], SUB, ["mixact"], ["dx"])
                CP(shiftst[:], hf[:, :, T - 1], ["mixact"], ["state"])
                if last:
                    ps_, pk = PSL()
                    TRP(ps_[0:16, 0:128], shiftst[:], ["state"], [pk])
                    CP(otile[0:16, 0:128], ps_[0:16, 0:128], [pk], ["xtile"])
                    S.dma(shift_p, otile[0:16, 0:128], reads=["xtile"])
            stage(3)

            def mix(dst, dkey, mi):
                TT(dst, dx[:, :, 0:T], bc(ptr("mu", 16 * mi, 16)), MULT, ["dx", "PT"], [dkey])
                TT(dst, dst, h[:, :, 0:T], ADD, [dkey, "h"], [dkey])

            xmix = [mixbf[:, 16 * TB * i:16 * TB * (i + 1)].rearrange("p (k t) -> p k t", t=TB)[:, :, 0:T] for i in range(3)]
            tmpx = zc[:, :, 0:T]
            for (W, ncol, mi, outs, func) in ((w1, 96, 1, [PV["tw"]], AF.Tanh), (a1, 96, 4, [PV["xa1"]], AF.Copy),
                                               (g1, 256, 5, [PV["sg0"], PV["sg1"]], AF.Sigmoid)):
                mix(tmpx, "zc", mi)
                for oi, o in enumerate(outs):
                    n = min(128, ncol - 128 * oi)
                    wv, wk = WLOAD(kview(W)[:, :, 128 * oi:128 * oi + n], (128, 16, n))
                    ps_, pk = PSL()
                    for kc in range(16):
                        MM(ps_[0:n, 0:T], wv[:, kc, :], tmpx[:, kc, :], kc == 0, kc == 15, [wk, "zc"], [pk])
                    ACT(o[0:n, 0:T], ps_[0:n, 0:T], func, [pk], ["lora"])
            mix(xmix[0], "mixact", 0)
            mix(xmix[1], "mixact", 2)
            mix(xmix[2], "mixact", 3)
            stage(4)

            def part1(p, alt):
                def proj(src, skey, col0, out, func, okey, bias=0.0):
                    wv, wk = WLOAD(kview(w_in)[:, :, col0 + 128 * p:col0 + 128 * p + 128], (128, 16, 128))
                    ps_, pk = PSL()
                    for kc in range(16):
                        MM(ps_[:, 0:T], wv[:, kc, :], src[:, kc, :], kc == 0, kc == 15, [wk, skey], [pk])
                    ACT(out[:, 0:T], ps_[:, 0:T], func, [pk, "PT"], [okey], bias=bias)

                hh = h[:, :, 0:T]
                proj(xmix[0], "mixact", 0, PV["r"], AF.Copy, "r")
                proj(xmix[1], "mixact", D, PV["k0"], AF.Copy, "k0")
                proj(xmix[2], "mixact", 2 * D, PV["v"], AF.Copy, "v")
                proj(hh, "h", 3 * D, PV["glu"], AF.Identity, "glu", bias=pt("b_glu", p))
                proj(hh, "h", 4 * D, PV["glb"], AF.Sigmoid, "glb", bias=pt("b_glu", 16 + p))
                proj(hh, "h", 5 * D, SGA[alt], AF.Sigmoid, f"sga{alt}")
                for (W, K, srcs, out, func, bias, okey) in ((w2, 96, [PV["tw"]], PV["lw"], AF.Sigmoid, pt("w0", p), "lw"),
                                                             (a2, 96, [PV["xa1"]], PV["a"], AF.Sigmoid, pt("a0", p), "a"),
                                                             (g2, 256, [PV["sg0"], PV["sg1"]], GG[alt], AF.Copy, 0.0, f"g{alt}")):
                    nk = len(srcs)
                    kp = min(K, 128)
                    wv, wk = WLOAD(W.rearrange("(kc ki) n -> ki kc n", ki=kp)[:, :, 128 * p:128 * p + 128], (kp, nk, 128))
                    ps_, pk = PSL()
                    for j in range(nk):
                        MM(ps_[:, 0:T], wv[:, j, :], srcs[j][0:kp, 0:T], j == 0, j == nk - 1, [wk, "lora"], [pk])
                    ACT(out[:, 0:T], ps_[:, 0:T], func, [pk, "PT"], [okey], bias=bias)

            pipelined = not smp
            if pipelined:
                part1(0, 0)
            for p in range(NHP):
                alt = (p % 2) if pipelined else 0
                if not pipelined:
                    part1(p, 0)
                stage(5)
                V = {n: PV[n][:, 0:T] for n in PV}
                TT(V["glu"], V["glu"], V["glb"], MULT, ["glu", "glb"], ["glu"])
                for j in range(31):
                    TSC(diag[:, j, :], ident[:], pt("dw_k", 16 * j + p), None, MULT, None, ["ident", "PT"], ["diag"], eng="pool")
                ps_, pk = PSL()
                if smp:
                    S.dma(otile[0:120, 0:512].rearrange("r (q c) -> r q c", q=4),
                          st_conv[:, :, 128 * p:128 * p + 128].rearrange("(q s) t c -> (s t) q c", q=4), writes=["xtile"])
                    for q in range(4):
                        pq, pqk = PSL()
                        TRP(pq[:, 0:120], otile[0:120, 128 * q:128 * q + 128], ["xtile"], [pqk])
                        CP(zsbf[:, 4 * q:4 * q + 4, 0:30], pq[:, 0:120].rearrange("p (s t) -> p s t", t=30), [pqk], ["zsbf"])
                    CP(zsbf[:, :, 30:34], V["glu"].rearrange("p (s t) -> p s t", t=4), ["glu"], ["zsbf"])
                    for j in range(31):
                        MM(ps_[:, 0:64], diag[:, j, :], zsbf[:, :, j:j + 4], j == 0, j == 30, ["diag", "zsbf"], [pk])
                    pq, pqk = PSL()
                    CP(PV["t4"][:, 0:64].rearrange("p (t s) -> p t s", t=4), V["glu"].rearrange("p (s t) -> p t s", t=4), ["glu"], ["t4"])
                    TRP(pq[0:64, 0:128], PV["t4"][:, 0:64], ["t4"], [pqk])
                    CP(Ytok[:, :], pq[0:64, 0:128], [pqk], ["Ytok"])
                    for t in range(4):
                        S.dma(conv_s[:, 26 + t, 128 * p:128 * p + 128], Ytok[16 * t:16 * t + 16, :], reads=["Ytok"])
                else:
                    CP(zext[:, 0:30], convhalo[:, p, :], ["state"], ["zext"])
                    CP(zext[:, 30:30 + T], V["glu"], ["glu"], ["zext"], eng="act")
                    for j in range(31):
                        MM(ps_[:, 0:T], diag[:, j, :], zext[:, j:j + T], j == 0, j == 30, ["diag", "zext"], [pk])
                    CP(convhalo[:, p, :], zext[:, T:T + 30], ["zext"], ["state"])
                    if last:
                        pq, pqk = PSL()
                        TRP(pq[0:30, 0:128], PV["glu"][:, T - 30:T], ["glu"], [pqk])
                        CP(Ytok[0:30, :], pq[0:30, 0:128], [pqk], ["Ytok"])
                        S.dma(conv_p[:, 128 * p:128 * p + 128], Ytok[0:30, :], reads=["Ytok"])
                TSC(zc[:, p, 0:T], ps_[:, 0:T], pt("dw_b", p), None, ADD, None, [pk, "PT"], [("zcp", p)])

                stage(6)
                TSC(V["kk"], V["k0"], pt("k_k", p), None, MULT, None, ["k0", "PT"], ["kk"])
                TT(V["t2"], V["kk"], V["kk"], MULT, ["kk"], ["t2"])
                ps_, pk = PSL()
                MM(ps_[:, 0:T], bones[:], V["t2"], True, True, ["bones", "t2"], [pk])
                RSQRT(V["t3"], ps_[:, 0:T], [pk], ["t3"], eps=1e-30)
                TT(V["kkn"], V["kk"], V["t3"], MULT, ["kk", "t3"], ["kkn"])
                TSC(V["t1"], V["a"], pt("k_a", p), omka[:, p:p + 1], MULT, ADD, ["a", "PT", "PT2"], ["t1"])
                TT(V["k"], V["k0"], V["t1"], MULT, ["k0", "t1"], ["k"])
                TT(V["b"], V["kkn"], V["a"], MULT, ["kkn", "a"], ["b"])
                STT(V["rk"], V["r"], pt("r_k", p), V["k"], MULT, MULT, ["r", "k", "PT"], ["rk"])
                ps_, pk = PSL()
                MM(ps_[:, 0:T], bones[:], V["rk"], True, True, ["bones", "rk"], [pk])
                TT(V["bonus"], ps_[:, 0:T], V["v"], MULT, [pk, "v"], ["bonus"])
                stage(7)
                L = 4 if smp else 64
                nseg = T // L
                S.op("dve", lambda e: e.tensor_tensor_scan(out=V["cs"], data0=ones[:, 0:T], data1=V["lw"], initial=0.0,
                                                           op0=MULT, op1=ADD), reads=["ones", "lw"], writes=["cs"])
                TT(V["cx"], V["cs"], V["lw"], SUB, ["cs", "lw"], ["cx"])
                cs3 = V["cs"].rearrange("p (s l) -> p s l", l=L)
                cx3 = V["cx"].rearrange("p (s l) -> p s l", l=L)
                CP(PV["t4"][:, 0:nseg], cx3[:, :, 0], ["cx"], ["t4"])
                base = PV["t4"][:, 0:nseg].unsqueeze(2).to_broadcast([128, nseg, L])
                TT(cs3, cs3, base, SUB, ["cs", "t4"], ["cs"])
                TT(cx3, cx3, base, SUB, ["cx", "t4"], ["cx"])
                ACT(V["eW"], V["cs"], AF.Exp, ["cs"], ["eW"], scale=-C0)
                ACT(V["eWi"], V["cs"], AF.Exp, ["cs"], ["eWi"], scale=C0)
                ACT(V["eWp"], V["cx"], AF.Exp, ["cx"], ["eWp"], scale=-C0)

                stage(8)
                msu, mu_, msl = (ms_su, ms_u, ms_sl) if smp else (m_su, m_u, m_sl)
                b2 = lambda m: m[:].unsqueeze(1).to_broadcast([64, 2, 64])
                h2v = lambda ap: ap.rearrange("p (h n) -> p h n", h=2)
                K_ = lambda n, c: f"{n}{c}"

                def bankA(c):
                    i = (0, 1, 3, 4)[c]
                    return pss[i][:, :], ("ps", i)

                def bankB(c):
                    i = (2, 2, 5, 5)[c]
                    return pss[i][:, 256 * (c % 2):256 * (c % 2) + 256], ("ps", i)

                def P0(c):
                    B = CS[c]
                    cs_ = slice(64 * c, 64 * c + 64)
                    TT(B["TL"][:, 0, :], PV["kkn"][:, cs_], PV["eWp"][:, cs_], MULT, ["kkn", "eWp"], [K_("TL", c)])
                    TT(B["TL"][:, 1, :], PV["r"][:, cs_], PV["eW"][:, cs_], MULT, ["r", "eW"], [K_("TL", c)])
                    TT(B["TR"][:, 0, :], PV["b"][:, cs_], PV["eWi"][:, cs_], MULT, ["b", "eWi"], [K_("TR", c)])
                    TT(B["TR"][:, 1, :], PV["k"][:, cs_], PV["eWi"][:, cs_], MULT, ["k", "eWi"], [K_("TR", c)])
                    for hj in range(2):
                        hs = slice(64 * hj, 64 * hj + 64)
                        CP(B["TLz"][hs, hj, :, :], B["TL"][hs, :, :], [K_("TL", c)], [K_("TLz", c)], eng="act")
                        CP(B["TRz"][hs, hj, :, :], B["TR"][hs, :, :], [K_("TR", c)], [K_("TRz", c)], eng="act")

                def P1(c):
                    B = CS[c]
                    A, ak = bankA(c)
                    Bk_, bk = bankB(c)
                    for hj in range(2):
                        TLh = B["TLz"][:, hj, :, :].rearrange("p a t -> p (a t)")
                        MM(A[0:64, 128 * hj:128 * hj + 128], B["TRz"][:, hj, 0, :], TLh, True, True, [K_("TRz", c), K_("TLz", c)], [ak])
                        MM(Bk_[0:64, 128 * hj:128 * hj + 128], B["TRz"][:, hj, 1, :], TLh, True, True, [K_("TRz", c), K_("TLz", c)], [bk])
                        MM(A[0:64, 256 + 64 * hj:256 + 64 * hj + 64], B["TLz"][:, hj, 0, :], B["TRz"][:, hj, 0, :], True, True,
                           [K_("TRz", c), K_("TLz", c)], [ak])

                def P2(c):
                    B = CS[c]
                    A, ak = bankA(c)
                    Bk_, bk = bankB(c)
                    pa3 = h2v(A[0:64, 0:256]); pb3 = h2v(Bk_[0:64, 0:256]); pc3 = h2v(A[0:64, 256:384])
                    pz = B["PZ"][0]
                    STT(pz[:, :, 1, :], pa3[:, :, 0:64], -1.0, b2(msu), MULT, MULT, [ak, "masks"], [K_("pz0_", c)])
                    STT(B["NRB"], pa3[:, :, 64:128], -1.0, b2(mu_), MULT, MULT, [ak, "masks"], [K_("NRB", c)])
                    STT(B["ZTa"], pc3, -1.0, b2(msl), MULT, MULT, [ak, "masks"], [K_("ZTa", c)])
                    TT(B["LKT"], pb3[:, :, 0:64], b2(msu), MULT, [bk, "masks"], [K_("LKT", c)])
                    TT(B["RKT"], pb3[:, :, 64:128], b2(mu_), MULT, [bk, "masks"], [K_("RKT", c)])
                    TT(pz[:, :, 0, :], pz[:, :, 1, :], ident[0:64, 0:64].unsqueeze(1).to_broadcast([64, 2, 64]), ADD,
                       [K_("pz0_", c), "ident"], [K_("pz0_", c)])

                def P3(c):
                    B = CS[c]
                    cs_ = slice(64 * c, 64 * c + 64)
                    A, ak = bankA(c)
                    TRP(A[0:64, 0:128], PV["v"][:, cs_], ["v"], [ak])
                    TRP(A[0:64, 128:256], B["TR"][:, 0, :], [K_("TR", c)], [ak])
                    TRP(A[0:64, 256:384], B["TR"][:, 1, :], [K_("TR", c)], [ak])
                    CP(B["Vtok"], A[0:64, 0:128], [ak], [K_("Vtok", c)], eng="act")
                    S.op("act", lambda e: e.mul(out=B["nbtok"], in_=A[0:64, 128:256], mul=-1.0), reads=[ak], writes=[K_("nbtok", c)])
                    CP(B["ktok"], A[0:64, 256:384], [ak], [K_("ktok", c)], eng="act")

                nstate = {}

                def NM(j):
                    def f(c):
                        B = CS[c]
                        cur, zt_cur = nstate.get(c, (0, "ZTa"))
                        pzc = B["PZ"][cur]
                        kc_ = K_(f"pz{cur}_", c)
                        A, ak = bankA(c)
                        Bk_, bk = bankB(c)
                        ztk = K_(zt_cur, c)
                        for hj in range(2):
                            if j == 0:
                                MM(A[0:64, 128 * hj + 64:128 * hj + 128], B[zt_cur][:, hj, :], pzc[:, hj, 1, :], True, True, [ztk, kc_], [ak])
                            elif j == 5:
                                MM(A[0:64, 128 * hj:128 * hj + 64], B[zt_cur][:, hj, :], pzc[:, hj, 0, :], True, False, [ztk, kc_], [ak])
                                MM(A[0:64, 128 * hj:128 * hj + 64], ident[0:64, 0:64], pzc[:, hj, 0, :], False, True, ["ident", kc_], [ak])
                            else:
                                MM(A[0:64, 128 * hj:128 * hj + 128], B[zt_cur][:, hj, :], pzc[:, hj, :, :].rearrange("p a t -> p (a t)"),
                                   True, False, [ztk, kc_], [ak])
                                MM(A[0:64, 128 * hj:128 * hj + 64], ident[0:64, 0:64], pzc[:, hj, 0, :], False, True, ["ident", kc_], [ak])
                            if j != 5:
                                MM(Bk_[0:64, 64 * hj:64 * hj + 64], pzc[:, hj, 1, :], B[zt_cur][:, hj, :], True, True, [ztk, kc_], [bk])
                    return f

                def NE(j):
                    def f(c):
                        B = CS[c]
                        cur, zt_cur = nstate.get(c, (0, "ZTa"))
                        zt_nxt = "ZTb" if zt_cur == "ZTa" else "ZTa"
                        pzc, pzn = B["PZ"][cur], B["PZ"][1 - cur]
                        kc_, kn_ = K_(f"pz{cur}_", c), K_(f"pz{1 - cur}_", c)
                        A, ak = bankA(c)
                        Bk_, bk = bankB(c)
                        q13 = h2v(A[0:64, 0:256])
                        if j == 0:
                            CP(pzn[:, :, 0, :], pzc[:, :, 0, :], [kc_], [kn_], eng="act")
                        else:
                            CP(pzn[:, :, 0, :], q13[:, :, 0:64], [ak], [kn_], eng="act")
                        if j != 5:
                            CP(pzn[:, :, 1, :], q13[:, :, 64:128], [ak], [kn_])
                            CP(B[zt_nxt], h2v(Bk_[0:64, 0:128]), [bk], [K_(zt_nxt, c)], eng="act")
                        nstate[c] = (1 - cur, zt_nxt)
                    return f

                for step in [P0, P1, P2, P3]:
                    for c in range(nch):
                        step(c)
                if pipelined and p + 1 < NHP:
                    psl_restrict[0] = True
                    part1(p + 1, (p + 1) % 2)
                    psl_restrict[0] = False
                for j in range(6):
                    for step in (NM(j), NE(j)):
                        for c in range(nch):
                            step(c)
                stage(11)
                if smp:
                    B = CS[0]
                    for s in range(NS):
                        S.dma(Stmp[:, :].rearrange("v (h k) -> v h k", h=2),
                              st_wkv[s, 2 * p:2 * p + 2, :, :].rearrange("h v k -> v h k"), writes=["Stmp"])
                        pq, pqk = PSL()
                        TRP(pq[:, 0:64], Stmp[:, :], ["Stmp"], [pqk])
                        for hj in range(2):
                            hs = slice(64 * hj, 64 * hj + 64)
                            CP(Msz[hs, hj, s, :], pq[hs, 0:64], [pqk], ["Ms"], eng="act")
                    TT(KKX, B["TL"][:, 0, :].unsqueeze(1).to_broadcast([128, 16, 64]), selm, MULT, ["TL0", "selm"], ["KKX"])
                    TT(RX, B["TL"][:, 1, :].unsqueeze(1).to_broadcast([128, 16, 64]), selm, MULT, ["TL0", "selm"], ["RX"])
                for c in range(nch):
                    B = CS[c]
                    cs_ = slice(64 * c, 64 * c + 64)
                    cur, _zt = nstate[c]
                    PTt, ptk = B["PZ"][cur], K_(f"pz{cur}_", c)
                    px, pxk = PSL()
                    px2, pxk2 = PSL()
                    py, pyk = PSL()
                    for hj in range(2):
                        hs = slice(64 * hj, 64 * hj + 64)
                        o_ = px[0:64, 64 * hj:64 * hj + 64]
                        oy = py[0:64, 64 * hj:64 * hj + 64]
                        if smp:
                            for s in range(NS):
                                MM(o_, KKX[:, s, :], Msz[:, hj, s, :], s == 0, s == NS - 1, ["KKX", "Ms"], [pxk])
                            for s in range(NS):
                                MM(oy, RX[:, s, :], Msz[:, hj, s, :], s == 0, s == NS - 1, ["RX", "Ms"], [pyk])
                        else:
                            MM(o_, B["TLz"][:, hj, 0, :], Mst[:, p, :], True, True, [K_("TLz", c), "state"], [pxk])
                            MM(oy, B["TLz"][:, hj, 1, :], Mst[:, p, :], True, True, [K_("TLz", c), "state"], [pyk])
                        MM(px2[0:64, 64 * hj:64 * hj + 64], B["LKT"][:, hj, :], B["Vtok"][:, hs], True, True,
                           [K_("LKT", c), K_("Vtok", c)], [pxk2])
                    CP(B["X"], h2v(px[0:64, 0:128]), [pxk], [K_("X", c)])
                    TT(B["X"], B["X"], h2v(px2[0:64, 0:128]), ADD, [K_("X", c), pxk2], [K_("X", c)])
                    CP(B["Ytok"], py[0:64, 0:128], [pyk], [K_("Ytok", c)], eng="act")
                    pu, puk = PSL()
                    for hj in range(2):
                        MM(pu[0:64, 64 * hj:64 * hj + 64], PTt[:, hj, 0, :], B["X"][:, hj, :], True, True, [ptk, K_("X", c)], [puk])
                    CP(B["U"], h2v(pu[0:64, 0:128]), [puk], [K_("U", c)])
                    stage(12)
                    if smp:
                        TT(NBX, B["nbtok"].unsqueeze(1).to_broadcast([64, 16, 128]),
                           rowsel[:].unsqueeze(2).to_broadcast([64, 16, 128]), MULT, ["nbtok0", "masks"], ["NBX"])
                        TT(KX, B["ktok"].unsqueeze(1).to_broadcast([64, 16, 128]),
                           rowsel[:].unsqueeze(2).to_broadcast([64, 16, 128]), MULT, ["ktok0", "masks"], ["KX"])
                        for s in range(NS):
                            pm, pmk = PSL()
                            for hj in range(2):
                                hs = slice(64 * hj, 64 * hj + 64)
                                o_ = pm[:, 64 * hj:64 * hj + 64]
                                MM(o_, NBX[:, s, :], B["U"][:, hj, :], True, False, ["NBX", "U0"], [pmk])
                                MM(o_, KX[:, s, :], B["Vtok"][:, hs], False, True, ["KX", "Vtok0"], [pmk])
                            for hj in range(2):
                                hs = slice(64 * hj, 64 * hj + 64)
                                TT(Msz[hs, hj, s, :], Msz[hs, hj, s, :], pm[hs, 64 * hj:64 * hj + 64], ADD, ["Ms", pmk], ["Ms"])
                            TSC(Msz[:, :, s, :], Msz[:, :, s, :], PV["eW"][:, 4 * s + 3:4 * s + 4], None, MULT, None, ["Ms", "eW"], ["Ms"])
                            TT(PV["t4"][:, 0:64], Msz[:, 0, s, :], Msz[:, 1, s, :], ADD, ["Ms"], ["t4"])
                            pq, pqk = PSL()
                            TRP(pq[0:64, 0:128], PV["t4"][:, 0:64], ["t4"], [pqk])
                            CP(Sout[:], pq[0:64, 0:128], [pqk], ["Sout"], eng="act")
                            S.dma(wkv_s[s, 2 * p:2 * p + 2, :, :].rearrange("h v k -> v h k"),
                                  Sout[:, :].rearrange("v (h k) -> v h k", h=2), reads=["Sout"])
                    else:
                        pm, pmk = PSL()
                        for hj in range(2):
                            hs = slice(64 * hj, 64 * hj + 64)
                            o_ = pm[:, 64 * hj:64 * hj + 64]
                            MM(o_, B["nbtok"], B["U"][:, hj, :], True, False, [K_("nbtok", c), K_("U", c)], [pmk])
                            MM(o_, B["ktok"], B["Vtok"][:, hs], False, True, [K_("ktok", c), K_("Vtok", c)], [pmk])
                        for hj in range(2):
                            hs = slice(64 * hj, 64 * hj + 64)
                            TT(Mst[hs, p, :], Mst[hs, p, :], pm[hs, 64 * hj:64 * hj + 64], ADD, ["state", pmk], ["state"])
                        TSC(Mst[:, p, :], Mst[:, p, :], PV["eW"][:, 64 * c + 63:64 * c + 64], None, MULT, None, ["state", "eW"], ["state"])
                    py2, pyk2 = PSL()
                    for hj in range(2):
                        hs = slice(64 * hj, 64 * hj + 64)
                        o2_ = py2[0:64, 64 * hj:64 * hj + 64]
                        MM(o2_, B["NRB"][:, hj, :], B["U"][:, hj, :], True, False, [K_("NRB", c), K_("U", c)], [pyk2])
                        MM(o2_, B["RKT"][:, hj, :], B["Vtok"][:, hs], False, True, [K_("RKT", c), K_("Vtok", c)], [pyk2])
                    TT(B["Ytok"], B["Ytok"], py2[0:64, 0:128], ADD, [K_("Ytok", c), pyk2], [K_("Ytok", c)])
                    pt_, ptk2 = PSL()
                    TRP(pt_[:, 0:64], B["Ytok"], [K_("Ytok", c)], [ptk2])
                    CP(PV["y"][:, cs_], pt_[:, 0:64], [ptk2], ["y"], eng="act")
                if last:
                    pq, pqk = PSL()
                    TRP(pq[0:64, 0:128], Mst[:, p, :], ["state"], [pqk])
                    CP(Sout[:], pq[0:64, 0:128], [pqk], ["Sout"], eng="act")
                    S.dma(wkv_p[2 * p:2 * p + 2, :, :].rearrange("h v k -> v h k"),
                          Sout[:, :].rearrange("v (h k) -> v h k", h=2), reads=["Sout"])
                stage(13)
                ps_, pk = PSL()
                MM(ps_[:, 0:T], bones[:], V["y"], True, True, ["bones", "y"], [pk])
                STT(V["t2"], ps_[:, 0:T], -1.0 / 64, V["y"], MULT, ADD, [pk, "y"], ["t2"])
                TT(V["t3"], V["t2"], V["t2"], MULT, ["t2"], ["t3"])
                ps_, pk = PSL()
                MM(ps_[:, 0:T], bones[:], V["t3"], True, True, ["bones", "t3"], [pk])
                RSQRT(V["t3"], ps_[:, 0:T], [pk], ["t3"], scale=1.0 / 64, eps=GN_EPS)
                TT(V["t2"], V["t2"], V["t3"], MULT, ["t2", "t3"], ["t2"])
                TSC(V["t2"], V["t2"], pt("lnx_g", p), pt("lnx_b", p), MULT, ADD, ["t2", "PT"], ["t2"])
                TT(V["t2"], V["t2"], V["bonus"], ADD, ["t2", "bonus"], ["t2"])
                TT(V["t2"], V["t2"], GG[alt][:, 0:T], MULT, ["t2", f"g{alt}"], ["t2"])
                TT(merged[:, p, 0:T], V["t2"], SGA[alt][:, 0:T], MULT, ["t2", f"sga{alt}"], [("mg", p)])

            stage(14)
            allzc = [("zcp", p) for p in range(16)]
            ps1, pk1 = PSL()
            for kc in range(16):
                MM(ps1[:, 0:T], ones_bf[:], zc[:, kc, 0:T], kc == 0, kc == 15, ["ones"] + allzc, [pk1])
            TSC(rstd[:, 0:T], ps1[:, 0:T], -1.0 / D, None, MULT, None, [pk1], ["rstd"])
            TT(dx[:, :, 0:T], zc[:, :, 0:T], zc[:, :, 0:T], MULT, allzc, ["dx"])
            ps1, pk1 = PSL()
            for kc in range(16):
                MM(ps1[:, 0:T], ones_bf[:], dx[:, kc, 0:T], kc == 0, kc == 15, ["ones", "dx"], [pk1])
            TT(rstd2[:, 0:T], rstd[:, 0:T], rstd[:, 0:T], MULT, ["rstd"], ["rstd2"])
            STT(rstd2[:, 0:T], ps1[:, 0:T], 1.0 / D, rstd2[:, 0:T], MULT, SUB, [pk1, "rstd2"], ["rstd2"])
            RSQRT(rstd2[:, 0:T], rstd2[:, 0:T], ["rstd2"], ["rstd2"], eps=1e-5)
            for p in range(NHP):
                wv, wk = WLOAD(kview(w_in)[:, :, 6 * D + 128 * p:6 * D + 128 * p + 128], (128, 16, 128))
                ps_, pk = PSL()
                for kc in range(16):
                    MM(ps_[:, 0:T], wv[:, kc, :], h[:, kc, 0:T], kc == 0, kc == 15, [wk, "h"], [pk])
                ACT(PV["gb"][:, 0:T], ps_[:, 0:T], AF.Sigmoid, [pk], ["gb"])
                t2 = PV["t2"][:, 0:T]
                TT(t2, zc[:, p, 0:T], rstd[:, 0:T], ADD, allzc + ["rstd"], ["t2"])
                TT(t2, t2, rstd2[:, 0:T], MULT, ["t2", "rstd2"], ["t2"])
                TSC(t2, t2, pt("ln_conv_g", p), pt("ln_conv_b", p), MULT, ADD, ["t2", "PT"], ["t2"])
                ACT(t2, t2, AF.Silu, ["t2"], ["t2"])
                TT(t2, t2, PV["gb"][:, 0:T], MULT, ["t2", "gb"], ["t2"])
                TT(merged[:, p, 0:T], merged[:, p, 0:T], t2, ADD, [("mg", p), "t2"], [("mg", p)])
            allmg = [("mg", p) for p in range(16)]
            stage(15)
            for m in range(16):
                wv, wk = WLOAD(kview(w_out)[:, :, 128 * m:128 * m + 128], (128, 16, 128))
                ps_, pk = PSL()
                for kc in range(16):
                    MM(ps_[:, 0:T], wv[:, kc, :], merged[:, kc, 0:T], kc == 0, kc == 15, [wk] + allmg, [pk])
                if smp:
                    t23 = PV["t2"][:, 0:T].rearrange("p (s t) -> p s t", t=4)
                    TT(t23, ps_[:, 0:T].rearrange("p (s t) -> p s t", t=4), TS[:, 2, m, :].unsqueeze(2).to_broadcast([128, 16, 4]),
                       MULT, [pk, "TS"], ["t2"])
                    TT(xT[:, m, 0:T], xT[:, m, 0:T], PV["t2"][:, 0:T], ADD, ["xT", "t2"], ["xT"])
                else:
                    STT(xT[:, m, 0:T], ps_[:, 0:T], TP[:, 2, m:m + 1], xT[:, m, 0:T], MULT, ADD, [pk, "TP", "xT"], ["xT"])
            stage(16)
            rms(xT, "xT", rstd)
            modnorm(xT, "xT", rstd, 3, 4)
            CP(h[:, :, 0:T], hf[:, :, 0:T], ["mixact"], ["h"], eng="act")
            act = mixbf[:, 0:NFC * TB].rearrange("p (f t) -> p f t", t=TB)
            for f in range(NFC):
                for half in range(2):
                    ch = f + NFC * half
                    wv, wk = WLOAD(kview(w_up)[:, :, 128 * ch:128 * ch + 128], (128, 16, 128))
                    ps_, pk = PSL()
                    for kc in range(16):
                        MM(ps_[:, 0:T], wv[:, kc, :], h[:, kc, 0:T], kc == 0, kc == 15, [wk, "h"], [pk])
                    k0_, k1_, k2_ = (pt("ffn_dw_k", 86 * j + ch) for j in range(3))
                    bb = pt("ffn_dw_b", ch)
                    o = PV["t2"] if half == 0 else PV["t3"]
                    okey = "t2" if half == 0 else "t3"
                    if smp:
                        ze = uext[:, 0:96].rearrange("p (s t) -> p s t", t=6)
                        S.dma(Stmp[0:32, 0:128], st_ffn[:, :, 128 * ch:128 * ch + 128].rearrange("s t c -> (s t) c"), writes=["Stmp"])
                        pq, pqk = PSL()
                        TRP(pq[:, 0:32], Stmp[0:32, 0:128], ["Stmp"], [pqk])
                        CP(ze[:, :, 0:2], pq[:, 0:32].rearrange("p (s t) -> p s t", t=2), [pqk], ["uext"])
                        CP(ze[:, :, 2:6], ps_[:, 0:64].rearrange("p (s t) -> p s t", t=4), [pk], ["uext"], eng="act")
                        o3 = o[:, 0:64].rearrange("p (s t) -> p s t", t=4)
                        TSC(o3, ze[:, :, 0:4], k0_, bb, MULT, ADD, ["uext", "PT"], [okey])
                        STT(o3, ze[:, :, 1:5], k1_, o3, MULT, ADD, ["uext", "PT", okey], [okey])
                        STT(o3, ze[:, :, 2:6], k2_, o3, MULT, ADD, ["uext", "PT", okey], [okey])
                        pq, pqk = PSL()
                        CP(PV["t4"][:, 0:64].rearrange("p (t s) -> p t s", t=4), ze[:, :, 2:6].rearrange("p s t -> p t s"), ["uext"], ["t4"])
                        TRP(pq[0:64, 0:128], PV["t4"][:, 0:64], ["t4"], [pqk])
                        CP(Ytok[:, :], pq[0:64, 0:128], [pqk], ["Ytok"])
                        for t in range(2):
                            S.dma(ffn_s[:, t, 128 * ch:128 * ch + 128], Ytok[32 + 16 * t:48 + 16 * t, :], reads=["Ytok"])
                    else:
                        CP(uext[:, 0:2], ffnhalo[:, ch, :], ["state"], ["uext"])
                        CP(uext[:, 2:2 + T], ps_[:, 0:T], [pk], ["uext"], eng="act")
                        CP(ffnhalo[:, ch, :], uext[:, T:T + 2], ["uext"], ["state"])
                        TSC(o[:, 0:T], uext[:, 0:T], k0_, bb, MULT, ADD, ["uext", "PT"], [okey])
                        STT(o[:, 0:T], uext[:, 1:1 + T], k1_, o[:, 0:T], MULT, ADD, ["uext", "PT", okey], [okey])
                        STT(o[:, 0:T], uext[:, 2:2 + T], k2_, o[:, 0:T], MULT, ADD, ["uext", "PT", okey], [okey])
                        if last:
                            pq, pqk = PSL()
                            TRP(pq[0:2, 0:128], uext[:, T:T + 2], ["uext"], [pqk])
                            CP(Ytok[0:2, :], pq[0:2, 0:128], [pqk], ["Ytok"])
                            S.dma(ffn_p[:, 128 * ch:128 * ch + 128], Ytok[0:2, :], reads=["Ytok"])
                ACT(PV["t2"][:, 0:T], PV["t2"][:, 0:T], AF.Silu, ["t2"], ["t2"])
                TT(act[:, f, 0:T], PV["t2"][:, 0:T], PV["t3"][:, 0:T], MULT, ["t2", "t3"], ["mixact"])
            for m in range(16):
                ps_, pk = PSL()
                for g0 in range(0, NFC, 16):
                    n = min(16, NFC - g0)
                    wv, wk = WLOAD(w_down[128 * g0:128 * (g0 + n), 128 * m:128 * m + 128].rearrange("(kc ki) n -> ki kc n", ki=128),
                                   (128, n, 128))
                    for j in range(n):
                        MM(ps_[:, 0:T], wv[:, j, :], act[:, g0 + j, 0:T], g0 + j == 0, g0 + j == NFC - 1, [wk, "mixact"], [pk])
                if smp:
                    t23 = PV["t2"][:, 0:T].rearrange("p (s t) -> p s t", t=4)
                    TT(t23, ps_[:, 0:T].rearrange("p (s t) -> p s t", t=4), TS[:, 5, m, :].unsqueeze(2).to_broadcast([128, 16, 4]),
                       MULT, [pk, "TS"], ["t2"])
                    TT(xT[:, m, 0:T], xT[:, m, 0:T], PV["t2"][:, 0:T], ADD, ["xT", "t2"], ["xT"])
                else:
                    STT(xT[:, m, 0:T], ps_[:, 0:T], TP[:, 5, m:m + 1], xT[:, m, 0:T], MULT, ADD, [pk, "TP", "xT"], ["xT"])
            stage(17)
            rms(xT, "xT", rstd)
            TT(hf[:, :, 0:T], xT[:, :, 0:T], rstd[:, 0:T].unsqueeze(1).to_broadcast([128, 16, T]), MULT, ["xT", "rstd"], ["mixact"])
            TT(hf[:, :, 0:T], hf[:, :, 0:T], bc(ptr("normf_g", 0, 16)), MULT, ["mixact", "PT"], ["mixact"])
            R = 64 if smp else 128
            for ti_ in range(max(1, T // 128)):
                for kc in range(16):
                    ps_, pk = PSL()
                    TRP(ps_[0:R, 0:128], hf[:, kc, 128 * ti_:128 * ti_ + R], ["mixact"], [pk])
                    CP(otile[0:R, 128 * kc:128 * kc + 128], ps_[0:R, 0:128], [pk], ["xtile"], eng="act" if kc % 2 else "dve")
                S.dma(ys if smp else yp[TB * bi + 128 * ti_:TB * bi + 128 * ti_ + 128, :], otile[0:R, :], reads=["xtile"])

        try:
            for bi in range(n_pblocks):
                emit_block("p", bi)
            if do_sample:
                S.dma(conv_s[:, 0:26, :], st_conv[:, 4:30, :])
                allset = [f"{n}{c}" for c in range(1, 4) for n in
                          ["TL", "TR", "TLz", "TRz", "Vtok", "nbtok", "ktok", "Ytok", "pz0_", "pz1_"] + CTN]
                S.op("pool", lambda e: e.memset(smpreg[:], 0.0), writes=allset + ["smpreg", "NBX", "KX", "KKX", "RX", "Ms", "selm", "sga1", "g1"])
                S.op("pool", lambda e: e.memset(selm, 1.0), writes=["selm"])
                selm4 = selm.rearrange("p s (a b) -> p s a b", b=4)
                S.op("pool", lambda e: e.affine_select(out=selm4, in_=selm4, pattern=[[1, 16], [-1, 16], [0, 4]], compare_op=ALU.is_equal,
                                                       fill=0.0, base=0, channel_multiplier=0), reads=["selm"], writes=["selm"])
                emit_block("s", 0)
        except StopEmit:
            pass
        S.finish()
        print("instructions:", S.ninstr, "sbuf left", nc.sbuf_bytes_remaining)
    return nc


_NC_CACHE = {}


def kernel(**inputs):
    f32 = lambda a: np.ascontiguousarray(np.asarray(a, dtype=np.float32))
    I = {k: f32(v) for k, v in inputs.items()}
    ncores = 8
    prm = np.zeros((NPROW, 128), np.float32)
    for name, cntc in _PSPEC:
        prm[POFF[name]:POFF[name] + cntc] = I[name].reshape(cntc, 128)
    if "nc" not in _NC_CACHE:
        _NC_CACHE["nc"] = build_nc()
    nc = _NC_CACHE["nc"]
    shared = {k: I[k] for k in ["w_ada", "w_in", "w1", "w2", "a1", "a2", "g1", "g2", "w_out", "w_up", "w_down"]}
    in_maps = []
    for c in range(ncores):
        m = dict(shared)
        m["params"] = prm
        if c < 4:
            m["xp"] = I["x_prompt"][c]
            cp = I["c_prompt"][c]
        else:
            m["xp"] = np.zeros((SEQ, D), np.float32)
            cp = np.zeros((D,), np.float32)
        sl = slice(NS * c, NS * c + NS)
        m["xs"] = I["x_sample"][sl].reshape(64, D)
        m["st_shift"] = I["state_shift"][sl]
        m["st_wkv"] = I["state_wkv"][sl]
        m["st_conv"] = I["state_conv"][sl]
        m["st_ffn"] = I["state_ffn"][sl]
        m["cvec"] = np.concatenate([cp[None], I["c_sample"][sl]], 0)
        in_maps.append({k: np.ascontiguousarray(v) for k, v in m.items()})
    res = run_bass_kernel_spmd(nc, in_maps, core_ids=list(range(ncores)))
    R = res.results
    y_p = np.stack([R[c]["yp"] for c in range(4)])
    shift_p = np.stack([R[c]["shift_p"].reshape(D) for c in range(4)])
    wkv_p = np.stack([R[c]["wkv_p"] for c in range(4)])
    conv_p = np.stack([R[c]["conv_p"] for c in range(4)])
    ffn_p = np.stack([R[c]["ffn_p"] for c in range(4)])
    y_s = np.concatenate([R[c]["ys"].reshape(NS, 4, D) for c in range(8)])
    shift_s = np.concatenate([R[c]["shift_s"] for c in range(8)])
    wkv_s = np.concatenate([R[c]["wkv_s"] for c in range(8)])
    conv_s = np.concatenate([R[c]["conv_s"] for c in range(8)])
    ffn_s = np.concatenate([R[c]["ffn_s"] for c in range(8)])
    return (y_p, y_s, shift_p, wkv_p, conv_p, ffn_p, shift_s, wkv_s, conv_s, ffn_s)
```

```python
import numpy as np
from contextlib import ExitStack
import concourse.bass as bass
import concourse.mybir as mybir
from concourse.bass_utils import run_bass_kernel_spmd

F32 = mybir.dt.float32
BF16 = mybir.dt.bfloat16
TB = 256
AF = mybir.ActivationFunctionType
ALU = mybir.AluOpType

D = 2048
NK = 16
SEQ = 2048
NS = 16
DFF = 5504
NFC = 43
F2 = 2 * DFF
NHP = 16
C0 = float(np.exp(-0.5))
GN_EPS = 64 * 1e-5

_PSPEC = [("norm1_g", 16), ("norm2_g", 16), ("normf_g", 16), ("b_ada", 96), ("mu", 96), ("b_glu", 32),
          ("w0", 16), ("a0", 16), ("k_k", 16), ("k_a", 16), ("r_k", 16), ("lnx_g", 16), ("lnx_b", 16),
          ("dw_b", 16), ("ln_conv_g", 16), ("ln_conv_b", 16), ("dw_k", 496), ("ffn_dw_k", 258), ("ffn_dw_b", 86)]
POFF = {}
_o = 0
for _n, _c in _PSPEC:
    POFF[_n] = _o
    _o += _c
NPROW = 1280


STOP = [None]


class StopEmit(Exception):
    pass


def stage(n):
    if STOP[0] is not None and STOP[0] == n:
        raise StopEmit()


class Sched:
    EPOCH = 20000

    def __init__(self, nc, stack, n_dma_sems=16):
        self.nc = nc
        self.stack = stack
        self.engs = {}
        for name, h in (("pe", nc.tensor), ("act", nc.scalar), ("dve", nc.vector),
                        ("pool", nc.gpsimd), ("sp", nc.sync)):
            self.engs[name] = dict(h=h, sems=[], count=0, seen={})
        self.dma_sems = [stack.enter_context(nc.semaphore(f"dq{i}")) for i in range(n_dma_sems)]
        self.dma_uses = [0] * n_dma_sems
        self.dma_next = 0
        self.dma_next_pool = 0
        self.last_w = {}
        self.readers = {}
        self.ninstr = 0

    def _sem_for(self, ename, idx):
        e = self.engs[ename]
        ep = idx // self.EPOCH
        while len(e["sems"]) <= ep:
            e["sems"].append(self.stack.enter_context(self.nc.semaphore(f"s_{ename}{len(e['sems'])}")))
        return e["sems"][ep], idx % self.EPOCH + 1

    def _wait(self, ename, ev):
        e = self.engs[ename]
        if ev[0] == "eng":
            _, src, idx = ev
            if src == ename and ename == "pe":
                return
            if e["seen"].get(src, -1) >= idx:
                return
            sem, val = self._sem_for(src, idx)
            e["h"].wait_ge(sem, val)
            e["seen"][src] = idx
        else:
            _, si, val = ev
            key = ("dma", si)
            if e["seen"].get(key, 0) >= val:
                return
            e["h"].wait_ge(self.dma_sems[si], val)
            e["seen"][key] = val

    def _deps(self, ename, reads, writes):
        evs = []
        for k in reads:
            if k in self.last_w:
                evs.append(self.last_w[k])
        for k in writes:
            if k in self.last_w:
                evs.append(self.last_w[k])
            for r in self.readers.get(k, ()):
                evs.append(r)
        for ev in evs:
            self._wait(ename, ev)

    def _record(self, ev, reads, writes):
        for k in reads:
            self.readers.setdefault(k, []).append(ev)
        for k in writes:
            self.last_w[k] = ev
            self.readers[k] = []

    def op(self, ename, fn, reads=(), writes=()):
        e = self.engs[ename]
        self._deps(ename, reads, writes)
        idx = e["count"]
        sem, val = self._sem_for(ename, idx)
        ins = fn(e["h"])
        ins.then_inc(sem, 1)
        e["count"] += 1
        self.ninstr += 1
        self._record(("eng", ename, idx), reads, writes)

    def dma(self, out, in_, reads=(), writes=(), q="sp", **kw):
        qname = q
        e = self.engs[qname]
        self._deps(qname, reads, writes)
        half = len(self.dma_sems) // 2
        if qname == "pool":
            si = half + self.dma_next_pool % half
            self.dma_next_pool += 1
        else:
            si = self.dma_next % half
            self.dma_next += 1
        prev = self.dma_uses[si]
        if prev > 0:
            self._wait(qname, ("dma", si, 16 * prev))
        self.dma_uses[si] = prev + 1
        e["h"].dma_start(out=out, in_=in_, **kw).then_inc(self.dma_sems[si], 16)
        self.ninstr += 1
        ev = ("dma", si, 16 * (prev + 1))
        self._record(ev, reads, writes)

    def finish(self):
        for si, uses in enumerate(self.dma_uses):
            if uses:
                self._wait("sp", ("dma", si, 16 * uses))
        for name, e in self.engs.items():
            if name != "sp" and e["count"]:
                self._wait("sp", ("eng", name, e["count"] - 1))


def build_nc(n_pblocks=SEQ // TB, do_sample=True):
    nc = bass.Bass("TRN2", target_bir_lowering=False)
    din = lambda n, s: nc.dram_tensor(n, s, F32, kind="ExternalInput").ap()
    dout = lambda n, s: nc.dram_tensor(n, s, F32, kind="ExternalOutput").ap()
    xp = din("xp", [SEQ, D]); xs = din("xs", [64, D]); st_shift = din("st_shift", [NS, D])
    st_wkv = din("st_wkv", [NS, 32, 64, 64]); st_conv = din("st_conv", [NS, 30, D]); st_ffn = din("st_ffn", [NS, 2, F2])
    cvec = din("cvec", [17, D]); params = din("params", [NPROW, 128])
    w_ada = din("w_ada", [D, 6 * D]); w_in = din("w_in", [D, 7 * D])
    w1 = din("w1", [D, 96]); w2 = din("w2", [96, D]); a1 = din("a1", [D, 96]); a2 = din("a2", [96, D])
    g1 = din("g1", [D, 256]); g2 = din("g2", [256, D]); w_out = din("w_out", [D, D])
    w_up = din("w_up", [D, F2]); w_down = din("w_down", [DFF, D])
    yp = dout("yp", [SEQ, D]); ys = dout("ys", [64, D]); shift_p = dout("shift_p", [16, 128])
    wkv_p = dout("wkv_p", [32, 64, 64]); conv_p = dout("conv_p", [30, D]); ffn_p = dout("ffn_p", [2, F2])
    shift_s = dout("shift_s", [NS, D]); wkv_s = dout("wkv_s", [NS, 32, 64, 64])
    conv_s = dout("conv_s", [NS, 30, D]); ffn_s = dout("ffn_s", [NS, 2, F2])

    kview = lambda W: W.rearrange("(kc ki) n -> ki kc n", ki=128)

    with ExitStack() as st:
        cnt = [0]

        def sb(shape, name=None, dt=F32):
            cnt[0] += 1
            return st.enter_context(nc.sbuf_tensor(name or f"t{cnt[0]}", shape, dt))

        ident = sb([128, 128], "ident"); ones = sb([128, TB], "ones"); bones = sb([128, 128], "bones")
        ones_bf = sb([128, 128], "ones_bf", BF16)
        m_su = sb([64, 64]); m_u = sb([64, 64]); m_sl = sb([64, 64]); blk = sb([64, 64])
        ms_su = sb([64, 64]); ms_u = sb([64, 64]); ms_sl = sb([64, 64])
        rowsel = sb([64, 16])
        PT = sb([128, NPROW], "PT")
        TP = sb([128, 6, 16], "TP"); TS = sb([128, 6, 16, 16], "TS")
        omu = sb([128, 96]); omka = sb([128, 16])
        shiftst = sb([128, 16]); Mst = sb([128, 16, 64]); convhalo = sb([128, 16, 30], None, BF16); ffnhalo = sb([128, 86, 2])
        xtile = sb([128, D], "xtile"); xT = sb([128, 16, TB], "xT"); h = sb([128, 16, TB], "h", BF16)
        dx = sb([128, 16, TB], "dx", BF16); mixact = sb([128, 6144], "mixact")
        mixbf = mixact[:, :].bitcast(BF16)
        hf = mixact[:, 0:16 * TB].rearrange("p (k t) -> p k t", t=TB)
        MT = mixact[:, 0:96 * 17].rearrange("p (m c) -> p m c", c=17)
        zc = sb([128, 16, TB], "zc", BF16); merged = sb([128, 16, TB], "merged", BF16)
        rstd = sb([128, TB]); rstd2 = sb([128, TB])
        ring = [sb([128, 16, 128], f"ring{i}", BF16) for i in range(4)]
        PV = {n: sb([128, TB], "pv_" + n) for n in
              ["r", "k0", "v", "lw", "a", "g", "sga", "glu", "glb", "kk", "kkn", "t1", "k", "b", "rk", "bonus",
               "cs", "cx", "eW", "eWi", "eWp", "y", "t2", "t3", "t4", "gb"]}
        for n in ["tw", "xa1", "sg0", "sg1"]:
            PV[n] = sb([128, TB], "pv_" + n, BF16)
        zext = sb([128, 32 + TB], "zext2", BF16); uext = sb([128, 8 + TB], "uext"); zsbf = sb([128, 16, 34], "zsbf", BF16)
        smpreg = sb([128, 9216], "smpreg")
        CTN = ["NRB", "LKT", "RKT", "X", "U", "ZTa", "ZTb"]

        def mkset(c):
            B = {}
            if c == 0:
                B["TL"] = sb([128, 2, 64])[:]; B["TR"] = sb([128, 2, 64])[:]
                B["TLz"] = sb([128, 2, 2, 64])[:]; B["TRz"] = sb([128, 2, 2, 64])[:]
                for n in CTN:
                    B[n] = sb([64, 2, 64], "ct_" + n)[:]
                B["PZ"] = [sb([64, 2, 2, 64], f"pz{i}")[:] for i in range(2)]
                for n in ["Vtok", "nbtok", "ktok", "Ytok"]:
                    B[n] = sb([64, 128])[:]
            else:
                o = [2688 * (c - 1)]

                def take(nparts, words):
                    v = smpreg[0:nparts, o[0]:o[0] + words]
                    o[0] += words
                    return v
                B["TL"] = take(128, 128).rearrange("p (a t) -> p a t", a=2)
                B["TR"] = take(128, 128).rearrange("p (a t) -> p a t", a=2)
                B["TLz"] = take(128, 256).rearrange("p (h a t) -> p h a t", h=2, a=2)
                B["TRz"] = take(128, 256).rearrange("p (h a t) -> p h a t", h=2, a=2)
                for n in CTN:
                    B[n] = take(64, 128).rearrange("p (h t) -> p h t", h=2)
                B["PZ"] = [take(64, 256).rearrange("p (h a t) -> p h a t", h=2, a=2) for i in range(2)]
                for n in ["Vtok", "nbtok", "ktok", "Ytok"]:
                    B[n] = take(64, 128)
            return B

        CS = [mkset(c) for c in range(4)]
        Ytok = sb([64, 128], "Ytok_st")
        NBX = smpreg[0:64, 0:2048].rearrange("p (s c) -> p s c", s=16)
        KX = smpreg[0:64, 2048:4096].rearrange("p (s c) -> p s c", s=16)
        KKX = smpreg[:, 4096:5120].rearrange("p (s c) -> p s c", s=16)
        RX = smpreg[:, 5120:6144].rearrange("p (s c) -> p s c", s=16)
        Msz = smpreg[:, 6144:8192].rearrange("p (h s c) -> p h s c", h=2, s=16)
        selm = smpreg[:, 8192:9216].rearrange("p (s c) -> p s c", s=16)
        Stmp = sb([64, 128]); Sout = sb([64, 128])
        SGA = [PV["sga"], smpreg[:, 8064:8064 + TB]]
        GG = [PV["g"], smpreg[:, 8064 + TB:8064 + 2 * TB]]
        otile = xtile
        cvt = xtile[0:17, :]; cT = sb([128, 16, 17], None, BF16); ptile = sb([128, 128])
        diag = sb([128, 31, 128], "diag", BF16)
        pss = [st.enter_context(nc.psum_tensor(f"ps{i}", [128, 512], F32)) for i in range(8)]
        block = st.enter_context(nc.Block())
        S = Sched(nc, st)

        psi = [0]

        psl_restrict = [False]

        def PSL():
            if psl_restrict[0]:
                i = 6 + psi[0] % 2
            else:
                i = psi[0] % 8
            psi[0] += 1
            return pss[i][:, :], ("ps", i)

        def MM(out, lhsT, rhs, start, stop, r, w):
            S.op("pe", lambda e: e.matmul(out, lhsT=lhsT, rhs=rhs, start=start, stop=stop), reads=r, writes=w)

        def TRP(out, in_, r, w):
            k = in_.shape[0]
            S.op("pe", lambda e: e.transpose(out=out, in_=in_, identity=ident[0:k, 0:k]), reads=list(r) + ["ident"], writes=w)

        def TT(out, a, b, op, r, w, eng="dve"):
            S.op(eng, lambda e: e.tensor_tensor(out=out, in0=a, in1=b, op=op), reads=r, writes=w)

        def TSC(out, a, s1, s2, op0, op1, r, w, eng="dve"):
            if s2 is None:
                S.op(eng, lambda e: e.tensor_scalar(out=out, in0=a, scalar1=s1, scalar2=None, op0=op0), reads=r, writes=w)
            else:
                S.op(eng, lambda e: e.tensor_scalar(out=out, in0=a, scalar1=s1, scalar2=s2, op0=op0, op1=op1), reads=r, writes=w)

        def STT(out, a, sc, b, op0, op1, r, w):
            S.op("dve", lambda e: e.scalar_tensor_tensor(out=out, in0=a, scalar=sc, in1=b, op0=op0, op1=op1), reads=r, writes=w)

        def CP(out, a, r, w, eng="dve"):
            if eng == "act":
                S.op("act", lambda e: e.copy(out=out, in_=a), reads=r, writes=w)
            else:
                S.op(eng, lambda e: e.tensor_copy(out=out, in_=a), reads=r, writes=w)

        def ACT(out, a, func, r, w, bias=0.0, scale=1.0):
            S.op("act", lambda e: e.activation(out=out, in_=a, func=func, bias=bias, scale=scale), reads=r, writes=w)

        def RSQRT(out, a, r, w, scale=1.0, eps=0.0):
            ACT(out, a, AF.Sqrt, r, w, bias=eps, scale=scale)
            S.op("dve", lambda e: e.reciprocal(out=out, in_=out), reads=w, writes=w)

        ringi = [0]

        def WLOAD(src, shape3):
            i = ringi[0] % 4
            ringi[0] += 1
            kp, nk, n = shape3
            dst = ring[i][0:kp, 0:nk, 0:n]
            S.dma(dst, src, writes=[("ring", i)], q="pool")
            return dst, ("ring", i)

        MULT, ADD, SUB = ALU.mult, ALU.add, ALU.subtract
        pt = lambda name, c: PT[:, POFF[name] + c:POFF[name] + c + 1]
        ptr = lambda name, c0, n: PT[:, POFF[name] + c0:POFF[name] + c0 + n]

        S.op("pool", lambda e: e.memset(ident[:], 0.0), writes=["ident"])
        S.op("pool", lambda e: e.affine_select(out=ident[:], in_=ident[:], pattern=[[-1, 128]], compare_op=ALU.not_equal,
                                               fill=1.0, base=0, channel_multiplier=1), reads=["ident"], writes=["ident"])
        S.op("pool", lambda e: e.memset(ones[:], 1.0), writes=["ones"])
        S.op("pool", lambda e: e.memset(ones_bf[:], 1.0), writes=["ones"])
        S.op("pool", lambda e: e.memset(bones[:], 0.0), writes=["bones"])
        S.op("pool", lambda e: e.memset(bones[0:64, 0:64], 1.0), reads=["bones"], writes=["bones"])
        S.op("pool", lambda e: e.memset(bones[64:128, 64:128], 1.0), reads=["bones"], writes=["bones"])
        for m, cm, pat, op in ((m_su, -1, 1, ALU.is_gt), (m_u, -1, 1, ALU.is_ge), (m_sl, 1, -1, ALU.is_gt)):
            S.op("pool", lambda e: e.memset(m[:], 1.0), writes=["masks"])
            S.op("pool", lambda e: e.affine_select(out=m[:], in_=m[:], pattern=[[pat, 64]], compare_op=op, fill=0.0,
                                                   base=0, channel_multiplier=cm), reads=["masks"], writes=["masks"])
        S.op("pool", lambda e: e.memset(blk[:], 1.0), writes=["masks"])
        blk3 = blk[:].rearrange("p (a b) -> p a b", b=4)
        S.op("pool", lambda e: e.affine_select(out=blk3, in_=blk3, pattern=[[-4, 16], [0, 4]], compare_op=ALU.is_ge, fill=0.0,
                                               base=0, channel_multiplier=1), reads=["masks"], writes=["masks"])
        S.op("pool", lambda e: e.affine_select(out=blk3, in_=blk3, pattern=[[4, 16], [0, 4]], compare_op=ALU.is_ge, fill=0.0,
                                               base=3, channel_multiplier=-1), reads=["masks"], writes=["masks"])
        for ms, m in ((ms_su, m_su), (ms_u, m_u), (ms_sl, m_sl)):
            TT(ms[:], m[:], blk[:], MULT, ["masks"], ["masks"], eng="pool")
        S.op("pool", lambda e: e.memset(rowsel[:], 1.0), writes=["masks"])
        S.op("pool", lambda e: e.affine_select(out=rowsel[:], in_=rowsel[:], pattern=[[-4, 16]], compare_op=ALU.is_ge, fill=0.0,
                                               base=0, channel_multiplier=1), reads=["masks"], writes=["masks"])
        S.op("pool", lambda e: e.affine_select(out=rowsel[:], in_=rowsel[:], pattern=[[4, 16]], compare_op=ALU.is_ge, fill=0.0,
                                               base=3, channel_multiplier=-1), reads=["masks"], writes=["masks"])
        for t_ in (shiftst, Mst, convhalo, ffnhalo):
            S.op("pool", lambda e: e.memset(t_[:], 0.0), writes=["state"])
        S.op("pool", lambda e: e.memset(smpreg[:], 0.0), writes=["smpreg"])
        for c in range(4):
            S.op("pool", lambda e: e.memset(CS[c]["TLz"], 0.0), reads=["smpreg"], writes=[f"TLz{c}"])
            S.op("pool", lambda e: e.memset(CS[c]["TRz"], 0.0), reads=["smpreg"], writes=[f"TRz{c}"])

        for i in range(NPROW // 128):
            S.dma(ptile[:], params[128 * i:128 * i + 128, :], writes=["ptile"])
            ps_, pk = PSL()
            TRP(ps_[:, 0:128], ptile[:], ["ptile"], [pk])
            CP(PT[:, 128 * i:128 * i + 128], ps_[:, 0:128], [pk], ["PT"])
        TSC(omu[:], ptr("mu", 0, 96), -1.0, 1.0, MULT, ADD, ["PT"], ["PT2"])
        TSC(omka[:], ptr("k_a", 0, 16), -1.0, 1.0, MULT, ADD, ["PT"], ["PT2"])

        S.dma(cvt, cvec, writes=["xtile"])
        ACT(cvt, cvt, AF.Silu, ["xtile"], ["xtile"])
        for kc in range(16):
            ps_, pk = PSL()
            TRP(ps_[:, 0:17], cvt[:, 128 * kc:128 * kc + 128], ["xtile"], [pk])
            CP(cT[:, kc, :], ps_[:, 0:17], [pk], ["cT"])
        for m in range(96):
            wv, wk = WLOAD(kview(w_ada)[:, :, 128 * m:128 * m + 128], (128, 16, 128))
            ps_, pk = PSL()
            for kc in range(16):
                MM(ps_[:, 0:17], wv[:, kc, :], cT[:, kc, :], kc == 0, kc == 15, [wk, "cT"], [pk])
            TSC(MT[:, m, :], ps_[:, 0:17], pt("b_ada", m), None, ADD, None, [pk, "PT"], ["mixact"])
        for (ti, mi, kind) in ((0, 1, "gs1"), (1, 0, "sh"), (2, 2, "gt"), (3, 4, "gs2"), (4, 3, "sh"), (5, 5, "gt")):
            src_p = MT[:, 16 * mi:16 * mi + 16, 0]
            src_s = MT[:, 16 * mi:16 * mi + 16, 1:17]
            if kind.startswith("gs"):
                g = ptr("norm1_g" if kind == "gs1" else "norm2_g", 0, 16)
                STT(TP[:, ti, :], src_p, 1.0, g, ADD, MULT, ["mixact", "PT"], ["TP"])
                STT(TS[:, ti, :, :], src_s, 1.0, g.unsqueeze(2).to_broadcast([128, 16, 16]), ADD, MULT, ["mixact", "PT"], ["TS"])
            else:
                CP(TP[:, ti, :], src_p, ["mixact"], ["TP"])
                CP(TS[:, ti, :, :], src_s, ["mixact"], ["TS"])

        def emit_block(kind, bi):
            smp = kind == "s"
            T = 64 if smp else TB
            nch = T // 64
            bc = lambda tab16: tab16.unsqueeze(2).to_broadcast([128, 16, T])
            v4 = lambda ap: ap.rearrange("p k (s t) -> p k s t", t=4)
            last = (not smp) and bi == n_pblocks - 1

            xsrc_rows = 64 if smp else 128
            for ti_ in range(max(1, T // 128)):
                xsrc = xs if smp else xp[TB * bi + 128 * ti_:TB * bi + 128 * ti_ + 128, :]
                S.dma(xtile[0:xsrc_rows, :], xsrc, writes=["xtile"])
                R = xsrc_rows
                for q in range(4):
                    ps_, pk0 = PSL()
                    for j in range(4):
                        kc = 4 * q + j
                        TRP(ps_[:, j * R:(j + 1) * R], xtile[0:R, 128 * kc:128 * kc + 128], ["xtile"], [pk0])
                    CP(xT[:, 4 * q:4 * q + 4, 128 * ti_:128 * ti_ + R], ps_[:, 0:4 * R].rearrange("p (a t) -> p a t", a=4),
                       [pk0], ["xT"], eng="act")
            stage(1)

            def rms(src, key, out_rstd):
                TT(dx[:, :, 0:T], src[:, :, 0:T], src[:, :, 0:T], MULT, [key], ["dx"])
                ps_, pk = PSL()
                for kc in range(16):
                    MM(ps_[:, 0:T], ones_bf[:], dx[:, kc, 0:T], kc == 0, kc == 15, ["ones", "dx"], [pk])
                RSQRT(out_rstd[:, 0:T], ps_[:, 0:T], [pk], ["rstd"], scale=1.0 / D, eps=1e-6)

            def modnorm(src, skey, rs, tg, tsft):
                d = hf[:, :, 0:T]
                TT(d, src[:, :, 0:T], rs[:, 0:T].unsqueeze(1).to_broadcast([128, 16, T]), MULT, [skey, "rstd"], ["mixact"])
                if smp:
                    for (tix, op) in ((tg, MULT), (tsft, ADD)):
                        TT(v4(d), v4(d), TS[:, tix, :, :].unsqueeze(3).to_broadcast([128, 16, 16, 4]), op, ["mixact", "TS"], ["mixact"])
                else:
                    TT(d, d, bc(TP[:, tg, :]), MULT, ["mixact", "TP"], ["mixact"])
                    TT(d, d, bc(TP[:, tsft, :]), ADD, ["mixact", "TP"], ["mixact"])

            rms(xT, "xT", rstd)
            modnorm(xT, "xT", rstd, 0, 1)
            CP(h[:, :, 0:T], hf[:, :, 0:T], ["mixact"], ["h"], eng="act")
            stage(2)
            if smp:
                S.dma(otile[0:16, :], st_shift, writes=["xtile"])
                h4 = v4(hf[:, :, 0:64])
                d4 = v4(dx[:, :, 0:64])
                for q in range(4):
                    ps_, pk = PSL()
                    for j in range(4):
                        TRP(ps_[:, 16 * j:16 * j + 16], otile[0:16, 128 * (4 * q + j):128 * (4 * q + j) + 128], ["xtile"], [pk])
                    TT(d4[:, 4 * q:4 * q + 4, :, 0], ps_[:, 0:64].rearrange("p (a s) -> p a s", a=4), h4[:, 4 * q:4 * q + 4, :, 0],
                       SUB, [pk, "mixact"], ["dx"])
                TT(d4[:, :, :, 1:4], h4[:, :, :, 0:3], h4[:, :, :, 1:4], SUB, ["mixact"], ["dx"])
                for kc in range(16):
                    ps_, pk = PSL()
                    TRP(ps_[0:16, 0:128], h4[:, kc, :, 3], ["mixact"], [pk])
                    CP(otile[0:16, 128 * kc:128 * kc + 128], ps_[0:16, 0:128], [pk], ["xtile"])
                S.dma(shift_s, otile[0:16, :], reads=["xtile"])
            else:
                TT(dx[:, :, 0], shiftst[:], hf[:, :, 0], SUB, ["state", "mixact"], ["dx"])
                TT(dx[:, :, 1:T], hf[:, :, 0:T - 1], hf[:, :, 1:T], SUB, ["mixact"], ["dx"])
                CP(shiftst[:], hf[:, :, T - 1], ["mixact"], ["state"])
                if last:
                    ps_, pk = PSL()
                    TRP(ps_[0:16, 0:128], shiftst[:], ["state"], [pk])
                    CP(otile[0:16, 0:128], ps_[0:16, 0:128], [pk], ["xtile"])
                    S.dma(shift_p, otile[0:16, 0:128], reads=["xtile"])
            stage(3)

            def mix(dst, dkey, mi):
                TT(dst, dx[:, :, 0:T], bc(ptr("mu", 16 * mi, 16)), MULT, ["dx", "PT"], [dkey])
                TT(dst, dst, h[:, :, 0:T], ADD, [dkey, "h"], [dkey])

            xmix = [mixbf[:, 16 * TB * i:16 * TB * (i + 1)].rearrange("p (k t) -> p k t", t=TB)[:, :, 0:T] for i in range(3)]
            tmpx = zc[:, :, 0:T]
            for (W, ncol, mi, outs, func) in ((w1, 96, 1, [PV["tw"]], AF.Tanh), (a1, 96, 4, [PV["xa1"]], AF.Copy),
                                               (g1, 256, 5, [PV["sg0"], PV["sg1"]], AF.Sigmoid)):
                mix(tmpx, "zc", mi)
                for oi, o in enumerate(outs):
                    n = min(128, ncol - 128 * oi)
                    wv, wk = WLOAD(kview(W)[:, :, 128 * oi:128 * oi + n], (128, 16, n))
                    ps_, pk = PSL()
                    for kc in range(16):
                        MM(ps_[0:n, 0:T], wv[:, kc, :], tmpx[:, kc, :], kc == 0, kc == 15, [wk, "zc"], [pk])
                    ACT(o[0:n, 0:T], ps_[0:n, 0:T], func, [pk], ["lora"])
            mix(xmix[0], "mixact", 0)
            mix(xmix[1], "mixact", 2)
            mix(xmix[2], "mixact", 3)
            stage(4)

            def part1(p, alt):
                def proj(src, skey, col0, out, func, okey, bias=0.0):
                    wv, wk = WLOAD(kview(w_in)[:, :, col0 + 128 * p:col0 + 128 * p + 128], (128, 16, 128))
                    ps_, pk = PSL()
                    for kc in range(16):
                        MM(ps_[:, 0:T], wv[:, kc, :], src[:, kc, :], kc == 0, kc == 15, [wk, skey], [pk])
                    ACT(out[:, 0:T], ps_[:, 0:T], func, [pk, "PT"], [okey], bias=bias)

                hh = h[:, :, 0:T]
                proj(xmix[0], "mixact", 0, PV["r"], AF.Copy, "r")
                proj(xmix[1], "mixact", D, PV["k0"], AF.Copy, "k0")
                proj(xmix[2], "mixact", 2 * D, PV["v"], AF.Copy, "v")
                proj(hh, "h", 3 * D, PV["glu"], AF.Identity, "glu", bias=pt("b_glu", p))
                proj(hh, "h", 4 * D, PV["glb"], AF.Sigmoid, "glb", bias=pt("b_glu", 16 + p))
                proj(hh, "h", 5 * D, SGA[alt], AF.Sigmoid, f"sga{alt}")
                for (W, K, srcs, out, func, bias, okey) in ((w2, 96, [PV["tw"]], PV["lw"], AF.Sigmoid, pt("w0", p), "lw"),
                                                             (a2, 96, [PV["xa1"]], PV["a"], AF.Sigmoid, pt("a0", p), "a"),
                                                             (g2, 256, [PV["sg0"], PV["sg1"]], GG[alt], AF.Copy, 0.0, f"g{alt}")):
                    nk = len(srcs)
                    kp = min(K, 128)
                    wv, wk = WLOAD(W.rearrange("(kc ki) n -> ki kc n", ki=kp)[:, :, 128 * p:128 * p + 128], (kp, nk, 128))
                    ps_, pk = PSL()
                    for j in range(nk):
                        MM(ps_[:, 0:T], wv[:, j, :], srcs[j][0:kp, 0:T], j == 0, j == nk - 1, [wk, "lora"], [pk])
                    ACT(out[:, 0:T], ps_[:, 0:T], func, [pk, "PT"], [okey], bias=bias)

            pipelined = not smp
            if pipelined:
                part1(0, 0)
            for p in range(NHP):
                alt = (p % 2) if pipelined else 0
                if not pipelined:
                    part1(p, 0)
                stage(5)
                V = {n: PV[n][:, 0:T] for n in PV}
                TT(V["glu"], V["glu"], V["glb"], MULT, ["glu", "glb"], ["glu"])
                for j in range(31):
                    TSC(diag[:, j, :], ident[:], pt("dw_k", 16 * j + p), None, MULT, None, ["ident", "PT"], ["diag"])
                ps_, pk = PSL()
                if smp:
                    S.dma(otile[0:120, 0:512].rearrange("r (q c) -> r q c", q=4),
                          st_conv[:, :, 128 * p:128 * p + 128].rearrange("(q s) t c -> (s t) q c", q=4), writes=["xtile"])
                    for q in range(4):
                        pq, pqk = PSL()
                        TRP(pq[:, 0:120], otile[0:120, 128 * q:128 * q + 128], ["xtile"], [pqk])
                        CP(zsbf[:, 4 * q:4 * q + 4, 0:30], pq[:, 0:120].rearrange("p (s t) -> p s t", t=30), [pqk], ["zsbf"])
                    CP(zsbf[:, :, 30:34], V["glu"].rearrange("p (s t) -> p s t", t=4), ["glu"], ["zsbf"])
                    for j in range(31):
                        MM(ps_[:, 0:64], diag[:, j, :], zsbf[:, :, j:j + 4], j == 0, j == 30, ["diag", "zsbf"], [pk])
                    pq, pqk = PSL()
                    CP(PV["t4"][:, 0:64].rearrange("p (t s) -> p t s", t=4), V["glu"].rearrange("p (s t) -> p t s", t=4), ["glu"], ["t4"])
                    TRP(pq[0:64, 0:128], PV["t4"][:, 0:64], ["t4"], [pqk])
                    CP(Ytok[:, :], pq[0:64, 0:128], [pqk], ["Ytok"])
                    for t in range(4):
                        S.dma(conv_s[:, 26 + t, 128 * p:128 * p + 128], Ytok[16 * t:16 * t + 16, :], reads=["Ytok"])
                else:
                    CP(zext[:, 0:30], convhalo[:, p, :], ["state"], ["zext"])
                    CP(zext[:, 30:30 + T], V["glu"], ["glu"], ["zext"], eng="act")
                    for j in range(31):
                        MM(ps_[:, 0:T], diag[:, j, :], zext[:, j:j + T], j == 0, j == 30, ["diag", "zext"], [pk])
                    CP(convhalo[:, p, :], zext[:, T:T + 30], ["zext"], ["state"])
                    if last:
                        pq, pqk = PSL()
                        TRP(pq[0:30, 0:128], PV["glu"][:, T - 30:T], ["glu"], [pqk])
                        CP(Ytok[0:30, :], pq[0:30, 0:128], [pqk], ["Ytok"])
                        S.dma(conv_p[:, 128 * p:128 * p + 128], Ytok[0:30, :], reads=["Ytok"])
                TSC(zc[:, p, 0:T], ps_[:, 0:T], pt("dw_b", p), None, ADD, None, [pk, "PT"], [("zcp", p)])

                stage(6)
                TSC(V["kk"], V["k0"], pt("k_k", p), None, MULT, None, ["k0", "PT"], ["kk"])
                TT(V["t2"], V["kk"], V["kk"], MULT, ["kk"], ["t2"])
                ps_, pk = PSL()
                MM(ps_[:, 0:T], bones[:], V["t2"], True, True, ["bones", "t2"], [pk])
                RSQRT(V["t3"], ps_[:, 0:T], [pk], ["t3"], eps=1e-30)
                TT(V["kkn"], V["kk"], V["t3"], MULT, ["kk", "t3"], ["kkn"])
                TSC(V["t1"], V["a"], pt("k_a", p), omka[:, p:p + 1], MULT, ADD, ["a", "PT", "PT2"], ["t1"])
                TT(V["k"], V["k0"], V["t1"], MULT, ["k0", "t1"], ["k"])
                TT(V["b"], V["kkn"], V["a"], MULT, ["kkn", "a"], ["b"])
                STT(V["rk"], V["r"], pt("r_k", p), V["k"], MULT, MULT, ["r", "k", "PT"], ["rk"])
                ps_, pk = PSL()
                MM(ps_[:, 0:T], bones[:], V["rk"], True, True, ["bones", "rk"], [pk])
                TT(V["bonus"], ps_[:, 0:T], V["v"], MULT, [pk, "v"], ["bonus"])
                stage(7)
                L = 4 if smp else 64
                nseg = T // L
                S.op("dve", lambda e: e.tensor_tensor_scan(out=V["cs"], data0=ones[:, 0:T], data1=V["lw"], initial=0.0,
                                                           op0=MULT, op1=ADD), reads=["ones", "lw"], writes=["cs"])
                TT(V["cx"], V["cs"], V["lw"], SUB, ["cs", "lw"], ["cx"])
                cs3 = V["cs"].rearrange("p (s l) -> p s l", l=L)
                cx3 = V["cx"].rearrange("p (s l) -> p s l", l=L)
                CP(PV["t4"][:, 0:nseg], cx3[:, :, 0], ["cx"], ["t4"])
                base = PV["t4"][:, 0:nseg].unsqueeze(2).to_broadcast([128, nseg, L])
                TT(cs3, cs3, base, SUB, ["cs", "t4"], ["cs"])
                TT(cx3, cx3, base, SUB, ["cx", "t4"], ["cx"])
                ACT(V["eW"], V["cs"], AF.Exp, ["cs"], ["eW"], scale=-C0)
                ACT(V["eWi"], V["cs"], AF.Exp, ["cs"], ["eWi"], scale=C0)
                ACT(V["eWp"], V["cx"], AF.Exp, ["cx"], ["eWp"], scale=-C0)

                stage(8)
                msu, mu_, msl = (ms_su, ms_u, ms_sl) if smp else (m_su, m_u, m_sl)
                b2 = lambda m: m[:].unsqueeze(1).to_broadcast([64, 2, 64])
                h2v = lambda ap: ap.rearrange("p (h n) -> p h n", h=2)
                K_ = lambda n, c: f"{n}{c}"

                def bankA(c):
                    i = (0, 1, 3, 4)[c]
                    return pss[i][:, :], ("ps", i)

                def bankB(c):
                    i = (2, 2, 5, 5)[c]
                    return pss[i][:, 256 * (c % 2):256 * (c % 2) + 256], ("ps", i)

                def P0(c):
                    B = CS[c]
                    cs_ = slice(64 * c, 64 * c + 64)
                    TT(B["TL"][:, 0, :], PV["kkn"][:, cs_], PV["eWp"][:, cs_], MULT, ["kkn", "eWp"], [K_("TL", c)])
                    TT(B["TL"][:, 1, :], PV["r"][:, cs_], PV["eW"][:, cs_], MULT, ["r", "eW"], [K_("TL", c)])
                    TT(B["TR"][:, 0, :], PV["b"][:, cs_], PV["eWi"][:, cs_], MULT, ["b", "eWi"], [K_("TR", c)])
                    TT(B["TR"][:, 1, :], PV["k"][:, cs_], PV["eWi"][:, cs_], MULT, ["k", "eWi"], [K_("TR", c)])
                    for hj in range(2):
                        hs = slice(64 * hj, 64 * hj + 64)
                        CP(B["TLz"][hs, hj, :, :], B["TL"][hs, :, :], [K_("TL", c)], [K_("TLz", c)], eng="act")
                        CP(B["TRz"][hs, hj, :, :], B["TR"][hs, :, :], [K_("TR", c)], [K_("TRz", c)], eng="act")

                def P1(c):
                    B = CS[c]
                    A, ak = bankA(c)
                    Bk_, bk = bankB(c)
                    for hj in range(2):
                        TLh = B["TLz"][:, hj, :, :].rearrange("p a t -> p (a t)")
                        MM(A[0:64, 128 * hj:128 * hj + 128], B["TRz"][:, hj, 0, :], TLh, True, True, [K_("TRz", c), K_("TLz", c)], [ak])
                        MM(Bk_[0:64, 128 * hj:128 * hj + 128], B["TRz"][:, hj, 1, :], TLh, True, True, [K_("TRz", c), K_("TLz", c)], [bk])
                        MM(A[0:64, 256 + 64 * hj:256 + 64 * hj + 64], B["TLz"][:, hj, 0, :], B["TRz"][:, hj, 0, :], True, True,
                           [K_("TRz", c), K_("TLz", c)], [ak])

                def P2(c):
                    B = CS[c]
                    A, ak = bankA(c)
                    Bk_, bk = bankB(c)
                    pa3 = h2v(A[0:64, 0:256]); pb3 = h2v(Bk_[0:64, 0:256]); pc3 = h2v(A[0:64, 256:384])
                    pz = B["PZ"][0]
                    STT(pz[:, :, 1, :], pa3[:, :, 0:64], -1.0, b2(msu), MULT, MULT, [ak, "masks"], [K_("pz0_", c)])
                    STT(B["NRB"], pa3[:, :, 64:128], -1.0, b2(mu_), MULT, MULT, [ak, "masks"], [K_("NRB", c)])
                    STT(B["ZTa"], pc3, -1.0, b2(msl), MULT, MULT, [ak, "masks"], [K_("ZTa", c)])
                    TT(B["LKT"], pb3[:, :, 0:64], b2(msu), MULT, [bk, "masks"], [K_("LKT", c)])
                    TT(B["RKT"], pb3[:, :, 64:128], b2(mu_), MULT, [bk, "masks"], [K_("RKT", c)])
                    TT(pz[:, :, 0, :], pz[:, :, 1, :], ident[0:64, 0:64].unsqueeze(1).to_broadcast([64, 2, 64]), ADD,
                       [K_("pz0_", c), "ident"], [K_("pz0_", c)])

                def P3(c):
                    B = CS[c]
                    cs_ = slice(64 * c, 64 * c + 64)
                    A, ak = bankA(c)
                    TRP(A[0:64, 0:128], PV["v"][:, cs_], ["v"], [ak])
                    TRP(A[0:64, 128:256], B["TR"][:, 0, :], [K_("TR", c)], [ak])
                    TRP(A[0:64, 256:384], B["TR"][:, 1, :], [K_("TR", c)], [ak])
                    CP(B["Vtok"], A[0:64, 0:128], [ak], [K_("Vtok", c)], eng="act")
                    S.op("act", lambda e: e.mul(out=B["nbtok"], in_=A[0:64, 128:256], mul=-1.0), reads=[ak], writes=[K_("nbtok", c)])
                    CP(B["ktok"], A[0:64, 256:384], [ak], [K_("ktok", c)], eng="act")

                nstate = {}

                def NM(j):
                    def f(c):
                        B = CS[c]
                        cur, zt_cur = nstate.get(c, (0, "ZTa"))
                        pzc = B["PZ"][cur]
                        kc_ = K_(f"pz{cur}_", c)
                        A, ak = bankA(c)
                        Bk_, bk = bankB(c)
                        ztk = K_(zt_cur, c)
                        for hj in range(2):
                            if j == 0:
                                MM(A[0:64, 128 * hj + 64:128 * hj + 128], B[zt_cur][:, hj, :], pzc[:, hj, 1, :], True, True, [ztk, kc_], [ak])
                            elif j == 5:
                                MM(A[0:64, 128 * hj:128 * hj + 64], B[zt_cur][:, hj, :], pzc[:, hj, 0, :], True, False, [ztk, kc_], [ak])
                                MM(A[0:64, 128 * hj:128 * hj + 64], ident[0:64, 0:64], pzc[:, hj, 0, :], False, True, ["ident", kc_], [ak])
                            else:
                                MM(A[0:64, 128 * hj:128 * hj + 128], B[zt_cur][:, hj, :], pzc[:, hj, :, :].rearrange("p a t -> p (a t)"),
                                   True, False, [ztk, kc_], [ak])
                                MM(A[0:64, 128 * hj:128 * hj + 64], ident[0:64, 0:64], pzc[:, hj, 0, :], False, True, ["ident", kc_], [ak])
                            if j != 5:
                                MM(Bk_[0:64, 64 * hj:64 * hj + 64], pzc[:, hj, 1, :], B[zt_cur][:, hj, :], True, True, [ztk, kc_], [bk])
                    return f

                def NE(j):
                    def f(c):
                        B = CS[c]
                        cur, zt_cur = nstate.get(c, (0, "ZTa"))
                        zt_nxt = "ZTb" if zt_cur == "ZTa" else "ZTa"
                        pzc, pzn = B["PZ"][cur], B["PZ"][1 - cur]
                        kc_, kn_ = K_(f"pz{cur}_", c), K_(f"pz{1 - cur}_", c)
                        A, ak = bankA(c)
                        Bk_, bk = bankB(c)
                        q13 = h2v(A[0:64, 0:256])
                        if j == 0:
                            CP(pzn[:, :, 0, :], pzc[:, :, 0, :], [kc_], [kn_], eng="act")
                        else:
                            CP(pzn[:, :, 0, :], q13[:, :, 0:64], [ak], [kn_], eng="act")
                        if j != 5:
                            CP(pzn[:, :, 1, :], q13[:, :, 64:128], [ak], [kn_])
                            CP(B[zt_nxt], h2v(Bk_[0:64, 0:128]), [bk], [K_(zt_nxt, c)], eng="act")
                        nstate[c] = (1 - cur, zt_nxt)
                    return f

                for step in [P0, P1, P2, P3]:
                    for c in range(nch):
                        step(c)
                if pipelined and p + 1 < NHP:
                    psl_restrict[0] = True
                    part1(p + 1, (p + 1) % 2)
                    psl_restrict[0] = False
                for j in range(6):
                    for step in (NM(j), NE(j)):
                        for c in range(nch):
                            step(c)
                stage(11)
                if smp:
                    B = CS[0]
                    for s in range(NS):
                        S.dma(Stmp[:, :].rearrange("v (h k) -> v h k", h=2),
                              st_wkv[s, 2 * p:2 * p + 2, :, :].rearrange("h v k -> v h k"), writes=["Stmp"])
                        pq, pqk = PSL()
                        TRP(pq[:, 0:64], Stmp[:, :], ["Stmp"], [pqk])
                        for hj in range(2):
                            hs = slice(64 * hj, 64 * hj + 64)
                            CP(Msz[hs, hj, s, :], pq[hs, 0:64], [pqk], ["Ms"], eng="act")
                    TT(KKX, B["TL"][:, 0, :].unsqueeze(1).to_broadcast([128, 16, 64]), selm, MULT, ["TL0", "selm"], ["KKX"])
                    TT(RX, B["TL"][:, 1, :].unsqueeze(1).to_broadcast([128, 16, 64]), selm, MULT, ["TL0", "selm"], ["RX"])
                for c in range(nch):
                    B = CS[c]
                    cs_ = slice(64 * c, 64 * c + 64)
                    cur, _zt = nstate[c]
                    PTt, ptk = B["PZ"][cur], K_(f"pz{cur}_", c)
                    px, pxk = PSL()
                    px2, pxk2 = PSL()
                    py, pyk = PSL()
                    for hj in range(2):
                        hs = slice(64 * hj, 64 * hj + 64)
                        o_ = px[0:64, 64 * hj:64 * hj + 64]
                        oy = py[0:64, 64 * hj:64 * hj + 64]
                        if smp:
                            for s in range(NS):
                                MM(o_, KKX[:, s, :], Msz[:, hj, s, :], s == 0, s == NS - 1, ["KKX", "Ms"], [pxk])
                            for s in range(NS):
                                MM(oy, RX[:, s, :], Msz[:, hj, s, :], s == 0, s == NS - 1, ["RX", "Ms"], [pyk])
                        else:
                            MM(o_, B["TLz"][:, hj, 0, :], Mst[:, p, :], True, True, [K_("TLz", c), "state"], [pxk])
                            MM(oy, B["TLz"][:, hj, 1, :], Mst[:, p, :], True, True, [K_("TLz", c), "state"], [pyk])
                        MM(px2[0:64, 64 * hj:64 * hj + 64], B["LKT"][:, hj, :], B["Vtok"][:, hs], True, True,
                           [K_("LKT", c), K_("Vtok", c)], [pxk2])
                    CP(B["X"], h2v(px[0:64, 0:128]), [pxk], [K_("X", c)])
                    TT(B["X"], B["X"], h2v(px2[0:64, 0:128]), ADD, [K_("X", c), pxk2], [K_("X", c)])
                    CP(B["Ytok"], py[0:64, 0:128], [pyk], [K_("Ytok", c)], eng="act")
                    pu, puk = PSL()
                    for hj in range(2):
                        MM(pu[0:64, 64 * hj:64 * hj + 64], PTt[:, hj, 0, :], B["X"][:, hj, :], True, True, [ptk, K_("X", c)], [puk])
                    CP(B["U"], h2v(pu[0:64, 0:128]), [puk], [K_("U", c)])
                    stage(12)
                    if smp:
                        TT(NBX, B["nbtok"].unsqueeze(1).to_broadcast([64, 16, 128]),
                           rowsel[:].unsqueeze(2).to_broadcast([64, 16, 128]), MULT, ["nbtok0", "masks"], ["NBX"])
                        TT(KX, B["ktok"].unsqueeze(1).to_broadcast([64, 16, 128]),
                           rowsel[:].unsqueeze(2).to_broadcast([64, 16, 128]), MULT, ["ktok0", "masks"], ["KX"])
                        for s in range(NS):
                            pm, pmk = PSL()
                            for hj in range(2):
                                hs = slice(64 * hj, 64 * hj + 64)
                                o_ = pm[:, 64 * hj:64 * hj + 64]
                                MM(o_, NBX[:, s, :], B["U"][:, hj, :], True, False, ["NBX", "U0"], [pmk])
                                MM(o_, KX[:, s, :], B["Vtok"][:, hs], False, True, ["KX", "Vtok0"], [pmk])
                            for hj in range(2):
                                hs = slice(64 * hj, 64 * hj + 64)
                                TT(Msz[hs, hj, s, :], Msz[hs, hj, s, :], pm[hs, 64 * hj:64 * hj + 64], ADD, ["Ms", pmk], ["Ms"])
                            TSC(Msz[:, :, s, :], Msz[:, :, s, :], PV["eW"][:, 4 * s + 3:4 * s + 4], None, MULT, None, ["Ms", "eW"], ["Ms"])
                            TT(PV["t4"][:, 0:64], Msz[:, 0, s, :], Msz[:, 1, s, :], ADD, ["Ms"], ["t4"])
                            pq, pqk = PSL()
                            TRP(pq[0:64, 0:128], PV["t4"][:, 0:64], ["t4"], [pqk])
                            CP(Sout[:], pq[0:64, 0:128], [pqk], ["Sout"], eng="act")
                            S.dma(wkv_s[s, 2 * p:2 * p + 2, :, :].rearrange("h v k -> v h k"),
                                  Sout[:, :].rearrange("v (h k) -> v h k", h=2), reads=["Sout"])
                    else:
                        pm, pmk = PSL()
                        for hj in range(2):
                            hs = slice(64 * hj, 64 * hj + 64)
                            o_ = pm[:, 64 * hj:64 * hj + 64]
                            MM(o_, B["nbtok"], B["U"][:, hj, :], True, False, [K_("nbtok", c), K_("U", c)], [pmk])
                            MM(o_, B["ktok"], B["Vtok"][:, hs], False, True, [K_("ktok", c), K_("Vtok", c)], [pmk])
                        for hj in range(2):
                            hs = slice(64 * hj, 64 * hj + 64)
                            TT(Mst[hs, p, :], Mst[hs, p, :], pm[hs, 64 * hj:64 * hj + 64], ADD, ["state", pmk], ["state"])
                        TSC(Mst[:, p, :], Mst[:, p, :], PV["eW"][:, 64 * c + 63:64 * c + 64], None, MULT, None, ["state", "eW"], ["state"])
                    py2, pyk2 = PSL()
                    for hj in range(2):
                        hs = slice(64 * hj, 64 * hj + 64)
                        o2_ = py2[0:64, 64 * hj:64 * hj + 64]
                        MM(o2_, B["NRB"][:, hj, :], B["U"][:, hj, :], True, False, [K_("NRB", c), K_("U", c)], [pyk2])
                        MM(o2_, B["RKT"][:, hj, :], B["Vtok"][:, hs], False, True, [K_("RKT", c), K_("Vtok", c)], [pyk2])
                    TT(B["Ytok"], B["Ytok"], py2[0:64, 0:128], ADD, [K_("Ytok", c), pyk2], [K_("Ytok", c)])
                    pt_, ptk2 = PSL()
                    TRP(pt_[:, 0:64], B["Ytok"], [K_("Ytok", c)], [ptk2])
                    CP(PV["y"][:, cs_], pt_[:, 0:64], [ptk2], ["y"], eng="act")
                if last:
                    pq, pqk = PSL()
                    TRP(pq[0:64, 0:128], Mst[:, p, :], ["state"], [pqk])
                    CP(Sout[:], pq[0:64, 0:128], [pqk], ["Sout"], eng="act")
                    S.dma(wkv_p[2 * p:2 * p + 2, :, :].rearrange("h v k -> v h k"),
                          Sout[:, :].rearrange("v (h k) -> v h k", h=2), reads=["Sout"])
                stage(13)
                ps_, pk = PSL()
                MM(ps_[:, 0:T], bones[:], V["y"], True, True, ["bones", "y"], [pk])
                STT(V["t2"], ps_[:, 0:T], -1.0 / 64, V["y"], MULT, ADD, [pk, "y"], ["t2"])
                TT(V["t3"], V["t2"], V["t2"], MULT, ["t2"], ["t3"])
                ps_, pk = PSL()
                MM(ps_[:, 0:T], bones[:], V["t3"], True, True, ["bones", "t3"], [pk])
                RSQRT(V["t3"], ps_[:, 0:T], [pk], ["t3"], scale=1.0 / 64, eps=GN_EPS)
                TT(V["t2"], V["t2"], V["t3"], MULT, ["t2", "t3"], ["t2"])
                TSC(V["t2"], V["t2"], pt("lnx_g", p), pt("lnx_b", p), MULT, ADD, ["t2", "PT"], ["t2"])
                TT(V["t2"], V["t2"], V["bonus"], ADD, ["t2", "bonus"], ["t2"])
                TT(V["t2"], V["t2"], GG[alt][:, 0:T], MULT, ["t2", f"g{alt}"], ["t2"])
                TT(merged[:, p, 0:T], V["t2"], SGA[alt][:, 0:T], MULT, ["t2", f"sga{alt}"], [("mg", p)])

            stage(14)
            allzc = [("zcp", p) for p in range(16)]
            ps1, pk1 = PSL()
            for kc in range(16):
                MM(ps1[:, 0:T], ones_bf[:], zc[:, kc, 0:T], kc == 0, kc == 15, ["ones"] + allzc, [pk1])
            TSC(rstd[:, 0:T], ps1[:, 0:T], -1.0 / D, None, MULT, None, [pk1], ["rstd"])
            TT(dx[:, :, 0:T], zc[:, :, 0:T], zc[:, :, 0:T], MULT, allzc, ["dx"])
            ps1, pk1 = PSL()
            for kc in range(16):
                MM(ps1[:, 0:T], ones_bf[:], dx[:, kc, 0:T], kc == 0, kc == 15, ["ones", "dx"], [pk1])
            TT(rstd2[:, 0:T], rstd[:, 0:T], rstd[:, 0:T], MULT, ["rstd"], ["rstd2"])
            STT(rstd2[:, 0:T], ps1[:, 0:T], 1.0 / D, rstd2[:, 0:T], MULT, SUB, [pk1, "rstd2"], ["rstd2"])
            RSQRT(rstd2[:, 0:T], rstd2[:, 0:T], ["rstd2"], ["rstd2"], eps=1e-5)
            for p in range(NHP):
                wv, wk = WLOAD(kview(w_in)[:, :, 6 * D + 128 * p:6 * D + 128 * p + 128], (128, 16, 128))
                ps_, pk = PSL()
                for kc in range(16):
                    MM(ps_[:, 0:T], wv[:, kc, :], h[:, kc, 0:T], kc == 0, kc == 15, [wk, "h"], [pk])
                ACT(PV["gb"][:, 0:T], ps_[:, 0:T], AF.Sigmoid, [pk], ["gb"])
                t2 = PV["t2"][:, 0:T]
                TT(t2, zc[:, p, 0:T], rstd[:, 0:T], ADD, allzc + ["rstd"], ["t2"])
                TT(t2, t2, rstd2[:, 0:T], MULT, ["t2", "rstd2"], ["t2"])
                TSC(t2, t2, pt("ln_conv_g", p), pt("ln_conv_b", p), MULT, ADD, ["t2", "PT"], ["t2"])
                ACT(t2, t2, AF.Silu, ["t2"], ["t2"])
                TT(t2, t2, PV["gb"][:, 0:T], MULT, ["t2", "gb"], ["t2"])
                TT(merged[:, p, 0:T], merged[:, p, 0:T], t2, ADD, [("mg", p), "t2"], [("mg", p)])
            allmg = [("mg", p) for p in range(16)]
            stage(15)
            for m in range(16):
                wv, wk = WLOAD(kview(w_out)[:, :, 128 * m:128 * m + 128], (128, 16, 128))
                ps_, pk = PSL()
                for kc in range(16):
                    MM(ps_[:, 0:T], wv[:, kc, :], merged[:, kc, 0:T], kc == 0, kc == 15, [wk] + allmg, [pk])
                if smp:
                    t23 = PV["t2"][:, 0:T].rearrange("p (s t) -> p s t", t=4)
                    TT(t23, ps_[:, 0:T].rearrange("p (s t) -> p s t", t=4), TS[:, 2, m, :].unsqueeze(2).to_broadcast([128, 16, 4]),
                       MULT, [pk, "TS"], ["t2"])
                    TT(xT[:, m, 0:T], xT[:, m, 0:T], PV["t2"][:, 0:T], ADD, ["xT", "t2"], ["xT"])
                else:
                    STT(xT[:, m, 0:T], ps_[:, 0:T], TP[:, 2, m:m + 1], xT[:, m, 0:T], MULT, ADD, [pk, "TP", "xT"], ["xT"])
            stage(16)
            rms(xT, "xT", rstd)
            modnorm(xT, "xT", rstd, 3, 4)
            CP(h[:, :, 0:T], hf[:, :, 0:T], ["mixact"], ["h"], eng="act")
            act = mixbf[:, 0:NFC * TB].rearrange("p (f t) -> p f t", t=TB)
            for f in range(NFC):
                for half in range(2):
                    ch = f + NFC * half
                    wv, wk = WLOAD(kview(w_up)[:, :, 128 * ch:128 * ch + 128], (128, 16, 128))
                    ps_, pk = PSL()
                    for kc in range(16):
                        MM(ps_[:, 0:T], wv[:, kc, :], h[:, kc, 0:T], kc == 0, kc == 15, [wk, "h"], [pk])
                    k0_, k1_, k2_ = (pt("ffn_dw_k", 86 * j + ch) for j in range(3))
                    bb = pt("ffn_dw_b", ch)
                    o = PV["t2"] if half == 0 else PV["t3"]
                    okey = "t2" if half == 0 else "t3"
                    if smp:
                        ze = uext[:, 0:96].rearrange("p (s t) -> p s t", t=6)
                        S.dma(Stmp[0:32, 0:128], st_ffn[:, :, 128 * ch:128 * ch + 128].rearrange("s t c -> (s t) c"), writes=["Stmp"])
                        pq, pqk = PSL()
                        TRP(pq[:, 0:32], Stmp[0:32, 0:128], ["Stmp"], [pqk])
                        CP(ze[:, :, 0:2], pq[:, 0:32].rearrange("p (s t) -> p s t", t=2), [pqk], ["uext"])
                        CP(ze[:, :, 2:6], ps_[:, 0:64].rearrange("p (s t) -> p s t", t=4), [pk], ["uext"], eng="act")
                        o3 = o[:, 0:64].rearrange("p (s t) -> p s t", t=4)
                        TSC(o3, ze[:, :, 0:4], k0_, bb, MULT, ADD, ["uext", "PT"], [okey])
                        STT(o3, ze[:, :, 1:5], k1_, o3, MULT, ADD, ["uext", "PT", okey], [okey])
                        STT(o3, ze[:, :, 2:6], k2_, o3, MULT, ADD, ["uext", "PT", okey], [okey])
                        pq, pqk = PSL()
                        CP(PV["t4"][:, 0:64].rearrange("p (t s) -> p t s", t=4), ze[:, :, 2:6].rearrange("p s t -> p t s"), ["uext"], ["t4"])
                        TRP(pq[0:64, 0:128], PV["t4"][:, 0:64], ["t4"], [pqk])
                        CP(Ytok[:, :], pq[0:64, 0:128], [pqk], ["Ytok"])
                        for t in range(2):
                            S.dma(ffn_s[:, t, 128 * ch:128 * ch + 128], Ytok[32 + 16 * t:48 + 16 * t, :], reads=["Ytok"])
                    else:
                        CP(uext[:, 0:2], ffnhalo[:, ch, :], ["state"], ["uext"])
                        CP(uext[:, 2:2 + T], ps_[:, 0:T], [pk], ["uext"], eng="act")
                        CP(ffnhalo[:, ch, :], uext[:, T:T + 2], ["uext"], ["state"])
                        TSC(o[:, 0:T], uext[:, 0:T], k0_, bb, MULT, ADD, ["uext", "PT"], [okey])
                        STT(o[:, 0:T], uext[:, 1:1 + T], k1_, o[:, 0:T], MULT, ADD, ["uext", "PT", okey], [okey])
                        STT(o[:, 0:T], uext[:, 2:2 + T], k2_, o[:, 0:T], MULT, ADD, ["uext", "PT", okey], [okey])
                        if last:
                            pq, pqk = PSL()
                            TRP(pq[0:2, 0:128], uext[:, T:T + 2], ["uext"], [pqk])
                            CP(Ytok[0:2, :], pq[0:2, 0:128], [pqk], ["Ytok"])
                            S.dma(ffn_p[:, 128 * ch:128 * ch + 128], Ytok[0:2, :], reads=["Ytok"])
                ACT(PV["t2"][:, 0:T], PV["t2"][:, 0:T], AF.Silu, ["t2"], ["t2"])
                TT(act[:, f, 0:T], PV["t2"][:, 0:T], PV["t3"][:, 0:T], MULT, ["t2", "t3"], ["mixact"])
            for m in range(16):
                ps_, pk = PSL()
                for g0 in range(0, NFC, 16):
                    n = min(16, NFC - g0)
                    wv, wk = WLOAD(w_down[128 * g0:128 * (g0 + n), 128 * m:128 * m + 128].rearrange("(kc ki) n -> ki kc n", ki=128),
                                   (128, n, 128))
                    for j in range(n):
                        MM(ps_[:, 0:T], wv[:, j, :], act[:, g0 + j, 0:T], g0 + j == 0, g0 + j == NFC - 1, [wk, "mixact"], [pk])
                if smp:
                    t23 = PV["t2"][:, 0:T].rearrange("p (s t) -> p s t", t=4)
                    TT(t23, ps_[:, 0:T].rearrange("p (s t) -> p s t", t=4), TS[:, 5, m, :].unsqueeze(2).to_broadcast([128, 16, 4]),
                       MULT, [pk, "TS"], ["t2"])
                    TT(xT[:, m, 0:T], xT[:, m, 0:T], PV["t2"][:, 0:T], ADD, ["xT", "t2"], ["xT"])
                else:
                    STT(xT[:, m, 0:T], ps_[:, 0:T], TP[:, 5, m:m + 1], xT[:, m, 0:T], MULT, ADD, [pk, "TP", "xT"], ["xT"])
            stage(17)
            rms(xT, "xT", rstd)
            TT(hf[:, :, 0:T], xT[:, :, 0:T], rstd[:, 0:T].unsqueeze(1).to_broadcast([128, 16, T]), MULT, ["xT", "rstd"], ["mixact"])
            TT(hf[:, :, 0:T], hf[:, :, 0:T], bc(ptr("normf_g", 0, 16)), MULT, ["mixact", "PT"], ["mixact"])
            R = 64 if smp else 128
            for ti_ in range(max(1, T // 128)):
                for kc in range(16):
                    ps_, pk = PSL()
                    TRP(ps_[0:R, 0:128], hf[:, kc, 128 * ti_:128 * ti_ + R], ["mixact"], [pk])
                    CP(otile[0:R, 128 * kc:128 * kc + 128], ps_[0:R, 0:128], [pk], ["xtile"], eng="act" if kc % 2 else "dve")
                S.dma(ys if smp else yp[TB * bi + 128 * ti_:TB * bi + 128 * ti_ + 128, :], otile[0:R, :], reads=["xtile"])

        try:
            for bi in range(n_pblocks):
                emit_block("p", bi)
            if do_sample:
                S.dma(conv_s[:, 0:26, :], st_conv[:, 4:30, :])
                allset = [f"{n}{c}" for c in range(1, 4) for n in
                          ["TL", "TR", "TLz", "TRz", "Vtok", "nbtok", "ktok", "Ytok", "pz0_", "pz1_"] + CTN]
                S.op("pool", lambda e: e.memset(smpreg[:], 0.0), writes=allset + ["smpreg", "NBX", "KX", "KKX", "RX", "Ms", "selm", "sga1", "g1"])
                S.op("pool", lambda e: e.memset(selm, 1.0), writes=["selm"])
                selm4 = selm.rearrange("p s (a b) -> p s a b", b=4)
                S.op("pool", lambda e: e.affine_select(out=selm4, in_=selm4, pattern=[[1, 16], [-1, 16], [0, 4]], compare_op=ALU.is_equal,
                                                       fill=0.0, base=0, channel_multiplier=0), reads=["selm"], writes=["selm"])
                emit_block("s", 0)
        except StopEmit:
            pass
        S.finish()
        print("instructions:", S.ninstr, "sbuf left", nc.sbuf_bytes_remaining)
    return nc


_NC_CACHE = {}


def kernel(**inputs):
    f32 = lambda a: np.ascontiguousarray(np.asarray(a, dtype=np.float32))
    I = {k: f32(v) for k, v in inputs.items()}
    ncores = 8
    prm = np.zeros((NPROW, 128), np.float32)
    for name, cntc in _PSPEC:
        prm[POFF[name]:POFF[name] + cntc] = I[name].reshape(cntc, 128)
    if "nc" not in _NC_CACHE:
        _NC_CACHE["nc"] = build_nc()
    nc = _NC_CACHE["nc"]
    shared = {k: I[k] for k in ["w_ada", "w_in", "w1", "w2", "a1", "a2", "g1", "g2", "w_out", "w_up", "w_down"]}
    in_maps = []
    for c in range(ncores):
        m = dict(shared)
        m["params"] = prm
        if c < 4:
            m["xp"] = I["x_prompt"][c]
            cp = I["c_prompt"][c]
        else:
            m["xp"] = np.zeros((SEQ, D), np.float32)
            cp = np.zeros((D,), np.float32)
        sl = slice(NS * c, NS * c + NS)
        m["xs"] = I["x_sample"][sl].reshape(64, D)
        m["st_shift"] = I["state_shift"][sl]
        m["st_wkv"] = I["state_wkv"][sl]
        m["st_conv"] = I["state_conv"][sl]
        m["st_ffn"] = I["state_ffn"][sl]
        m["cvec"] = np.concatenate([cp[None], I["c_sample"][sl]], 0)
        in_maps.append({k: np.ascontiguousarray(v) for k, v in m.items()})
    res = run_bass_kernel_spmd(nc, in_maps, core_ids=list(range(ncores)))
    R = res.results
    y_p = np.stack([R[c]["yp"] for c in range(4)])
    shift_p = np.stack([R[c]["shift_p"].reshape(D) for c in range(4)])
    wkv_p = np.stack([R[c]["wkv_p"] for c in range(4)])
    conv_p = np.stack([R[c]["conv_p"] for c in range(4)])
    ffn_p = np.stack([R[c]["ffn_p"] for c in range(4)])
    y_s = np.concatenate([R[c]["ys"].reshape(NS, 4, D) for c in range(8)])
    shift_s = np.concatenate([R[c]["shift_s"] for c in range(8)])
    wkv_s = np.concatenate([R[c]["wkv_s"] for c in range(8)])
    conv_s = np.concatenate([R[c]["conv_s"] for c in range(8)])
    ffn_s = np.concatenate([R[c]["ffn_s"] for c in range(8)])
    return (y_p, y_s, shift_p, wkv_p, conv_p, ffn_p, shift_s, wkv_s, conv_s, ffn_s)
```

```python
import numpy as np
from contextlib import ExitStack
import concourse.bass as bass
import concourse.mybir as mybir
from concourse.bass_utils import run_bass_kernel_spmd

F32 = mybir.dt.float32
BF16 = mybir.dt.bfloat16
TB = 256
NPRE = 3
NMAIN = 5
AF = mybir.ActivationFunctionType
ALU = mybir.AluOpType

D = 2048
NK = 16
SEQ = 2048
NS = 16
DFF = 5504
NFC = 43
F2 = 2 * DFF
NHP = 16
C0 = float(np.exp(-0.5))
GN_EPS = 64 * 1e-5

_PSPEC = [("norm1_g", 16), ("norm2_g", 16), ("normf_g", 16), ("b_ada", 96), ("mu", 96), ("b_glu", 32),
          ("w0", 16), ("a0", 16), ("k_k", 16), ("k_a", 16), ("r_k", 16), ("lnx_g", 16), ("lnx_b", 16),
          ("dw_b", 16), ("ln_conv_g", 16), ("ln_conv_b", 16), ("dw_k", 496), ("ffn_dw_k", 258), ("ffn_dw_b", 86)]
POFF = {}
_o = 0
for _n, _c in _PSPEC:
    POFF[_n] = _o
    _o += _c
NPROW = 1280


STOP = [None]


class StopEmit(Exception):
    pass


def stage(n):
    if STOP[0] is not None and STOP[0] == n:
        raise StopEmit()


class Sched:
    EPOCH = 20000

    def __init__(self, nc, stack, n_dma_sems=16):
        self.nc = nc
        self.stack = stack
        self.engs = {}
        for name, h in (("pe", nc.tensor), ("act", nc.scalar), ("dve", nc.vector),
                        ("pool", nc.gpsimd), ("sp", nc.sync)):
            self.engs[name] = dict(h=h, sems=[], count=0, seen={})
        self.dma_sems = [stack.enter_context(nc.semaphore(f"dq{i}")) for i in range(n_dma_sems)]
        self.dma_uses = [0] * n_dma_sems
        self.dma_next = 0
        self.dma_next_pool = 0
        self.last_w = {}
        self.readers = {}
        self.ninstr = 0

    def _sem_for(self, ename, idx):
        e = self.engs[ename]
        ep = idx // self.EPOCH
        while len(e["sems"]) <= ep:
            e["sems"].append(self.stack.enter_context(self.nc.semaphore(f"s_{ename}{len(e['sems'])}")))
        return e["sems"][ep], idx % self.EPOCH + 1

    def _wait(self, ename, ev):
        e = self.engs[ename]
        if ev[0] == "eng":
            _, src, idx = ev
            if src == ename and ename == "pe":
                return
            if e["seen"].get(src, -1) >= idx:
                return
            sem, val = self._sem_for(src, idx)
            e["h"].wait_ge(sem, val)
            e["seen"][src] = idx
        else:
            _, si, val = ev
            key = ("dma", si)
            if e["seen"].get(key, 0) >= val:
                return
            e["h"].wait_ge(self.dma_sems[si], val)
            e["seen"][key] = val

    def _deps(self, ename, reads, writes):
        evs = []
        for k in reads:
            if k in self.last_w:
                evs.append(self.last_w[k])
        for k in writes:
            if k in self.last_w:
                evs.append(self.last_w[k])
            for r in self.readers.get(k, ()):
                evs.append(r)
        for ev in evs:
            self._wait(ename, ev)

    def _record(self, ev, reads, writes):
        for k in reads:
            self.readers.setdefault(k, []).append(ev)
        for k in writes:
            self.last_w[k] = ev
            self.readers[k] = []

    def op(self, ename, fn, reads=(), writes=()):
        e = self.engs[ename]
        self._deps(ename, reads, writes)
        idx = e["count"]
        sem, val = self._sem_for(ename, idx)
        ins = fn(e["h"])
        ins.then_inc(sem, 1)
        e["count"] += 1
        self.ninstr += 1
        self._record(("eng", ename, idx), reads, writes)

    def dma(self, out, in_, reads=(), writes=(), q="sp", **kw):
        qname = q
        e = self.engs[qname]
        self._deps(qname, reads, writes)
        half = len(self.dma_sems) // 2
        if qname == "pool":
            si = half + self.dma_next_pool % half
            self.dma_next_pool += 1
        else:
            si = self.dma_next % half
            self.dma_next += 1
        prev = self.dma_uses[si]
        if prev > 0:
            self._wait(qname, ("dma", si, 16 * prev))
        self.dma_uses[si] = prev + 1
        e["h"].dma_start(out=out, in_=in_, **kw).then_inc(self.dma_sems[si], 16)
        self.ninstr += 1
        ev = ("dma", si, 16 * (prev + 1))
        self._record(ev, reads, writes)

    def finish(self):
        for si, uses in enumerate(self.dma_uses):
            if uses:
                self._wait("sp", ("dma", si, 16 * uses))
        for name, e in self.engs.items():
            if name != "sp" and e["count"]:
                self._wait("sp", ("eng", name, e["count"] - 1))


def build_nc(n_pblocks=NMAIN, do_sample=True, n_pre=NPRE):
    nc = bass.Bass("TRN2", target_bir_lowering=False)
    din = lambda n, s: nc.dram_tensor(n, s, F32, kind="ExternalInput").ap()
    dout = lambda n, s: nc.dram_tensor(n, s, F32, kind="ExternalOutput").ap()
    xp = din("xp", [n_pblocks * TB, D]); xq = din("xq", [max(n_pre, 1) * TB, D]); flag = din("flag", [128, 1]); xs = din("xs", [64, D]); st_shift = din("st_shift", [NS, D])
    st_wkv = din("st_wkv", [NS, 32, 64, 64]); st_conv = din("st_conv", [NS, 30, D]); st_ffn = din("st_ffn", [NS, 2, F2])
    cvec = din("cvec", [17, D]); params = din("params", [NPROW, 128])
    w_ada = din("w_ada", [D, 6 * D]); w_in = din("w_in", [D, 7 * D])
    w1 = din("w1", [D, 96]); w2 = din("w2", [96, D]); a1 = din("a1", [D, 96]); a2 = din("a2", [96, D])
    g1 = din("g1", [D, 256]); g2 = din("g2", [256, D]); w_out = din("w_out", [D, D])
    w_up = din("w_up", [D, F2]); w_down = din("w_down", [DFF, D])
    yp = dout("yp", [n_pblocks * TB, D]); ys = dout("ys", [64, D]); shift_p = dout("shift_p", [16, 128])
    wkv_p = dout("wkv_p", [32, 64, 64]); conv_p = dout("conv_p", [30, D]); ffn_p = dout("ffn_p", [2, F2])
    shift_s = dout("shift_s", [NS, D]); wkv_s = dout("wkv_s", [NS, 32, 64, 64])
    conv_s = dout("conv_s", [NS, 30, D]); ffn_s = dout("ffn_s", [NS, 2, F2])

    kview = lambda W: W.rearrange("(kc ki) n -> ki kc n", ki=128)

    with ExitStack() as st:
        cnt = [0]

        def sb(shape, name=None, dt=F32):
            cnt[0] += 1
            return st.enter_context(nc.sbuf_tensor(name or f"t{cnt[0]}", shape, dt))

        ident = sb([128, 128], "ident"); ones = sb([128, TB], "ones"); bones = sb([128, 128], "bones")
        ones_bf = sb([128, 128], "ones_bf", BF16)
        m_su = sb([64, 64]); m_u = sb([64, 64]); m_sl = sb([64, 64]); blk = sb([64, 64])
        ms_su = sb([64, 64]); ms_u = sb([64, 64]); ms_sl = sb([64, 64])
        rowsel = sb([64, 16])
        PT = sb([128, NPROW], "PT")
        TP = sb([128, 6, 16], "TP"); TS = sb([128, 6, 16, 16], "TS")
        omu = sb([128, 96]); omka = sb([128, 16]); flg = sb([128, 1], "flg")
        shiftst = sb([128, 16]); Mst = sb([128, 16, 64]); convhalo = sb([128, 16, 30], None, BF16); ffnhalo = sb([128, 86, 2])
        xtile = sb([128, D], "xtile"); xT = sb([128, 16, TB], "xT"); h = sb([128, 16, TB], "h", BF16)
        dx = sb([128, 16, TB], "dx", BF16); mixact = sb([128, 6144], "mixact")
        mixbf = mixact[:, :].bitcast(BF16)
        hf = mixact[:, 0:16 * TB].rearrange("p (k t) -> p k t", t=TB)
        MT = mixact[:, 0:96 * 17].rearrange("p (m c) -> p m c", c=17)
        zc = sb([128, 16, TB], "zc", BF16); merged = sb([128, 16, TB], "merged", BF16)
        rstd = sb([128, TB]); rstd2 = sb([128, TB])
        ring = [sb([128, 16, 128], f"ring{i}", BF16) for i in range(4)]
        PV = {n: sb([128, TB], "pv_" + n) for n in
              ["r", "k0", "v", "lw", "a", "g", "sga", "glu", "glb", "kk", "kkn", "t1", "k", "b", "rk", "bonus",
               "cs", "cx", "eW", "eWi", "eWp", "y", "t2", "t3", "t4", "gb"]}
        for n in ["tw", "xa1", "sg0", "sg1"]:
            PV[n] = sb([128, TB], "pv_" + n, BF16)
        zext = sb([128, 32 + TB], "zext2", BF16); uext = sb([128, 8 + TB], "uext"); zsbf = sb([128, 16, 34], "zsbf", BF16)
        smpreg = sb([128, 9216], "smpreg")
        CTN = ["NRB", "LKT", "RKT", "X", "U", "ZTa", "ZTb"]

        def mkset(c):
            B = {}
            if c == 0:
                B["TL"] = sb([128, 2, 64])[:]; B["TR"] = sb([128, 2, 64])[:]
                B["TLz"] = sb([128, 2, 2, 64])[:]; B["TRz"] = sb([128, 2, 2, 64])[:]
                for n in CTN:
                    B[n] = sb([64, 2, 64], "ct_" + n)[:]
                B["PZ"] = [sb([64, 2, 2, 64], f"pz{i}")[:] for i in range(2)]
                for n in ["Vtok", "nbtok", "ktok", "Ytok"]:
                    B[n] = sb([64, 128])[:]
            else:
                o = [2688 * (c - 1)]

                def take(nparts, words):
                    v = smpreg[0:nparts, o[0]:o[0] + words]
                    o[0] += words
                    return v
                B["TL"] = take(128, 128).rearrange("p (a t) -> p a t", a=2)
                B["TR"] = take(128, 128).rearrange("p (a t) -> p a t", a=2)
                B["TLz"] = take(128, 256).rearrange("p (h a t) -> p h a t", h=2, a=2)
                B["TRz"] = take(128, 256).rearrange("p (h a t) -> p h a t", h=2, a=2)
                for n in CTN:
                    B[n] = take(64, 128).rearrange("p (h t) -> p h t", h=2)
                B["PZ"] = [take(64, 256).rearrange("p (h a t) -> p h a t", h=2, a=2) for i in range(2)]
                for n in ["Vtok", "nbtok", "ktok", "Ytok"]:
                    B[n] = take(64, 128)
            return B

        CS = [mkset(c) for c in range(4)]
        Ytok = sb([64, 128], "Ytok_st")
        NBX = smpreg[0:64, 0:2048].rearrange("p (s c) -> p s c", s=16)
        KX = smpreg[0:64, 2048:4096].rearrange("p (s c) -> p s c", s=16)
        KKX = smpreg[:, 4096:5120].rearrange("p (s c) -> p s c", s=16)
        RX = smpreg[:, 5120:6144].rearrange("p (s c) -> p s c", s=16)
        Msz = smpreg[:, 6144:8192].rearrange("p (h s c) -> p h s c", h=2, s=16)
        selm = smpreg[:, 8192:9216].rearrange("p (s c) -> p s c", s=16)
        Stmp = sb([64, 128]); Sout = sb([64, 128])
        otile = xtile
        cvt = xtile[0:17, :]; cT = sb([128, 16, 17], None, BF16); ptile = sb([128, 128])
        diag = sb([128, 31, 128], "diag", BF16)
        pss = [st.enter_context(nc.psum_tensor(f"ps{i}", [128, 512], F32)) for i in range(8)]
        block = st.enter_context(nc.Block())
        S = Sched(nc, st)

        psi = [0]

        def PSL():
            i = psi[0] % 8
            psi[0] += 1
            return pss[i][:, :], ("ps", i)

        def MM(out, lhsT, rhs, start, stop, r, w):
            S.op("pe", lambda e: e.matmul(out, lhsT=lhsT, rhs=rhs, start=start, stop=stop), reads=r, writes=w)

        def TRP(out, in_, r, w):
            k = in_.shape[0]
            S.op("pe", lambda e: e.transpose(out=out, in_=in_, identity=ident[0:k, 0:k]), reads=list(r) + ["ident"], writes=w)

        def TT(out, a, b, op, r, w, eng="dve"):
            S.op(eng, lambda e: e.tensor_tensor(out=out, in0=a, in1=b, op=op), reads=r, writes=w)

        def TSC(out, a, s1, s2, op0, op1, r, w, eng="dve"):
            if s2 is None:
                S.op(eng, lambda e: e.tensor_scalar(out=out, in0=a, scalar1=s1, scalar2=None, op0=op0), reads=r, writes=w)
            else:
                S.op(eng, lambda e: e.tensor_scalar(out=out, in0=a, scalar1=s1, scalar2=s2, op0=op0, op1=op1), reads=r, writes=w)

        def STT(out, a, sc, b, op0, op1, r, w):
            S.op("dve", lambda e: e.scalar_tensor_tensor(out=out, in0=a, scalar=sc, in1=b, op0=op0, op1=op1), reads=r, writes=w)

        def CP(out, a, r, w, eng="dve"):
            if eng == "act":
                S.op("act", lambda e: e.copy(out=out, in_=a), reads=r, writes=w)
            else:
                S.op(eng, lambda e: e.tensor_copy(out=out, in_=a), reads=r, writes=w)

        def ACT(out, a, func, r, w, bias=0.0, scale=1.0):
            S.op("act", lambda e: e.activation(out=out, in_=a, func=func, bias=bias, scale=scale), reads=r, writes=w)

        def RSQRT(out, a, r, w, scale=1.0, eps=0.0):
            ACT(out, a, AF.Sqrt, r, w, bias=eps, scale=scale)
            S.op("dve", lambda e: e.reciprocal(out=out, in_=out), reads=w, writes=w)

        ringi = [0]

        def WLOAD(src, shape3):
            i = ringi[0] % 4
            ringi[0] += 1
            kp, nk, n = shape3
            dst = ring[i][0:kp, 0:nk, 0:n]
            S.dma(dst, src, writes=[("ring", i)], q="pool")
            return dst, ("ring", i)

        MULT, ADD, SUB = ALU.mult, ALU.add, ALU.subtract
        pt = lambda name, c: PT[:, POFF[name] + c:POFF[name] + c + 1]
        ptr = lambda name, c0, n: PT[:, POFF[name] + c0:POFF[name] + c0 + n]

        S.op("pool", lambda e: e.memset(ident[:], 0.0), writes=["ident"])
        S.op("pool", lambda e: e.affine_select(out=ident[:], in_=ident[:], pattern=[[-1, 128]], compare_op=ALU.not_equal,
                                               fill=1.0, base=0, channel_multiplier=1), reads=["ident"], writes=["ident"])
        S.op("pool", lambda e: e.memset(ones[:], 1.0), writes=["ones"])
        S.op("pool", lambda e: e.memset(ones_bf[:], 1.0), writes=["ones"])
        S.op("pool", lambda e: e.memset(bones[:], 0.0), writes=["bones"])
        S.op("pool", lambda e: e.memset(bones[0:64, 0:64], 1.0), reads=["bones"], writes=["bones"])
        S.op("pool", lambda e: e.memset(bones[64:128, 64:128], 1.0), reads=["bones"], writes=["bones"])
        for m, cm, pat, op in ((m_su, -1, 1, ALU.is_gt), (m_u, -1, 1, ALU.is_ge), (m_sl, 1, -1, ALU.is_gt)):
            S.op("pool", lambda e: e.memset(m[:], 1.0), writes=["masks"])
            S.op("pool", lambda e: e.affine_select(out=m[:], in_=m[:], pattern=[[pat, 64]], compare_op=op, fill=0.0,
                                                   base=0, channel_multiplier=cm), reads=["masks"], writes=["masks"])
        S.op("pool", lambda e: e.memset(blk[:], 1.0), writes=["masks"])
        blk3 = blk[:].rearrange("p (a b) -> p a b", b=4)
        S.op("pool", lambda e: e.affine_select(out=blk3, in_=blk3, pattern=[[-4, 16], [0, 4]], compare_op=ALU.is_ge, fill=0.0,
                                               base=0, channel_multiplier=1), reads=["masks"], writes=["masks"])
        S.op("pool", lambda e: e.affine_select(out=blk3, in_=blk3, pattern=[[4, 16], [0, 4]], compare_op=ALU.is_ge, fill=0.0,
                                               base=3, channel_multiplier=-1), reads=["masks"], writes=["masks"])
        for ms, m in ((ms_su, m_su), (ms_u, m_u), (ms_sl, m_sl)):
            TT(ms[:], m[:], blk[:], MULT, ["masks"], ["masks"], eng="pool")
        S.op("pool", lambda e: e.memset(rowsel[:], 1.0), writes=["masks"])
        S.op("pool", lambda e: e.affine_select(out=rowsel[:], in_=rowsel[:], pattern=[[-4, 16]], compare_op=ALU.is_ge, fill=0.0,
                                               base=0, channel_multiplier=1), reads=["masks"], writes=["masks"])
        S.op("pool", lambda e: e.affine_select(out=rowsel[:], in_=rowsel[:], pattern=[[4, 16]], compare_op=ALU.is_ge, fill=0.0,
                                               base=3, channel_multiplier=-1), reads=["masks"], writes=["masks"])
        for t_ in (shiftst, Mst, convhalo, ffnhalo):
            S.op("pool", lambda e: e.memset(t_[:], 0.0), writes=["state"])
        S.op("pool", lambda e: e.memset(smpreg[:], 0.0), writes=["smpreg"])
        for c in range(4):
            S.op("pool", lambda e: e.memset(CS[c]["TLz"], 0.0), reads=["smpreg"], writes=[f"TLz{c}"])
            S.op("pool", lambda e: e.memset(CS[c]["TRz"], 0.0), reads=["smpreg"], writes=[f"TRz{c}"])

        S.dma(flg[:], flag, writes=["flg"])
        S.op("pool", lambda e: e.memset(PV["r"][:], 0.0), writes=["r"])
        for i in range(NPROW // 128):
            S.dma(ptile[:], params[128 * i:128 * i + 128, :], writes=["ptile"])
            ps_, pk = PSL()
            TRP(ps_[:, 0:128], ptile[:], ["ptile"], [pk])
            CP(PT[:, 128 * i:128 * i + 128], ps_[:, 0:128], [pk], ["PT"])
        TSC(omu[:], ptr("mu", 0, 96), -1.0, 1.0, MULT, ADD, ["PT"], ["PT2"])
        TSC(omka[:], ptr("k_a", 0, 16), -1.0, 1.0, MULT, ADD, ["PT"], ["PT2"])

        S.dma(cvt, cvec, writes=["xtile"])
        ACT(cvt, cvt, AF.Silu, ["xtile"], ["xtile"])
        for kc in range(16):
            ps_, pk = PSL()
            TRP(ps_[:, 0:17], cvt[:, 128 * kc:128 * kc + 128], ["xtile"], [pk])
            CP(cT[:, kc, :], ps_[:, 0:17], [pk], ["cT"])
        for m in range(96):
            wv, wk = WLOAD(kview(w_ada)[:, :, 128 * m:128 * m + 128], (128, 16, 128))
            ps_, pk = PSL()
            for kc in range(16):
                MM(ps_[:, 0:17], wv[:, kc, :], cT[:, kc, :], kc == 0, kc == 15, [wk, "cT"], [pk])
            TSC(MT[:, m, :], ps_[:, 0:17], pt("b_ada", m), None, ADD, None, [pk, "PT"], ["mixact"])
        for (ti, mi, kind) in ((0, 1, "gs1"), (1, 0, "sh"), (2, 2, "gt"), (3, 4, "gs2"), (4, 3, "sh"), (5, 5, "gt")):
            src_p = MT[:, 16 * mi:16 * mi + 16, 0]
            src_s = MT[:, 16 * mi:16 * mi + 16, 1:17]
            if kind.startswith("gs"):
                g = ptr("norm1_g" if kind == "gs1" else "norm2_g", 0, 16)
                STT(TP[:, ti, :], src_p, 1.0, g, ADD, MULT, ["mixact", "PT"], ["TP"])
                STT(TS[:, ti, :, :], src_s, 1.0, g.unsqueeze(2).to_broadcast([128, 16, 16]), ADD, MULT, ["mixact", "PT"], ["TS"])
            else:
                CP(TP[:, ti, :], src_p, ["mixact"], ["TP"])
                CP(TS[:, ti, :, :], src_s, ["mixact"], ["TS"])

        def emit_block(kind, bi):
            smp = kind == "s"
            pre = kind == "q"
            msk = pre or (kind == "p" and bi == 0)
            T = 64 if smp else TB
            nch = T // 64
            bc = lambda tab16: tab16.unsqueeze(2).to_broadcast([128, 16, T])
            v4 = lambda ap: ap.rearrange("p k (s t) -> p k s t", t=4)
            last = kind == "p" and bi == n_pblocks - 1

            xsrc_rows = 64 if smp else 128
            for ti_ in range(max(1, T // 128)):
                xsrc = xs if smp else (xq if pre else xp)[TB * bi + 128 * ti_:TB * bi + 128 * ti_ + 128, :]
                S.dma(xtile[0:xsrc_rows, :], xsrc, writes=["xtile"])
                R = xsrc_rows
                for q in range(4):
                    ps_, pk0 = PSL()
                    for j in range(4):
                        kc = 4 * q + j
                        TRP(ps_[:, j * R:(j + 1) * R], xtile[0:R, 128 * kc:128 * kc + 128], ["xtile"], [pk0])
                    CP(xT[:, 4 * q:4 * q + 4, 128 * ti_:128 * ti_ + R], ps_[:, 0:4 * R].rearrange("p (a t) -> p a t", a=4),
                       [pk0], ["xT"], eng="act")
            stage(1)

            def rms(src, key, out_rstd):
                TT(dx[:, :, 0:T], src[:, :, 0:T], src[:, :, 0:T], MULT, [key], ["dx"])
                ps_, pk = PSL()
                for kc in range(16):
                    MM(ps_[:, 0:T], ones_bf[:], dx[:, kc, 0:T], kc == 0, kc == 15, ["ones", "dx"], [pk])
                RSQRT(out_rstd[:, 0:T], ps_[:, 0:T], [pk], ["rstd"], scale=1.0 / D, eps=1e-6)

            def modnorm(src, skey, rs, tg, tsft):
                d = hf[:, :, 0:T]
                TT(d, src[:, :, 0:T], rs[:, 0:T].unsqueeze(1).to_broadcast([128, 16, T]), MULT, [skey, "rstd"], ["mixact"])
                if smp:
                    for (tix, op) in ((tg, MULT), (tsft, ADD)):
                        TT(v4(d), v4(d), TS[:, tix, :, :].unsqueeze(3).to_broadcast([128, 16, 16, 4]), op, ["mixact", "TS"], ["mixact"])
                else:
                    TT(d, d, bc(TP[:, tg, :]), MULT, ["mixact", "TP"], ["mixact"])
                    TT(d, d, bc(TP[:, tsft, :]), ADD, ["mixact", "TP"], ["mixact"])
                    if msk:
                        TSC(d, d, flg[:, 0:1], None, MULT, None, ["mixact", "flg"], ["mixact"])

            rms(xT, "xT", rstd)
            modnorm(xT, "xT", rstd, 0, 1)
            CP(h[:, :, 0:T], hf[:, :, 0:T], ["mixact"], ["h"], eng="act")
            stage(2)
            if smp:
                S.dma(otile[0:16, :], st_shift, writes=["xtile"])
                h4 = v4(hf[:, :, 0:64])
                d4 = v4(dx[:, :, 0:64])
                for q in range(4):
                    ps_, pk = PSL()
                    for j in range(4):
                        TRP(ps_[:, 16 * j:16 * j + 16], otile[0:16, 128 * (4 * q + j):128 * (4 * q + j) + 128], ["xtile"], [pk])
                    TT(d4[:, 4 * q:4 * q + 4, :, 0], ps_[:, 0:64].rearrange("p (a s) -> p a s", a=4), h4[:, 4 * q:4 * q + 4, :, 0],
                       SUB, [pk, "mixact"], ["dx"])
                TT(d4[:, :, :, 1:4], h4[:, :, :, 0:3], h4[:, :, :, 1:4], SUB, ["mixact"], ["dx"])
                for kc in range(16):
                    ps_, pk = PSL()
                    TRP(ps_[0:16, 0:128], h4[:, kc, :, 3], ["mixact"], [pk])
                    CP(otile[0:16, 128 * kc:128 * kc + 128], ps_[0:16, 0:128], [pk], ["xtile"])
                S.dma(shift_s, otile[0:16, :], reads=["xtile"])
            else:
                TT(dx[:, :, 0], shiftst[:], hf[:, :, 0], SUB, ["state", "mixact"], ["dx"])
                TT(dx[:, :, 1:T], hf[:, :, 0:T - 1], hf[:, :, 1:T], SUB, ["mixact"], ["dx"])
                CP(shiftst[:], hf[:, :, T - 1], ["mixact"], ["state"])
                if last:
                    ps_, pk = PSL()
                    TRP(ps_[0:16, 0:128], shiftst[:], ["state"], [pk])
                    CP(otile[0:16, 0:128], ps_[0:16, 0:128], [pk], ["xtile"])
                    S.dma(shift_p, otile[0:16, 0:128], reads=["xtile"])
            stage(3)

            def mix(dst, dkey, mi):
                TT(dst, dx[:, :, 0:T], bc(ptr("mu", 16 * mi, 16)), MULT, ["dx", "PT"], [dkey])
                TT(dst, dst, h[:, :, 0:T], ADD, [dkey, "h"], [dkey])

            xmix = [mixbf[:, 16 * TB * i:16 * TB * (i + 1)].rearrange("p (k t) -> p k t", t=TB)[:, :, 0:T] for i in range(3)]
            tmpx = zc[:, :, 0:T]
            lora1 = [(w1, 96, 1, [PV["tw"]], AF.Tanh), (a1, 96, 4, [PV["xa1"]], AF.Copy)]
            if not pre:
                lora1.append((g1, 256, 5, [PV["sg0"], PV["sg1"]], AF.Sigmoid))
            for (W, ncol, mi, outs, func) in lora1:
                mix(tmpx, "zc", mi)
                for oi, o in enumerate(outs):
                    n = min(128, ncol - 128 * oi)
                    wv, wk = WLOAD(kview(W)[:, :, 128 * oi:128 * oi + n], (128, 16, n))
                    ps_, pk = PSL()
                    for kc in range(16):
                        MM(ps_[0:n, 0:T], wv[:, kc, :], tmpx[:, kc, :], kc == 0, kc == 15, [wk, "zc"], [pk])
                    ACT(o[0:n, 0:T], ps_[0:n, 0:T], func, [pk], ["lora"])
            if not pre:
                mix(xmix[0], "mixact", 0)
            mix(xmix[1], "mixact", 2)
            mix(xmix[2], "mixact", 3)
            stage(4)

            for p in range(NHP):
                def proj(src, skey, col0, out, func, okey, bias=0.0):
                    wv, wk = WLOAD(kview(w_in)[:, :, col0 + 128 * p:col0 + 128 * p + 128], (128, 16, 128))
                    ps_, pk = PSL()
                    for kc in range(16):
                        MM(ps_[:, 0:T], wv[:, kc, :], src[:, kc, :], kc == 0, kc == 15, [wk, skey], [pk])
                    ACT(out[:, 0:T], ps_[:, 0:T], func, [pk, "PT"], [okey], bias=bias)

                hh = h[:, :, 0:T]
                if not pre:
                    proj(xmix[0], "mixact", 0, PV["r"], AF.Copy, "r")
                proj(xmix[1], "mixact", D, PV["k0"], AF.Copy, "k0")
                proj(xmix[2], "mixact", 2 * D, PV["v"], AF.Copy, "v")
                if not pre:
                    proj(hh, "h", 3 * D, PV["glu"], AF.Identity, "glu", bias=pt("b_glu", p))
                    proj(hh, "h", 4 * D, PV["glb"], AF.Sigmoid, "glb", bias=pt("b_glu", 16 + p))
                    proj(hh, "h", 5 * D, PV["sga"], AF.Sigmoid, "sga")
                lora2 = [(w2, 96, [PV["tw"]], PV["lw"], AF.Sigmoid, pt("w0", p)),
                         (a2, 96, [PV["xa1"]], PV["a"], AF.Sigmoid, pt("a0", p))]
                if not pre:
                    lora2.append((g2, 256, [PV["sg0"], PV["sg1"]], PV["g"], AF.Copy, 0.0))
                for (W, K, srcs, out, func, bias) in lora2:
                    nk = len(srcs)
                    kp = min(K, 128)
                    wv, wk = WLOAD(W.rearrange("(kc ki) n -> ki kc n", ki=kp)[:, :, 128 * p:128 * p + 128], (kp, nk, 128))
                    ps_, pk = PSL()
                    for j in range(nk):
                        MM(ps_[:, 0:T], wv[:, j, :], srcs[j][0:kp, 0:T], j == 0, j == nk - 1, [wk, "lora"], [pk])
                    ACT(out[:, 0:T], ps_[:, 0:T], func, [pk, "PT"], ["lwag"], bias=bias)
                stage(5)
                V = {n: PV[n][:, 0:T] for n in PV}
                if not pre:
                    TT(V["glu"], V["glu"], V["glb"], MULT, ["glu", "glb"], ["glu"])
                    if msk:
                        TSC(V["glu"], V["glu"], flg[:, 0:1], None, MULT, None, ["glu", "flg"], ["glu"])
                    for j in range(31):
                        TSC(diag[:, j, :], ident[:], pt("dw_k", 16 * j + p), None, MULT, None, ["ident", "PT"], ["diag"])
                    ps_, pk = PSL()
                    if smp:
                        S.dma(otile[0:120, 0:512].rearrange("r (q c) -> r q c", q=4),
                              st_conv[:, :, 128 * p:128 * p + 128].rearrange("(q s) t c -> (s t) q c", q=4), writes=["xtile"])
                        for q in range(4):
                            pq, pqk = PSL()
                            TRP(pq[:, 0:120], otile[0:120, 128 * q:128 * q + 128], ["xtile"], [pqk])
                            CP(zsbf[:, 4 * q:4 * q + 4, 0:30], pq[:, 0:120].rearrange("p (s t) -> p s t", t=30), [pqk], ["zsbf"])
                        CP(zsbf[:, :, 30:34], V["glu"].rearrange("p (s t) -> p s t", t=4), ["glu"], ["zsbf"])
                        for j in range(31):
                            MM(ps_[:, 0:64], diag[:, j, :], zsbf[:, :, j:j + 4], j == 0, j == 30, ["diag", "zsbf"], [pk])
                        pq, pqk = PSL()
                        CP(PV["t4"][:, 0:64].rearrange("p (t s) -> p t s", t=4), V["glu"].rearrange("p (s t) -> p t s", t=4), ["glu"], ["t4"])
                        TRP(pq[0:64, 0:128], PV["t4"][:, 0:64], ["t4"], [pqk])
                        CP(Ytok[:, :], pq[0:64, 0:128], [pqk], ["Ytok"])
                        for t in range(4):
                            S.dma(conv_s[:, 26 + t, 128 * p:128 * p + 128], Ytok[16 * t:16 * t + 16, :], reads=["Ytok"])
                    else:
                        CP(zext[:, 0:30], convhalo[:, p, :], ["state"], ["zext"])
                        CP(zext[:, 30:30 + T], V["glu"], ["glu"], ["zext"], eng="act")
                        for j in range(31):
                            MM(ps_[:, 0:T], diag[:, j, :], zext[:, j:j + T], j == 0, j == 30, ["diag", "zext"], [pk])
                        CP(convhalo[:, p, :], zext[:, T:T + 30], ["zext"], ["state"])
                        if last:
                            pq, pqk = PSL()
                            TRP(pq[0:30, 0:128], PV["glu"][:, T - 30:T], ["glu"], [pqk])
                            CP(Ytok[0:30, :], pq[0:30, 0:128], [pqk], ["Ytok"])
                            S.dma(conv_p[:, 128 * p:128 * p + 128], Ytok[0:30, :], reads=["Ytok"])
                    TSC(zc[:, p, 0:T], ps_[:, 0:T], pt("dw_b", p), None, ADD, None, [pk, "PT"], [("zcp", p)])

                stage(6)
                TSC(V["kk"], V["k0"], pt("k_k", p), None, MULT, None, ["k0", "PT"], ["kk"])
                TT(V["t2"], V["kk"], V["kk"], MULT, ["kk"], ["t2"])
                ps_, pk = PSL()
                MM(ps_[:, 0:T], bones[:], V["t2"], True, True, ["bones", "t2"], [pk])
                RSQRT(V["t3"], ps_[:, 0:T], [pk], ["t3"], eps=1e-30)
                TT(V["kkn"], V["kk"], V["t3"], MULT, ["kk", "t3"], ["kkn"])
                TSC(V["t1"], V["a"], pt("k_a", p), omka[:, p:p + 1], MULT, ADD, ["lwag", "PT", "PT2"], ["t1"])
                TT(V["k"], V["k0"], V["t1"], MULT, ["k0", "t1"], ["k"])
                TT(V["b"], V["kkn"], V["a"], MULT, ["kkn", "lwag"], ["b"])
                if not pre:
                    STT(V["rk"], V["r"], pt("r_k", p), V["k"], MULT, MULT, ["r", "k", "PT"], ["rk"])
                    ps_, pk = PSL()
                    MM(ps_[:, 0:T], bones[:], V["rk"], True, True, ["bones", "rk"], [pk])
                    TT(V["bonus"], ps_[:, 0:T], V["v"], MULT, [pk, "v"], ["bonus"])
                stage(7)
                L = 4 if smp else 64
                nseg = T // L
                S.op("dve", lambda e: e.tensor_tensor_scan(out=V["cs"], data0=ones[:, 0:T], data1=V["lw"], initial=0.0,
                                                           op0=MULT, op1=ADD), reads=["ones", "lwag"], writes=["cs"])
                TT(V["cx"], V["cs"], V["lw"], SUB, ["cs", "lwag"], ["cx"])
                cs3 = V["cs"].rearrange("p (s l) -> p s l", l=L)
                cx3 = V["cx"].rearrange("p (s l) -> p s l", l=L)
                CP(PV["t4"][:, 0:nseg], cx3[:, :, 0], ["cx"], ["t4"])
                base = PV["t4"][:, 0:nseg].unsqueeze(2).to_broadcast([128, nseg, L])
                TT(cs3, cs3, base, SUB, ["cs", "t4"], ["cs"])
                TT(cx3, cx3, base, SUB, ["cx", "t4"], ["cx"])
                ACT(V["eW"], V["cs"], AF.Exp, ["cs"], ["eW"], scale=-C0)
                ACT(V["eWi"], V["cs"], AF.Exp, ["cs"], ["eWi"], scale=C0)
                ACT(V["eWp"], V["cx"], AF.Exp, ["cx"], ["eWp"], scale=-C0)

                stage(8)
                msu, mu_, msl = (ms_su, ms_u, ms_sl) if smp else (m_su, m_u, m_sl)
                b2 = lambda m: m[:].unsqueeze(1).to_broadcast([64, 2, 64])
                h2v = lambda ap: ap.rearrange("p (h n) -> p h n", h=2)
                K_ = lambda n, c: f"{n}{c}"

                def bankA(c):
                    return pss[2 * c][:, :], ("ps", 2 * c)

                def bankB(c):
                    return pss[2 * c + 1][:, :], ("ps", 2 * c + 1)

                def P0(c):
                    B = CS[c]
                    cs_ = slice(64 * c, 64 * c + 64)
                    TT(B["TL"][:, 0, :], PV["kkn"][:, cs_], PV["eWp"][:, cs_], MULT, ["kkn", "eWp"], [K_("TL", c)])
                    TT(B["TL"][:, 1, :], PV["r"][:, cs_], PV["eW"][:, cs_], MULT, ["r", "eW"], [K_("TL", c)])
                    TT(B["TR"][:, 0, :], PV["b"][:, cs_], PV["eWi"][:, cs_], MULT, ["b", "eWi"], [K_("TR", c)])
                    TT(B["TR"][:, 1, :], PV["k"][:, cs_], PV["eWi"][:, cs_], MULT, ["k", "eWi"], [K_("TR", c)])
                    for hj in range(2):
                        hs = slice(64 * hj, 64 * hj + 64)
                        CP(B["TLz"][hs, hj, :, :], B["TL"][hs, :, :], [K_("TL", c)], [K_("TLz", c)], eng="act")
                        CP(B["TRz"][hs, hj, :, :], B["TR"][hs, :, :], [K_("TR", c)], [K_("TRz", c)], eng="act")

                def P1(c):
                    B = CS[c]
                    A, ak = bankA(c)
                    Bk_, bk = bankB(c)
                    for hj in range(2):
                        TLh = B["TLz"][:, hj, :, :].rearrange("p a t -> p (a t)")
                        MM(A[0:64, 128 * hj:128 * hj + 128], B["TRz"][:, hj, 0, :], TLh, True, True, [K_("TRz", c), K_("TLz", c)], [ak])
                        MM(Bk_[0:64, 128 * hj:128 * hj + 128], B["TRz"][:, hj, 1, :], TLh, True, True, [K_("TRz", c), K_("TLz", c)], [bk])
                        MM(A[0:64, 256 + 64 * hj:256 + 64 * hj + 64], B["TLz"][:, hj, 0, :], B["TRz"][:, hj, 0, :], True, True,
                           [K_("TRz", c), K_("TLz", c)], [ak])

                def P2(c):
                    B = CS[c]
                    A, ak = bankA(c)
                    Bk_, bk = bankB(c)
                    pa3 = h2v(A[0:64, 0:256]); pb3 = h2v(Bk_[0:64, 0:256]); pc3 = h2v(A[0:64, 256:384])
                    pz = B["PZ"][0]
                    STT(pz[:, :, 1, :], pa3[:, :, 0:64], -1.0, b2(msu), MULT, MULT, [ak, "masks"], [K_("pz0_", c)])
                    STT(B["NRB"], pa3[:, :, 64:128], -1.0, b2(mu_), MULT, MULT, [ak, "masks"], [K_("NRB", c)])
                    STT(B["ZTa"], pc3, -1.0, b2(msl), MULT, MULT, [ak, "masks"], [K_("ZTa", c)])
                    TT(B["LKT"], pb3[:, :, 0:64], b2(msu), MULT, [bk, "masks"], [K_("LKT", c)])
                    TT(B["RKT"], pb3[:, :, 64:128], b2(mu_), MULT, [bk, "masks"], [K_("RKT", c)])
                    TT(pz[:, :, 0, :], pz[:, :, 1, :], ident[0:64, 0:64].unsqueeze(1).to_broadcast([64, 2, 64]), ADD,
                       [K_("pz0_", c), "ident"], [K_("pz0_", c)])

                def P3(c):
                    B = CS[c]
                    cs_ = slice(64 * c, 64 * c + 64)
                    A, ak = bankA(c)
                    TRP(A[0:64, 0:128], PV["v"][:, cs_], ["v"], [ak])
                    TRP(A[0:64, 128:256], B["TR"][:, 0, :], [K_("TR", c)], [ak])
                    TRP(A[0:64, 256:384], B["TR"][:, 1, :], [K_("TR", c)], [ak])
                    CP(B["Vtok"], A[0:64, 0:128], [ak], [K_("Vtok", c)], eng="act")
                    S.op("act", lambda e: e.mul(out=B["nbtok"], in_=A[0:64, 128:256], mul=-1.0), reads=[ak], writes=[K_("nbtok", c)])
                    CP(B["ktok"], A[0:64, 256:384], [ak], [K_("ktok", c)], eng="act")

                nstate = {}

                def NM(j):
                    def f(c):
                        B = CS[c]
                        cur, zt_cur = nstate.get(c, (0, "ZTa"))
                        pzc = B["PZ"][cur]
                        kc_ = K_(f"pz{cur}_", c)
                        A, ak = bankA(c)
                        Bk_, bk = bankB(c)
                        ztk = K_(zt_cur, c)
                        for hj in range(2):
                            if j == 0:
                                MM(A[0:64, 128 * hj + 64:128 * hj + 128], B[zt_cur][:, hj, :], pzc[:, hj, 1, :], True, True, [ztk, kc_], [ak])
                            elif j == 5:
                                MM(A[0:64, 128 * hj:128 * hj + 64], B[zt_cur][:, hj, :], pzc[:, hj, 0, :], True, True, [ztk, kc_], [ak])
                            else:
                                MM(A[0:64, 128 * hj:128 * hj + 128], B[zt_cur][:, hj, :], pzc[:, hj, :, :].rearrange("p a t -> p (a t)"),
                                   True, True, [ztk, kc_], [ak])
                            if j != 5:
                                MM(Bk_[0:64, 64 * hj:64 * hj + 64], pzc[:, hj, 1, :], B[zt_cur][:, hj, :], True, True, [ztk, kc_], [bk])
                    return f

                def NE(j):
                    def f(c):
                        B = CS[c]
                        cur, zt_cur = nstate.get(c, (0, "ZTa"))
                        zt_nxt = "ZTb" if zt_cur == "ZTa" else "ZTa"
                        pzc, pzn = B["PZ"][cur], B["PZ"][1 - cur]
                        kc_, kn_ = K_(f"pz{cur}_", c), K_(f"pz{1 - cur}_", c)
                        A, ak = bankA(c)
                        Bk_, bk = bankB(c)
                        q13 = h2v(A[0:64, 0:256])
                        if j == 0:
                            CP(pzn[:, :, 0, :], pzc[:, :, 0, :], [kc_], [kn_])
                        else:
                            TT(pzn[:, :, 0, :], pzc[:, :, 0, :], q13[:, :, 0:64], ADD, [kc_, ak], [kn_])
                        if j != 5:
                            CP(pzn[:, :, 1, :], q13[:, :, 64:128], [ak], [kn_])
                            CP(B[zt_nxt], h2v(Bk_[0:64, 0:128]), [bk], [K_(zt_nxt, c)], eng="act")
                        nstate[c] = (1 - cur, zt_nxt)
                    return f

                par_steps = [P0, P1, P2, P3]
                for j in range(6):
                    par_steps += [NM(j), NE(j)]
                for step in par_steps:
                    for c in range(nch):
                        step(c)
                stage(11)
                if smp:
                    B = CS[0]
                    for s in range(NS):
                        S.dma(Stmp[:, :].rearrange("v (h k) -> v h k", h=2),
                              st_wkv[s, 2 * p:2 * p + 2, :, :].rearrange("h v k -> v h k"), writes=["Stmp"])
                        pq, pqk = PSL()
                        TRP(pq[:, 0:64], Stmp[:, :], ["Stmp"], [pqk])
                        for hj in range(2):
                            hs = slice(64 * hj, 64 * hj + 64)
                            CP(Msz[hs, hj, s, :], pq[hs, 0:64], [pqk], ["Ms"], eng="act")
                    TT(KKX, B["TL"][:, 0, :].unsqueeze(1).to_broadcast([128, 16, 64]), selm, MULT, ["TL0", "selm"], ["KKX"])
                    TT(RX, B["TL"][:, 1, :].unsqueeze(1).to_broadcast([128, 16, 64]), selm, MULT, ["TL0", "selm"], ["RX"])
                for c in range(nch):
                    B = CS[c]
                    cs_ = slice(64 * c, 64 * c + 64)
                    cur, _zt = nstate[c]
                    PTt, ptk = B["PZ"][cur], K_(f"pz{cur}_", c)
                    px, pxk = PSL()
                    px2, pxk2 = PSL()
                    py, pyk = PSL()
                    for hj in range(2):
                        hs = slice(64 * hj, 64 * hj + 64)
                        o_ = px[0:64, 64 * hj:64 * hj + 64]
                        oy = py[0:64, 64 * hj:64 * hj + 64]
                        if smp:
                            for s in range(NS):
                                MM(o_, KKX[:, s, :], Msz[:, hj, s, :], s == 0, s == NS - 1, ["KKX", "Ms"], [pxk])
                            for s in range(NS):
                                MM(oy, RX[:, s, :], Msz[:, hj, s, :], s == 0, s == NS - 1, ["RX", "Ms"], [pyk])
                        else:
                            MM(o_, B["TLz"][:, hj, 0, :], Mst[:, p, :], True, True, [K_("TLz", c), "state"], [pxk])
                            MM(oy, B["TLz"][:, hj, 1, :], Mst[:, p, :], True, True, [K_("TLz", c), "state"], [pyk])
                        MM(px2[0:64, 64 * hj:64 * hj + 64], B["LKT"][:, hj, :], B["Vtok"][:, hs], True, True,
                           [K_("LKT", c), K_("Vtok", c)], [pxk2])
                    CP(B["X"], h2v(px[0:64, 0:128]), [pxk], [K_("X", c)])
                    TT(B["X"], B["X"], h2v(px2[0:64, 0:128]), ADD, [K_("X", c), pxk2], [K_("X", c)])
                    CP(B["Ytok"], py[0:64, 0:128], [pyk], [K_("Ytok", c)], eng="act")
                    pu, puk = PSL()
                    for hj in range(2):
                        MM(pu[0:64, 64 * hj:64 * hj + 64], PTt[:, hj, 0, :], B["X"][:, hj, :], True, True, [ptk, K_("X", c)], [puk])
                    CP(B["U"], h2v(pu[0:64, 0:128]), [puk], [K_("U", c)])
                    stage(12)
                    if smp:
                        TT(NBX, B["nbtok"].unsqueeze(1).to_broadcast([64, 16, 128]),
                           rowsel[:].unsqueeze(2).to_broadcast([64, 16, 128]), MULT, ["nbtok0", "masks"], ["NBX"])
                        TT(KX, B["ktok"].unsqueeze(1).to_broadcast([64, 16, 128]),
                           rowsel[:].unsqueeze(2).to_broadcast([64, 16, 128]), MULT, ["ktok0", "masks"], ["KX"])
                        for s in range(NS):
                            pm, pmk = PSL()
                            for hj in range(2):
                                hs = slice(64 * hj, 64 * hj + 64)
                                o_ = pm[:, 64 * hj:64 * hj + 64]
                                MM(o_, NBX[:, s, :], B["U"][:, hj, :], True, False, ["NBX", "U0"], [pmk])
                                MM(o_, KX[:, s, :], B["Vtok"][:, hs], False, True, ["KX", "Vtok0"], [pmk])
                            for hj in range(2):
                                hs = slice(64 * hj, 64 * hj + 64)
                                TT(Msz[hs, hj, s, :], Msz[hs, hj, s, :], pm[hs, 64 * hj:64 * hj + 64], ADD, ["Ms", pmk], ["Ms"])
                            TSC(Msz[:, :, s, :], Msz[:, :, s, :], PV["eW"][:, 4 * s + 3:4 * s + 4], None, MULT, None, ["Ms", "eW"], ["Ms"])
                            TT(PV["t4"][:, 0:64], Msz[:, 0, s, :], Msz[:, 1, s, :], ADD, ["Ms"], ["t4"])
                            pq, pqk = PSL()
                            TRP(pq[0:64, 0:128], PV["t4"][:, 0:64], ["t4"], [pqk])
                            CP(Sout[:], pq[0:64, 0:128], [pqk], ["Sout"], eng="act")
                            S.dma(wkv_s[s, 2 * p:2 * p + 2, :, :].rearrange("h v k -> v h k"),
                                  Sout[:, :].rearrange("v (h k) -> v h k", h=2), reads=["Sout"])
                    else:
                        pm, pmk = PSL()
                        for hj in range(2):
                            hs = slice(64 * hj, 64 * hj + 64)
                            o_ = pm[:, 64 * hj:64 * hj + 64]
                            MM(o_, B["nbtok"], B["U"][:, hj, :], True, False, [K_("nbtok", c), K_("U", c)], [pmk])
                            MM(o_, B["ktok"], B["Vtok"][:, hs], False, True, [K_("ktok", c), K_("Vtok", c)], [pmk])
                        for hj in range(2):
                            hs = slice(64 * hj, 64 * hj + 64)
                            TT(Mst[hs, p, :], Mst[hs, p, :], pm[hs, 64 * hj:64 * hj + 64], ADD, ["state", pmk], ["state"])
                        TSC(Mst[:, p, :], Mst[:, p, :], PV["eW"][:, 64 * c + 63:64 * c + 64], None, MULT, None, ["state", "eW"], ["state"])
                    if not pre:
                        py2, pyk2 = PSL()
                        for hj in range(2):
                            hs = slice(64 * hj, 64 * hj + 64)
                            o2_ = py2[0:64, 64 * hj:64 * hj + 64]
                            MM(o2_, B["NRB"][:, hj, :], B["U"][:, hj, :], True, False, [K_("NRB", c), K_("U", c)], [pyk2])
                            MM(o2_, B["RKT"][:, hj, :], B["Vtok"][:, hs], False, True, [K_("RKT", c), K_("Vtok", c)], [pyk2])
                        TT(B["Ytok"], B["Ytok"], py2[0:64, 0:128], ADD, [K_("Ytok", c), pyk2], [K_("Ytok", c)])
                        pt_, ptk2 = PSL()
                        TRP(pt_[:, 0:64], B["Ytok"], [K_("Ytok", c)], [ptk2])
                        CP(PV["y"][:, cs_], pt_[:, 0:64], [ptk2], ["y"], eng="act")
                if last:
                    pq, pqk = PSL()
                    TRP(pq[0:64, 0:128], Mst[:, p, :], ["state"], [pqk])
                    CP(Sout[:], pq[0:64, 0:128], [pqk], ["Sout"], eng="act")
                    S.dma(wkv_p[2 * p:2 * p + 2, :, :].rearrange("h v k -> v h k"),
                          Sout[:, :].rearrange("v (h k) -> v h k", h=2), reads=["Sout"])
                if not pre:
                    stage(13)
                    ps_, pk = PSL()
                    MM(ps_[:, 0:T], bones[:], V["y"], True, True, ["bones", "y"], [pk])
                    STT(V["t2"], ps_[:, 0:T], -1.0 / 64, V["y"], MULT, ADD, [pk, "y"], ["t2"])
                    TT(V["t3"], V["t2"], V["t2"], MULT, ["t2"], ["t3"])
                    ps_, pk = PSL()
                    MM(ps_[:, 0:T], bones[:], V["t3"], True, True, ["bones", "t3"], [pk])
                    RSQRT(V["t3"], ps_[:, 0:T], [pk], ["t3"], scale=1.0 / 64, eps=GN_EPS)
                    TT(V["t2"], V["t2"], V["t3"], MULT, ["t2", "t3"], ["t2"])
                    TSC(V["t2"], V["t2"], pt("lnx_g", p), pt("lnx_b", p), MULT, ADD, ["t2", "PT"], ["t2"])
                    TT(V["t2"], V["t2"], V["bonus"], ADD, ["t2", "bonus"], ["t2"])
                    TT(V["t2"], V["t2"], V["g"], MULT, ["t2", "lwag"], ["t2"])
                    TT(merged[:, p, 0:T], V["t2"], V["sga"], MULT, ["t2", "sga"], [("mg", p)])

            if not pre:
                stage(14)
                allzc = [("zcp", p) for p in range(16)]
                ps1, pk1 = PSL()
                for kc in range(16):
                    MM(ps1[:, 0:T], ones_bf[:], zc[:, kc, 0:T], kc == 0, kc == 15, ["ones"] + allzc, [pk1])
                TSC(rstd[:, 0:T], ps1[:, 0:T], -1.0 / D, None, MULT, None, [pk1], ["rstd"])
                TT(dx[:, :, 0:T], zc[:, :, 0:T], zc[:, :, 0:T], MULT, allzc, ["dx"])
                ps1, pk1 = PSL()
                for kc in range(16):
                    MM(ps1[:, 0:T], ones_bf[:], dx[:, kc, 0:T], kc == 0, kc == 15, ["ones", "dx"], [pk1])
                TT(rstd2[:, 0:T], rstd[:, 0:T], rstd[:, 0:T], MULT, ["rstd"], ["rstd2"])
                STT(rstd2[:, 0:T], ps1[:, 0:T], 1.0 / D, rstd2[:, 0:T], MULT, SUB, [pk1, "rstd2"], ["rstd2"])
                RSQRT(rstd2[:, 0:T], rstd2[:, 0:T], ["rstd2"], ["rstd2"], eps=1e-5)
                for p in range(NHP):
                    wv, wk = WLOAD(kview(w_in)[:, :, 6 * D + 128 * p:6 * D + 128 * p + 128], (128, 16, 128))
                    ps_, pk = PSL()
                    for kc in range(16):
                        MM(ps_[:, 0:T], wv[:, kc, :], h[:, kc, 0:T], kc == 0, kc == 15, [wk, "h"], [pk])
                    ACT(PV["gb"][:, 0:T], ps_[:, 0:T], AF.Sigmoid, [pk], ["gb"])
                    t2 = PV["t2"][:, 0:T]
                    TT(t2, zc[:, p, 0:T], rstd[:, 0:T], ADD, allzc + ["rstd"], ["t2"])
                    TT(t2, t2, rstd2[:, 0:T], MULT, ["t2", "rstd2"], ["t2"])
                    TSC(t2, t2, pt("ln_conv_g", p), pt("ln_conv_b", p), MULT, ADD, ["t2", "PT"], ["t2"])
                    ACT(t2, t2, AF.Silu, ["t2"], ["t2"])
                    TT(t2, t2, PV["gb"][:, 0:T], MULT, ["t2", "gb"], ["t2"])
                    TT(merged[:, p, 0:T], merged[:, p, 0:T], t2, ADD, [("mg", p), "t2"], [("mg", p)])
                allmg = [("mg", p) for p in range(16)]
                stage(15)
                for m in range(16):
                    wv, wk = WLOAD(kview(w_out)[:, :, 128 * m:128 * m + 128], (128, 16, 128))
                    ps_, pk = PSL()
                    for kc in range(16):
                        MM(ps_[:, 0:T], wv[:, kc, :], merged[:, kc, 0:T], kc == 0, kc == 15, [wk] + allmg, [pk])
                    if smp:
                        t23 = PV["t2"][:, 0:T].rearrange("p (s t) -> p s t", t=4)
                        TT(t23, ps_[:, 0:T].rearrange("p (s t) -> p s t", t=4), TS[:, 2, m, :].unsqueeze(2).to_broadcast([128, 16, 4]),
                           MULT, [pk, "TS"], ["t2"])
                        TT(xT[:, m, 0:T], xT[:, m, 0:T], PV["t2"][:, 0:T], ADD, ["xT", "t2"], ["xT"])
                    else:
                        STT(xT[:, m, 0:T], ps_[:, 0:T], TP[:, 2, m:m + 1], xT[:, m, 0:T], MULT, ADD, [pk, "TP", "xT"], ["xT"])
                stage(16)
                rms(xT, "xT", rstd)
                modnorm(xT, "xT", rstd, 3, 4)
                CP(h[:, :, 0:T], hf[:, :, 0:T], ["mixact"], ["h"], eng="act")
                act = mixbf[:, 0:NFC * TB].rearrange("p (f t) -> p f t", t=TB)
                for f in range(NFC):
                    for half in range(2):
                        ch = f + NFC * half
                        wv, wk = WLOAD(kview(w_up)[:, :, 128 * ch:128 * ch + 128], (128, 16, 128))
                        ps_, pk = PSL()
                        for kc in range(16):
                            MM(ps_[:, 0:T], wv[:, kc, :], h[:, kc, 0:T], kc == 0, kc == 15, [wk, "h"], [pk])
                        k0_, k1_, k2_ = (pt("ffn_dw_k", 86 * j + ch) for j in range(3))
                        bb = pt("ffn_dw_b", ch)
                        o = PV["t2"] if half == 0 else PV["t3"]
                        okey = "t2" if half == 0 else "t3"
                        if smp:
                            ze = uext[:, 0:96].rearrange("p (s t) -> p s t", t=6)
                            S.dma(Stmp[0:32, 0:128], st_ffn[:, :, 128 * ch:128 * ch + 128].rearrange("s t c -> (s t) c"), writes=["Stmp"])
                            pq, pqk = PSL()
                            TRP(pq[:, 0:32], Stmp[0:32, 0:128], ["Stmp"], [pqk])
                            CP(ze[:, :, 0:2], pq[:, 0:32].rearrange("p (s t) -> p s t", t=2), [pqk], ["uext"])
                            CP(ze[:, :, 2:6], ps_[:, 0:64].rearrange("p (s t) -> p s t", t=4), [pk], ["uext"], eng="act")
                            o3 = o[:, 0:64].rearrange("p (s t) -> p s t", t=4)
                            TSC(o3, ze[:, :, 0:4], k0_, bb, MULT, ADD, ["uext", "PT"], [okey])
                            STT(o3, ze[:, :, 1:5], k1_, o3, MULT, ADD, ["uext", "PT", okey], [okey])
                            STT(o3, ze[:, :, 2:6], k2_, o3, MULT, ADD, ["uext", "PT", okey], [okey])
                            pq, pqk = PSL()
                            CP(PV["t4"][:, 0:64].rearrange("p (t s) -> p t s", t=4), ze[:, :, 2:6].rearrange("p s t -> p t s"), ["uext"], ["t4"])
                            TRP(pq[0:64, 0:128], PV["t4"][:, 0:64], ["t4"], [pqk])
                            CP(Ytok[:, :], pq[0:64, 0:128], [pqk], ["Ytok"])
                            for t in range(2):
                                S.dma(ffn_s[:, t, 128 * ch:128 * ch + 128], Ytok[32 + 16 * t:48 + 16 * t, :], reads=["Ytok"])
                        else:
                            CP(uext[:, 0:2], ffnhalo[:, ch, :], ["state"], ["uext"])
                            CP(uext[:, 2:2 + T], ps_[:, 0:T], [pk], ["uext"], eng="act")
                            CP(ffnhalo[:, ch, :], uext[:, T:T + 2], ["uext"], ["state"])
                            TSC(o[:, 0:T], uext[:, 0:T], k0_, bb, MULT, ADD, ["uext", "PT"], [okey])
                            STT(o[:, 0:T], uext[:, 1:1 + T], k1_, o[:, 0:T], MULT, ADD, ["uext", "PT", okey], [okey])
                            STT(o[:, 0:T], uext[:, 2:2 + T], k2_, o[:, 0:T], MULT, ADD, ["uext", "PT", okey], [okey])
                            if last:
                                pq, pqk = PSL()
                                TRP(pq[0:2, 0:128], uext[:, T:T + 2], ["uext"], [pqk])
                                CP(Ytok[0:2, :], pq[0:2, 0:128], [pqk], ["Ytok"])
                                S.dma(ffn_p[:, 128 * ch:128 * ch + 128], Ytok[0:2, :], reads=["Ytok"])
                    ACT(PV["t2"][:, 0:T], PV["t2"][:, 0:T], AF.Silu, ["t2"], ["t2"])
                    TT(act[:, f, 0:T], PV["t2"][:, 0:T], PV["t3"][:, 0:T], MULT, ["t2", "t3"], ["mixact"])
                for m in range(16):
                    ps_, pk = PSL()
                    for g0 in range(0, NFC, 16):
                        n = min(16, NFC - g0)
                        wv, wk = WLOAD(w_down[128 * g0:128 * (g0 + n), 128 * m:128 * m + 128].rearrange("(kc ki) n -> ki kc n", ki=128),
                                       (128, n, 128))
                        for j in range(n):
                            MM(ps_[:, 0:T], wv[:, j, :], act[:, g0 + j, 0:T], g0 + j == 0, g0 + j == NFC - 1, [wk, "mixact"], [pk])
                    if smp:
                        t23 = PV["t2"][:, 0:T].rearrange("p (s t) -> p s t", t=4)
                        TT(t23, ps_[:, 0:T].rearrange("p (s t) -> p s t", t=4), TS[:, 5, m, :].unsqueeze(2).to_broadcast([128, 16, 4]),
                           MULT, [pk, "TS"], ["t2"])
                        TT(xT[:, m, 0:T], xT[:, m, 0:T], PV["t2"][:, 0:T], ADD, ["xT", "t2"], ["xT"])
                    else:
                        STT(xT[:, m, 0:T], ps_[:, 0:T], TP[:, 5, m:m + 1], xT[:, m, 0:T], MULT, ADD, [pk, "TP", "xT"], ["xT"])
                stage(17)
                rms(xT, "xT", rstd)
                TT(hf[:, :, 0:T], xT[:, :, 0:T], rstd[:, 0:T].unsqueeze(1).to_broadcast([128, 16, T]), MULT, ["xT", "rstd"], ["mixact"])
                TT(hf[:, :, 0:T], hf[:, :, 0:T], bc(ptr("normf_g", 0, 16)), MULT, ["mixact", "PT"], ["mixact"])
                R = 64 if smp else 128
                for ti_ in range(max(1, T // 128)):
                    for kc in range(16):
                        ps_, pk = PSL()
                        TRP(ps_[0:R, 0:128], hf[:, kc, 128 * ti_:128 * ti_ + R], ["mixact"], [pk])
                        CP(otile[0:R, 128 * kc:128 * kc + 128], ps_[0:R, 0:128], [pk], ["xtile"], eng="act" if kc % 2 else "dve")
                    S.dma(ys if smp else yp[TB * bi + 128 * ti_:TB * bi + 128 * ti_ + 128, :], otile[0:R, :], reads=["xtile"])

        try:
            for bi in range(n_pre):
                emit_block("q", bi)
            for bi in range(n_pblocks):
                emit_block("p", bi)
            if do_sample:
                S.dma(conv_s[:, 0:26, :], st_conv[:, 4:30, :])
                allset = [f"{n}{c}" for c in range(1, 4) for n in
                          ["TL", "TR", "TLz", "TRz", "Vtok", "nbtok", "ktok", "Ytok", "pz0_", "pz1_"] + CTN]
                S.op("pool", lambda e: e.memset(smpreg[:], 0.0), writes=allset + ["smpreg", "NBX", "KX", "KKX", "RX", "Ms", "selm"])
                S.op("pool", lambda e: e.memset(selm, 1.0), writes=["selm"])
                selm4 = selm.rearrange("p s (a b) -> p s a b", b=4)
                S.op("pool", lambda e: e.affine_select(out=selm4, in_=selm4, pattern=[[1, 16], [-1, 16], [0, 4]], compare_op=ALU.is_equal,
                                                       fill=0.0, base=0, channel_multiplier=0), reads=["selm"], writes=["selm"])
                emit_block("s", 0)
        except StopEmit:
            pass
        S.finish()
        print("instructions:", S.ninstr, "sbuf left", nc.sbuf_bytes_remaining)
    return nc


_NC_CACHE = {}
_CFG = {}


def kernel(**inputs):
    f32 = lambda a: np.ascontiguousarray(np.asarray(a, dtype=np.float32))
    I = {k: f32(v) for k, v in inputs.items()}
    ncores = 8
    npre, nmain = _CFG.get("npre", NPRE), _CFG.get("nmain", NMAIN)
    prm = np.zeros((NPROW, 128), np.float32)
    for name, cntc in _PSPEC:
        prm[POFF[name]:POFF[name] + cntc] = I[name].reshape(cntc, 128)
    if "nc" not in _NC_CACHE:
        _NC_CACHE["nc"] = build_nc()
    nc = _NC_CACHE["nc"]
    shared = {k: I[k] for k in ["w_ada", "w_in", "w1", "w2", "a1", "a2", "g1", "g2", "w_out", "w_up", "w_down"]}
    in_maps = []
    for c in range(ncores):
        m = dict(shared)
        m["params"] = prm
        seq, half = c // 2, c % 2
        x = I["x_prompt"][seq]
        if half:
            m["xq"] = x[0:npre * TB] if npre else np.zeros((TB, D), np.float32)
            m["xp"] = x[npre * TB:(npre + nmain) * TB]
        else:
            m["xq"] = np.zeros((max(npre, 1) * TB, D), np.float32)
            m["xp"] = np.concatenate([np.zeros((TB, D), np.float32), x[0:(nmain - 1) * TB]], 0)
        m["flag"] = np.full((128, 1), float(half), np.float32)
        sl = slice(NS * c, NS * c + NS)
        m["xs"] = I["x_sample"][sl].reshape(64, D)
        m["st_shift"] = I["state_shift"][sl]
        m["st_wkv"] = I["state_wkv"][sl]
        m["st_conv"] = I["state_conv"][sl]
        m["st_ffn"] = I["state_ffn"][sl]
        m["cvec"] = np.concatenate([I["c_prompt"][seq][None], I["c_sample"][sl]], 0)
        in_maps.append({k: np.ascontiguousarray(v) for k, v in m.items()})
    res = run_bass_kernel_spmd(nc, in_maps, core_ids=list(range(ncores)))
    R = res.results
    nv = (nmain - 1) * TB
    y_p = np.zeros((4, SEQ, D), np.float32)
    for c in range(ncores):
        seq, half = c // 2, c % 2
        t0 = (npre + 1) * TB if half else 0
        y_p[seq, t0:t0 + nv] = R[c]["yp"][TB:TB + nv]
    odd = [2 * s_ + 1 for s_ in range(4)]
    shift_p = np.stack([R[c]["shift_p"].reshape(D) for c in odd])
    wkv_p = np.stack([R[c]["wkv_p"] for c in odd])
    conv_p = np.stack([R[c]["conv_p"] for c in odd])
    ffn_p = np.stack([R[c]["ffn_p"] for c in odd])
    y_s = np.concatenate([R[c]["ys"].reshape(NS, 4, D) for c in range(8)])
    shift_s = np.concatenate([R[c]["shift_s"] for c in range(8)])
    wkv_s = np.concatenate([R[c]["wkv_s"] for c in range(8)])
    conv_s = np.concatenate([R[c]["conv_s"] for c in range(8)])
    ffn_s = np.concatenate([R[c]["ffn_s"] for c in range(8)])
    return (y_p, y_s, shift_p, wkv_p, conv_p, ffn_p, shift_s, wkv_s, conv_s, ffn_s)
```

```python
import numpy as np
from contextlib import ExitStack
import concourse.bass as bass
import concourse.mybir as mybir
from concourse.bass_utils import run_bass_kernel_spmd

F32 = mybir.dt.float32
BF16 = mybir.dt.bfloat16
TB = 256
NPRE = 3
NMAIN = 5
AF = mybir.ActivationFunctionType
ALU = mybir.AluOpType

D = 2048
NK = 16
SEQ = 2048
NS = 16
DFF = 5504
NFC = 43
F2 = 2 * DFF
NHP = 16
C0 = float(np.exp(-0.5))
GN_EPS = 64 * 1e-5

_PSPEC = [("norm1_g", 16), ("norm2_g", 16), ("normf_g", 16), ("b_ada", 96), ("mu", 96), ("b_glu", 32),
          ("w0", 16), ("a0", 16), ("k_k", 16), ("k_a", 16), ("r_k", 16), ("lnx_g", 16), ("lnx_b", 16),
          ("dw_b", 16), ("ln_conv_g", 16), ("ln_conv_b", 16), ("dw_k", 496), ("ffn_dw_k", 258), ("ffn_dw_b", 86)]
POFF = {}
_o = 0
for _n, _c in _PSPEC:
    POFF[_n] = _o
    _o += _c
NPROW = 1280


STOP = [None]


class StopEmit(Exception):
    pass


def stage(n):
    if STOP[0] is not None and STOP[0] == n:
        raise StopEmit()


class Sched:
    EPOCH = 20000

    def __init__(self, nc, stack, n_dma_sems=16):
        self.nc = nc
        self.stack = stack
        self.engs = {}
        for name, h in (("pe", nc.tensor), ("act", nc.scalar), ("dve", nc.vector),
                        ("pool", nc.gpsimd), ("sp", nc.sync)):
            self.engs[name] = dict(h=h, sems=[], count=0, seen={})
        self.dma_sems = [stack.enter_context(nc.semaphore(f"dq{i}")) for i in range(n_dma_sems)]
        self.dma_uses = [0] * n_dma_sems
        self.dma_next = 0
        self.dma_next_pool = 0
        self.last_w = {}
        self.readers = {}
        self.ninstr = 0

    def _sem_for(self, ename, idx):
        e = self.engs[ename]
        ep = idx // self.EPOCH
        while len(e["sems"]) <= ep:
            e["sems"].append(self.stack.enter_context(self.nc.semaphore(f"s_{ename}{len(e['sems'])}")))
        return e["sems"][ep], idx % self.EPOCH + 1

    def _wait(self, ename, ev):
        e = self.engs[ename]
        if ev[0] == "eng":
            _, src, idx = ev
            if src == ename and ename == "pe":
                return
            if e["seen"].get(src, -1) >= idx:
                return
            sem, val = self._sem_for(src, idx)
            e["h"].wait_ge(sem, val)
            e["seen"][src] = idx
        else:
            _, si, val = ev
            key = ("dma", si)
            if e["seen"].get(key, 0) >= val:
                return
            e["h"].wait_ge(self.dma_sems[si], val)
            e["seen"][key] = val

    def _deps(self, ename, reads, writes):
        evs = []
        for k in reads:
            if k in self.last_w:
                evs.append(self.last_w[k])
        for k in writes:
            if k in self.last_w:
                evs.append(self.last_w[k])
            for r in self.readers.get(k, ()):
                evs.append(r)
        for ev in evs:
            self._wait(ename, ev)

    def _record(self, ev, reads, writes):
        for k in reads:
            self.readers.setdefault(k, []).append(ev)
        for k in writes:
            self.last_w[k] = ev
            self.readers[k] = []

    def op(self, ename, fn, reads=(), writes=()):
        e = self.engs[ename]
        self._deps(ename, reads, writes)
        idx = e["count"]
        sem, val = self._sem_for(ename, idx)
        ins = fn(e["h"])
        ins.then_inc(sem, 1)
        e["count"] += 1
        self.ninstr += 1
        self._record(("eng", ename, idx), reads, writes)

    def dma(self, out, in_, reads=(), writes=(), q="sp", **kw):
        qname = q
        e = self.engs[qname]
        self._deps(qname, reads, writes)
        half = len(self.dma_sems) // 2
        if qname == "pool":
            si = half + self.dma_next_pool % half
            self.dma_next_pool += 1
        else:
            si = self.dma_next % half
            self.dma_next += 1
        prev = self.dma_uses[si]
        if prev > 0:
            self._wait(qname, ("dma", si, 16 * prev))
        self.dma_uses[si] = prev + 1
        e["h"].dma_start(out=out, in_=in_, **kw).then_inc(self.dma_sems[si], 16)
        self.ninstr += 1
        ev = ("dma", si, 16 * (prev + 1))
        self._record(ev, reads, writes)

    def finish(self):
        for si, uses in enumerate(self.dma_uses):
            if uses:
                self._wait("sp", ("dma", si, 16 * uses))
        for name, e in self.engs.items():
            if name != "sp" and e["count"]:
                self._wait("sp", ("eng", name, e["count"] - 1))


def build_nc(n_pblocks=NMAIN, do_sample=True, n_pre=NPRE):
    nc = bass.Bass("TRN2", target_bir_lowering=False)
    din = lambda n, s: nc.dram_tensor(n, s, F32, kind="ExternalInput").ap()
    dout = lambda n, s: nc.dram_tensor(n, s, F32, kind="ExternalOutput").ap()
    xp = din("xp", [n_pblocks * TB, D]); xq = din("xq", [max(n_pre, 1) * TB, D]); flag = din("flag", [128, 1]); xs = din("xs", [64, D]); st_shift = din("st_shift", [NS, D])
    st_wkv = din("st_wkv", [NS, 32, 64, 64]); st_conv = din("st_conv", [NS, 30, D]); st_ffn = din("st_ffn", [NS, 2, F2])
    cvec = din("cvec", [17, D]); params = din("params", [NPROW, 128])
    w_ada = din("w_ada", [D, 6 * D]); w_in = din("w_in", [D, 7 * D])
    w1 = din("w1", [D, 96]); w2 = din("w2", [96, D]); a1 = din("a1", [D, 96]); a2 = din("a2", [96, D])
    g1 = din("g1", [D, 256]); g2 = din("g2", [256, D]); w_out = din("w_out", [D, D])
    w_up = din("w_up", [D, F2]); w_down = din("w_down", [DFF, D])
    yp = dout("yp", [n_pblocks * TB, D]); ys = dout("ys", [64, D]); shift_p = dout("shift_p", [16, 128])
    wkv_p = dout("wkv_p", [32, 64, 64]); conv_p = dout("conv_p", [30, D]); ffn_p = dout("ffn_p", [2, F2])
    shift_s = dout("shift_s", [NS, D]); wkv_s = dout("wkv_s", [NS, 32, 64, 64])
    conv_s = dout("conv_s", [NS, 30, D]); ffn_s = dout("ffn_s", [NS, 2, F2])

    kview = lambda W: W.rearrange("(kc ki) n -> ki kc n", ki=128)

    with ExitStack() as st:
        cnt = [0]

        def sb(shape, name=None, dt=F32):
            cnt[0] += 1
            return st.enter_context(nc.sbuf_tensor(name or f"t{cnt[0]}", shape, dt))

        ident = sb([128, 128], "ident"); ones = sb([128, TB], "ones"); bones = sb([128, 128], "bones")
        ones_bf = sb([128, 128], "ones_bf", BF16)
        m_su = sb([64, 64]); m_u = sb([64, 64]); m_sl = sb([64, 64]); blk = sb([64, 64])
        ms_su = sb([64, 64]); ms_u = sb([64, 64]); ms_sl = sb([64, 64])
        rowsel = sb([64, 16])
        PT = sb([128, NPROW], "PT")
        TP = sb([128, 6, 16], "TP"); TS = sb([128, 6, 16, 16], "TS")
        omu = sb([128, 96]); omka = sb([128, 16]); flg = sb([128, 1], "flg")
        shiftst = sb([128, 16]); Mst = sb([128, 16, 64]); convhalo = sb([128, 16, 30], None, BF16); ffnhalo = sb([128, 86, 2])
        xtile = sb([128, D], "xtile"); xT = sb([128, 16, TB], "xT"); h = sb([128, 16, TB], "h", BF16)
        dx = sb([128, 16, TB], "dx", BF16); mixact = sb([128, 6144], "mixact")
        mixbf = mixact[:, :].bitcast(BF16)
        hf = mixact[:, 0:16 * TB].rearrange("p (k t) -> p k t", t=TB)
        MT = mixact[:, 0:96 * 17].rearrange("p (m c) -> p m c", c=17)
        zc = sb([128, 16, TB], "zc", BF16); merged = sb([128, 16, TB], "merged", BF16)
        rstd = sb([128, TB]); rstd2 = sb([128, TB])
        ring = [sb([128, 16, 128], f"ring{i}", BF16) for i in range(4)]
        PV = {n: sb([128, TB], "pv_" + n) for n in
              ["r", "k0", "v", "lw", "a", "g", "sga", "glu", "glb", "kk", "kkn", "t1", "k", "b", "rk", "bonus",
               "cs", "cx", "eW", "eWi", "eWp", "y", "t2", "t3", "t4", "gb"]}
        for n in ["tw", "xa1", "sg0", "sg1"]:
            PV[n] = sb([128, TB], "pv_" + n, BF16)
        zext = sb([128, 32 + TB], "zext2", BF16); uext = sb([128, 8 + TB], "uext"); zsbf = sb([128, 16, 34], "zsbf", BF16)
        smpreg = sb([128, 9216], "smpreg")
        CTN = ["NRB", "LKT", "RKT", "X", "U", "ZTa", "ZTb"]

        def mkset(c):
            B = {}
            if c == 0:
                B["TL"] = sb([128, 2, 64])[:]; B["TR"] = sb([128, 2, 64])[:]
                B["TLz"] = sb([128, 2, 2, 64])[:]; B["TRz"] = sb([128, 2, 2, 64])[:]
                for n in CTN:
                    B[n] = sb([64, 2, 64], "ct_" + n)[:]
                B["PZ"] = [sb([64, 2, 2, 64], f"pz{i}")[:] for i in range(2)]
                for n in ["Vtok", "nbtok", "ktok", "Ytok"]:
                    B[n] = sb([64, 128])[:]
            else:
                o = [2688 * (c - 1)]

                def take(nparts, words):
                    v = smpreg[0:nparts, o[0]:o[0] + words]
                    o[0] += words
                    return v
                B["TL"] = take(128, 128).rearrange("p (a t) -> p a t", a=2)
                B["TR"] = take(128, 128).rearrange("p (a t) -> p a t", a=2)
                B["TLz"] = take(128, 256).rearrange("p (h a t) -> p h a t", h=2, a=2)
                B["TRz"] = take(128, 256).rearrange("p (h a t) -> p h a t", h=2, a=2)
                for n in CTN:
                    B[n] = take(64, 128).rearrange("p (h t) -> p h t", h=2)
                B["PZ"] = [take(64, 256).rearrange("p (h a t) -> p h a t", h=2, a=2) for i in range(2)]
                for n in ["Vtok", "nbtok", "ktok", "Ytok"]:
                    B[n] = take(64, 128)
            return B

        CS = [mkset(c) for c in range(4)]
        Ytok = sb([64, 128], "Ytok_st")
        NBX = smpreg[0:64, 0:2048].rearrange("p (s c) -> p s c", s=16)
        KX = smpreg[0:64, 2048:4096].rearrange("p (s c) -> p s c", s=16)
        KKX = smpreg[:, 4096:5120].rearrange("p (s c) -> p s c", s=16)
        RX = smpreg[:, 5120:6144].rearrange("p (s c) -> p s c", s=16)
        Msz = smpreg[:, 6144:8192].rearrange("p (h s c) -> p h s c", h=2, s=16)
        selm = smpreg[:, 8192:9216].rearrange("p (s c) -> p s c", s=16)
        Stmp = sb([64, 128]); Sout = sb([64, 128])
        SGA = [PV["sga"], smpreg[:, 8064:8064 + TB]]
        GG = [PV["g"], smpreg[:, 8064 + TB:8064 + 2 * TB]]
        otile = xtile
        cvt = xtile[0:17, :]; cT = sb([128, 16, 17], None, BF16); ptile = sb([128, 128])
        diag = sb([128, 31, 128], "diag", BF16)
        pss = [st.enter_context(nc.psum_tensor(f"ps{i}", [128, 512], F32)) for i in range(8)]
        block = st.enter_context(nc.Block())
        S = Sched(nc, st)

        psi = [0]

        def PSL():
            i = psi[0] % 8
            psi[0] += 1
            return pss[i][:, :], ("ps", i)

        def MM(out, lhsT, rhs, start, stop, r, w):
            S.op("pe", lambda e: e.matmul(out, lhsT=lhsT, rhs=rhs, start=start, stop=stop), reads=r, writes=w)

        def TRP(out, in_, r, w):
            k = in_.shape[0]
            S.op("pe", lambda e: e.transpose(out=out, in_=in_, identity=ident[0:k, 0:k]), reads=list(r) + ["ident"], writes=w)

        def TT(out, a, b, op, r, w, eng="dve"):
            S.op(eng, lambda e: e.tensor_tensor(out=out, in0=a, in1=b, op=op), reads=r, writes=w)

        def TSC(out, a, s1, s2, op0, op1, r, w, eng="dve"):
            if s2 is None:
                S.op(eng, lambda e: e.tensor_scalar(out=out, in0=a, scalar1=s1, scalar2=None, op0=op0), reads=r, writes=w)
            else:
                S.op(eng, lambda e: e.tensor_scalar(out=out, in0=a, scalar1=s1, scalar2=s2, op0=op0, op1=op1), reads=r, writes=w)

        def STT(out, a, sc, b, op0, op1, r, w):
            S.op("dve", lambda e: e.scalar_tensor_tensor(out=out, in0=a, scalar=sc, in1=b, op0=op0, op1=op1), reads=r, writes=w)

        def CP(out, a, r, w, eng="dve"):
            if eng == "act":
                S.op("act", lambda e: e.copy(out=out, in_=a), reads=r, writes=w)
            else:
                S.op(eng, lambda e: e.tensor_copy(out=out, in_=a), reads=r, writes=w)

        def ACT(out, a, func, r, w, bias=0.0, scale=1.0):
            S.op("act", lambda e: e.activation(out=out, in_=a, func=func, bias=bias, scale=scale), reads=r, writes=w)

        def RSQRT(out, a, r, w, scale=1.0, eps=0.0):
            ACT(out, a, AF.Sqrt, r, w, bias=eps, scale=scale)
            S.op("dve", lambda e: e.reciprocal(out=out, in_=out), reads=w, writes=w)

        ringi = [0]

        def WLOAD(src, shape3):
            i = ringi[0] % 4
            ringi[0] += 1
            kp, nk, n = shape3
            dst = ring[i][0:kp, 0:nk, 0:n]
            S.dma(dst, src, writes=[("ring", i)], q="pool")
            return dst, ("ring", i)

        MULT, ADD, SUB = ALU.mult, ALU.add, ALU.subtract
        pt = lambda name, c: PT[:, POFF[name] + c:POFF[name] + c + 1]
        ptr = lambda name, c0, n: PT[:, POFF[name] + c0:POFF[name] + c0 + n]

        S.op("pool", lambda e: e.memset(ident[:], 0.0), writes=["ident"])
        S.op("pool", lambda e: e.affine_select(out=ident[:], in_=ident[:], pattern=[[-1, 128]], compare_op=ALU.not_equal,
                                               fill=1.0, base=0, channel_multiplier=1), reads=["ident"], writes=["ident"])
        S.op("pool", lambda e: e.memset(ones[:], 1.0), writes=["ones"])
        S.op("pool", lambda e: e.memset(ones_bf[:], 1.0), writes=["ones"])
        S.op("pool", lambda e: e.memset(bones[:], 0.0), writes=["bones"])
        S.op("pool", lambda e: e.memset(bones[0:64, 0:64], 1.0), reads=["bones"], writes=["bones"])
        S.op("pool", lambda e: e.memset(bones[64:128, 64:128], 1.0), reads=["bones"], writes=["bones"])
        for m, cm, pat, op in ((m_su, -1, 1, ALU.is_gt), (m_u, -1, 1, ALU.is_ge), (m_sl, 1, -1, ALU.is_gt)):
            S.op("pool", lambda e: e.memset(m[:], 1.0), writes=["masks"])
            S.op("pool", lambda e: e.affine_select(out=m[:], in_=m[:], pattern=[[pat, 64]], compare_op=op, fill=0.0,
                                                   base=0, channel_multiplier=cm), reads=["masks"], writes=["masks"])
        S.op("pool", lambda e: e.memset(blk[:], 1.0), writes=["masks"])
        blk3 = blk[:].rearrange("p (a b) -> p a b", b=4)
        S.op("pool", lambda e: e.affine_select(out=blk3, in_=blk3, pattern=[[-4, 16], [0, 4]], compare_op=ALU.is_ge, fill=0.0,
                                               base=0, channel_multiplier=1), reads=["masks"], writes=["masks"])
        S.op("pool", lambda e: e.affine_select(out=blk3, in_=blk3, pattern=[[4, 16], [0, 4]], compare_op=ALU.is_ge, fill=0.0,
                                               base=3, channel_multiplier=-1), reads=["masks"], writes=["masks"])
        for ms, m in ((ms_su, m_su), (ms_u, m_u), (ms_sl, m_sl)):
            TT(ms[:], m[:], blk[:], MULT, ["masks"], ["masks"], eng="pool")
        S.op("pool", lambda e: e.memset(rowsel[:], 1.0), writes=["masks"])
        S.op("pool", lambda e: e.affine_select(out=rowsel[:], in_=rowsel[:], pattern=[[-4, 16]], compare_op=ALU.is_ge, fill=0.0,
                                               base=0, channel_multiplier=1), reads=["masks"], writes=["masks"])
        S.op("pool", lambda e: e.affine_select(out=rowsel[:], in_=rowsel[:], pattern=[[4, 16]], compare_op=ALU.is_ge, fill=0.0,
                                               base=3, channel_multiplier=-1), reads=["masks"], writes=["masks"])
        for t_ in (shiftst, Mst, convhalo, ffnhalo):
            S.op("pool", lambda e: e.memset(t_[:], 0.0), writes=["state"])
        S.op("pool", lambda e: e.memset(smpreg[:], 0.0), writes=["smpreg"])
        for c in range(4):
            S.op("pool", lambda e: e.memset(CS[c]["TLz"], 0.0), reads=["smpreg"], writes=[f"TLz{c}"])
            S.op("pool", lambda e: e.memset(CS[c]["TRz"], 0.0), reads=["smpreg"], writes=[f"TRz{c}"])

        S.dma(flg[:], flag, writes=["flg"])
        S.op("pool", lambda e: e.memset(PV["r"][:], 0.0), writes=["r"])
        for i in range(NPROW // 128):
            S.dma(ptile[:], params[128 * i:128 * i + 128, :], writes=["ptile"])
            ps_, pk = PSL()
            TRP(ps_[:, 0:128], ptile[:], ["ptile"], [pk])
            CP(PT[:, 128 * i:128 * i + 128], ps_[:, 0:128], [pk], ["PT"])
        TSC(omu[:], ptr("mu", 0, 96), -1.0, 1.0, MULT, ADD, ["PT"], ["PT2"])
        TSC(omka[:], ptr("k_a", 0, 16), -1.0, 1.0, MULT, ADD, ["PT"], ["PT2"])

        S.dma(cvt, cvec, writes=["xtile"])
        ACT(cvt, cvt, AF.Silu, ["xtile"], ["xtile"])
        for kc in range(16):
            ps_, pk = PSL()
            TRP(ps_[:, 0:17], cvt[:, 128 * kc:128 * kc + 128], ["xtile"], [pk])
            CP(cT[:, kc, :], ps_[:, 0:17], [pk], ["cT"])
        for m in range(96):
            wv, wk = WLOAD(kview(w_ada)[:, :, 128 * m:128 * m + 128], (128, 16, 128))
            ps_, pk = PSL()
            for kc in range(16):
                MM(ps_[:, 0:17], wv[:, kc, :], cT[:, kc, :], kc == 0, kc == 15, [wk, "cT"], [pk])
            TSC(MT[:, m, :], ps_[:, 0:17], pt("b_ada", m), None, ADD, None, [pk, "PT"], ["mixact"])
        for (ti, mi, kind) in ((0, 1, "gs1"), (1, 0, "sh"), (2, 2, "gt"), (3, 4, "gs2"), (4, 3, "sh"), (5, 5, "gt")):
            src_p = MT[:, 16 * mi:16 * mi + 16, 0]
            src_s = MT[:, 16 * mi:16 * mi + 16, 1:17]
            if kind.startswith("gs"):
                g = ptr("norm1_g" if kind == "gs1" else "norm2_g", 0, 16)
                STT(TP[:, ti, :], src_p, 1.0, g, ADD, MULT, ["mixact", "PT"], ["TP"])
                STT(TS[:, ti, :, :], src_s, 1.0, g.unsqueeze(2).to_broadcast([128, 16, 16]), ADD, MULT, ["mixact", "PT"], ["TS"])
            else:
                CP(TP[:, ti, :], src_p, ["mixact"], ["TP"])
                CP(TS[:, ti, :, :], src_s, ["mixact"], ["TS"])

        def emit_block(kind, bi):
            smp = kind == "s"
            pre = kind == "q"
            msk = pre or (kind == "p" and bi == 0)
            T = 64 if smp else TB
            nch = T // 64
            bc = lambda tab16: tab16.unsqueeze(2).to_broadcast([128, 16, T])
            v4 = lambda ap: ap.rearrange("p k (s t) -> p k s t", t=4)
            last = kind == "p" and bi == n_pblocks - 1

            xsrc_rows = 64 if smp else 128
            for ti_ in range(max(1, T // 128)):
                xsrc = xs if smp else (xq if pre else xp)[TB * bi + 128 * ti_:TB * bi + 128 * ti_ + 128, :]
                S.dma(xtile[0:xsrc_rows, :], xsrc, writes=["xtile"])
                R = xsrc_rows
                for q in range(4):
                    ps_, pk0 = PSL()
                    for j in range(4):
                        kc = 4 * q + j
                        TRP(ps_[:, j * R:(j + 1) * R], xtile[0:R, 128 * kc:128 * kc + 128], ["xtile"], [pk0])
                    CP(xT[:, 4 * q:4 * q + 4, 128 * ti_:128 * ti_ + R], ps_[:, 0:4 * R].rearrange("p (a t) -> p a t", a=4),
                       [pk0], ["xT"], eng="act")
            stage(1)

            def rms(src, key, out_rstd):
                TT(dx[:, :, 0:T], src[:, :, 0:T], src[:, :, 0:T], MULT, [key], ["dx"])
                ps_, pk = PSL()
                for kc in range(16):
                    MM(ps_[:, 0:T], ones_bf[:], dx[:, kc, 0:T], kc == 0, kc == 15, ["ones", "dx"], [pk])
                RSQRT(out_rstd[:, 0:T], ps_[:, 0:T], [pk], ["rstd"], scale=1.0 / D, eps=1e-6)

            def modnorm(src, skey, rs, tg, tsft):
                d = hf[:, :, 0:T]
                TT(d, src[:, :, 0:T], rs[:, 0:T].unsqueeze(1).to_broadcast([128, 16, T]), MULT, [skey, "rstd"], ["mixact"])
                if smp:
                    for (tix, op) in ((tg, MULT), (tsft, ADD)):
                        TT(v4(d), v4(d), TS[:, tix, :, :].unsqueeze(3).to_broadcast([128, 16, 16, 4]), op, ["mixact", "TS"], ["mixact"])
                else:
                    TT(d, d, bc(TP[:, tg, :]), MULT, ["mixact", "TP"], ["mixact"])
                    TT(d, d, bc(TP[:, tsft, :]), ADD, ["mixact", "TP"], ["mixact"])
                    if msk:
                        TSC(d, d, flg[:, 0:1], None, MULT, None, ["mixact", "flg"], ["mixact"])

            rms(xT, "xT", rstd)
            modnorm(xT, "xT", rstd, 0, 1)
            CP(h[:, :, 0:T], hf[:, :, 0:T], ["mixact"], ["h"], eng="act")
            stage(2)
            if smp:
                S.dma(otile[0:16, :], st_shift, writes=["xtile"])
                h4 = v4(hf[:, :, 0:64])
                d4 = v4(dx[:, :, 0:64])
                for q in range(4):
                    ps_, pk = PSL()
                    for j in range(4):
                        TRP(ps_[:, 16 * j:16 * j + 16], otile[0:16, 128 * (4 * q + j):128 * (4 * q + j) + 128], ["xtile"], [pk])
                    TT(d4[:, 4 * q:4 * q + 4, :, 0], ps_[:, 0:64].rearrange("p (a s) -> p a s", a=4), h4[:, 4 * q:4 * q + 4, :, 0],
                       SUB, [pk, "mixact"], ["dx"])
                TT(d4[:, :, :, 1:4], h4[:, :, :, 0:3], h4[:, :, :, 1:4], SUB, ["mixact"], ["dx"])
                for kc in range(16):
                    ps_, pk = PSL()
                    TRP(ps_[0:16, 0:128], h4[:, kc, :, 3], ["mixact"], [pk])
                    CP(otile[0:16, 128 * kc:128 * kc + 128], ps_[0:16, 0:128], [pk], ["xtile"])
                S.dma(shift_s, otile[0:16, :], reads=["xtile"])
            else:
                TT(dx[:, :, 0], shiftst[:], hf[:, :, 0], SUB, ["state", "mixact"], ["dx"])
                TT(dx[:, :, 1:T], hf[:, :, 0:T - 1], hf[:, :, 1:T], SUB, ["mixact"], ["dx"])
                CP(shiftst[:], hf[:, :, T - 1], ["mixact"], ["state"])
                if last:
                    ps_, pk = PSL()
                    TRP(ps_[0:16, 0:128], shiftst[:], ["state"], [pk])
                    CP(otile[0:16, 0:128], ps_[0:16, 0:128], [pk], ["xtile"])
                    S.dma(shift_p, otile[0:16, 0:128], reads=["xtile"])
            stage(3)

            def mix(dst, dkey, mi):
                TT(dst, dx[:, :, 0:T], bc(ptr("mu", 16 * mi, 16)), MULT, ["dx", "PT"], [dkey])
                TT(dst, dst, h[:, :, 0:T], ADD, [dkey, "h"], [dkey])

            xmix = [mixbf[:, 16 * TB * i:16 * TB * (i + 1)].rearrange("p (k t) -> p k t", t=TB)[:, :, 0:T] for i in range(3)]
            tmpx = zc[:, :, 0:T]
            lora1 = [(w1, 96, 1, [PV["tw"]], AF.Tanh), (a1, 96, 4, [PV["xa1"]], AF.Copy)]
            if not pre:
                lora1.append((g1, 256, 5, [PV["sg0"], PV["sg1"]], AF.Sigmoid))
            for (W, ncol, mi, outs, func) in lora1:
                mix(tmpx, "zc", mi)
                for oi, o in enumerate(outs):
                    n = min(128, ncol - 128 * oi)
                    wv, wk = WLOAD(kview(W)[:, :, 128 * oi:128 * oi + n], (128, 16, n))
                    ps_, pk = PSL()
                    for kc in range(16):
                        MM(ps_[0:n, 0:T], wv[:, kc, :], tmpx[:, kc, :], kc == 0, kc == 15, [wk, "zc"], [pk])
                    ACT(o[0:n, 0:T], ps_[0:n, 0:T], func, [pk], ["lora"])
            if not pre:
                mix(xmix[0], "mixact", 0)
            mix(xmix[1], "mixact", 2)
            mix(xmix[2], "mixact", 3)
            stage(4)

            def part1_groups(p, alt):
                hh = h[:, :, 0:T]
                specs = []
                if not pre:
                    specs.append(("w", xmix[0], "mixact", 0, PV["r"], AF.Copy, "r", 0.0))
                specs.append(("w", xmix[1], "mixact", D, PV["k0"], AF.Copy, "k0", 0.0))
                specs.append(("w", xmix[2], "mixact", 2 * D, PV["v"], AF.Copy, "v", 0.0))
                if not pre:
                    specs.append(("w", hh, "h", 3 * D, PV["glu"], AF.Identity, "glu", pt("b_glu", p)))
                    specs.append(("w", hh, "h", 4 * D, PV["glb"], AF.Sigmoid, "glb", pt("b_glu", 16 + p)))
                    specs.append(("w", hh, "h", 5 * D, SGA[alt], AF.Sigmoid, f"sga{alt}", 0.0))
                specs.append(("l", w2, 96, [PV["tw"]], PV["lw"], AF.Sigmoid, pt("w0", p), "lw"))
                specs.append(("l", a2, 96, [PV["xa1"]], PV["a"], AF.Sigmoid, pt("a0", p), "a"))
                if not pre:
                    specs.append(("l", g2, 256, [PV["sg0"], PV["sg1"]], GG[alt], AF.Copy, 0.0, f"g{alt}"))
                groups = []
                for sp_ in specs:
                    st_ = {}
                    if sp_[0] == "w":
                        _, src, skey, col0, out, func, okey, bias = sp_

                        def load(st_=st_, col0=col0):
                            st_["w"] = WLOAD(kview(w_in)[:, :, col0 + 128 * p:col0 + 128 * p + 128], (128, 16, 128))

                        def comp(st_=st_, src=src, skey=skey, out=out, func=func, okey=okey, bias=bias):
                            wv, wk = st_["w"]
                            ps_, pk = PSL()
                            for kc in range(16):
                                MM(ps_[:, 0:T], wv[:, kc, :], src[:, kc, :], kc == 0, kc == 15, [wk, skey], [pk])
                            ACT(out[:, 0:T], ps_[:, 0:T], func, [pk, "PT"], [okey], bias=bias)
                    else:
                        _, W, K, srcs, out, func, bias, okey = sp_
                        nk = len(srcs)
                        kp = min(K, 128)

                        def load(st_=st_, W=W, kp=kp, nk=nk):
                            st_["w"] = WLOAD(W.rearrange("(kc ki) n -> ki kc n", ki=kp)[:, :, 128 * p:128 * p + 128], (kp, nk, 128))

                        def comp(st_=st_, srcs=srcs, out=out, func=func, bias=bias, okey=okey, kp=kp, nk=nk):
                            wv, wk = st_["w"]
                            ps_, pk = PSL()
                            for j in range(nk):
                                MM(ps_[:, 0:T], wv[:, j, :], srcs[j][0:kp, 0:T], j == 0, j == nk - 1, [wk, "lora"], [pk])
                            ACT(out[:, 0:T], ps_[:, 0:T], func, [pk, "PT"], [okey], bias=bias)
                    groups.append((load, comp))
                return groups

            def run_groups(groups):
                for i, (ld, cp_) in enumerate(groups):
                    if i == 0:
                        ld()
                        if len(groups) > 1:
                            groups[1][0]()
                    cp_()
                    if i + 2 < len(groups):
                        groups[i + 2][0]()

            pipelined = not smp
            pend = []
            pstate = {"i": 0}

            def pump(n=1):
                for _ in range(n):
                    i = pstate["i"]
                    if i >= len(pend):
                        return
                    if i == 0:
                        pend[0][0]()
                        if len(pend) > 1:
                            pend[1][0]()
                    pend[i][1]()
                    if i + 2 < len(pend):
                        pend[i + 2][0]()
                    pstate["i"] = i + 1

            if pipelined:
                run_groups(part1_groups(0, 0))
            for p in range(NHP):
                alt = (p % 2) if pipelined else 0
                if not pipelined:
                    run_groups(part1_groups(p, 0))
                stage(5)
                V = {n: PV[n][:, 0:T] for n in PV}
                if not pre:
                    TT(V["glu"], V["glu"], V["glb"], MULT, ["glu", "glb"], ["glu"])
                    if msk:
                        TSC(V["glu"], V["glu"], flg[:, 0:1], None, MULT, None, ["glu", "flg"], ["glu"])
                    for j in range(31):
                        TSC(diag[:, j, :], ident[:], pt("dw_k", 16 * j + p), None, MULT, None, ["ident", "PT"], ["diag"])
                    ps_, pk = PSL()
                    if smp:
                        S.dma(otile[0:120, 0:512].rearrange("r (q c) -> r q c", q=4),
                              st_conv[:, :, 128 * p:128 * p + 128].rearrange("(q s) t c -> (s t) q c", q=4), writes=["xtile"])
                        for q in range(4):
                            pq, pqk = PSL()
                            TRP(pq[:, 0:120], otile[0:120, 128 * q:128 * q + 128], ["xtile"], [pqk])
                            CP(zsbf[:, 4 * q:4 * q + 4, 0:30], pq[:, 0:120].rearrange("p (s t) -> p s t", t=30), [pqk], ["zsbf"])
                        CP(zsbf[:, :, 30:34], V["glu"].rearrange("p (s t) -> p s t", t=4), ["glu"], ["zsbf"])
                        for j in range(31):
                            MM(ps_[:, 0:64], diag[:, j, :], zsbf[:, :, j:j + 4], j == 0, j == 30, ["diag", "zsbf"], [pk])
                        pq, pqk = PSL()
                        CP(PV["t4"][:, 0:64].rearrange("p (t s) -> p t s", t=4), V["glu"].rearrange("p (s t) -> p t s", t=4), ["glu"], ["t4"])
                        TRP(pq[0:64, 0:128], PV["t4"][:, 0:64], ["t4"], [pqk])
                        CP(Ytok[:, :], pq[0:64, 0:128], [pqk], ["Ytok"])
                        for t in range(4):
                            S.dma(conv_s[:, 26 + t, 128 * p:128 * p + 128], Ytok[16 * t:16 * t + 16, :], reads=["Ytok"])
                    else:
                        CP(zext[:, 0:30], convhalo[:, p, :], ["state"], ["zext"])
                        CP(zext[:, 30:30 + T], V["glu"], ["glu"], ["zext"], eng="act")
                        for j in range(31):
                            MM(ps_[:, 0:T], diag[:, j, :], zext[:, j:j + T], j == 0, j == 30, ["diag", "zext"], [pk])
                        CP(convhalo[:, p, :], zext[:, T:T + 30], ["zext"], ["state"])
                        if last:
                            pq, pqk = PSL()
                            TRP(pq[0:30, 0:128], PV["glu"][:, T - 30:T], ["glu"], [pqk])
                            CP(Ytok[0:30, :], pq[0:30, 0:128], [pqk], ["Ytok"])
                            S.dma(conv_p[:, 128 * p:128 * p + 128], Ytok[0:30, :], reads=["Ytok"])
                    TSC(zc[:, p, 0:T], ps_[:, 0:T], pt("dw_b", p), None, ADD, None, [pk, "PT"], [("zcp", p)])

                stage(6)
                TSC(V["kk"], V["k0"], pt("k_k", p), None, MULT, None, ["k0", "PT"], ["kk"])
                TT(V["t2"], V["kk"], V["kk"], MULT, ["kk"], ["t2"])
                ps_, pk = PSL()
                MM(ps_[:, 0:T], bones[:], V["t2"], True, True, ["bones", "t2"], [pk])
                RSQRT(V["t3"], ps_[:, 0:T], [pk], ["t3"], eps=1e-30)
                TT(V["kkn"], V["kk"], V["t3"], MULT, ["kk", "t3"], ["kkn"])
                TSC(V["t1"], V["a"], pt("k_a", p), omka[:, p:p + 1], MULT, ADD, ["a", "PT", "PT2"], ["t1"])
                TT(V["k"], V["k0"], V["t1"], MULT, ["k0", "t1"], ["k"])
                TT(V["b"], V["kkn"], V["a"], MULT, ["kkn", "a"], ["b"])
                if not pre:
                    STT(V["rk"], V["r"], pt("r_k", p), V["k"], MULT, MULT, ["r", "k", "PT"], ["rk"])
                    ps_, pk = PSL()
                    MM(ps_[:, 0:T], bones[:], V["rk"], True, True, ["bones", "rk"], [pk])
                    TT(V["bonus"], ps_[:, 0:T], V["v"], MULT, [pk, "v"], ["bonus"])
                stage(7)
                L = 4 if smp else 64
                nseg = T // L
                S.op("dve", lambda e: e.tensor_tensor_scan(out=V["cs"], data0=ones[:, 0:T], data1=V["lw"], initial=0.0,
                                                           op0=MULT, op1=ADD), reads=["ones", "lw"], writes=["cs"])
                TT(V["cx"], V["cs"], V["lw"], SUB, ["cs", "lw"], ["cx"])
                cs3 = V["cs"].rearrange("p (s l) -> p s l", l=L)
                cx3 = V["cx"].rearrange("p (s l) -> p s l", l=L)
                CP(PV["t4"][:, 0:nseg], cx3[:, :, 0], ["cx"], ["t4"])
                base = PV["t4"][:, 0:nseg].unsqueeze(2).to_broadcast([128, nseg, L])
                TT(cs3, cs3, base, SUB, ["cs", "t4"], ["cs"])
                TT(cx3, cx3, base, SUB, ["cx", "t4"], ["cx"])
                ACT(V["eW"], V["cs"], AF.Exp, ["cs"], ["eW"], scale=-C0)
                ACT(V["eWi"], V["cs"], AF.Exp, ["cs"], ["eWi"], scale=C0)
                ACT(V["eWp"], V["cx"], AF.Exp, ["cx"], ["eWp"], scale=-C0)

                stage(8)
                msu, mu_, msl = (ms_su, ms_u, ms_sl) if smp else (m_su, m_u, m_sl)
                b2 = lambda m: m[:].unsqueeze(1).to_broadcast([64, 2, 64])
                h2v = lambda ap: ap.rearrange("p (h n) -> p h n", h=2)
                K_ = lambda n, c: f"{n}{c}"

                def bankA(c):
                    return pss[2 * c][:, :], ("ps", 2 * c)

                def bankB(c):
                    return pss[2 * c + 1][:, :], ("ps", 2 * c + 1)

                def P0(c):
                    B = CS[c]
                    cs_ = slice(64 * c, 64 * c + 64)
                    TT(B["TL"][:, 0, :], PV["kkn"][:, cs_], PV["eWp"][:, cs_], MULT, ["kkn", "eWp"], [K_("TL", c)])
                    TT(B["TL"][:, 1, :], PV["r"][:, cs_], PV["eW"][:, cs_], MULT, ["r", "eW"], [K_("TL", c)])
                    TT(B["TR"][:, 0, :], PV["b"][:, cs_], PV["eWi"][:, cs_], MULT, ["b", "eWi"], [K_("TR", c)])
                    TT(B["TR"][:, 1, :], PV["k"][:, cs_], PV["eWi"][:, cs_], MULT, ["k", "eWi"], [K_("TR", c)])
                    for hj in range(2):
                        hs = slice(64 * hj, 64 * hj + 64)
                        CP(B["TLz"][hs, hj, :, :], B["TL"][hs, :, :], [K_("TL", c)], [K_("TLz", c)], eng="act")
                        CP(B["TRz"][hs, hj, :, :], B["TR"][hs, :, :], [K_("TR", c)], [K_("TRz", c)], eng="act")

                def P1(c):
                    B = CS[c]
                    A, ak = bankA(c)
                    Bk_, bk = bankB(c)
                    for hj in range(2):
                        TLh = B["TLz"][:, hj, :, :].rearrange("p a t -> p (a t)")
                        MM(A[0:64, 128 * hj:128 * hj + 128], B["TRz"][:, hj, 0, :], TLh, True, True, [K_("TRz", c), K_("TLz", c)], [ak])
                        MM(Bk_[0:64, 128 * hj:128 * hj + 128], B["TRz"][:, hj, 1, :], TLh, True, True, [K_("TRz", c), K_("TLz", c)], [bk])
                        MM(A[0:64, 256 + 64 * hj:256 + 64 * hj + 64], B["TLz"][:, hj, 0, :], B["TRz"][:, hj, 0, :], True, True,
                           [K_("TRz", c), K_("TLz", c)], [ak])

                def P2(c):
                    B = CS[c]
                    A, ak = bankA(c)
                    Bk_, bk = bankB(c)
                    pa3 = h2v(A[0:64, 0:256]); pb3 = h2v(Bk_[0:64, 0:256]); pc3 = h2v(A[0:64, 256:384])
                    pz = B["PZ"][0]
                    STT(pz[:, :, 1, :], pa3[:, :, 0:64], -1.0, b2(msu), MULT, MULT, [ak, "masks"], [K_("pz0_", c)])
                    STT(B["NRB"], pa3[:, :, 64:128], -1.0, b2(mu_), MULT, MULT, [ak, "masks"], [K_("NRB", c)])
                    STT(B["ZTa"], pc3, -1.0, b2(msl), MULT, MULT, [ak, "masks"], [K_("ZTa", c)])
                    TT(B["LKT"], pb3[:, :, 0:64], b2(msu), MULT, [bk, "masks"], [K_("LKT", c)])
                    TT(B["RKT"], pb3[:, :, 64:128], b2(mu_), MULT, [bk, "masks"], [K_("RKT", c)])
                    TT(pz[:, :, 0, :], pz[:, :, 1, :], ident[0:64, 0:64].unsqueeze(1).to_broadcast([64, 2, 64]), ADD,
                       [K_("pz0_", c), "ident"], [K_("pz0_", c)])

                def P3(c):
                    B = CS[c]
                    cs_ = slice(64 * c, 64 * c + 64)
                    A, ak = bankA(c)
                    TRP(A[0:64, 0:128], PV["v"][:, cs_], ["v"], [ak])
                    TRP(A[0:64, 128:256], B["TR"][:, 0, :], [K_("TR", c)], [ak])
                    TRP(A[0:64, 256:384], B["TR"][:, 1, :], [K_("TR", c)], [ak])
                    CP(B["Vtok"], A[0:64, 0:128], [ak], [K_("Vtok", c)], eng="act")
                    S.op("act", lambda e: e.mul(out=B["nbtok"], in_=A[0:64, 128:256], mul=-1.0), reads=[ak], writes=[K_("nbtok", c)])
                    CP(B["ktok"], A[0:64, 256:384], [ak], [K_("ktok", c)], eng="act")

                nstate = {}

                def NM(j):
                    def f(c):
                        B = CS[c]
                        cur, zt_cur = nstate.get(c, (0, "ZTa"))
                        pzc = B["PZ"][cur]
                        kc_ = K_(f"pz{cur}_", c)
                        A, ak = bankA(c)
                        Bk_, bk = bankB(c)
                        ztk = K_(zt_cur, c)
                        for hj in range(2):
                            if j == 0:
                                MM(A[0:64, 128 * hj + 64:128 * hj + 128], B[zt_cur][:, hj, :], pzc[:, hj, 1, :], True, True, [ztk, kc_], [ak])
                            elif j == 5:
                                MM(A[0:64, 128 * hj:128 * hj + 64], B[zt_cur][:, hj, :], pzc[:, hj, 0, :], True, True, [ztk, kc_], [ak])
                            else:
                                MM(A[0:64, 128 * hj:128 * hj + 128], B[zt_cur][:, hj, :], pzc[:, hj, :, :].rearrange("p a t -> p (a t)"),
                                   True, True, [ztk, kc_], [ak])
                            if j != 5:
                                MM(Bk_[0:64, 64 * hj:64 * hj + 64], pzc[:, hj, 1, :], B[zt_cur][:, hj, :], True, True, [ztk, kc_], [bk])
                    return f

                def NE(j):
                    def f(c):
                        B = CS[c]
                        cur, zt_cur = nstate.get(c, (0, "ZTa"))
                        zt_nxt = "ZTb" if zt_cur == "ZTa" else "ZTa"
                        pzc, pzn = B["PZ"][cur], B["PZ"][1 - cur]
                        kc_, kn_ = K_(f"pz{cur}_", c), K_(f"pz{1 - cur}_", c)
                        A, ak = bankA(c)
                        Bk_, bk = bankB(c)
                        q13 = h2v(A[0:64, 0:256])
                        if j == 0:
                            CP(pzn[:, :, 0, :], pzc[:, :, 0, :], [kc_], [kn_])
                        else:
                            TT(pzn[:, :, 0, :], pzc[:, :, 0, :], q13[:, :, 0:64], ADD, [kc_, ak], [kn_])
                        if j != 5:
                            CP(pzn[:, :, 1, :], q13[:, :, 64:128], [ak], [kn_])
                            CP(B[zt_nxt], h2v(Bk_[0:64, 0:128]), [bk], [K_(zt_nxt, c)], eng="act")
                        nstate[c] = (1 - cur, zt_nxt)
                    return f

                par_steps = [P0, P1, P2, P3]
                for j in range(6):
                    par_steps += [NM(j), NE(j)]
                for step in par_steps:
                    for c in range(nch):
                        step(c)
                stage(11)
                if smp:
                    B = CS[0]
                    for s in range(NS):
                        S.dma(Stmp[:, :].rearrange("v (h k) -> v h k", h=2),
                              st_wkv[s, 2 * p:2 * p + 2, :, :].rearrange("h v k -> v h k"), writes=["Stmp"])
                        pq, pqk = PSL()
                        TRP(pq[:, 0:64], Stmp[:, :], ["Stmp"], [pqk])
                        for hj in range(2):
                            hs = slice(64 * hj, 64 * hj + 64)
                            CP(Msz[hs, hj, s, :], pq[hs, 0:64], [pqk], ["Ms"], eng="act")
                    TT(KKX, B["TL"][:, 0, :].unsqueeze(1).to_broadcast([128, 16, 64]), selm, MULT, ["TL0", "selm"], ["KKX"])
                    TT(RX, B["TL"][:, 1, :].unsqueeze(1).to_broadcast([128, 16, 64]), selm, MULT, ["TL0", "selm"], ["RX"])
                if pipelined and p + 1 < NHP:
                    pend[:] = part1_groups(p + 1, (p + 1) % 2)
                    pstate["i"] = 0
                else:
                    pend[:] = []
                    pstate["i"] = 0
                for c in range(nch):
                    B = CS[c]
                    cs_ = slice(64 * c, 64 * c + 64)
                    cur, _zt = nstate[c]
                    PTt, ptk = B["PZ"][cur], K_(f"pz{cur}_", c)
                    px, pxk = PSL()
                    px2, pxk2 = PSL()
                    py, pyk = PSL()
                    for hj in range(2):
                        hs = slice(64 * hj, 64 * hj + 64)
                        o_ = px[0:64, 64 * hj:64 * hj + 64]
                        oy = py[0:64, 64 * hj:64 * hj + 64]
                        if smp:
                            for s in range(NS):
                                MM(o_, KKX[:, s, :], Msz[:, hj, s, :], s == 0, s == NS - 1, ["KKX", "Ms"], [pxk])
                            for s in range(NS):
                                MM(oy, RX[:, s, :], Msz[:, hj, s, :], s == 0, s == NS - 1, ["RX", "Ms"], [pyk])
                        else:
                            MM(o_, B["TLz"][:, hj, 0, :], Mst[:, p, :], True, True, [K_("TLz", c), "state"], [pxk])
                            MM(oy, B["TLz"][:, hj, 1, :], Mst[:, p, :], True, True, [K_("TLz", c), "state"], [pyk])
                        MM(px2[0:64, 64 * hj:64 * hj + 64], B["LKT"][:, hj, :], B["Vtok"][:, hs], True, True,
                           [K_("LKT", c), K_("Vtok", c)], [pxk2])
                    CP(B["X"], h2v(px[0:64, 0:128]), [pxk], [K_("X", c)])
                    TT(B["X"], B["X"], h2v(px2[0:64, 0:128]), ADD, [K_("X", c), pxk2], [K_("X", c)])
                    CP(B["Ytok"], py[0:64, 0:128], [pyk], [K_("Ytok", c)], eng="act")
                    pump()
                    pu, puk = PSL()
                    for hj in range(2):
                        MM(pu[0:64, 64 * hj:64 * hj + 64], PTt[:, hj, 0, :], B["X"][:, hj, :], True, True, [ptk, K_("X", c)], [puk])
                    CP(B["U"], h2v(pu[0:64, 0:128]), [puk], [K_("U", c)])
                    pump()
                    stage(12)
                    if smp:
                        TT(NBX, B["nbtok"].unsqueeze(1).to_broadcast([64, 16, 128]),
                           rowsel[:].unsqueeze(2).to_broadcast([64, 16, 128]), MULT, ["nbtok0", "masks"], ["NBX"])
                        TT(KX, B["ktok"].unsqueeze(1).to_broadcast([64, 16, 128]),
                           rowsel[:].unsqueeze(2).to_broadcast([64, 16, 128]), MULT, ["ktok0", "masks"], ["KX"])
                        for s in range(NS):
                            pm, pmk = PSL()
                            for hj in range(2):
                                hs = slice(64 * hj, 64 * hj + 64)
                                o_ = pm[:, 64 * hj:64 * hj + 64]
                                MM(o_, NBX[:, s, :], B["U"][:, hj, :], True, False, ["NBX", "U0"], [pmk])
                                MM(o_, KX[:, s, :], B["Vtok"][:, hs], False, True, ["KX", "Vtok0"], [pmk])
                            for hj in range(2):
                                hs = slice(64 * hj, 64 * hj + 64)
                                TT(Msz[hs, hj, s, :], Msz[hs, hj, s, :], pm[hs, 64 * hj:64 * hj + 64], ADD, ["Ms", pmk], ["Ms"])
                            TSC(Msz[:, :, s, :], Msz[:, :, s, :], PV["eW"][:, 4 * s + 3:4 * s + 4], None, MULT, None, ["Ms", "eW"], ["Ms"])
                            TT(PV["t4"][:, 0:64], Msz[:, 0, s, :], Msz[:, 1, s, :], ADD, ["Ms"], ["t4"])
                            pq, pqk = PSL()
                            TRP(pq[0:64, 0:128], PV["t4"][:, 0:64], ["t4"], [pqk])
                            CP(Sout[:], pq[0:64, 0:128], [pqk], ["Sout"], eng="act")
                            S.dma(wkv_s[s, 2 * p:2 * p + 2, :, :].rearrange("h v k -> v h k"),
                                  Sout[:, :].rearrange("v (h k) -> v h k", h=2), reads=["Sout"])
                    else:
                        pm, pmk = PSL()
                        for hj in range(2):
                            hs = slice(64 * hj, 64 * hj + 64)
                            o_ = pm[:, 64 * hj:64 * hj + 64]
                            MM(o_, B["nbtok"], B["U"][:, hj, :], True, False, [K_("nbtok", c), K_("U", c)], [pmk])
                            MM(o_, B["ktok"], B["Vtok"][:, hs], False, True, [K_("ktok", c), K_("Vtok", c)], [pmk])
                        for hj in range(2):
                            hs = slice(64 * hj, 64 * hj + 64)
                            TT(Mst[hs, p, :], Mst[hs, p, :], pm[hs, 64 * hj:64 * hj + 64], ADD, ["state", pmk], ["state"])
                        TSC(Mst[:, p, :], Mst[:, p, :], PV["eW"][:, 64 * c + 63:64 * c + 64], None, MULT, None, ["state", "eW"], ["state"])
                        pump()
                    if not pre:
                        py2, pyk2 = PSL()
                        for hj in range(2):
                            hs = slice(64 * hj, 64 * hj + 64)
                            o2_ = py2[0:64, 64 * hj:64 * hj + 64]
                            MM(o2_, B["NRB"][:, hj, :], B["U"][:, hj, :], True, False, [K_("NRB", c), K_("U", c)], [pyk2])
                            MM(o2_, B["RKT"][:, hj, :], B["Vtok"][:, hs], False, True, [K_("RKT", c), K_("Vtok", c)], [pyk2])
                        TT(B["Ytok"], B["Ytok"], py2[0:64, 0:128], ADD, [K_("Ytok", c), pyk2], [K_("Ytok", c)])
                        pt_, ptk2 = PSL()
                        TRP(pt_[:, 0:64], B["Ytok"], [K_("Ytok", c)], [ptk2])
                        CP(PV["y"][:, cs_], pt_[:, 0:64], [ptk2], ["y"], eng="act")
                pump(100)
                if last:
                    pq, pqk = PSL()
                    TRP(pq[0:64, 0:128], Mst[:, p, :], ["state"], [pqk])
                    CP(Sout[:], pq[0:64, 0:128], [pqk], ["Sout"], eng="act")
                    S.dma(wkv_p[2 * p:2 * p + 2, :, :].rearrange("h v k -> v h k"),
                          Sout[:, :].rearrange("v (h k) -> v h k", h=2), reads=["Sout"])
                if not pre:
                    stage(13)
                    ps_, pk = PSL()
                    MM(ps_[:, 0:T], bones[:], V["y"], True, True, ["bones", "y"], [pk])
                    STT(V["t2"], ps_[:, 0:T], -1.0 / 64, V["y"], MULT, ADD, [pk, "y"], ["t2"])
                    TT(V["t3"], V["t2"], V["t2"], MULT, ["t2"], ["t3"])
                    ps_, pk = PSL()
                    MM(ps_[:, 0:T], bones[:], V["t3"], True, True, ["bones", "t3"], [pk])
                    RSQRT(V["t3"], ps_[:, 0:T], [pk], ["t3"], scale=1.0 / 64, eps=GN_EPS)
                    TT(V["t2"], V["t2"], V["t3"], MULT, ["t2", "t3"], ["t2"])
                    TSC(V["t2"], V["t2"], pt("lnx_g", p), pt("lnx_b", p), MULT, ADD, ["t2", "PT"], ["t2"])
                    TT(V["t2"], V["t2"], V["bonus"], ADD, ["t2", "bonus"], ["t2"])
                    TT(V["t2"], V["t2"], GG[alt][:, 0:T], MULT, ["t2", f"g{alt}"], ["t2"])
                    TT(merged[:, p, 0:T], V["t2"], SGA[alt][:, 0:T], MULT, ["t2", f"sga{alt}"], [("mg", p)])

            if not pre:
                stage(14)
                allzc = [("zcp", p) for p in range(16)]
                ps1, pk1 = PSL()
                for kc in range(16):
                    MM(ps1[:, 0:T], ones_bf[:], zc[:, kc, 0:T], kc == 0, kc == 15, ["ones"] + allzc, [pk1])
                TSC(rstd[:, 0:T], ps1[:, 0:T], -1.0 / D, None, MULT, None, [pk1], ["rstd"])
                TT(dx[:, :, 0:T], zc[:, :, 0:T], zc[:, :, 0:T], MULT, allzc, ["dx"])
                ps1, pk1 = PSL()
                for kc in range(16):
                    MM(ps1[:, 0:T], ones_bf[:], dx[:, kc, 0:T], kc == 0, kc == 15, ["ones", "dx"], [pk1])
                TT(rstd2[:, 0:T], rstd[:, 0:T], rstd[:, 0:T], MULT, ["rstd"], ["rstd2"])
                STT(rstd2[:, 0:T], ps1[:, 0:T], 1.0 / D, rstd2[:, 0:T], MULT, SUB, [pk1, "rstd2"], ["rstd2"])
                RSQRT(rstd2[:, 0:T], rstd2[:, 0:T], ["rstd2"], ["rstd2"], eps=1e-5)
                for p in range(NHP):
                    wv, wk = WLOAD(kview(w_in)[:, :, 6 * D + 128 * p:6 * D + 128 * p + 128], (128, 16, 128))
                    ps_, pk = PSL()
                    for kc in range(16):
                        MM(ps_[:, 0:T], wv[:, kc, :], h[:, kc, 0:T], kc == 0, kc == 15, [wk, "h"], [pk])
                    ACT(PV["gb"][:, 0:T], ps_[:, 0:T], AF.Sigmoid, [pk], ["gb"])
                    t2 = PV["t2"][:, 0:T]
                    TT(t2, zc[:, p, 0:T], rstd[:, 0:T], ADD, allzc + ["rstd"], ["t2"])
                    TT(t2, t2, rstd2[:, 0:T], MULT, ["t2", "rstd2"], ["t2"])
                    TSC(t2, t2, pt("ln_conv_g", p), pt("ln_conv_b", p), MULT, ADD, ["t2", "PT"], ["t2"])
                    ACT(t2, t2, AF.Silu, ["t2"], ["t2"])
                    TT(t2, t2, PV["gb"][:, 0:T], MULT, ["t2", "gb"], ["t2"])
                    TT(merged[:, p, 0:T], merged[:, p, 0:T], t2, ADD, [("mg", p), "t2"], [("mg", p)])
                allmg = [("mg", p) for p in range(16)]
                stage(15)
                for m in range(16):
                    wv, wk = WLOAD(kview(w_out)[:, :, 128 * m:128 * m + 128], (128, 16, 128))
                    ps_, pk = PSL()
                    for kc in range(16):
                        MM(ps_[:, 0:T], wv[:, kc, :], merged[:, kc, 0:T], kc == 0, kc == 15, [wk] + allmg, [pk])
                    if smp:
                        t23 = PV["t2"][:, 0:T].rearrange("p (s t) -> p s t", t=4)
                        TT(t23, ps_[:, 0:T].rearrange("p (s t) -> p s t", t=4), TS[:, 2, m, :].unsqueeze(2).to_broadcast([128, 16, 4]),
                           MULT, [pk, "TS"], ["t2"])
                        TT(xT[:, m, 0:T], xT[:, m, 0:T], PV["t2"][:, 0:T], ADD, ["xT", "t2"], ["xT"])
                    else:
                        STT(xT[:, m, 0:T], ps_[:, 0:T], TP[:, 2, m:m + 1], xT[:, m, 0:T], MULT, ADD, [pk, "TP", "xT"], ["xT"])
                stage(16)
                rms(xT, "xT", rstd)
                modnorm(xT, "xT", rstd, 3, 4)
                CP(h[:, :, 0:T], hf[:, :, 0:T], ["mixact"], ["h"], eng="act")
                act = mixbf[:, 0:NFC * TB].rearrange("p (f t) -> p f t", t=TB)
                for f in range(NFC):
                    for half in range(2):
                        ch = f + NFC * half
                        wv, wk = WLOAD(kview(w_up)[:, :, 128 * ch:128 * ch + 128], (128, 16, 128))
                        ps_, pk = PSL()
                        for kc in range(16):
                            MM(ps_[:, 0:T], wv[:, kc, :], h[:, kc, 0:T], kc == 0, kc == 15, [wk, "h"], [pk])
                        k0_, k1_, k2_ = (pt("ffn_dw_k", 86 * j + ch) for j in range(3))
                        bb = pt("ffn_dw_b", ch)
                        o = PV["t2"] if half == 0 else PV["t3"]
                        okey = "t2" if half == 0 else "t3"
                        if smp:
                            ze = uext[:, 0:96].rearrange("p (s t) -> p s t", t=6)
                            S.dma(Stmp[0:32, 0:128], st_ffn[:, :, 128 * ch:128 * ch + 128].rearrange("s t c -> (s t) c"), writes=["Stmp"])
                            pq, pqk = PSL()
                            TRP(pq[:, 0:32], Stmp[0:32, 0:128], ["Stmp"], [pqk])
                            CP(ze[:, :, 0:2], pq[:, 0:32].rearrange("p (s t) -> p s t", t=2), [pqk], ["uext"])
                            CP(ze[:, :, 2:6], ps_[:, 0:64].rearrange("p (s t) -> p s t", t=4), [pk], ["uext"], eng="act")
                            o3 = o[:, 0:64].rearrange("p (s t) -> p s t", t=4)
                            TSC(o3, ze[:, :, 0:4], k0_, bb, MULT, ADD, ["uext", "PT"], [okey])
                            STT(o3, ze[:, :, 1:5], k1_, o3, MULT, ADD, ["uext", "PT", okey], [okey])
                            STT(o3, ze[:, :, 2:6], k2_, o3, MULT, ADD, ["uext", "PT", okey], [okey])
                            pq, pqk = PSL()
                            CP(PV["t4"][:, 0:64].rearrange("p (t s) -> p t s", t=4), ze[:, :, 2:6].rearrange("p s t -> p t s"), ["uext"], ["t4"])
                            TRP(pq[0:64, 0:128], PV["t4"][:, 0:64], ["t4"], [pqk])
                            CP(Ytok[:, :], pq[0:64, 0:128], [pqk], ["Ytok"])
                            for t in range(2):
                                S.dma(ffn_s[:, t, 128 * ch:128 * ch + 128], Ytok[32 + 16 * t:48 + 16 * t, :], reads=["Ytok"])
                        else:
                            CP(uext[:, 0:2], ffnhalo[:, ch, :], ["state"], ["uext"])
                            CP(uext[:, 2:2 + T], ps_[:, 0:T], [pk], ["uext"], eng="act")
                            CP(ffnhalo[:, ch, :], uext[:, T:T + 2], ["uext"], ["state"])
                            TSC(o[:, 0:T], uext[:, 0:T], k0_, bb, MULT, ADD, ["uext", "PT"], [okey])
                            STT(o[:, 0:T], uext[:, 1:1 + T], k1_, o[:, 0:T], MULT, ADD, ["uext", "PT", okey], [okey])
                            STT(o[:, 0:T], uext[:, 2:2 + T], k2_, o[:, 0:T], MULT, ADD, ["uext", "PT", okey], [okey])
                            if last:
                                pq, pqk = PSL()
                                TRP(pq[0:2, 0:128], uext[:, T:T + 2], ["uext"], [pqk])
                                CP(Ytok[0:2, :], pq[0:2, 0:128], [pqk], ["Ytok"])
                                S.dma(ffn_p[:, 128 * ch:128 * ch + 128], Ytok[0:2, :], reads=["Ytok"])
                    ACT(PV["t2"][:, 0:T], PV["t2"][:, 0:T], AF.Silu, ["t2"], ["t2"])
                    TT(act[:, f, 0:T], PV["t2"][:, 0:T], PV["t3"][:, 0:T], MULT, ["t2", "t3"], ["mixact"])
                for m in range(16):
                    ps_, pk = PSL()
                    for g0 in range(0, NFC, 16):
                        n = min(16, NFC - g0)
                        wv, wk = WLOAD(w_down[128 * g0:128 * (g0 + n), 128 * m:128 * m + 128].rearrange("(kc ki) n -> ki kc n", ki=128),
                                       (128, n, 128))
                        for j in range(n):
                            MM(ps_[:, 0:T], wv[:, j, :], act[:, g0 + j, 0:T], g0 + j == 0, g0 + j == NFC - 1, [wk, "mixact"], [pk])
                    if smp:
                        t23 = PV["t2"][:, 0:T].rearrange("p (s t) -> p s t", t=4)
                        TT(t23, ps_[:, 0:T].rearrange("p (s t) -> p s t", t=4), TS[:, 5, m, :].unsqueeze(2).to_broadcast([128, 16, 4]),
                           MULT, [pk, "TS"], ["t2"])
                        TT(xT[:, m, 0:T], xT[:, m, 0:T], PV["t2"][:, 0:T], ADD, ["xT", "t2"], ["xT"])
                    else:
                        STT(xT[:, m, 0:T], ps_[:, 0:T], TP[:, 5, m:m + 1], xT[:, m, 0:T], MULT, ADD, [pk, "TP", "xT"], ["xT"])
                stage(17)
                rms(xT, "xT", rstd)
                TT(hf[:, :, 0:T], xT[:, :, 0:T], rstd[:, 0:T].unsqueeze(1).to_broadcast([128, 16, T]), MULT, ["xT", "rstd"], ["mixact"])
                TT(hf[:, :, 0:T], hf[:, :, 0:T], bc(ptr("normf_g", 0, 16)), MULT, ["mixact", "PT"], ["mixact"])
                R = 64 if smp else 128
                for ti_ in range(max(1, T // 128)):
                    for kc in range(16):
                        ps_, pk = PSL()
                        TRP(ps_[0:R, 0:128], hf[:, kc, 128 * ti_:128 * ti_ + R], ["mixact"], [pk])
                        CP(otile[0:R, 128 * kc:128 * kc + 128], ps_[0:R, 0:128], [pk], ["xtile"], eng="act" if kc % 2 else "dve")
                    S.dma(ys if smp else yp[TB * bi + 128 * ti_:TB * bi + 128 * ti_ + 128, :], otile[0:R, :], reads=["xtile"])

        try:
            for bi in range(n_pre):
                emit_block("q", bi)
            for bi in range(n_pblocks):
                emit_block("p", bi)
            if do_sample:
                S.dma(conv_s[:, 0:26, :], st_conv[:, 4:30, :])
                allset = [f"{n}{c}" for c in range(1, 4) for n in
                          ["TL", "TR", "TLz", "TRz", "Vtok", "nbtok", "ktok", "Ytok", "pz0_", "pz1_"] + CTN]
                S.op("pool", lambda e: e.memset(smpreg[:], 0.0), writes=allset + ["smpreg", "NBX", "KX", "KKX", "RX", "Ms", "selm", "sga1", "g1"])
                S.op("pool", lambda e: e.memset(selm, 1.0), writes=["selm"])
                selm4 = selm.rearrange("p s (a b) -> p s a b", b=4)
                S.op("pool", lambda e: e.affine_select(out=selm4, in_=selm4, pattern=[[1, 16], [-1, 16], [0, 4]], compare_op=ALU.is_equal,
                                                       fill=0.0, base=0, channel_multiplier=0), reads=["selm"], writes=["selm"])
                emit_block("s", 0)
        except StopEmit:
            pass
        S.finish()
        print("instructions:", S.ninstr, "sbuf left", nc.sbuf_bytes_remaining)
    return nc


_NC_CACHE = {}
_CFG = {}


def kernel(**inputs):
    f32 = lambda a: np.ascontiguousarray(np.asarray(a, dtype=np.float32))
    I = {k: f32(v) for k, v in inputs.items()}
    ncores = 8
    npre, nmain = _CFG.get("npre", NPRE), _CFG.get("nmain", NMAIN)
    prm = np.zeros((NPROW, 128), np.float32)
    for name, cntc in _PSPEC:
        prm[POFF[name]:POFF[name] + cntc] = I[name].reshape(cntc, 128)
    if "nc" not in _NC_CACHE:
        _NC_CACHE["nc"] = build_nc()
    nc = _NC_CACHE["nc"]
    shared = {k: I[k] for k in ["w_ada", "w_in", "w1", "w2", "a1", "a2", "g1", "g2", "w_out", "w_up", "w_down"]}
    in_maps = []
    for c in range(ncores):
        m = dict(shared)
        m["params"] = prm
        seq, half = c // 2, c % 2
        x = I["x_prompt"][seq]
        if half:
            m["xq"] = x[0:npre * TB] if npre else np.zeros((TB, D), np.float32)
            m["xp"] = x[npre * TB:(npre + nmain) * TB]
        else:
            m["xq"] = np.zeros((max(npre, 1) * TB, D), np.float32)
            m["xp"] = np.concatenate([np.zeros((TB, D), np.float32), x[0:(nmain - 1) * TB]], 0)
        m["flag"] = np.full((128, 1), float(half), np.float32)
        sl = slice(NS * c, NS * c + NS)
        m["xs"] = I["x_sample"][sl].reshape(64, D)
        m["st_shift"] = I["state_shift"][sl]
        m["st_wkv"] = I["state_wkv"][sl]
        m["st_conv"] = I["state_conv"][sl]
        m["st_ffn"] = I["state_ffn"][sl]
        m["cvec"] = np.concatenate([I["c_prompt"][seq][None], I["c_sample"][sl]], 0)
        in_maps.append({k: np.ascontiguousarray(v) for k, v in m.items()})
    res = run_bass_kernel_spmd(nc, in_maps, core_ids=list(range(ncores)))
    R = res.results
    nv = (nmain - 1) * TB
    y_p = np.zeros((4, SEQ, D), np.float32)
    for c in range(ncores):
        seq, half = c // 2, c % 2
        t0 = (npre + 1) * TB if half else 0
        y_p[seq, t0:t0 + nv] = R[c]["yp"][TB:TB + nv]
    odd = [2 * s_ + 1 for s_ in range(4)]
    shift_p = np.stack([R[c]["shift_p"].reshape(D) for c in odd])
    wkv_p = np.stack([R[c]["wkv_p"] for c in odd])
    conv_p = np.stack([R[c]["conv_p"] for c in odd])
    ffn_p = np.stack([R[c]["ffn_p"] for c in odd])
    y_s = np.concatenate([R[c]["ys"].reshape(NS, 4, D) for c in range(8)])
    shift_s = np.concatenate([R[c]["shift_s"] for c in range(8)])
    wkv_s = np.concatenate([R[c]["wkv_s"] for c in range(8)])
    conv_s = np.concatenate([R[c]["conv_s"] for c in range(8)])
    ffn_s = np.concatenate([R[c]["ffn_s"] for c in range(8)])
    return (y_p, y_s, shift_p, wkv_p, conv_p, ffn_p, shift_s, wkv_s, conv_s, ffn_s)
```

```python
import numpy as np
from contextlib import ExitStack
import concourse.bass as bass
import concourse.mybir as mybir
from concourse.bass_utils import run_bass_kernel_spmd

F32 = mybir.dt.float32
BF16 = mybir.dt.bfloat16
TB = 256
NPRE = 3
NMAIN = 5
AF = mybir.ActivationFunctionType
ALU = mybir.AluOpType

D = 2048
NK = 16
SEQ = 2048
NS = 16
DFF = 5504
NFC = 43
F2 = 2 * DFF
NHP = 16
C0 = float(np.exp(-0.5))
GN_EPS = 64 * 1e-5

_PSPEC = [("norm1_g", 16), ("norm2_g", 16), ("normf_g", 16), ("b_ada", 96), ("mu", 96), ("b_glu", 32),
          ("w0", 16), ("a0", 16), ("k_k", 16), ("k_a", 16), ("r_k", 16), ("lnx_g", 16), ("lnx_b", 16),
          ("dw_b", 16), ("ln_conv_g", 16), ("ln_conv_b", 16), ("dw_k", 496), ("ffn_dw_k", 258), ("ffn_dw_b", 86)]
POFF = {}
_o = 0
for _n, _c in _PSPEC:
    POFF[_n] = _o
    _o += _c
NPROW = 1280


STOP = [None]


class StopEmit(Exception):
    pass


def stage(n):
    if STOP[0] is not None and STOP[0] == n:
        raise StopEmit()


class Sched:
    EPOCH = 20000

    def __init__(self, nc, stack, n_dma_sems=16):
        self.nc = nc
        self.stack = stack
        self.engs = {}
        for name, h in (("pe", nc.tensor), ("act", nc.scalar), ("dve", nc.vector),
                        ("pool", nc.gpsimd), ("sp", nc.sync)):
            self.engs[name] = dict(h=h, sems=[], count=0, seen={})
        self.dma_sems = [stack.enter_context(nc.semaphore(f"dq{i}")) for i in range(n_dma_sems)]
        self.dma_uses = [0] * n_dma_sems
        self.dma_next = 0
        self.dma_next_pool = 0
        self.last_w = {}
        self.readers = {}
        self.ninstr = 0

    def _sem_for(self, ename, idx):
        e = self.engs[ename]
        ep = idx // self.EPOCH
        while len(e["sems"]) <= ep:
            e["sems"].append(self.stack.enter_context(self.nc.semaphore(f"s_{ename}{len(e['sems'])}")))
        return e["sems"][ep], idx % self.EPOCH + 1

    def _wait(self, ename, ev):
        e = self.engs[ename]
        if ev[0] == "eng":
            _, src, idx = ev
            if src == ename and ename == "pe":
                return
            if e["seen"].get(src, -1) >= idx:
                return
            sem, val = self._sem_for(src, idx)
            e["h"].wait_ge(sem, val)
            e["seen"][src] = idx
        else:
            _, si, val = ev
            key = ("dma", si)
            if e["seen"].get(key, 0) >= val:
                return
            e["h"].wait_ge(self.dma_sems[si], val)
            e["seen"][key] = val

    def _deps(self, ename, reads, writes):
        evs = []
        for k in reads:
            if k in self.last_w:
                evs.append(self.last_w[k])
        for k in writes:
            if k in self.last_w:
                evs.append(self.last_w[k])
            for r in self.readers.get(k, ()):
                evs.append(r)
        for ev in evs:
            self._wait(ename, ev)

    def _record(self, ev, reads, writes):
        for k in reads:
            self.readers.setdefault(k, []).append(ev)
        for k in writes:
            self.last_w[k] = ev
            self.readers[k] = []

    def op(self, ename, fn, reads=(), writes=()):
        e = self.engs[ename]
        self._deps(ename, reads, writes)
        idx = e["count"]
        sem, val = self._sem_for(ename, idx)
        ins = fn(e["h"])
        ins.then_inc(sem, 1)
        e["count"] += 1
        self.ninstr += 1
        self._record(("eng", ename, idx), reads, writes)

    def dma(self, out, in_, reads=(), writes=(), q="sp", **kw):
        qname = q
        e = self.engs[qname]
        self._deps(qname, reads, writes)
        half = len(self.dma_sems) // 2
        if qname == "pool":
            si = half + self.dma_next_pool % half
            self.dma_next_pool += 1
        else:
            si = self.dma_next % half
            self.dma_next += 1
        prev = self.dma_uses[si]
        if prev > 0:
            self._wait(qname, ("dma", si, 16 * prev))
        self.dma_uses[si] = prev + 1
        e["h"].dma_start(out=out, in_=in_, **kw).then_inc(self.dma_sems[si], 16)
        self.ninstr += 1
        ev = ("dma", si, 16 * (prev + 1))
        self._record(ev, reads, writes)

    def finish(self):
        for si, uses in enumerate(self.dma_uses):
            if uses:
                self._wait("sp", ("dma", si, 16 * uses))
        for name, e in self.engs.items():
            if name != "sp" and e["count"]:
                self._wait("sp", ("eng", name, e["count"] - 1))


def build_nc(n_pblocks=NMAIN, do_sample=True, n_pre=NPRE):
    nc = bass.Bass("TRN2", target_bir_lowering=False)
    din = lambda n, s: nc.dram_tensor(n, s, F32, kind="ExternalInput").ap()
    dout = lambda n, s: nc.dram_tensor(n, s, F32, kind="ExternalOutput").ap()
    xp = din("xp", [n_pblocks * TB, D]); xq = din("xq", [max(n_pre, 1) * TB, D]); flag = din("flag", [128, 1]); xs = din("xs", [64, D]); st_shift = din("st_shift", [NS, D])
    st_wkv = din("st_wkv", [NS, 32, 64, 64]); st_conv = din("st_conv", [NS, 30, D]); st_ffn = din("st_ffn", [NS, 2, F2])
    cvec = din("cvec", [17, D]); params = din("params", [NPROW, 128])
    w_ada = din("w_ada", [D, 6 * D]); w_in = din("w_in", [D, 7 * D])
    w1 = din("w1", [D, 96]); w2 = din("w2", [96, D]); a1 = din("a1", [D, 96]); a2 = din("a2", [96, D])
    g1 = din("g1", [D, 256]); g2 = din("g2", [256, D]); w_out = din("w_out", [D, D])
    w_up = din("w_up", [D, F2]); w_down = din("w_down", [DFF, D])
    yp = dout("yp", [n_pblocks * TB, D]); ys = dout("ys", [64, D]); shift_p = dout("shift_p", [16, 128])
    wkv_p = dout("wkv_p", [32, 64, 64]); conv_p = dout("conv_p", [30, D]); ffn_p = dout("ffn_p", [2, F2])
    shift_s = dout("shift_s", [NS, D]); wkv_s = dout("wkv_s", [NS, 32, 64, 64])
    conv_s = dout("conv_s", [NS, 30, D]); ffn_s = dout("ffn_s", [NS, 2, F2])

    kview = lambda W: W.rearrange("(kc ki) n -> ki kc n", ki=128)

    with ExitStack() as st:
        cnt = [0]

        def sb(shape, name=None, dt=F32):
            cnt[0] += 1
            return st.enter_context(nc.sbuf_tensor(name or f"t{cnt[0]}", shape, dt))

        ident = sb([128, 128], "ident"); ones = sb([128, TB], "ones"); bones = sb([128, 128], "bones")
        ones_bf = sb([128, 128], "ones_bf", BF16)
        m_su = sb([64, 64]); m_u = sb([64, 64]); m_sl = sb([64, 64]); blk = sb([64, 64])
        ms_su = sb([64, 64]); ms_u = sb([64, 64]); ms_sl = sb([64, 64])
        rowsel = sb([64, 16])
        PT = sb([128, NPROW], "PT")
        TP = sb([128, 6, 16], "TP"); TS = sb([128, 6, 16, 16], "TS")
        omu = sb([128, 96]); omka = sb([128, 16]); flg = sb([128, 1], "flg")
        shiftst = sb([128, 16]); Mst = sb([128, 16, 64]); convhalo = sb([128, 16, 30], None, BF16); ffnhalo = sb([128, 86, 2])
        xtile = sb([128, D], "xtile"); xT = sb([128, 16, TB], "xT"); h = sb([128, 16, TB], "h", BF16)
        dx = sb([128, 16, TB], "dx", BF16); mixact = sb([128, 6144], "mixact")
        mixbf = mixact[:, :].bitcast(BF16)
        hf = mixact[:, 0:16 * TB].rearrange("p (k t) -> p k t", t=TB)
        MT = mixact[:, 0:96 * 17].rearrange("p (m c) -> p m c", c=17)
        zc = sb([128, 16, TB], "zc", BF16); merged = sb([128, 16, TB], "merged", BF16)
        rstd = sb([128, TB]); rstd2 = sb([128, TB])
        ring = [sb([128, 16, 128], f"ring{i}", BF16) for i in range(4)]
        PV = {n: sb([128, TB], "pv_" + n) for n in
              ["r", "k0", "v", "lw", "a", "g", "sga", "glu", "glb", "kk", "kkn", "t1", "k", "b", "rk", "bonus",
               "cs", "cx", "eW", "eWi", "eWp", "y", "t2", "t3", "t4", "gb"]}
        for n in ["tw", "xa1", "sg0", "sg1"]:
            PV[n] = sb([128, TB], "pv_" + n, BF16)
        zext = sb([128, 32 + TB], "zext2", BF16); uext = sb([128, 8 + TB], "uext"); zsbf = sb([128, 16, 34], "zsbf", BF16)
        smpreg = sb([128, 9216], "smpreg")
        CTN = ["NRB", "LKT", "RKT", "X", "U", "ZTa", "ZTb"]

        def mkset(c):
            B = {}
            if c == 0:
                B["TL"] = sb([128, 2, 64])[:]; B["TR"] = sb([128, 2, 64])[:]
                B["TLz"] = sb([128, 2, 2, 64])[:]; B["TRz"] = sb([128, 2, 2, 64])[:]
                for n in CTN:
                    B[n] = sb([64, 2, 64], "ct_" + n)[:]
                B["PZ"] = [sb([64, 2, 2, 64], f"pz{i}")[:] for i in range(2)]
                for n in ["Vtok", "nbtok", "ktok", "Ytok"]:
                    B[n] = sb([64, 128])[:]
            else:
                o = [2688 * (c - 1)]

                def take(nparts, words):
                    v = smpreg[0:nparts, o[0]:o[0] + words]
                    o[0] += words
                    return v
                B["TL"] = take(128, 128).rearrange("p (a t) -> p a t", a=2)
                B["TR"] = take(128, 128).rearrange("p (a t) -> p a t", a=2)
                B["TLz"] = take(128, 256).rearrange("p (h a t) -> p h a t", h=2, a=2)
                B["TRz"] = take(128, 256).rearrange("p (h a t) -> p h a t", h=2, a=2)
                for n in CTN:
                    B[n] = take(64, 128).rearrange("p (h t) -> p h t", h=2)
                B["PZ"] = [take(64, 256).rearrange("p (h a t) -> p h a t", h=2, a=2) for i in range(2)]
                for n in ["Vtok", "nbtok", "ktok", "Ytok"]:
                    B[n] = take(64, 128)
            return B

        CS = [mkset(c) for c in range(4)]
        Ytok = sb([64, 128], "Ytok_st")
        NBX = smpreg[0:64, 0:2048].rearrange("p (s c) -> p s c", s=16)
        KX = smpreg[0:64, 2048:4096].rearrange("p (s c) -> p s c", s=16)
        KKX = smpreg[:, 4096:5120].rearrange("p (s c) -> p s c", s=16)
        RX = smpreg[:, 5120:6144].rearrange("p (s c) -> p s c", s=16)
        Msz = smpreg[:, 6144:8192].rearrange("p (h s c) -> p h s c", h=2, s=16)
        selm = smpreg[:, 8192:9216].rearrange("p (s c) -> p s c", s=16)
        Stmp = sb([64, 128]); Sout = sb([64, 128])
        StB = [Stmp[:, :]] + [PV[n][0:64, 128:256] for n in ("bonus", "rk", "kk")]
        SoB = [Sout[:, :]] + [PV[n][0:64, 128:256] for n in ("t1", "kkn", "k")]
        SGA = [PV["sga"], smpreg[:, 8064:8064 + TB]]
        GG = [PV["g"], smpreg[:, 8064 + TB:8064 + 2 * TB]]
        otile = xtile
        cvt = xtile[0:17, :]; cT = sb([128, 16, 17], None, BF16); ptile = sb([128, 128])
        diag = sb([128, 31, 128], "diag", BF16)
        pss = [st.enter_context(nc.psum_tensor(f"ps{i}", [128, 512], F32)) for i in range(8)]
        block = st.enter_context(nc.Block())
        S = Sched(nc, st)

        psi = [0]

        def PSL():
            i = psi[0] % 8
            psi[0] += 1
            return pss[i][:, :], ("ps", i)

        def MM(out, lhsT, rhs, start, stop, r, w):
            S.op("pe", lambda e: e.matmul(out, lhsT=lhsT, rhs=rhs, start=start, stop=stop), reads=r, writes=w)

        def TRP(out, in_, r, w):
            k = in_.shape[0]
            S.op("pe", lambda e: e.transpose(out=out, in_=in_, identity=ident[0:k, 0:k]), reads=list(r) + ["ident"], writes=w)

        def TT(out, a, b, op, r, w, eng="dve"):
            S.op(eng, lambda e: e.tensor_tensor(out=out, in0=a, in1=b, op=op), reads=r, writes=w)

        def TSC(out, a, s1, s2, op0, op1, r, w, eng="dve"):
            if s2 is None:
                S.op(eng, lambda e: e.tensor_scalar(out=out, in0=a, scalar1=s1, scalar2=None, op0=op0), reads=r, writes=w)
            else:
                S.op(eng, lambda e: e.tensor_scalar(out=out, in0=a, scalar1=s1, scalar2=s2, op0=op0, op1=op1), reads=r, writes=w)

        def STT(out, a, sc, b, op0, op1, r, w):
            S.op("dve", lambda e: e.scalar_tensor_tensor(out=out, in0=a, scalar=sc, in1=b, op0=op0, op1=op1), reads=r, writes=w)

        def CP(out, a, r, w, eng="dve"):
            if eng == "act":
                S.op("act", lambda e: e.copy(out=out, in_=a), reads=r, writes=w)
            else:
                S.op(eng, lambda e: e.tensor_copy(out=out, in_=a), reads=r, writes=w)

        def ACT(out, a, func, r, w, bias=0.0, scale=1.0):
            S.op("act", lambda e: e.activation(out=out, in_=a, func=func, bias=bias, scale=scale), reads=r, writes=w)

        def RSQRT(out, a, r, w, scale=1.0, eps=0.0):
            ACT(out, a, AF.Sqrt, r, w, bias=eps, scale=scale)
            S.op("dve", lambda e: e.reciprocal(out=out, in_=out), reads=w, writes=w)

        ringi = [0]

        def WLOAD(src, shape3):
            i = ringi[0] % 4
            ringi[0] += 1
            kp, nk, n = shape3
            dst = ring[i][0:kp, 0:nk, 0:n]
            S.dma(dst, src, writes=[("ring", i)], q="pool")
            return dst, ("ring", i)

        MULT, ADD, SUB = ALU.mult, ALU.add, ALU.subtract
        pt = lambda name, c: PT[:, POFF[name] + c:POFF[name] + c + 1]
        ptr = lambda name, c0, n: PT[:, POFF[name] + c0:POFF[name] + c0 + n]

        S.op("pool", lambda e: e.memset(ident[:], 0.0), writes=["ident"])
        S.op("pool", lambda e: e.affine_select(out=ident[:], in_=ident[:], pattern=[[-1, 128]], compare_op=ALU.not_equal,
                                               fill=1.0, base=0, channel_multiplier=1), reads=["ident"], writes=["ident"])
        S.op("pool", lambda e: e.memset(ones[:], 1.0), writes=["ones"])
        S.op("pool", lambda e: e.memset(ones_bf[:], 1.0), writes=["ones"])
        S.op("pool", lambda e: e.memset(bones[:], 0.0), writes=["bones"])
        S.op("pool", lambda e: e.memset(bones[0:64, 0:64], 1.0), reads=["bones"], writes=["bones"])
        S.op("pool", lambda e: e.memset(bones[64:128, 64:128], 1.0), reads=["bones"], writes=["bones"])
        for m, cm, pat, op in ((m_su, -1, 1, ALU.is_gt), (m_u, -1, 1, ALU.is_ge), (m_sl, 1, -1, ALU.is_gt)):
            S.op("pool", lambda e: e.memset(m[:], 1.0), writes=["masks"])
            S.op("pool", lambda e: e.affine_select(out=m[:], in_=m[:], pattern=[[pat, 64]], compare_op=op, fill=0.0,
                                                   base=0, channel_multiplier=cm), reads=["masks"], writes=["masks"])
        S.op("pool", lambda e: e.memset(blk[:], 1.0), writes=["masks"])
        blk3 = blk[:].rearrange("p (a b) -> p a b", b=4)
        S.op("pool", lambda e: e.affine_select(out=blk3, in_=blk3, pattern=[[-4, 16], [0, 4]], compare_op=ALU.is_ge, fill=0.0,
                                               base=0, channel_multiplier=1), reads=["masks"], writes=["masks"])
        S.op("pool", lambda e: e.affine_select(out=blk3, in_=blk3, pattern=[[4, 16], [0, 4]], compare_op=ALU.is_ge, fill=0.0,
                                               base=3, channel_multiplier=-1), reads=["masks"], writes=["masks"])
        for ms, m in ((ms_su, m_su), (ms_u, m_u), (ms_sl, m_sl)):
            TT(ms[:], m[:], blk[:], MULT, ["masks"], ["masks"], eng="pool")
        S.op("pool", lambda e: e.memset(rowsel[:], 1.0), writes=["masks"])
        S.op("pool", lambda e: e.affine_select(out=rowsel[:], in_=rowsel[:], pattern=[[-4, 16]], compare_op=ALU.is_ge, fill=0.0,
                                               base=0, channel_multiplier=1), reads=["masks"], writes=["masks"])
        S.op("pool", lambda e: e.affine_select(out=rowsel[:], in_=rowsel[:], pattern=[[4, 16]], compare_op=ALU.is_ge, fill=0.0,
                                               base=3, channel_multiplier=-1), reads=["masks"], writes=["masks"])
        for t_ in (shiftst, Mst, convhalo, ffnhalo):
            S.op("pool", lambda e: e.memset(t_[:], 0.0), writes=["state"])
        S.op("pool", lambda e: e.memset(smpreg[:], 0.0), writes=["smpreg"])
        for c in range(4):
            S.op("pool", lambda e: e.memset(CS[c]["TLz"], 0.0), reads=["smpreg"], writes=[f"TLz{c}"])
            S.op("pool", lambda e: e.memset(CS[c]["TRz"], 0.0), reads=["smpreg"], writes=[f"TRz{c}"])

        S.dma(flg[:], flag, writes=["flg"])
        S.op("pool", lambda e: e.memset(PV["r"][:], 0.0), writes=["r"])
        for i in range(NPROW // 128):
            S.dma(ptile[:], params[128 * i:128 * i + 128, :], writes=["ptile"])
            ps_, pk = PSL()
            TRP(ps_[:, 0:128], ptile[:], ["ptile"], [pk])
            CP(PT[:, 128 * i:128 * i + 128], ps_[:, 0:128], [pk], ["PT"])
        TSC(omu[:], ptr("mu", 0, 96), -1.0, 1.0, MULT, ADD, ["PT"], ["PT2"])
        TSC(omka[:], ptr("k_a", 0, 16), -1.0, 1.0, MULT, ADD, ["PT"], ["PT2"])

        S.dma(cvt, cvec, writes=["xtile"])
        ACT(cvt, cvt, AF.Silu, ["xtile"], ["xtile"])
        for kc in range(16):
            ps_, pk = PSL()
            TRP(ps_[:, 0:17], cvt[:, 128 * kc:128 * kc + 128], ["xtile"], [pk])
            CP(cT[:, kc, :], ps_[:, 0:17], [pk], ["cT"])
        for m in range(96):
            wv, wk = WLOAD(kview(w_ada)[:, :, 128 * m:128 * m + 128], (128, 16, 128))
            ps_, pk = PSL()
            for kc in range(16):
                MM(ps_[:, 0:17], wv[:, kc, :], cT[:, kc, :], kc == 0, kc == 15, [wk, "cT"], [pk])
            TSC(MT[:, m, :], ps_[:, 0:17], pt("b_ada", m), None, ADD, None, [pk, "PT"], ["mixact"])
        for (ti, mi, kind) in ((0, 1, "gs1"), (1, 0, "sh"), (2, 2, "gt"), (3, 4, "gs2"), (4, 3, "sh"), (5, 5, "gt")):
            src_p = MT[:, 16 * mi:16 * mi + 16, 0]
            src_s = MT[:, 16 * mi:16 * mi + 16, 1:17]
            if kind.startswith("gs"):
                g = ptr("norm1_g" if kind == "gs1" else "norm2_g", 0, 16)
                STT(TP[:, ti, :], src_p, 1.0, g, ADD, MULT, ["mixact", "PT"], ["TP"])
                STT(TS[:, ti, :, :], src_s, 1.0, g.unsqueeze(2).to_broadcast([128, 16, 16]), ADD, MULT, ["mixact", "PT"], ["TS"])
            else:
                CP(TP[:, ti, :], src_p, ["mixact"], ["TP"])
                CP(TS[:, ti, :, :], src_s, ["mixact"], ["TS"])

        def emit_block(kind, bi):
            smp = kind == "s"
            pre = kind == "q"
            msk = pre or (kind == "p" and bi == 0)
            T = 64 if smp else TB
            nch = T // 64
            bc = lambda tab16: tab16.unsqueeze(2).to_broadcast([128, 16, T])
            v4 = lambda ap: ap.rearrange("p k (s t) -> p k s t", t=4)
            last = kind == "p" and bi == n_pblocks - 1

            xsrc_rows = 64 if smp else 128
            for ti_ in range(max(1, T // 128)):
                xsrc = xs if smp else (xq if pre else xp)[TB * bi + 128 * ti_:TB * bi + 128 * ti_ + 128, :]
                S.dma(xtile[0:xsrc_rows, :], xsrc, writes=["xtile"])
                R = xsrc_rows
                for q in range(4):
                    ps_, pk0 = PSL()
                    for j in range(4):
                        kc = 4 * q + j
                        TRP(ps_[:, j * R:(j + 1) * R], xtile[0:R, 128 * kc:128 * kc + 128], ["xtile"], [pk0])
                    CP(xT[:, 4 * q:4 * q + 4, 128 * ti_:128 * ti_ + R], ps_[:, 0:4 * R].rearrange("p (a t) -> p a t", a=4),
                       [pk0], ["xT"], eng="act")
            stage(1)

            def rms(src, key, out_rstd):
                TT(dx[:, :, 0:T], src[:, :, 0:T], src[:, :, 0:T], MULT, [key], ["dx"])
                ps_, pk = PSL()
                for kc in range(16):
                    MM(ps_[:, 0:T], ones_bf[:], dx[:, kc, 0:T], kc == 0, kc == 15, ["ones", "dx"], [pk])
                RSQRT(out_rstd[:, 0:T], ps_[:, 0:T], [pk], ["rstd"], scale=1.0 / D, eps=1e-6)

            def modnorm(src, skey, rs, tg, tsft):
                d = hf[:, :, 0:T]
                TT(d, src[:, :, 0:T], rs[:, 0:T].unsqueeze(1).to_broadcast([128, 16, T]), MULT, [skey, "rstd"], ["mixact"])
                if smp:
                    for (tix, op) in ((tg, MULT), (tsft, ADD)):
                        TT(v4(d), v4(d), TS[:, tix, :, :].unsqueeze(3).to_broadcast([128, 16, 16, 4]), op, ["mixact", "TS"], ["mixact"])
                else:
                    TT(d, d, bc(TP[:, tg, :]), MULT, ["mixact", "TP"], ["mixact"])
                    TT(d, d, bc(TP[:, tsft, :]), ADD, ["mixact", "TP"], ["mixact"])
                    if msk:
                        TSC(d, d, flg[:, 0:1], None, MULT, None, ["mixact", "flg"], ["mixact"])

            rms(xT, "xT", rstd)
            modnorm(xT, "xT", rstd, 0, 1)
            CP(h[:, :, 0:T], hf[:, :, 0:T], ["mixact"], ["h"], eng="act")
            stage(2)
            if smp:
                S.dma(otile[0:16, :], st_shift, writes=["xtile"])
                h4 = v4(hf[:, :, 0:64])
                d4 = v4(dx[:, :, 0:64])
                for q in range(4):
                    ps_, pk = PSL()
                    for j in range(4):
                        TRP(ps_[:, 16 * j:16 * j + 16], otile[0:16, 128 * (4 * q + j):128 * (4 * q + j) + 128], ["xtile"], [pk])
                    TT(d4[:, 4 * q:4 * q + 4, :, 0], ps_[:, 0:64].rearrange("p (a s) -> p a s", a=4), h4[:, 4 * q:4 * q + 4, :, 0],
                       SUB, [pk, "mixact"], ["dx"])
                TT(d4[:, :, :, 1:4], h4[:, :, :, 0:3], h4[:, :, :, 1:4], SUB, ["mixact"], ["dx"])
                for kc in range(16):
                    ps_, pk = PSL()
                    TRP(ps_[0:16, 0:128], h4[:, kc, :, 3], ["mixact"], [pk])
                    CP(otile[0:16, 128 * kc:128 * kc + 128], ps_[0:16, 0:128], [pk], ["xtile"])
                S.dma(shift_s, otile[0:16, :], reads=["xtile"])
            else:
                TT(dx[:, :, 0], shiftst[:], hf[:, :, 0], SUB, ["state", "mixact"], ["dx"])
                TT(dx[:, :, 1:T], hf[:, :, 0:T - 1], hf[:, :, 1:T], SUB, ["mixact"], ["dx"])
                CP(shiftst[:], hf[:, :, T - 1], ["mixact"], ["state"])
                if last:
                    ps_, pk = PSL()
                    TRP(ps_[0:16, 0:128], shiftst[:], ["state"], [pk])
                    CP(otile[0:16, 0:128], ps_[0:16, 0:128], [pk], ["xtile"])
                    S.dma(shift_p, otile[0:16, 0:128], reads=["xtile"])
            stage(3)

            def mix(dst, dkey, mi):
                TT(dst, dx[:, :, 0:T], bc(ptr("mu", 16 * mi, 16)), MULT, ["dx", "PT"], [dkey])
                TT(dst, dst, h[:, :, 0:T], ADD, [dkey, "h"], [dkey])

            xmix = [mixbf[:, 16 * TB * i:16 * TB * (i + 1)].rearrange("p (k t) -> p k t", t=TB)[:, :, 0:T] for i in range(3)]
            tmpx = zc[:, :, 0:T]
            lora1 = [(w1, 96, 1, [PV["tw"]], AF.Tanh), (a1, 96, 4, [PV["xa1"]], AF.Copy)]
            if not pre:
                lora1.append((g1, 256, 5, [PV["sg0"], PV["sg1"]], AF.Sigmoid))
            for (W, ncol, mi, outs, func) in lora1:
                mix(tmpx, "zc", mi)
                for oi, o in enumerate(outs):
                    n = min(128, ncol - 128 * oi)
                    wv, wk = WLOAD(kview(W)[:, :, 128 * oi:128 * oi + n], (128, 16, n))
                    ps_, pk = PSL()
                    for kc in range(16):
                        MM(ps_[0:n, 0:T], wv[:, kc, :], tmpx[:, kc, :], kc == 0, kc == 15, [wk, "zc"], [pk])
                    ACT(o[0:n, 0:T], ps_[0:n, 0:T], func, [pk], ["lora"])
            if not pre:
                mix(xmix[0], "mixact", 0)
            mix(xmix[1], "mixact", 2)
            mix(xmix[2], "mixact", 3)
            stage(4)

            def part1_groups(p, alt):
                hh = h[:, :, 0:T]
                specs = []
                if not pre:
                    specs.append(("w", xmix[0], "mixact", 0, PV["r"], AF.Copy, "r", 0.0))
                specs.append(("w", xmix[1], "mixact", D, PV["k0"], AF.Copy, "k0", 0.0))
                specs.append(("w", xmix[2], "mixact", 2 * D, PV["v"], AF.Copy, "v", 0.0))
                if not pre:
                    specs.append(("w", hh, "h", 3 * D, PV["glu"], AF.Identity, "glu", pt("b_glu", p)))
                    specs.append(("w", hh, "h", 4 * D, PV["glb"], AF.Sigmoid, "glb", pt("b_glu", 16 + p)))
                    specs.append(("w", hh, "h", 5 * D, SGA[alt], AF.Sigmoid, f"sga{alt}", 0.0))
                specs.append(("l", w2, 96, [PV["tw"]], PV["lw"], AF.Sigmoid, pt("w0", p), "lw"))
                specs.append(("l", a2, 96, [PV["xa1"]], PV["a"], AF.Sigmoid, pt("a0", p), "a"))
                if not pre:
                    specs.append(("l", g2, 256, [PV["sg0"], PV["sg1"]], GG[alt], AF.Copy, 0.0, f"g{alt}"))
                groups = []
                for sp_ in specs:
                    st_ = {}
                    if sp_[0] == "w":
                        _, src, skey, col0, out, func, okey, bias = sp_

                        def load(st_=st_, col0=col0):
                            st_["w"] = WLOAD(kview(w_in)[:, :, col0 + 128 * p:col0 + 128 * p + 128], (128, 16, 128))

                        def comp(st_=st_, src=src, skey=skey, out=out, func=func, okey=okey, bias=bias):
                            wv, wk = st_["w"]
                            ps_, pk = PSL()
                            for kc in range(16):
                                MM(ps_[:, 0:T], wv[:, kc, :], src[:, kc, :], kc == 0, kc == 15, [wk, skey], [pk])
                            ACT(out[:, 0:T], ps_[:, 0:T], func, [pk, "PT"], [okey], bias=bias)
                    else:
                        _, W, K, srcs, out, func, bias, okey = sp_
                        nk = len(srcs)
                        kp = min(K, 128)

                        def load(st_=st_, W=W, kp=kp, nk=nk):
                            st_["w"] = WLOAD(W.rearrange("(kc ki) n -> ki kc n", ki=kp)[:, :, 128 * p:128 * p + 128], (kp, nk, 128))

                        def comp(st_=st_, srcs=srcs, out=out, func=func, bias=bias, okey=okey, kp=kp, nk=nk):
                            wv, wk = st_["w"]
                            ps_, pk = PSL()
                            for j in range(nk):
                                MM(ps_[:, 0:T], wv[:, j, :], srcs[j][0:kp, 0:T], j == 0, j == nk - 1, [wk, "lora"], [pk])
                            ACT(out[:, 0:T], ps_[:, 0:T], func, [pk, "PT"], [okey], bias=bias)
                    groups.append((load, comp))
                return groups

            def run_groups(groups):
                for i, (ld, cp_) in enumerate(groups):
                    if i == 0:
                        ld()
                        if len(groups) > 1:
                            groups[1][0]()
                    cp_()
                    if i + 2 < len(groups):
                        groups[i + 2][0]()

            pipelined = not smp
            pend = []
            pstate = {"i": 0}

            def pump(n=1):
                for _ in range(n):
                    i = pstate["i"]
                    if i >= len(pend):
                        return
                    if i == 0:
                        pend[0][0]()
                        if len(pend) > 1:
                            pend[1][0]()
                    pend[i][1]()
                    if i + 2 < len(pend):
                        pend[i + 2][0]()
                    pstate["i"] = i + 1

            if pipelined:
                run_groups(part1_groups(0, 0))
            for p in range(NHP):
                alt = (p % 2) if pipelined else 0
                if not pipelined:
                    run_groups(part1_groups(p, 0))
                stage(5)
                V = {n: PV[n][:, 0:T] for n in PV}
                if not pre:
                    TT(V["glu"], V["glu"], V["glb"], MULT, ["glu", "glb"], ["glu"])
                    if msk:
                        TSC(V["glu"], V["glu"], flg[:, 0:1], None, MULT, None, ["glu", "flg"], ["glu"])
                    for j in range(31):
                        TSC(diag[:, j, :], ident[:], pt("dw_k", 16 * j + p), None, MULT, None, ["ident", "PT"], ["diag"])
                    ps_, pk = PSL()
                    if smp:
                        S.dma(otile[0:120, 0:512].rearrange("r (q c) -> r q c", q=4),
                              st_conv[:, :, 128 * p:128 * p + 128].rearrange("(q s) t c -> (s t) q c", q=4), writes=["xtile"])
                        for q in range(4):
                            pq, pqk = PSL()
                            TRP(pq[:, 0:120], otile[0:120, 128 * q:128 * q + 128], ["xtile"], [pqk])
                            CP(zsbf[:, 4 * q:4 * q + 4, 0:30], pq[:, 0:120].rearrange("p (s t) -> p s t", t=30), [pqk], ["zsbf"])
                        CP(zsbf[:, :, 30:34], V["glu"].rearrange("p (s t) -> p s t", t=4), ["glu"], ["zsbf"])
                        for j in range(31):
                            MM(ps_[:, 0:64], diag[:, j, :], zsbf[:, :, j:j + 4], j == 0, j == 30, ["diag", "zsbf"], [pk])
                        pq, pqk = PSL()
                        CP(PV["t4"][:, 0:64].rearrange("p (t s) -> p t s", t=4), V["glu"].rearrange("p (s t) -> p t s", t=4), ["glu"], ["t4"])
                        TRP(pq[0:64, 0:128], PV["t4"][:, 0:64], ["t4"], [pqk])
                        CP(Ytok[:, :], pq[0:64, 0:128], [pqk], ["Ytok"])
                        for t in range(4):
                            S.dma(conv_s[:, 26 + t, 128 * p:128 * p + 128], Ytok[16 * t:16 * t + 16, :], reads=["Ytok"])
                    else:
                        CP(zext[:, 0:30], convhalo[:, p, :], ["state"], ["zext"])
                        CP(zext[:, 30:30 + T], V["glu"], ["glu"], ["zext"], eng="act")
                        for j in range(31):
                            MM(ps_[:, 0:T], diag[:, j, :], zext[:, j:j + T], j == 0, j == 30, ["diag", "zext"], [pk])
                        CP(convhalo[:, p, :], zext[:, T:T + 30], ["zext"], ["state"])
                        if last:
                            pq, pqk = PSL()
                            TRP(pq[0:30, 0:128], PV["glu"][:, T - 30:T], ["glu"], [pqk])
                            CP(Ytok[0:30, :], pq[0:30, 0:128], [pqk], ["Ytok"])
                            S.dma(conv_p[:, 128 * p:128 * p + 128], Ytok[0:30, :], reads=["Ytok"])
                    TSC(zc[:, p, 0:T], ps_[:, 0:T], pt("dw_b", p), None, ADD, None, [pk, "PT"], [("zcp", p)])

                stage(6)
                TSC(V["kk"], V["k0"], pt("k_k", p), None, MULT, None, ["k0", "PT"], ["kk"])
                TT(V["t2"], V["kk"], V["kk"], MULT, ["kk"], ["t2"])
                ps_, pk = PSL()
                MM(ps_[:, 0:T], bones[:], V["t2"], True, True, ["bones", "t2"], [pk])
                RSQRT(V["t3"], ps_[:, 0:T], [pk], ["t3"], eps=1e-30)
                TT(V["kkn"], V["kk"], V["t3"], MULT, ["kk", "t3"], ["kkn"])
                TSC(V["t1"], V["a"], pt("k_a", p), omka[:, p:p + 1], MULT, ADD, ["a", "PT", "PT2"], ["t1"])
                TT(V["k"], V["k0"], V["t1"], MULT, ["k0", "t1"], ["k"])
                TT(V["b"], V["kkn"], V["a"], MULT, ["kkn", "a"], ["b"])
                if not pre:
                    STT(V["rk"], V["r"], pt("r_k", p), V["k"], MULT, MULT, ["r", "k", "PT"], ["rk"])
                    ps_, pk = PSL()
                    MM(ps_[:, 0:T], bones[:], V["rk"], True, True, ["bones", "rk"], [pk])
                    TT(V["bonus"], ps_[:, 0:T], V["v"], MULT, [pk, "v"], ["bonus"])
                stage(7)
                L = 4 if smp else 64
                nseg = T // L
                S.op("dve", lambda e: e.tensor_tensor_scan(out=V["cs"], data0=ones[:, 0:T], data1=V["lw"], initial=0.0,
                                                           op0=MULT, op1=ADD), reads=["ones", "lw"], writes=["cs"])
                TT(V["cx"], V["cs"], V["lw"], SUB, ["cs", "lw"], ["cx"])
                cs3 = V["cs"].rearrange("p (s l) -> p s l", l=L)
                cx3 = V["cx"].rearrange("p (s l) -> p s l", l=L)
                CP(PV["t4"][:, 0:nseg], cx3[:, :, 0], ["cx"], ["t4"])
                base = PV["t4"][:, 0:nseg].unsqueeze(2).to_broadcast([128, nseg, L])
                TT(cs3, cs3, base, SUB, ["cs", "t4"], ["cs"])
                TT(cx3, cx3, base, SUB, ["cx", "t4"], ["cx"])
                ACT(V["eW"], V["cs"], AF.Exp, ["cs"], ["eW"], scale=-C0)
                ACT(V["eWi"], V["cs"], AF.Exp, ["cs"], ["eWi"], scale=C0)
                ACT(V["eWp"], V["cx"], AF.Exp, ["cx"], ["eWp"], scale=-C0)

                stage(8)
                msu, mu_, msl = (ms_su, ms_u, ms_sl) if smp else (m_su, m_u, m_sl)
                b2 = lambda m: m[:].unsqueeze(1).to_broadcast([64, 2, 64])
                h2v = lambda ap: ap.rearrange("p (h n) -> p h n", h=2)
                K_ = lambda n, c: f"{n}{c}"

                def bankA(c):
                    return pss[2 * c][:, :], ("ps", 2 * c)

                def bankB(c):
                    return pss[2 * c + 1][:, :], ("ps", 2 * c + 1)

                def P0(c):
                    B = CS[c]
                    cs_ = slice(64 * c, 64 * c + 64)
                    TT(B["TL"][:, 0, :], PV["kkn"][:, cs_], PV["eWp"][:, cs_], MULT, ["kkn", "eWp"], [K_("TL", c)])
                    TT(B["TL"][:, 1, :], PV["r"][:, cs_], PV["eW"][:, cs_], MULT, ["r", "eW"], [K_("TL", c)])
                    TT(B["TR"][:, 0, :], PV["b"][:, cs_], PV["eWi"][:, cs_], MULT, ["b", "eWi"], [K_("TR", c)])
                    TT(B["TR"][:, 1, :], PV["k"][:, cs_], PV["eWi"][:, cs_], MULT, ["k", "eWi"], [K_("TR", c)])
                    for hj in range(2):
                        hs = slice(64 * hj, 64 * hj + 64)
                        CP(B["TLz"][hs, hj, :, :], B["TL"][hs, :, :], [K_("TL", c)], [K_("TLz", c)], eng="act")
                        CP(B["TRz"][hs, hj, :, :], B["TR"][hs, :, :], [K_("TR", c)], [K_("TRz", c)], eng="act")

                def P1(c):
                    B = CS[c]
                    A, ak = bankA(c)
                    Bk_, bk = bankB(c)
                    for hj in range(2):
                        TLh = B["TLz"][:, hj, :, :].rearrange("p a t -> p (a t)")
                        MM(A[0:64, 128 * hj:128 * hj + 128], B["TRz"][:, hj, 0, :], TLh, True, True, [K_("TRz", c), K_("TLz", c)], [ak])
                        MM(Bk_[0:64, 128 * hj:128 * hj + 128], B["TRz"][:, hj, 1, :], TLh, True, True, [K_("TRz", c), K_("TLz", c)], [bk])
                        MM(A[0:64, 256 + 64 * hj:256 + 64 * hj + 64], B["TLz"][:, hj, 0, :], B["TRz"][:, hj, 0, :], True, True,
                           [K_("TRz", c), K_("TLz", c)], [ak])

                def P2(c):
                    B = CS[c]
                    A, ak = bankA(c)
                    Bk_, bk = bankB(c)
                    pa3 = h2v(A[0:64, 0:256]); pb3 = h2v(Bk_[0:64, 0:256]); pc3 = h2v(A[0:64, 256:384])
                    pz = B["PZ"][0]
                    STT(pz[:, :, 1, :], pa3[:, :, 0:64], -1.0, b2(msu), MULT, MULT, [ak, "masks"], [K_("pz0_", c)])
                    STT(B["NRB"], pa3[:, :, 64:128], -1.0, b2(mu_), MULT, MULT, [ak, "masks"], [K_("NRB", c)])
                    STT(B["ZTa"], pc3, -1.0, b2(msl), MULT, MULT, [ak, "masks"], [K_("ZTa", c)])
                    TT(B["LKT"], pb3[:, :, 0:64], b2(msu), MULT, [bk, "masks"], [K_("LKT", c)])
                    TT(B["RKT"], pb3[:, :, 64:128], b2(mu_), MULT, [bk, "masks"], [K_("RKT", c)])
                    TT(pz[:, :, 0, :], pz[:, :, 1, :], ident[0:64, 0:64].unsqueeze(1).to_broadcast([64, 2, 64]), ADD,
                       [K_("pz0_", c), "ident"], [K_("pz0_", c)])

                def P3(c):
                    B = CS[c]
                    cs_ = slice(64 * c, 64 * c + 64)
                    A, ak = bankA(c)
                    TRP(A[0:64, 0:128], PV["v"][:, cs_], ["v"], [ak])
                    TRP(A[0:64, 128:256], B["TR"][:, 0, :], [K_("TR", c)], [ak])
                    TRP(A[0:64, 256:384], B["TR"][:, 1, :], [K_("TR", c)], [ak])
                    CP(B["Vtok"], A[0:64, 0:128], [ak], [K_("Vtok", c)], eng="act")
                    S.op("act", lambda e: e.mul(out=B["nbtok"], in_=A[0:64, 128:256], mul=-1.0), reads=[ak], writes=[K_("nbtok", c)])
                    CP(B["ktok"], A[0:64, 256:384], [ak], [K_("ktok", c)], eng="act")

                nstate = {}

                def NM(j):
                    def f(c):
                        B = CS[c]
                        cur, zt_cur = nstate.get(c, (0, "ZTa"))
                        pzc = B["PZ"][cur]
                        kc_ = K_(f"pz{cur}_", c)
                        A, ak = bankA(c)
                        Bk_, bk = bankB(c)
                        ztk = K_(zt_cur, c)
                        for hj in range(2):
                            if j == 0:
                                MM(A[0:64, 128 * hj + 64:128 * hj + 128], B[zt_cur][:, hj, :], pzc[:, hj, 1, :], True, True, [ztk, kc_], [ak])
                            elif j == 5:
                                MM(A[0:64, 128 * hj:128 * hj + 64], B[zt_cur][:, hj, :], pzc[:, hj, 0, :], True, True, [ztk, kc_], [ak])
                            else:
                                MM(A[0:64, 128 * hj:128 * hj + 128], B[zt_cur][:, hj, :], pzc[:, hj, :, :].rearrange("p a t -> p (a t)"),
                                   True, True, [ztk, kc_], [ak])
                            if j != 5:
                                MM(Bk_[0:64, 64 * hj:64 * hj + 64], pzc[:, hj, 1, :], B[zt_cur][:, hj, :], True, True, [ztk, kc_], [bk])
                    return f

                def NE(j):
                    def f(c):
                        B = CS[c]
                        cur, zt_cur = nstate.get(c, (0, "ZTa"))
                        zt_nxt = "ZTb" if zt_cur == "ZTa" else "ZTa"
                        pzc, pzn = B["PZ"][cur], B["PZ"][1 - cur]
                        kc_, kn_ = K_(f"pz{cur}_", c), K_(f"pz{1 - cur}_", c)
                        A, ak = bankA(c)
                        Bk_, bk = bankB(c)
                        q13 = h2v(A[0:64, 0:256])
                        if j == 0:
                            CP(pzn[:, :, 0, :], pzc[:, :, 0, :], [kc_], [kn_])
                        else:
                            TT(pzn[:, :, 0, :], pzc[:, :, 0, :], q13[:, :, 0:64], ADD, [kc_, ak], [kn_])
                        if j != 5:
                            CP(pzn[:, :, 1, :], q13[:, :, 64:128], [ak], [kn_])
                            CP(B[zt_nxt], h2v(Bk_[0:64, 0:128]), [bk], [K_(zt_nxt, c)], eng="act")
                        nstate[c] = (1 - cur, zt_nxt)
                    return f

                par_steps = [P0, P1, P2, P3]
                for j in range(6):
                    par_steps += [NM(j), NE(j)]
                for step in par_steps:
                    for c in range(nch):
                        step(c)
                stage(11)
                if smp:
                    B = CS[0]
                    for s in range(NS):
                        stb, stk = StB[s % 4], f"Stmp{s % 4}"
                        S.dma(stb.rearrange("v (h k) -> v h k", h=2),
                              st_wkv[s, 2 * p:2 * p + 2, :, :].rearrange("h v k -> v h k"), writes=[stk])
                        pq, pqk = PSL()
                        TRP(pq[:, 0:64], stb, [stk], [pqk])
                        for hj in range(2):
                            hs = slice(64 * hj, 64 * hj + 64)
                            CP(Msz[hs, hj, s, :], pq[hs, 0:64], [pqk], ["Ms"], eng="act")
                    TT(KKX, B["TL"][:, 0, :].unsqueeze(1).to_broadcast([128, 16, 64]), selm, MULT, ["TL0", "selm"], ["KKX"])
                    TT(RX, B["TL"][:, 1, :].unsqueeze(1).to_broadcast([128, 16, 64]), selm, MULT, ["TL0", "selm"], ["RX"])
                if pipelined and p + 1 < NHP:
                    pend[:] = part1_groups(p + 1, (p + 1) % 2)
                    pstate["i"] = 0
                else:
                    pend[:] = []
                    pstate["i"] = 0
                for c in range(nch):
                    B = CS[c]
                    cs_ = slice(64 * c, 64 * c + 64)
                    cur, _zt = nstate[c]
                    PTt, ptk = B["PZ"][cur], K_(f"pz{cur}_", c)
                    px, pxk = PSL()
                    px2, pxk2 = PSL()
                    py, pyk = PSL()
                    for hj in range(2):
                        hs = slice(64 * hj, 64 * hj + 64)
                        o_ = px[0:64, 64 * hj:64 * hj + 64]
                        oy = py[0:64, 64 * hj:64 * hj + 64]
                        if smp:
                            for s in range(NS):
                                MM(o_, KKX[:, s, :], Msz[:, hj, s, :], s == 0, s == NS - 1, ["KKX", "Ms"], [pxk])
                            for s in range(NS):
                                MM(oy, RX[:, s, :], Msz[:, hj, s, :], s == 0, s == NS - 1, ["RX", "Ms"], [pyk])
                        else:
                            MM(o_, B["TLz"][:, hj, 0, :], Mst[:, p, :], True, True, [K_("TLz", c), "state"], [pxk])
                            MM(oy, B["TLz"][:, hj, 1, :], Mst[:, p, :], True, True, [K_("TLz", c), "state"], [pyk])
                        MM(px2[0:64, 64 * hj:64 * hj + 64], B["LKT"][:, hj, :], B["Vtok"][:, hs], True, True,
                           [K_("LKT", c), K_("Vtok", c)], [pxk2])
                    CP(B["X"], h2v(px[0:64, 0:128]), [pxk], [K_("X", c)])
                    TT(B["X"], B["X"], h2v(px2[0:64, 0:128]), ADD, [K_("X", c), pxk2], [K_("X", c)])
                    CP(B["Ytok"], py[0:64, 0:128], [pyk], [K_("Ytok", c)], eng="act")
                    pump()
                    pu, puk = PSL()
                    for hj in range(2):
                        MM(pu[0:64, 64 * hj:64 * hj + 64], PTt[:, hj, 0, :], B["X"][:, hj, :], True, True, [ptk, K_("X", c)], [puk])
                    CP(B["U"], h2v(pu[0:64, 0:128]), [puk], [K_("U", c)])
                    pump()
                    stage(12)
                    if smp:
                        TT(NBX, B["nbtok"].unsqueeze(1).to_broadcast([64, 16, 128]),
                           rowsel[:].unsqueeze(2).to_broadcast([64, 16, 128]), MULT, ["nbtok0", "masks"], ["NBX"])
                        TT(KX, B["ktok"].unsqueeze(1).to_broadcast([64, 16, 128]),
                           rowsel[:].unsqueeze(2).to_broadcast([64, 16, 128]), MULT, ["ktok0", "masks"], ["KX"])
                        for s in range(NS):
                            pm, pmk = PSL()
                            for hj in range(2):
                                hs = slice(64 * hj, 64 * hj + 64)
                                o_ = pm[:, 64 * hj:64 * hj + 64]
                                MM(o_, NBX[:, s, :], B["U"][:, hj, :], True, False, ["NBX", "U0"], [pmk])
                                MM(o_, KX[:, s, :], B["Vtok"][:, hs], False, True, ["KX", "Vtok0"], [pmk])
                            for hj in range(2):
                                hs = slice(64 * hj, 64 * hj + 64)
                                TT(Msz[hs, hj, s, :], Msz[hs, hj, s, :], pm[hs, 64 * hj:64 * hj + 64], ADD, ["Ms", pmk], ["Ms"])
                            TSC(Msz[:, :, s, :], Msz[:, :, s, :], PV["eW"][:, 4 * s + 3:4 * s + 4], None, MULT, None, ["Ms", "eW"], ["Ms"])
                            TT(PV["t4"][:, 0:64], Msz[:, 0, s, :], Msz[:, 1, s, :], ADD, ["Ms"], ["t4"])
                            pq, pqk = PSL()
                            TRP(pq[0:64, 0:128], PV["t4"][:, 0:64], ["t4"], [pqk])
                            sob, sok = SoB[s % 4], f"Sout{s % 4}"
                            CP(sob, pq[0:64, 0:128], [pqk], [sok], eng="act")
                            S.dma(wkv_s[s, 2 * p:2 * p + 2, :, :].rearrange("h v k -> v h k"),
                                  sob.rearrange("v (h k) -> v h k", h=2), reads=[sok])
                    else:
                        pm, pmk = PSL()
                        for hj in range(2):
                            hs = slice(64 * hj, 64 * hj + 64)
                            o_ = pm[:, 64 * hj:64 * hj + 64]
                            MM(o_, B["nbtok"], B["U"][:, hj, :], True, False, [K_("nbtok", c), K_("U", c)], [pmk])
                            MM(o_, B["ktok"], B["Vtok"][:, hs], False, True, [K_("ktok", c), K_("Vtok", c)], [pmk])
                        for hj in range(2):
                            hs = slice(64 * hj, 64 * hj + 64)
                            TT(Mst[hs, p, :], Mst[hs, p, :], pm[hs, 64 * hj:64 * hj + 64], ADD, ["state", pmk], ["state"])
                        TSC(Mst[:, p, :], Mst[:, p, :], PV["eW"][:, 64 * c + 63:64 * c + 64], None, MULT, None, ["state", "eW"], ["state"])
                        pump()
                    if not pre:
                        py2, pyk2 = PSL()
                        for hj in range(2):
                            hs = slice(64 * hj, 64 * hj + 64)
                            o2_ = py2[0:64, 64 * hj:64 * hj + 64]
                            MM(o2_, B["NRB"][:, hj, :], B["U"][:, hj, :], True, False, [K_("NRB", c), K_("U", c)], [pyk2])
                            MM(o2_, B["RKT"][:, hj, :], B["Vtok"][:, hs], False, True, [K_("RKT", c), K_("Vtok", c)], [pyk2])
                        TT(B["Ytok"], B["Ytok"], py2[0:64, 0:128], ADD, [K_("Ytok", c), pyk2], [K_("Ytok", c)])
                        pt_, ptk2 = PSL()
                        TRP(pt_[:, 0:64], B["Ytok"], [K_("Ytok", c)], [ptk2])
                        CP(PV["y"][:, cs_], pt_[:, 0:64], [ptk2], ["y"], eng="act")
                pump(100)
                if last:
                    pq, pqk = PSL()
                    TRP(pq[0:64, 0:128], Mst[:, p, :], ["state"], [pqk])
                    CP(Sout[:], pq[0:64, 0:128], [pqk], ["Sout0"], eng="act")
                    S.dma(wkv_p[2 * p:2 * p + 2, :, :].rearrange("h v k -> v h k"),
                          Sout[:, :].rearrange("v (h k) -> v h k", h=2), reads=["Sout0"])
                if not pre:
                    stage(13)
                    ps_, pk = PSL()
                    MM(ps_[:, 0:T], bones[:], V["y"], True, True, ["bones", "y"], [pk])
                    STT(V["t2"], ps_[:, 0:T], -1.0 / 64, V["y"], MULT, ADD, [pk, "y"], ["t2"])
                    TT(V["t3"], V["t2"], V["t2"], MULT, ["t2"], ["t3"])
                    ps_, pk = PSL()
                    MM(ps_[:, 0:T], bones[:], V["t3"], True, True, ["bones", "t3"], [pk])
                    RSQRT(V["t3"], ps_[:, 0:T], [pk], ["t3"], scale=1.0 / 64, eps=GN_EPS)
                    TT(V["t2"], V["t2"], V["t3"], MULT, ["t2", "t3"], ["t2"])
                    TSC(V["t2"], V["t2"], pt("lnx_g", p), pt("lnx_b", p), MULT, ADD, ["t2", "PT"], ["t2"])
                    TT(V["t2"], V["t2"], V["bonus"], ADD, ["t2", "bonus"], ["t2"])
                    TT(V["t2"], V["t2"], GG[alt][:, 0:T], MULT, ["t2", f"g{alt}"], ["t2"])
                    TT(merged[:, p, 0:T], V["t2"], SGA[alt][:, 0:T], MULT, ["t2", f"sga{alt}"], [("mg", p)])

            if not pre:
                stage(14)
                allzc = [("zcp", p) for p in range(16)]
                ps1, pk1 = PSL()
                for kc in range(16):
                    MM(ps1[:, 0:T], ones_bf[:], zc[:, kc, 0:T], kc == 0, kc == 15, ["ones"] + allzc, [pk1])
                TSC(rstd[:, 0:T], ps1[:, 0:T], -1.0 / D, None, MULT, None, [pk1], ["rstd"])
                TT(dx[:, :, 0:T], zc[:, :, 0:T], zc[:, :, 0:T], MULT, allzc, ["dx"])
                ps1, pk1 = PSL()
                for kc in range(16):
                    MM(ps1[:, 0:T], ones_bf[:], dx[:, kc, 0:T], kc == 0, kc == 15, ["ones", "dx"], [pk1])
                TT(rstd2[:, 0:T], rstd[:, 0:T], rstd[:, 0:T], MULT, ["rstd"], ["rstd2"])
                STT(rstd2[:, 0:T], ps1[:, 0:T], 1.0 / D, rstd2[:, 0:T], MULT, SUB, [pk1, "rstd2"], ["rstd2"])
                RSQRT(rstd2[:, 0:T], rstd2[:, 0:T], ["rstd2"], ["rstd2"], eps=1e-5)
                for p in range(NHP):
                    wv, wk = WLOAD(kview(w_in)[:, :, 6 * D + 128 * p:6 * D + 128 * p + 128], (128, 16, 128))
                    ps_, pk = PSL()
                    for kc in range(16):
                        MM(ps_[:, 0:T], wv[:, kc, :], h[:, kc, 0:T], kc == 0, kc == 15, [wk, "h"], [pk])
                    ACT(PV["gb"][:, 0:T], ps_[:, 0:T], AF.Sigmoid, [pk], ["gb"])
                    t2 = PV["t2"][:, 0:T]
                    TT(t2, zc[:, p, 0:T], rstd[:, 0:T], ADD, allzc + ["rstd"], ["t2"])
                    TT(t2, t2, rstd2[:, 0:T], MULT, ["t2", "rstd2"], ["t2"])
                    TSC(t2, t2, pt("ln_conv_g", p), pt("ln_conv_b", p), MULT, ADD, ["t2", "PT"], ["t2"])
                    ACT(t2, t2, AF.Silu, ["t2"], ["t2"])
                    TT(t2, t2, PV["gb"][:, 0:T], MULT, ["t2", "gb"], ["t2"])
                    TT(merged[:, p, 0:T], merged[:, p, 0:T], t2, ADD, [("mg", p), "t2"], [("mg", p)])
                allmg = [("mg", p) for p in range(16)]
                stage(15)
                for m in range(16):
                    wv, wk = WLOAD(kview(w_out)[:, :, 128 * m:128 * m + 128], (128, 16, 128))
                    ps_, pk = PSL()
                    for kc in range(16):
                        MM(ps_[:, 0:T], wv[:, kc, :], merged[:, kc, 0:T], kc == 0, kc == 15, [wk] + allmg, [pk])
                    if smp:
                        t23 = PV["t2"][:, 0:T].rearrange("p (s t) -> p s t", t=4)
                        TT(t23, ps_[:, 0:T].rearrange("p (s t) -> p s t", t=4), TS[:, 2, m, :].unsqueeze(2).to_broadcast([128, 16, 4]),
                           MULT, [pk, "TS"], ["t2"])
                        TT(xT[:, m, 0:T], xT[:, m, 0:T], PV["t2"][:, 0:T], ADD, ["xT", "t2"], ["xT"])
                    else:
                        STT(xT[:, m, 0:T], ps_[:, 0:T], TP[:, 2, m:m + 1], xT[:, m, 0:T], MULT, ADD, [pk, "TP", "xT"], ["xT"])
                stage(16)
                rms(xT, "xT", rstd)
                modnorm(xT, "xT", rstd, 3, 4)
                CP(h[:, :, 0:T], hf[:, :, 0:T], ["mixact"], ["h"], eng="act")
                act = mixbf[:, 0:NFC * TB].rearrange("p (f t) -> p f t", t=TB)
                for f in range(NFC):
                    for half in range(2):
                        ch = f + NFC * half
                        wv, wk = WLOAD(kview(w_up)[:, :, 128 * ch:128 * ch + 128], (128, 16, 128))
                        ps_, pk = PSL()
                        for kc in range(16):
                            MM(ps_[:, 0:T], wv[:, kc, :], h[:, kc, 0:T], kc == 0, kc == 15, [wk, "h"], [pk])
                        k0_, k1_, k2_ = (pt("ffn_dw_k", 86 * j + ch) for j in range(3))
                        bb = pt("ffn_dw_b", ch)
                        o = PV["t2"] if half == 0 else PV["t3"]
                        okey = "t2" if half == 0 else "t3"
                        if smp:
                            ze = uext[:, 0:96].rearrange("p (s t) -> p s t", t=6)
                            S.dma(Stmp[0:32, 0:128], st_ffn[:, :, 128 * ch:128 * ch + 128].rearrange("s t c -> (s t) c"), writes=["Stmp0"])
                            pq, pqk = PSL()
                            TRP(pq[:, 0:32], Stmp[0:32, 0:128], ["Stmp0"], [pqk])
                            CP(ze[:, :, 0:2], pq[:, 0:32].rearrange("p (s t) -> p s t", t=2), [pqk], ["uext"])
                            CP(ze[:, :, 2:6], ps_[:, 0:64].rearrange("p (s t) -> p s t", t=4), [pk], ["uext"], eng="act")
                            o3 = o[:, 0:64].rearrange("p (s t) -> p s t", t=4)
                            TSC(o3, ze[:, :, 0:4], k0_, bb, MULT, ADD, ["uext", "PT"], [okey])
                            STT(o3, ze[:, :, 1:5], k1_, o3, MULT, ADD, ["uext", "PT", okey], [okey])
                            STT(o3, ze[:, :, 2:6], k2_, o3, MULT, ADD, ["uext", "PT", okey], [okey])
                            pq, pqk = PSL()
                            CP(PV["t4"][:, 0:64].rearrange("p (t s) -> p t s", t=4), ze[:, :, 2:6].rearrange("p s t -> p t s"), ["uext"], ["t4"])
                            TRP(pq[0:64, 0:128], PV["t4"][:, 0:64], ["t4"], [pqk])
                            CP(Ytok[:, :], pq[0:64, 0:128], [pqk], ["Ytok"])
                            for t in range(2):
                                S.dma(ffn_s[:, t, 128 * ch:128 * ch + 128], Ytok[32 + 16 * t:48 + 16 * t, :], reads=["Ytok"])
                        else:
                            CP(uext[:, 0:2], ffnhalo[:, ch, :], ["state"], ["uext"])
                            CP(uext[:, 2:2 + T], ps_[:, 0:T], [pk], ["uext"], eng="act")
                            CP(ffnhalo[:, ch, :], uext[:, T:T + 2], ["uext"], ["state"])
                            TSC(o[:, 0:T], uext[:, 0:T], k0_, bb, MULT, ADD, ["uext", "PT"], [okey])
                            STT(o[:, 0:T], uext[:, 1:1 + T], k1_, o[:, 0:T], MULT, ADD, ["uext", "PT", okey], [okey])
                            STT(o[:, 0:T], uext[:, 2:2 + T], k2_, o[:, 0:T], MULT, ADD, ["uext", "PT", okey], [okey])
                            if last:
                                pq, pqk = PSL()
                                TRP(pq[0:2, 0:128], uext[:, T:T + 2], ["uext"], [pqk])
                                CP(Ytok[0:2, :], pq[0:2, 0:128], [pqk], ["Ytok"])
                                S.dma(ffn_p[:, 128 * ch:128 * ch + 128], Ytok[0:2, :], reads=["Ytok"])
                    ACT(PV["t2"][:, 0:T], PV["t2"][:, 0:T], AF.Silu, ["t2"], ["t2"])
                    TT(act[:, f, 0:T], PV["t2"][:, 0:T], PV["t3"][:, 0:T], MULT, ["t2", "t3"], ["mixact"])
                for m in range(16):
                    ps_, pk = PSL()
                    for g0 in range(0, NFC, 16):
                        n = min(16, NFC - g0)
                        wv, wk = WLOAD(w_down[128 * g0:128 * (g0 + n), 128 * m:128 * m + 128].rearrange("(kc ki) n -> ki kc n", ki=128),
                                       (128, n, 128))
                        for j in range(n):
                            MM(ps_[:, 0:T], wv[:, j, :], act[:, g0 + j, 0:T], g0 + j == 0, g0 + j == NFC - 1, [wk, "mixact"], [pk])
                    if smp:
                        t23 = PV["t2"][:, 0:T].rearrange("p (s t) -> p s t", t=4)
                        TT(t23, ps_[:, 0:T].rearrange("p (s t) -> p s t", t=4), TS[:, 5, m, :].unsqueeze(2).to_broadcast([128, 16, 4]),
                           MULT, [pk, "TS"], ["t2"])
                        TT(xT[:, m, 0:T], xT[:, m, 0:T], PV["t2"][:, 0:T], ADD, ["xT", "t2"], ["xT"])
                    else:
                        STT(xT[:, m, 0:T], ps_[:, 0:T], TP[:, 5, m:m + 1], xT[:, m, 0:T], MULT, ADD, [pk, "TP", "xT"], ["xT"])
                stage(17)
                rms(xT, "xT", rstd)
                TT(hf[:, :, 0:T], xT[:, :, 0:T], rstd[:, 0:T].unsqueeze(1).to_broadcast([128, 16, T]), MULT, ["xT", "rstd"], ["mixact"])
                TT(hf[:, :, 0:T], hf[:, :, 0:T], bc(ptr("normf_g", 0, 16)), MULT, ["mixact", "PT"], ["mixact"])
                R = 64 if smp else 128
                for ti_ in range(max(1, T // 128)):
                    for kc in range(16):
                        ps_, pk = PSL()
                        TRP(ps_[0:R, 0:128], hf[:, kc, 128 * ti_:128 * ti_ + R], ["mixact"], [pk])
                        CP(otile[0:R, 128 * kc:128 * kc + 128], ps_[0:R, 0:128], [pk], ["xtile"], eng="act" if kc % 2 else "dve")
                    S.dma(ys if smp else yp[TB * bi + 128 * ti_:TB * bi + 128 * ti_ + 128, :], otile[0:R, :], reads=["xtile"])

        try:
            for bi in range(n_pre):
                emit_block("q", bi)
            for bi in range(n_pblocks):
                emit_block("p", bi)
            if do_sample:
                S.dma(conv_s[:, 0:26, :], st_conv[:, 4:30, :])
                allset = [f"{n}{c}" for c in range(1, 4) for n in
                          ["TL", "TR", "TLz", "TRz", "Vtok", "nbtok", "ktok", "Ytok", "pz0_", "pz1_"] + CTN]
                S.op("pool", lambda e: e.memset(smpreg[:], 0.0), writes=allset + ["smpreg", "NBX", "KX", "KKX", "RX", "Ms", "selm", "sga1", "g1", "bonus", "rk", "kk", "t1", "kkn", "k"]
                     + [f"Stmp{i}" for i in range(1, 4)] + [f"Sout{i}" for i in range(1, 4)])
                S.op("pool", lambda e: e.memset(selm, 1.0), writes=["selm"])
                selm4 = selm.rearrange("p s (a b) -> p s a b", b=4)
                S.op("pool", lambda e: e.affine_select(out=selm4, in_=selm4, pattern=[[1, 16], [-1, 16], [0, 4]], compare_op=ALU.is_equal,
                                                       fill=0.0, base=0, channel_multiplier=0), reads=["selm"], writes=["selm"])
                emit_block("s", 0)
        except StopEmit:
            pass
        S.finish()
        print("instructions:", S.ninstr, "sbuf left", nc.sbuf_bytes_remaining)
    return nc


_NC_CACHE = {}
_CFG = {}


def kernel(**inputs):
    f32 = lambda a: np.ascontiguousarray(np.asarray(a, dtype=np.float32))
    I = {k: f32(v) for k, v in inputs.items()}
    ncores = 8
    npre, nmain = _CFG.get("npre", NPRE), _CFG.get("nmain", NMAIN)
    prm = np.zeros((NPROW, 128), np.float32)
    for name, cntc in _PSPEC:
        prm[POFF[name]:POFF[name] + cntc] = I[name].reshape(cntc, 128)
    if "nc" not in _NC_CACHE:
        _NC_CACHE["nc"] = build_nc()
    nc = _NC_CACHE["nc"]
    shared = {k: I[k] for k in ["w_ada", "w_in", "w1", "w2", "a1", "a2", "g1", "g2", "w_out", "w_up", "w_down"]}
    in_maps = []
    for c in range(ncores):
        m = dict(shared)
        m["params"] = prm
        seq, half = c // 2, c % 2
        x = I["x_prompt"][seq]
        if half:
            m["xq"] = x[0:npre * TB] if npre else np.zeros((TB, D), np.float32)
            m["xp"] = x[npre * TB:(npre + nmain) * TB]
        else:
            m["xq"] = np.zeros((max(npre, 1) * TB, D), np.float32)
            m["xp"] = np.concatenate([np.zeros((TB, D), np.float32), x[0:(nmain - 1) * TB]], 0)
        m["flag"] = np.full((128, 1), float(half), np.float32)
        sl = slice(NS * c, NS * c + NS)
        m["xs"] = I["x_sample"][sl].reshape(64, D)
        m["st_shift"] = I["state_shift"][sl]
        m["st_wkv"] = I["state_wkv"][sl]
        m["st_conv"] = I["state_conv"][sl]
        m["st_ffn"] = I["state_ffn"][sl]
        m["cvec"] = np.concatenate([I["c_prompt"][seq][None], I["c_sample"][sl]], 0)
        in_maps.append({k: np.ascontiguousarray(v) for k, v in m.items()})
    res = run_bass_kernel_spmd(nc, in_maps, core_ids=list(range(ncores)))
    R = res.results
    nv = (nmain - 1) * TB
    y_p = np.zeros((4, SEQ, D), np.float32)
    for c in range(ncores):
        seq, half = c // 2, c % 2
        t0 = (npre + 1) * TB if half else 0
        y_p[seq, t0:t0 + nv] = R[c]["yp"][TB:TB + nv]
    odd = [2 * s_ + 1 for s_ in range(4)]
    shift_p = np.stack([R[c]["shift_p"].reshape(D) for c in odd])
    wkv_p = np.stack([R[c]["wkv_p"] for c in odd])
    conv_p = np.stack([R[c]["conv_p"] for c in odd])
    ffn_p = np.stack([R[c]["ffn_p"] for c in odd])
    y_s = np.concatenate([R[c]["ys"].reshape(NS, 4, D) for c in range(8)])
    shift_s = np.concatenate([R[c]["shift_s"] for c in range(8)])
    wkv_s = np.concatenate([R[c]["wkv_s"] for c in range(8)])
    conv_s = np.concatenate([R[c]["conv_s"] for c in range(8)])
    ffn_s = np.concatenate([R[c]["ffn_s"] for c in range(8)])
    return (y_p, y_s, shift_p, wkv_p, conv_p, ffn_p, shift_s, wkv_s, conv_s, ffn_s)
```

```python
import numpy as np
from contextlib import ExitStack
import concourse.bass as bass
import concourse.mybir as mybir
from concourse.bass_utils import run_bass_kernel_spmd

F32 = mybir.dt.float32
BF16 = mybir.dt.bfloat16
TB = 256
NPRE = 3
NMAIN = 5
AF = mybir.ActivationFunctionType
ALU = mybir.AluOpType

D = 2048
NK = 16
SEQ = 2048
NS = 16
DFF = 5504
NFC = 43
F2 = 2 * DFF
NHP = 16
C0 = float(np.exp(-0.5))
GN_EPS = 64 * 1e-5

_PSPEC = [("norm1_g", 16), ("norm2_g", 16), ("normf_g", 16), ("b_ada", 96), ("mu", 96), ("b_glu", 32),
          ("w0", 16), ("a0", 16), ("k_k", 16), ("k_a", 16), ("r_k", 16), ("lnx_g", 16), ("lnx_b", 16),
          ("dw_b", 16), ("ln_conv_g", 16), ("ln_conv_b", 16), ("dw_k", 496), ("ffn_dw_k", 258), ("ffn_dw_b", 86)]
POFF = {}
_o = 0
for _n, _c in _PSPEC:
    POFF[_n] = _o
    _o += _c
NPROW = 1280


STOP = [None]


class StopEmit(Exception):
    pass


def stage(n):
    if STOP[0] is not None and STOP[0] == n:
        raise StopEmit()


class Sched:
    EPOCH = 20000

    def __init__(self, nc, stack, n_dma_sems=16):
        self.nc = nc
        self.stack = stack
        self.engs = {}
        for name, h in (("pe", nc.tensor), ("act", nc.scalar), ("dve", nc.vector),
                        ("pool", nc.gpsimd), ("sp", nc.sync)):
            self.engs[name] = dict(h=h, sems=[], count=0, seen={})
        self.dma_sems = [stack.enter_context(nc.semaphore(f"dq{i}")) for i in range(n_dma_sems)]
        self.dma_uses = [0] * n_dma_sems
        self.dma_next = 0
        self.dma_next_pool = 0
        self.last_w = {}
        self.readers = {}
        self.ninstr = 0

    def _sem_for(self, ename, idx):
        e = self.engs[ename]
        ep = idx // self.EPOCH
        while len(e["sems"]) <= ep:
            e["sems"].append(self.stack.enter_context(self.nc.semaphore(f"s_{ename}{len(e['sems'])}")))
        return e["sems"][ep], idx % self.EPOCH + 1

    def _wait(self, ename, ev):
        e = self.engs[ename]
        if ev[0] == "eng":
            _, src, idx = ev
            if src == ename and ename == "pe":
                return
            if e["seen"].get(src, -1) >= idx:
                return
            sem, val = self._sem_for(src, idx)
            e["h"].wait_ge(sem, val)
            e["seen"][src] = idx
        else:
            _, si, val = ev
            key = ("dma", si)
            if e["seen"].get(key, 0) >= val:
                return
            e["h"].wait_ge(self.dma_sems[si], val)
            e["seen"][key] = val

    def _deps(self, ename, reads, writes):
        evs = []
        for k in reads:
            if k in self.last_w:
                evs.append(self.last_w[k])
        for k in writes:
            if k in self.last_w:
                evs.append(self.last_w[k])
            for r in self.readers.get(k, ()):
                evs.append(r)
        for ev in evs:
            self._wait(ename, ev)

    def _record(self, ev, reads, writes):
        for k in reads:
            self.readers.setdefault(k, []).append(ev)
        for k in writes:
            self.last_w[k] = ev
            self.readers[k] = []

    def op(self, ename, fn, reads=(), writes=()):
        e = self.engs[ename]
        self._deps(ename, reads, writes)
        idx = e["count"]
        sem, val = self._sem_for(ename, idx)
        ins = fn(e["h"])
        ins.then_inc(sem, 1)
        e["count"] += 1
        self.ninstr += 1
        self._record(("eng", ename, idx), reads, writes)

    def dma(self, out, in_, reads=(), writes=(), q="sp", **kw):
        qname = q
        e = self.engs[qname]
        self._deps(qname, reads, writes)
        half = len(self.dma_sems) // 2
        if qname == "pool":
            si = half + self.dma_next_pool % half
            self.dma_next_pool += 1
        else:
            si = self.dma_next % half
            self.dma_next += 1
        prev = self.dma_uses[si]
        if prev > 0:
            self._wait(qname, ("dma", si, 16 * prev))
        self.dma_uses[si] = prev + 1
        e["h"].dma_start(out=out, in_=in_, **kw).then_inc(self.dma_sems[si], 16)
        self.ninstr += 1
        ev = ("dma", si, 16 * (prev + 1))
        self._record(ev, reads, writes)

    def finish(self):
        for si, uses in enumerate(self.dma_uses):
            if uses:
                self._wait("sp", ("dma", si, 16 * uses))
        for name, e in self.engs.items():
            if name != "sp" and e["count"]:
                self._wait("sp", ("eng", name, e["count"] - 1))


def build_nc(n_pblocks=NMAIN, do_sample=True, n_pre=NPRE):
    nc = bass.Bass("TRN2", target_bir_lowering=False)
    din = lambda n, s: nc.dram_tensor(n, s, F32, kind="ExternalInput").ap()
    dout = lambda n, s: nc.dram_tensor(n, s, F32, kind="ExternalOutput").ap()
    xp = din("xp", [n_pblocks * TB, D]); xq = din("xq", [max(n_pre, 1) * TB, D]); flag = din("flag", [128, 1]); xs = din("xs", [64, D]); st_shift = din("st_shift", [NS, D])
    st_wkv = din("st_wkv", [NS, 32, 64, 64]); st_conv = din("st_conv", [NS, 30, D]); st_ffn = din("st_ffn", [NS, 2, F2])
    cvec = din("cvec", [17, D]); params = din("params", [NPROW, 128])
    w_ada = din("w_ada", [D, 6 * D]); w_in = din("w_in", [D, 7 * D])
    w1 = din("w1", [D, 96]); w2 = din("w2", [96, D]); a1 = din("a1", [D, 96]); a2 = din("a2", [96, D])
    g1 = din("g1", [D, 256]); g2 = din("g2", [256, D]); w_out = din("w_out", [D, D])
    w_up = din("w_up", [D, F2]); w_down = din("w_down", [DFF, D])
    yp = dout("yp", [n_pblocks * TB, D]); ys = dout("ys", [64, D]); shift_p = dout("shift_p", [16, 128])
    wkv_p = dout("wkv_p", [32, 64, 64]); conv_p = dout("conv_p", [30, D]); ffn_p = dout("ffn_p", [2, F2])
    shift_s = dout("shift_s", [NS, D]); wkv_s = dout("wkv_s", [NS, 32, 64, 64])
    conv_s = dout("conv_s", [NS, 30, D]); ffn_s = dout("ffn_s", [NS, 2, F2])

    kview = lambda W: W.rearrange("(kc ki) n -> ki kc n", ki=128)

    with ExitStack() as st:
        cnt = [0]

        def sb(shape, name=None, dt=F32):
            cnt[0] += 1
            return st.enter_context(nc.sbuf_tensor(name or f"t{cnt[0]}", shape, dt))

        ident = sb([128, 128], "ident"); ones = sb([128, TB], "ones"); bones = sb([128, 128], "bones")
        ones_bf = sb([128, 128], "ones_bf", BF16)
        m_su = sb([64, 64]); m_u = sb([64, 64]); m_sl = sb([64, 64]); blk = sb([64, 64])
        ms_su = sb([64, 64]); ms_u = sb([64, 64]); ms_sl = sb([64, 64])
        rowsel = sb([64, 16])
        PT = sb([128, NPROW], "PT")
        TP = sb([128, 6, 16], "TP"); TS = sb([128, 6, 16, 16], "TS")
        omu = sb([128, 96]); omka = sb([128, 16]); flg = sb([128, 1], "flg")
        shiftst = sb([128, 16]); Mst = sb([128, 16, 64]); convhalo = sb([128, 16, 30], None, BF16); ffnhalo = sb([128, 86, 2])
        xtile = sb([128, D], "xtile"); xT = sb([128, 16, TB], "xT"); h = sb([128, 16, TB], "h", BF16)
        dx = sb([128, 16, TB], "dx", BF16); mixact = sb([128, 6144], "mixact")
        mixbf = mixact[:, :].bitcast(BF16)
        hf = mixact[:, 0:16 * TB].rearrange("p (k t) -> p k t", t=TB)
        MT = mixact[:, 0:96 * 17].rearrange("p (m c) -> p m c", c=17)
        zc = sb([128, 16, TB], "zc", BF16); merged = sb([128, 16, TB], "merged", BF16)
        rstd = sb([128, TB]); rstd2 = sb([128, TB])
        ring = [sb([128, 16, 128], f"ring{i}", BF16) for i in range(4)]
        PV = {n: sb([128, TB], "pv_" + n) for n in
              ["r", "k0", "v", "lw", "a", "g", "sga", "glu", "glb", "kk", "kkn", "t1", "k", "b", "rk", "bonus",
               "cs", "cx", "eW", "eWi", "eWp", "y", "t2", "t3", "t4", "gb"]}
        for n in ["tw", "xa1", "sg0", "sg1"]:
            PV[n] = sb([128, TB], "pv_" + n, BF16)
        zext = sb([128, 32 + TB], "zext2", BF16); uext = sb([128, 8 + TB], "uext"); zsbf = sb([128, 16, 34], "zsbf", BF16)
        smpreg = sb([128, 9216], "smpreg")
        CTN = ["NRB", "LKT", "RKT", "X", "U", "ZTa", "ZTb"]

        def mkset(c):
            B = {}
            if c == 0:
                B["TL"] = sb([128, 2, 64])[:]; B["TR"] = sb([128, 2, 64])[:]
                B["TLz"] = sb([128, 2, 2, 64])[:]; B["TRz"] = sb([128, 2, 2, 64])[:]
                for n in CTN:
                    B[n] = sb([64, 2, 64], "ct_" + n)[:]
                B["PZ"] = [sb([64, 2, 2, 64], f"pz{i}")[:] for i in range(2)]
                for n in ["Vtok", "nbtok", "ktok", "Ytok"]:
                    B[n] = sb([64, 128])[:]
            else:
                o = [2688 * (c - 1)]

                def take(nparts, words):
                    v = smpreg[0:nparts, o[0]:o[0] + words]
                    o[0] += words
                    return v
                B["TL"] = take(128, 128).rearrange("p (a t) -> p a t", a=2)
                B["TR"] = take(128, 128).rearrange("p (a t) -> p a t", a=2)
                B["TLz"] = take(128, 256).rearrange("p (h a t) -> p h a t", h=2, a=2)
                B["TRz"] = take(128, 256).rearrange("p (h a t) -> p h a t", h=2, a=2)
                for n in CTN:
                    B[n] = take(64, 128).rearrange("p (h t) -> p h t", h=2)
                B["PZ"] = [take(64, 256).rearrange("p (h a t) -> p h a t", h=2, a=2) for i in range(2)]
                for n in ["Vtok", "nbtok", "ktok", "Ytok"]:
                    B[n] = take(64, 128)
            return B

        CS = [mkset(c) for c in range(4)]
        Ytok = sb([64, 128], "Ytok_st")
        NBX = smpreg[0:64, 0:2048].rearrange("p (s c) -> p s c", s=16)
        KX = smpreg[0:64, 2048:4096].rearrange("p (s c) -> p s c", s=16)
        KKX = smpreg[:, 4096:5120].rearrange("p (s c) -> p s c", s=16)
        RX = smpreg[:, 5120:6144].rearrange("p (s c) -> p s c", s=16)
        Msz = smpreg[:, 6144:8192].rearrange("p (h s c) -> p h s c", h=2, s=16)
        selm = smpreg[:, 8192:9216].rearrange("p (s c) -> p s c", s=16)
        Stmp = sb([64, 128]); Sout = sb([64, 128])
        StB = [Stmp[:, :]] + [PV[n][0:64, 128:256] for n in ("bonus", "rk", "kk")]
        SoB = [Sout[:, :]] + [PV[n][0:64, 128:256] for n in ("t1", "kkn", "k")]
        SGA = [PV["sga"], smpreg[:, 8064:8064 + TB]]
        GG = [PV["g"], smpreg[:, 8064 + TB:8064 + 2 * TB]]
        otile = xtile
        cvt = xtile[0:17, :]; cT = sb([128, 16, 17], None, BF16); ptile = sb([128, 128])
        diag = sb([128, 31, 128], "diag", BF16)
        pss = [st.enter_context(nc.psum_tensor(f"ps{i}", [128, 512], F32)) for i in range(8)]
        block = st.enter_context(nc.Block())
        S = Sched(nc, st)

        psi = [0]

        def PSL():
            i = psi[0] % 8
            psi[0] += 1
            return pss[i][:, :], ("ps", i)

        def MM(out, lhsT, rhs, start, stop, r, w):
            S.op("pe", lambda e: e.matmul(out, lhsT=lhsT, rhs=rhs, start=start, stop=stop), reads=r, writes=w)

        def TRP(out, in_, r, w):
            k = in_.shape[0]
            S.op("pe", lambda e: e.transpose(out=out, in_=in_, identity=ident[0:k, 0:k]), reads=list(r) + ["ident"], writes=w)

        def TT(out, a, b, op, r, w, eng="dve"):
            S.op(eng, lambda e: e.tensor_tensor(out=out, in0=a, in1=b, op=op), reads=r, writes=w)

        def TSC(out, a, s1, s2, op0, op1, r, w, eng="dve"):
            if s2 is None:
                S.op(eng, lambda e: e.tensor_scalar(out=out, in0=a, scalar1=s1, scalar2=None, op0=op0), reads=r, writes=w)
            else:
                S.op(eng, lambda e: e.tensor_scalar(out=out, in0=a, scalar1=s1, scalar2=s2, op0=op0, op1=op1), reads=r, writes=w)

        def STT(out, a, sc, b, op0, op1, r, w):
            S.op("dve", lambda e: e.scalar_tensor_tensor(out=out, in0=a, scalar=sc, in1=b, op0=op0, op1=op1), reads=r, writes=w)

        def CP(out, a, r, w, eng="dve"):
            if eng == "act":
                S.op("act", lambda e: e.copy(out=out, in_=a), reads=r, writes=w)
            else:
                S.op(eng, lambda e: e.tensor_copy(out=out, in_=a), reads=r, writes=w)

        def ACT(out, a, func, r, w, bias=0.0, scale=1.0):
            S.op("act", lambda e: e.activation(out=out, in_=a, func=func, bias=bias, scale=scale), reads=r, writes=w)

        def RSQRT(out, a, r, w, scale=1.0, eps=0.0):
            ACT(out, a, AF.Ln, r, w, bias=eps, scale=scale)
            ACT(out, out, AF.Exp, w, w, scale=-0.5)

        ringi = [0]

        def WLOAD(src, shape3):
            i = ringi[0] % 4
            ringi[0] += 1
            kp, nk, n = shape3
            dst = ring[i][0:kp, 0:nk, 0:n]
            S.dma(dst, src, writes=[("ring", i)], q="pool")
            return dst, ("ring", i)

        MULT, ADD, SUB = ALU.mult, ALU.add, ALU.subtract
        pt = lambda name, c: PT[:, POFF[name] + c:POFF[name] + c + 1]
        ptr = lambda name, c0, n: PT[:, POFF[name] + c0:POFF[name] + c0 + n]

        S.op("pool", lambda e: e.memset(ident[:], 0.0), writes=["ident"])
        S.op("pool", lambda e: e.affine_select(out=ident[:], in_=ident[:], pattern=[[-1, 128]], compare_op=ALU.not_equal,
                                               fill=1.0, base=0, channel_multiplier=1), reads=["ident"], writes=["ident"])
        S.op("pool", lambda e: e.memset(ones[:], 1.0), writes=["ones"])
        S.op("pool", lambda e: e.memset(ones_bf[:], 1.0), writes=["ones"])
        S.op("pool", lambda e: e.memset(bones[:], 0.0), writes=["bones"])
        S.op("pool", lambda e: e.memset(bones[0:64, 0:64], 1.0), reads=["bones"], writes=["bones"])
        S.op("pool", lambda e: e.memset(bones[64:128, 64:128], 1.0), reads=["bones"], writes=["bones"])
        for m, cm, pat, op in ((m_su, -1, 1, ALU.is_gt), (m_u, -1, 1, ALU.is_ge), (m_sl, 1, -1, ALU.is_gt)):
            S.op("pool", lambda e: e.memset(m[:], 1.0), writes=["masks"])
            S.op("pool", lambda e: e.affine_select(out=m[:], in_=m[:], pattern=[[pat, 64]], compare_op=op, fill=0.0,
                                                   base=0, channel_multiplier=cm), reads=["masks"], writes=["masks"])
        S.op("pool", lambda e: e.memset(blk[:], 1.0), writes=["masks"])
        blk3 = blk[:].rearrange("p (a b) -> p a b", b=4)
        S.op("pool", lambda e: e.affine_select(out=blk3, in_=blk3, pattern=[[-4, 16], [0, 4]], compare_op=ALU.is_ge, fill=0.0,
                                               base=0, channel_multiplier=1), reads=["masks"], writes=["masks"])
        S.op("pool", lambda e: e.affine_select(out=blk3, in_=blk3, pattern=[[4, 16], [0, 4]], compare_op=ALU.is_ge, fill=0.0,
                                               base=3, channel_multiplier=-1), reads=["masks"], writes=["masks"])
        for ms, m in ((ms_su, m_su), (ms_u, m_u), (ms_sl, m_sl)):
            TT(ms[:], m[:], blk[:], MULT, ["masks"], ["masks"], eng="pool")
        S.op("pool", lambda e: e.memset(rowsel[:], 1.0), writes=["masks"])
        S.op("pool", lambda e: e.affine_select(out=rowsel[:], in_=rowsel[:], pattern=[[-4, 16]], compare_op=ALU.is_ge, fill=0.0,
                                               base=0, channel_multiplier=1), reads=["masks"], writes=["masks"])
        S.op("pool", lambda e: e.affine_select(out=rowsel[:], in_=rowsel[:], pattern=[[4, 16]], compare_op=ALU.is_ge, fill=0.0,
                                               base=3, channel_multiplier=-1), reads=["masks"], writes=["masks"])
        for t_ in (shiftst, Mst, convhalo, ffnhalo):
            S.op("pool", lambda e: e.memset(t_[:], 0.0), writes=["state"])
        S.op("pool", lambda e: e.memset(smpreg[:], 0.0), writes=["smpreg"])
        for c in range(4):
            S.op("pool", lambda e: e.memset(CS[c]["TLz"], 0.0), reads=["smpreg"], writes=[f"TLz{c}"])
            S.op("pool", lambda e: e.memset(CS[c]["TRz"], 0.0), reads=["smpreg"], writes=[f"TRz{c}"])

        S.dma(flg[:], flag, writes=["flg"])
        S.op("pool", lambda e: e.memset(PV["r"][:], 0.0), writes=["r"])
        for i in range(NPROW // 128):
            S.dma(ptile[:], params[128 * i:128 * i + 128, :], writes=["ptile"])
            ps_, pk = PSL()
            TRP(ps_[:, 0:128], ptile[:], ["ptile"], [pk])
            CP(PT[:, 128 * i:128 * i + 128], ps_[:, 0:128], [pk], ["PT"])
        TSC(omu[:], ptr("mu", 0, 96), -1.0, 1.0, MULT, ADD, ["PT"], ["PT2"])
        TSC(omka[:], ptr("k_a", 0, 16), -1.0, 1.0, MULT, ADD, ["PT"], ["PT2"])

        S.dma(cvt, cvec, writes=["xtile"])
        ACT(cvt, cvt, AF.Silu, ["xtile"], ["xtile"])
        for kc in range(16):
            ps_, pk = PSL()
            TRP(ps_[:, 0:17], cvt[:, 128 * kc:128 * kc + 128], ["xtile"], [pk])
            CP(cT[:, kc, :], ps_[:, 0:17], [pk], ["cT"])
        for m in range(96):
            wv, wk = WLOAD(kview(w_ada)[:, :, 128 * m:128 * m + 128], (128, 16, 128))
            ps_, pk = PSL()
            for kc in range(16):
                MM(ps_[:, 0:17], wv[:, kc, :], cT[:, kc, :], kc == 0, kc == 15, [wk, "cT"], [pk])
            TSC(MT[:, m, :], ps_[:, 0:17], pt("b_ada", m), None, ADD, None, [pk, "PT"], ["mixact"])
        for (ti, mi, kind) in ((0, 1, "gs1"), (1, 0, "sh"), (2, 2, "gt"), (3, 4, "gs2"), (4, 3, "sh"), (5, 5, "gt")):
            src_p = MT[:, 16 * mi:16 * mi + 16, 0]
            src_s = MT[:, 16 * mi:16 * mi + 16, 1:17]
            if kind.startswith("gs"):
                g = ptr("norm1_g" if kind == "gs1" else "norm2_g", 0, 16)
                STT(TP[:, ti, :], src_p, 1.0, g, ADD, MULT, ["mixact", "PT"], ["TP"])
                STT(TS[:, ti, :, :], src_s, 1.0, g.unsqueeze(2).to_broadcast([128, 16, 16]), ADD, MULT, ["mixact", "PT"], ["TS"])
            else:
                CP(TP[:, ti, :], src_p, ["mixact"], ["TP"])
                CP(TS[:, ti, :, :], src_s, ["mixact"], ["TS"])

        def emit_block(kind, bi):
            smp = kind == "s"
            pre = kind == "q"
            msk = pre or (kind == "p" and bi == 0)
            T = 64 if smp else TB
            nch = T // 64
            bc = lambda tab16: tab16.unsqueeze(2).to_broadcast([128, 16, T])
            v4 = lambda ap: ap.rearrange("p k (s t) -> p k s t", t=4)
            last = kind == "p" and bi == n_pblocks - 1

            xsrc_rows = 64 if smp else 128
            for ti_ in range(max(1, T // 128)):
                xsrc = xs if smp else (xq if pre else xp)[TB * bi + 128 * ti_:TB * bi + 128 * ti_ + 128, :]
                S.dma(xtile[0:xsrc_rows, :], xsrc, writes=["xtile"])
                R = xsrc_rows
                for q in range(4):
                    ps_, pk0 = PSL()
                    for j in range(4):
                        kc = 4 * q + j
                        TRP(ps_[:, j * R:(j + 1) * R], xtile[0:R, 128 * kc:128 * kc + 128], ["xtile"], [pk0])
                    CP(xT[:, 4 * q:4 * q + 4, 128 * ti_:128 * ti_ + R], ps_[:, 0:4 * R].rearrange("p (a t) -> p a t", a=4),
                       [pk0], ["xT"], eng="act")
            stage(1)

            def rms(src, key, out_rstd):
                TT(dx[:, :, 0:T], src[:, :, 0:T], src[:, :, 0:T], MULT, [key], ["dx"])
                ps_, pk = PSL()
                for kc in range(16):
                    MM(ps_[:, 0:T], ones_bf[:], dx[:, kc, 0:T], kc == 0, kc == 15, ["ones", "dx"], [pk])
                RSQRT(out_rstd[:, 0:T], ps_[:, 0:T], [pk], ["rstd"], scale=1.0 / D, eps=1e-6)

            def modnorm(src, skey, rs, tg, tsft):
                d = hf[:, :, 0:T]
                TT(d, src[:, :, 0:T], rs[:, 0:T].unsqueeze(1).to_broadcast([128, 16, T]), MULT, [skey, "rstd"], ["mixact"])
                if smp:
                    for (tix, op) in ((tg, MULT), (tsft, ADD)):
                        TT(v4(d), v4(d), TS[:, tix, :, :].unsqueeze(3).to_broadcast([128, 16, 16, 4]), op, ["mixact", "TS"], ["mixact"])
                else:
                    TT(d, d, bc(TP[:, tg, :]), MULT, ["mixact", "TP"], ["mixact"])
                    TT(d, d, bc(TP[:, tsft, :]), ADD, ["mixact", "TP"], ["mixact"])
                    if msk:
                        TSC(d, d, flg[:, 0:1], None, MULT, None, ["mixact", "flg"], ["mixact"])

            rms(xT, "xT", rstd)
            modnorm(xT, "xT", rstd, 0, 1)
            CP(h[:, :, 0:T], hf[:, :, 0:T], ["mixact"], ["h"], eng="act")
            stage(2)
            if smp:
                S.dma(otile[0:16, :], st_shift, writes=["xtile"])
                h4 = v4(hf[:, :, 0:64])
                d4 = v4(dx[:, :, 0:64])
                for q in range(4):
                    ps_, pk = PSL()
                    for j in range(4):
                        TRP(ps_[:, 16 * j:16 * j + 16], otile[0:16, 128 * (4 * q + j):128 * (4 * q + j) + 128], ["xtile"], [pk])
                    TT(d4[:, 4 * q:4 * q + 4, :, 0], ps_[:, 0:64].rearrange("p (a s) -> p a s", a=4), h4[:, 4 * q:4 * q + 4, :, 0],
                       SUB, [pk, "mixact"], ["dx"])
                TT(d4[:, :, :, 1:4], h4[:, :, :, 0:3], h4[:, :, :, 1:4], SUB, ["mixact"], ["dx"])
                for kc in range(16):
                    ps_, pk = PSL()
                    TRP(ps_[0:16, 0:128], h4[:, kc, :, 3], ["mixact"], [pk])
                    CP(otile[0:16, 128 * kc:128 * kc + 128], ps_[0:16, 0:128], [pk], ["xtile"])
                S.dma(shift_s, otile[0:16, :], reads=["xtile"])
            else:
                TT(dx[:, :, 0], shiftst[:], hf[:, :, 0], SUB, ["state", "mixact"], ["dx"])
                TT(dx[:, :, 1:T], hf[:, :, 0:T - 1], hf[:, :, 1:T], SUB, ["mixact"], ["dx"])
                CP(shiftst[:], hf[:, :, T - 1], ["mixact"], ["state"])
                if last:
                    ps_, pk = PSL()
                    TRP(ps_[0:16, 0:128], shiftst[:], ["state"], [pk])
                    CP(otile[0:16, 0:128], ps_[0:16, 0:128], [pk], ["xtile"])
                    S.dma(shift_p, otile[0:16, 0:128], reads=["xtile"])
            stage(3)

            def mix(dst, dkey, mi):
                TT(dst, dx[:, :, 0:T], bc(ptr("mu", 16 * mi, 16)), MULT, ["dx", "PT"], [dkey])
                TT(dst, dst, h[:, :, 0:T], ADD, [dkey, "h"], [dkey])

            xmix = [mixbf[:, 16 * TB * i:16 * TB * (i + 1)].rearrange("p (k t) -> p k t", t=TB)[:, :, 0:T] for i in range(3)]
            tmpx = zc[:, :, 0:T]
            lora1 = [(w1, 96, 1, [PV["tw"]], AF.Tanh), (a1, 96, 4, [PV["xa1"]], AF.Copy)]
            if not pre:
                lora1.append((g1, 256, 5, [PV["sg0"], PV["sg1"]], AF.Sigmoid))
            for (W, ncol, mi, outs, func) in lora1:
                mix(tmpx, "zc", mi)
                for oi, o in enumerate(outs):
                    n = min(128, ncol - 128 * oi)
                    wv, wk = WLOAD(kview(W)[:, :, 128 * oi:128 * oi + n], (128, 16, n))
                    ps_, pk = PSL()
                    for kc in range(16):
                        MM(ps_[0:n, 0:T], wv[:, kc, :], tmpx[:, kc, :], kc == 0, kc == 15, [wk, "zc"], [pk])
                    ACT(o[0:n, 0:T], ps_[0:n, 0:T], func, [pk], ["lora"])
            if not pre:
                mix(xmix[0], "mixact", 0)
            mix(xmix[1], "mixact", 2)
            mix(xmix[2], "mixact", 3)
            stage(4)

            def part1_groups(p, alt):
                hh = h[:, :, 0:T]
                specs = []
                if not pre:
                    specs.append(("w", xmix[0], "mixact", 0, PV["r"], AF.Copy, "r", 0.0))
                specs.append(("w", xmix[1], "mixact", D, PV["k0"], AF.Copy, "k0", 0.0))
                specs.append(("w", xmix[2], "mixact", 2 * D, PV["v"], AF.Copy, "v", 0.0))
                if not pre:
                    specs.append(("w", hh, "h", 3 * D, PV["glu"], AF.Identity, "glu", pt("b_glu", p)))
                    specs.append(("w", hh, "h", 4 * D, PV["glb"], AF.Sigmoid, "glb", pt("b_glu", 16 + p)))
                    specs.append(("w", hh, "h", 5 * D, SGA[alt], AF.Sigmoid, f"sga{alt}", 0.0))
                specs.append(("l", w2, 96, [PV["tw"]], PV["lw"], AF.Sigmoid, pt("w0", p), "lw"))
                specs.append(("l", a2, 96, [PV["xa1"]], PV["a"], AF.Sigmoid, pt("a0", p), "a"))
                if not pre:
                    specs.append(("l", g2, 256, [PV["sg0"], PV["sg1"]], GG[alt], AF.Copy, 0.0, f"g{alt}"))
                groups = []
                for sp_ in specs:
                    st_ = {}
                    if sp_[0] == "w":
                        _, src, skey, col0, out, func, okey, bias = sp_

                        def load(st_=st_, col0=col0):
                            st_["w"] = WLOAD(kview(w_in)[:, :, col0 + 128 * p:col0 + 128 * p + 128], (128, 16, 128))

                        def comp(st_=st_, src=src, skey=skey, out=out, func=func, okey=okey, bias=bias):
                            wv, wk = st_["w"]
                            ps_, pk = PSL()
                            for kc in range(16):
                                MM(ps_[:, 0:T], wv[:, kc, :], src[:, kc, :], kc == 0, kc == 15, [wk, skey], [pk])
                            ACT(out[:, 0:T], ps_[:, 0:T], func, [pk, "PT"], [okey], bias=bias)
                    else:
                        _, W, K, srcs, out, func, bias, okey = sp_
                        nk = len(srcs)
                        kp = min(K, 128)

                        def load(st_=st_, W=W, kp=kp, nk=nk):
                            st_["w"] = WLOAD(W.rearrange("(kc ki) n -> ki kc n", ki=kp)[:, :, 128 * p:128 * p + 128], (kp, nk, 128))

                        def comp(st_=st_, srcs=srcs, out=out, func=func, bias=bias, okey=okey, kp=kp, nk=nk):
                            wv, wk = st_["w"]
                            ps_, pk = PSL()
                            for j in range(nk):
                                MM(ps_[:, 0:T], wv[:, j, :], srcs[j][0:kp, 0:T], j == 0, j == nk - 1, [wk, "lora"], [pk])
                            ACT(out[:, 0:T], ps_[:, 0:T], func, [pk, "PT"], [okey], bias=bias)
                    groups.append((load, comp))
                return groups

            def run_groups(groups):
                for i, (ld, cp_) in enumerate(groups):
                    if i == 0:
                        ld()
                        if len(groups) > 1:
                            groups[1][0]()
                    cp_()
                    if i + 2 < len(groups):
                        groups[i + 2][0]()

            pipelined = not smp
            pend = []
            pstate = {"i": 0}

            def pump(n=1):
                for _ in range(n):
                    i = pstate["i"]
                    if i >= len(pend):
                        return
                    if i == 0:
                        pend[0][0]()
                        if len(pend) > 1:
                            pend[1][0]()
                    pend[i][1]()
                    if i + 2 < len(pend):
                        pend[i + 2][0]()
                    pstate["i"] = i + 1

            if pipelined:
                run_groups(part1_groups(0, 0))
            for p in range(NHP):
                alt = (p % 2) if pipelined else 0
                if not pipelined:
                    run_groups(part1_groups(p, 0))
                stage(5)
                V = {n: PV[n][:, 0:T] for n in PV}
                if not pre:
                    TT(V["glu"], V["glu"], V["glb"], MULT, ["glu", "glb"], ["glu"])
                    if msk:
                        TSC(V["glu"], V["glu"], flg[:, 0:1], None, MULT, None, ["glu", "flg"], ["glu"])
                    for j in range(31):
                        TSC(diag[:, j, :], ident[:], pt("dw_k", 16 * j + p), None, MULT, None, ["ident", "PT"], ["diag"])
                    ps_, pk = PSL()
                    if smp:
                        S.dma(otile[0:120, 0:512].rearrange("r (q c) -> r q c", q=4),
                              st_conv[:, :, 128 * p:128 * p + 128].rearrange("(q s) t c -> (s t) q c", q=4), writes=["xtile"])
                        for q in range(4):
                            pq, pqk = PSL()
                            TRP(pq[:, 0:120], otile[0:120, 128 * q:128 * q + 128], ["xtile"], [pqk])
                            CP(zsbf[:, 4 * q:4 * q + 4, 0:30], pq[:, 0:120].rearrange("p (s t) -> p s t", t=30), [pqk], ["zsbf"])
                        CP(zsbf[:, :, 30:34], V["glu"].rearrange("p (s t) -> p s t", t=4), ["glu"], ["zsbf"])
                        for j in range(31):
                            MM(ps_[:, 0:64], diag[:, j, :], zsbf[:, :, j:j + 4], j == 0, j == 30, ["diag", "zsbf"], [pk])
                        pq, pqk = PSL()
                        CP(PV["t4"][:, 0:64].rearrange("p (t s) -> p t s", t=4), V["glu"].rearrange("p (s t) -> p t s", t=4), ["glu"], ["t4"])
                        TRP(pq[0:64, 0:128], PV["t4"][:, 0:64], ["t4"], [pqk])
                        CP(Ytok[:, :], pq[0:64, 0:128], [pqk], ["Ytok"])
                        for t in range(4):
                            S.dma(conv_s[:, 26 + t, 128 * p:128 * p + 128], Ytok[16 * t:16 * t + 16, :], reads=["Ytok"])
                    else:
                        CP(zext[:, 0:30], convhalo[:, p, :], ["state"], ["zext"])
                        CP(zext[:, 30:30 + T], V["glu"], ["glu"], ["zext"], eng="act")
                        for j in range(31):
                            MM(ps_[:, 0:T], diag[:, j, :], zext[:, j:j + T], j == 0, j == 30, ["diag", "zext"], [pk])
                        CP(convhalo[:, p, :], zext[:, T:T + 30], ["zext"], ["state"])
                        if last:
                            pq, pqk = PSL()
                            TRP(pq[0:30, 0:128], PV["glu"][:, T - 30:T], ["glu"], [pqk])
                            CP(Ytok[0:30, :], pq[0:30, 0:128], [pqk], ["Ytok"])
                            S.dma(conv_p[:, 128 * p:128 * p + 128], Ytok[0:30, :], reads=["Ytok"])
                    TSC(zc[:, p, 0:T], ps_[:, 0:T], pt("dw_b", p), None, ADD, None, [pk, "PT"], [("zcp", p)])

                stage(6)
                TSC(V["kk"], V["k0"], pt("k_k", p), None, MULT, None, ["k0", "PT"], ["kk"])
                TT(V["t2"], V["kk"], V["kk"], MULT, ["kk"], ["t2"])
                ps_, pk = PSL()
                MM(ps_[:, 0:T], bones[:], V["t2"], True, True, ["bones", "t2"], [pk])
                RSQRT(V["t3"], ps_[:, 0:T], [pk], ["t3"], eps=1e-30)
                TT(V["kkn"], V["kk"], V["t3"], MULT, ["kk", "t3"], ["kkn"])
                TSC(V["t1"], V["a"], pt("k_a", p), omka[:, p:p + 1], MULT, ADD, ["a", "PT", "PT2"], ["t1"])
                TT(V["k"], V["k0"], V["t1"], MULT, ["k0", "t1"], ["k"])
                TT(V["b"], V["kkn"], V["a"], MULT, ["kkn", "a"], ["b"])
                if not pre:
                    STT(V["rk"], V["r"], pt("r_k", p), V["k"], MULT, MULT, ["r", "k", "PT"], ["rk"])
                    ps_, pk = PSL()
                    MM(ps_[:, 0:T], bones[:], V["rk"], True, True, ["bones", "rk"], [pk])
                    TT(V["bonus"], ps_[:, 0:T], V["v"], MULT, [pk, "v"], ["bonus"])
                stage(7)
                L = 4 if smp else 64
                nseg = T // L
                S.op("dve", lambda e: e.tensor_tensor_scan(out=V["cs"], data0=ones[:, 0:T], data1=V["lw"], initial=0.0,
                                                           op0=MULT, op1=ADD), reads=["ones", "lw"], writes=["cs"])
                TT(V["cx"], V["cs"], V["lw"], SUB, ["cs", "lw"], ["cx"])
                cs3 = V["cs"].rearrange("p (s l) -> p s l", l=L)
                cx3 = V["cx"].rearrange("p (s l) -> p s l", l=L)
                CP(PV["t4"][:, 0:nseg], cx3[:, :, 0], ["cx"], ["t4"])
                base = PV["t4"][:, 0:nseg].unsqueeze(2).to_broadcast([128, nseg, L])
                TT(cs3, cs3, base, SUB, ["cs", "t4"], ["cs"])
                TT(cx3, cx3, base, SUB, ["cx", "t4"], ["cx"])
                ACT(V["eW"], V["cs"], AF.Exp, ["cs"], ["eW"], scale=-C0)
                ACT(V["eWi"], V["cs"], AF.Exp, ["cs"], ["eWi"], scale=C0)
                ACT(V["eWp"], V["cx"], AF.Exp, ["cx"], ["eWp"], scale=-C0)

                stage(8)
                msu, mu_, msl = (ms_su, ms_u, ms_sl) if smp else (m_su, m_u, m_sl)
                b2 = lambda m: m[:].unsqueeze(1).to_broadcast([64, 2, 64])
                h2v = lambda ap: ap.rearrange("p (h n) -> p h n", h=2)
                K_ = lambda n, c: f"{n}{c}"

                def bankA(c):
                    return pss[2 * c][:, :], ("ps", 2 * c)

                def bankB(c):
                    return pss[2 * c + 1][:, :], ("ps", 2 * c + 1)

                def P0(c):
                    B = CS[c]
                    cs_ = slice(64 * c, 64 * c + 64)
                    TT(B["TL"][:, 0, :], PV["kkn"][:, cs_], PV["eWp"][:, cs_], MULT, ["kkn", "eWp"], [K_("TL", c)])
                    TT(B["TL"][:, 1, :], PV["r"][:, cs_], PV["eW"][:, cs_], MULT, ["r", "eW"], [K_("TL", c)])
                    TT(B["TR"][:, 0, :], PV["b"][:, cs_], PV["eWi"][:, cs_], MULT, ["b", "eWi"], [K_("TR", c)])
                    TT(B["TR"][:, 1, :], PV["k"][:, cs_], PV["eWi"][:, cs_], MULT, ["k", "eWi"], [K_("TR", c)])
                    for hj in range(2):
                        hs = slice(64 * hj, 64 * hj + 64)
                        CP(B["TLz"][hs, hj, :, :], B["TL"][hs, :, :], [K_("TL", c)], [K_("TLz", c)], eng="act")
                        CP(B["TRz"][hs, hj, :, :], B["TR"][hs, :, :], [K_("TR", c)], [K_("TRz", c)], eng="act")

                def P1(c):
                    B = CS[c]
                    A, ak = bankA(c)
                    Bk_, bk = bankB(c)
                    for hj in range(2):
                        TLh = B["TLz"][:, hj, :, :].rearrange("p a t -> p (a t)")
                        MM(A[0:64, 128 * hj:128 * hj + 128], B["TRz"][:, hj, 0, :], TLh, True, True, [K_("TRz", c), K_("TLz", c)], [ak])
                        MM(Bk_[0:64, 128 * hj:128 * hj + 128], B["TRz"][:, hj, 1, :], TLh, True, True, [K_("TRz", c), K_("TLz", c)], [bk])
                        MM(A[0:64, 256 + 64 * hj:256 + 64 * hj + 64], B["TLz"][:, hj, 0, :], B["TRz"][:, hj, 0, :], True, True,
                           [K_("TRz", c), K_("TLz", c)], [ak])

                def P2(c):
                    B = CS[c]
                    A, ak = bankA(c)
                    Bk_, bk = bankB(c)
                    pa3 = h2v(A[0:64, 0:256]); pb3 = h2v(Bk_[0:64, 0:256]); pc3 = h2v(A[0:64, 256:384])
                    pz = B["PZ"][0]
                    STT(pz[:, :, 1, :], pa3[:, :, 0:64], -1.0, b2(msu), MULT, MULT, [ak, "masks"], [K_("pz0_", c)])
                    STT(B["NRB"], pa3[:, :, 64:128], -1.0, b2(mu_), MULT, MULT, [ak, "masks"], [K_("NRB", c)])
                    STT(B["ZTa"], pc3, -1.0, b2(msl), MULT, MULT, [ak, "masks"], [K_("ZTa", c)])
                    TT(B["LKT"], pb3[:, :, 0:64], b2(msu), MULT, [bk, "masks"], [K_("LKT", c)])
                    TT(B["RKT"], pb3[:, :, 64:128], b2(mu_), MULT, [bk, "masks"], [K_("RKT", c)])
                    TT(pz[:, :, 0, :], pz[:, :, 1, :], ident[0:64, 0:64].unsqueeze(1).to_broadcast([64, 2, 64]), ADD,
                       [K_("pz0_", c), "ident"], [K_("pz0_", c)])

                def P3(c):
                    B = CS[c]
                    cs_ = slice(64 * c, 64 * c + 64)
                    A, ak = bankA(c)
                    TRP(A[0:64, 0:128], PV["v"][:, cs_], ["v"], [ak])
                    TRP(A[0:64, 128:256], B["TR"][:, 0, :], [K_("TR", c)], [ak])
                    TRP(A[0:64, 256:384], B["TR"][:, 1, :], [K_("TR", c)], [ak])
                    CP(B["Vtok"], A[0:64, 0:128], [ak], [K_("Vtok", c)], eng="act")
                    S.op("act", lambda e: e.mul(out=B["nbtok"], in_=A[0:64, 128:256], mul=-1.0), reads=[ak], writes=[K_("nbtok", c)])
                    CP(B["ktok"], A[0:64, 256:384], [ak], [K_("ktok", c)], eng="act")

                nstate = {}

                def NM(j):
                    def f(c):
                        B = CS[c]
                        cur, zt_cur = nstate.get(c, (0, "ZTa"))
                        pzc = B["PZ"][cur]
                        kc_ = K_(f"pz{cur}_", c)
                        A, ak = bankA(c)
                        Bk_, bk = bankB(c)
                        ztk = K_(zt_cur, c)
                        for hj in range(2):
                            if j == 0:
                                MM(A[0:64, 128 * hj + 64:128 * hj + 128], B[zt_cur][:, hj, :], pzc[:, hj, 1, :], True, True, [ztk, kc_], [ak])
                            elif j == 5:
                                MM(A[0:64, 128 * hj:128 * hj + 64], B[zt_cur][:, hj, :], pzc[:, hj, 0, :], True, True, [ztk, kc_], [ak])
                            else:
                                MM(A[0:64, 128 * hj:128 * hj + 128], B[zt_cur][:, hj, :], pzc[:, hj, :, :].rearrange("p a t -> p (a t)"),
                                   True, True, [ztk, kc_], [ak])
                            if j != 5:
                                MM(Bk_[0:64, 64 * hj:64 * hj + 64], pzc[:, hj, 1, :], B[zt_cur][:, hj, :], True, True, [ztk, kc_], [bk])
                    return f

                def NE(j):
                    def f(c):
                        B = CS[c]
                        cur, zt_cur = nstate.get(c, (0, "ZTa"))
                        zt_nxt = "ZTb" if zt_cur == "ZTa" else "ZTa"
                        pzc, pzn = B["PZ"][cur], B["PZ"][1 - cur]
                        kc_, kn_ = K_(f"pz{cur}_", c), K_(f"pz{1 - cur}_", c)
                        A, ak = bankA(c)
                        Bk_, bk = bankB(c)
                        q13 = h2v(A[0:64, 0:256])
                        if j == 0:
                            CP(pzn[:, :, 0, :], pzc[:, :, 0, :], [kc_], [kn_])
                        else:
                            TT(pzn[:, :, 0, :], pzc[:, :, 0, :], q13[:, :, 0:64], ADD, [kc_, ak], [kn_])
                        if j != 5:
                            CP(pzn[:, :, 1, :], q13[:, :, 64:128], [ak], [kn_], eng=("act" if c % 2 else "dve"))
                            CP(B[zt_nxt], h2v(Bk_[0:64, 0:128]), [bk], [K_(zt_nxt, c)], eng="act")
                        nstate[c] = (1 - cur, zt_nxt)
                    return f

                par_steps = [P0, P1, P2, P3]
                for j in range(6):
                    par_steps += [NM(j), NE(j)]
                for step in par_steps:
                    for c in range(nch):
                        step(c)
                stage(11)
                if smp:
                    B = CS[0]
                    for s in range(NS):
                        stb, stk = StB[s % 4], f"Stmp{s % 4}"
                        S.dma(stb.rearrange("v (h k) -> v h k", h=2),
                              st_wkv[s, 2 * p:2 * p + 2, :, :].rearrange("h v k -> v h k"), writes=[stk])
                        pq, pqk = PSL()
                        TRP(pq[:, 0:64], stb, [stk], [pqk])
                        for hj in range(2):
                            hs = slice(64 * hj, 64 * hj + 64)
                            CP(Msz[hs, hj, s, :], pq[hs, 0:64], [pqk], ["Ms"], eng="act")
                    TT(KKX, B["TL"][:, 0, :].unsqueeze(1).to_broadcast([128, 16, 64]), selm, MULT, ["TL0", "selm"], ["KKX"])
                    TT(RX, B["TL"][:, 1, :].unsqueeze(1).to_broadcast([128, 16, 64]), selm, MULT, ["TL0", "selm"], ["RX"])
                if pipelined and p + 1 < NHP:
                    pend[:] = part1_groups(p + 1, (p + 1) % 2)
                    pstate["i"] = 0
                else:
                    pend[:] = []
                    pstate["i"] = 0
                for c in range(nch):
                    B = CS[c]
                    cs_ = slice(64 * c, 64 * c + 64)
                    cur, _zt = nstate[c]
                    PTt, ptk = B["PZ"][cur], K_(f"pz{cur}_", c)
                    px, pxk = PSL()
                    px2, pxk2 = PSL()
                    py, pyk = PSL()
                    for hj in range(2):
                        hs = slice(64 * hj, 64 * hj + 64)
                        o_ = px[0:64, 64 * hj:64 * hj + 64]
                        oy = py[0:64, 64 * hj:64 * hj + 64]
                        if smp:
                            for s in range(NS):
                                MM(o_, KKX[:, s, :], Msz[:, hj, s, :], s == 0, s == NS - 1, ["KKX", "Ms"], [pxk])
                            for s in range(NS):
                                MM(oy, RX[:, s, :], Msz[:, hj, s, :], s == 0, s == NS - 1, ["RX", "Ms"], [pyk])
                        else:
                            MM(o_, B["TLz"][:, hj, 0, :], Mst[:, p, :], True, True, [K_("TLz", c), "state"], [pxk])
                            MM(oy, B["TLz"][:, hj, 1, :], Mst[:, p, :], True, True, [K_("TLz", c), "state"], [pyk])
                        MM(px2[0:64, 64 * hj:64 * hj + 64], B["LKT"][:, hj, :], B["Vtok"][:, hs], True, True,
                           [K_("LKT", c), K_("Vtok", c)], [pxk2])
                    CP(B["X"], h2v(px[0:64, 0:128]), [pxk], [K_("X", c)])
                    TT(B["X"], B["X"], h2v(px2[0:64, 0:128]), ADD, [K_("X", c), pxk2], [K_("X", c)])
                    CP(B["Ytok"], py[0:64, 0:128], [pyk], [K_("Ytok", c)], eng="act")
                    pump()
                    pu, puk = PSL()
                    for hj in range(2):
                        MM(pu[0:64, 64 * hj:64 * hj + 64], PTt[:, hj, 0, :], B["X"][:, hj, :], True, True, [ptk, K_("X", c)], [puk])
                    CP(B["U"], h2v(pu[0:64, 0:128]), [puk], [K_("U", c)])
                    pump()
                    stage(12)
                    if smp:
                        TT(NBX, B["nbtok"].unsqueeze(1).to_broadcast([64, 16, 128]),
                           rowsel[:].unsqueeze(2).to_broadcast([64, 16, 128]), MULT, ["nbtok0", "masks"], ["NBX"])
                        TT(KX, B["ktok"].unsqueeze(1).to_broadcast([64, 16, 128]),
                           rowsel[:].unsqueeze(2).to_broadcast([64, 16, 128]), MULT, ["ktok0", "masks"], ["KX"])
                        for s in range(NS):
                            pm, pmk = PSL()
                            for hj in range(2):
                                hs = slice(64 * hj, 64 * hj + 64)
                                o_ = pm[:, 64 * hj:64 * hj + 64]
                                MM(o_, NBX[:, s, :], B["U"][:, hj, :], True, False, ["NBX", "U0"], [pmk])
                                MM(o_, KX[:, s, :], B["Vtok"][:, hs], False, True, ["KX", "Vtok0"], [pmk])
                            for hj in range(2):
                                hs = slice(64 * hj, 64 * hj + 64)
                                TT(Msz[hs, hj, s, :], Msz[hs, hj, s, :], pm[hs, 64 * hj:64 * hj + 64], ADD, ["Ms", pmk], ["Ms"])
                            TSC(Msz[:, :, s, :], Msz[:, :, s, :], PV["eW"][:, 4 * s + 3:4 * s + 4], None, MULT, None, ["Ms", "eW"], ["Ms"])
                            TT(PV["t4"][:, 0:64], Msz[:, 0, s, :], Msz[:, 1, s, :], ADD, ["Ms"], ["t4"])
                            pq, pqk = PSL()
                            TRP(pq[0:64, 0:128], PV["t4"][:, 0:64], ["t4"], [pqk])
                            sob, sok = SoB[s % 4], f"Sout{s % 4}"
                            CP(sob, pq[0:64, 0:128], [pqk], [sok], eng="act")
                            S.dma(wkv_s[s, 2 * p:2 * p + 2, :, :].rearrange("h v k -> v h k"),
                                  sob.rearrange("v (h k) -> v h k", h=2), reads=[sok])
                    else:
                        pm, pmk = PSL()
                        for hj in range(2):
                            hs = slice(64 * hj, 64 * hj + 64)
                            o_ = pm[:, 64 * hj:64 * hj + 64]
                            MM(o_, B["nbtok"], B["U"][:, hj, :], True, False, [K_("nbtok", c), K_("U", c)], [pmk])
                            MM(o_, B["ktok"], B["Vtok"][:, hs], False, True, [K_("ktok", c), K_("Vtok", c)], [pmk])
                        for hj in range(2):
                            hs = slice(64 * hj, 64 * hj + 64)
                            TT(Mst[hs, p, :], Mst[hs, p, :], pm[hs, 64 * hj:64 * hj + 64], ADD, ["state", pmk], ["state"])
                        TSC(Mst[:, p, :], Mst[:, p, :], PV["eW"][:, 64 * c + 63:64 * c + 64], None, MULT, None, ["state", "eW"], ["state"])
                        pump()
                    if not pre:
                        py2, pyk2 = PSL()
                        for hj in range(2):
                            hs = slice(64 * hj, 64 * hj + 64)
                            o2_ = py2[0:64, 64 * hj:64 * hj + 64]
                            MM(o2_, B["NRB"][:, hj, :], B["U"][:, hj, :], True, False, [K_("NRB", c), K_("U", c)], [pyk2])
                            MM(o2_, B["RKT"][:, hj, :], B["Vtok"][:, hs], False, True, [K_("RKT", c), K_("Vtok", c)], [pyk2])
                        TT(B["Ytok"], B["Ytok"], py2[0:64, 0:128], ADD, [K_("Ytok", c), pyk2], [K_("Ytok", c)])
                        pt_, ptk2 = PSL()
                        TRP(pt_[:, 0:64], B["Ytok"], [K_("Ytok", c)], [ptk2])
                        CP(PV["y"][:, cs_], pt_[:, 0:64], [ptk2], ["y"], eng="act")
                pump(100)
                if last:
                    pq, pqk = PSL()
                    TRP(pq[0:64, 0:128], Mst[:, p, :], ["state"], [pqk])
                    CP(Sout[:], pq[0:64, 0:128], [pqk], ["Sout0"], eng="act")
                    S.dma(wkv_p[2 * p:2 * p + 2, :, :].rearrange("h v k -> v h k"),
                          Sout[:, :].rearrange("v (h k) -> v h k", h=2), reads=["Sout0"])
                if not pre:
                    stage(13)
                    ps_, pk = PSL()
                    MM(ps_[:, 0:T], bones[:], V["y"], True, True, ["bones", "y"], [pk])
                    STT(V["t2"], ps_[:, 0:T], -1.0 / 64, V["y"], MULT, ADD, [pk, "y"], ["t2"])
                    TT(V["t3"], V["t2"], V["t2"], MULT, ["t2"], ["t3"])
                    ps_, pk = PSL()
                    MM(ps_[:, 0:T], bones[:], V["t3"], True, True, ["bones", "t3"], [pk])
                    RSQRT(V["t3"], ps_[:, 0:T], [pk], ["t3"], scale=1.0 / 64, eps=GN_EPS)
                    TT(V["t2"], V["t2"], V["t3"], MULT, ["t2", "t3"], ["t2"])
                    TSC(V["t2"], V["t2"], pt("lnx_g", p), pt("lnx_b", p), MULT, ADD, ["t2", "PT"], ["t2"])
                    TT(V["t2"], V["t2"], V["bonus"], ADD, ["t2", "bonus"], ["t2"])
                    TT(V["t2"], V["t2"], GG[alt][:, 0:T], MULT, ["t2", f"g{alt}"], ["t2"])
                    TT(merged[:, p, 0:T], V["t2"], SGA[alt][:, 0:T], MULT, ["t2", f"sga{alt}"], [("mg", p)])

            if not pre:
                stage(14)
                allzc = [("zcp", p) for p in range(16)]
                ps1, pk1 = PSL()
                for kc in range(16):
                    MM(ps1[:, 0:T], ones_bf[:], zc[:, kc, 0:T], kc == 0, kc == 15, ["ones"] + allzc, [pk1])
                TSC(rstd[:, 0:T], ps1[:, 0:T], -1.0 / D, None, MULT, None, [pk1], ["rstd"])
                TT(dx[:, :, 0:T], zc[:, :, 0:T], zc[:, :, 0:T], MULT, allzc, ["dx"])
                ps1, pk1 = PSL()
                for kc in range(16):
                    MM(ps1[:, 0:T], ones_bf[:], dx[:, kc, 0:T], kc == 0, kc == 15, ["ones", "dx"], [pk1])
                TT(rstd2[:, 0:T], rstd[:, 0:T], rstd[:, 0:T], MULT, ["rstd"], ["rstd2"])
                STT(rstd2[:, 0:T], ps1[:, 0:T], 1.0 / D, rstd2[:, 0:T], MULT, SUB, [pk1, "rstd2"], ["rstd2"])
                RSQRT(rstd2[:, 0:T], rstd2[:, 0:T], ["rstd2"], ["rstd2"], eps=1e-5)
                for p in range(NHP):
                    wv, wk = WLOAD(kview(w_in)[:, :, 6 * D + 128 * p:6 * D + 128 * p + 128], (128, 16, 128))
                    ps_, pk = PSL()
                    for kc in range(16):
                        MM(ps_[:, 0:T], wv[:, kc, :], h[:, kc, 0:T], kc == 0, kc == 15, [wk, "h"], [pk])
                    ACT(PV["gb"][:, 0:T], ps_[:, 0:T], AF.Sigmoid, [pk], ["gb"])
                    t2 = PV["t2"][:, 0:T]
                    TT(t2, zc[:, p, 0:T], rstd[:, 0:T], ADD, allzc + ["rstd"], ["t2"])
                    TT(t2, t2, rstd2[:, 0:T], MULT, ["t2", "rstd2"], ["t2"])
                    TSC(t2, t2, pt("ln_conv_g", p), pt("ln_conv_b", p), MULT, ADD, ["t2", "PT"], ["t2"])
                    ACT(t2, t2, AF.Silu, ["t2"], ["t2"])
                    TT(t2, t2, PV["gb"][:, 0:T], MULT, ["t2", "gb"], ["t2"])
                    TT(merged[:, p, 0:T], merged[:, p, 0:T], t2, ADD, [("mg", p), "t2"], [("mg", p)])
                allmg = [("mg", p) for p in range(16)]
                stage(15)
                for m in range(16):
                    wv, wk = WLOAD(kview(w_out)[:, :, 128 * m:128 * m + 128], (128, 16, 128))
                    ps_, pk = PSL()
                    for kc in range(16):
                        MM(ps_[:, 0:T], wv[:, kc, :], merged[:, kc, 0:T], kc == 0, kc == 15, [wk] + allmg, [pk])
                    if smp:
                        t23 = PV["t2"][:, 0:T].rearrange("p (s t) -> p s t", t=4)
                        TT(t23, ps_[:, 0:T].rearrange("p (s t) -> p s t", t=4), TS[:, 2, m, :].unsqueeze(2).to_broadcast([128, 16, 4]),
                           MULT, [pk, "TS"], ["t2"])
                        TT(xT[:, m, 0:T], xT[:, m, 0:T], PV["t2"][:, 0:T], ADD, ["xT", "t2"], ["xT"])
                    else:
                        STT(xT[:, m, 0:T], ps_[:, 0:T], TP[:, 2, m:m + 1], xT[:, m, 0:T], MULT, ADD, [pk, "TP", "xT"], ["xT"])
                stage(16)
                rms(xT, "xT", rstd)
                modnorm(xT, "xT", rstd, 3, 4)
                CP(h[:, :, 0:T], hf[:, :, 0:T], ["mixact"], ["h"], eng="act")
                act = mixbf[:, 0:NFC * TB].rearrange("p (f t) -> p f t", t=TB)
                for f in range(NFC):
                    for half in range(2):
                        ch = f + NFC * half
                        wv, wk = WLOAD(kview(w_up)[:, :, 128 * ch:128 * ch + 128], (128, 16, 128))
                        ps_, pk = PSL()
                        for kc in range(16):
                            MM(ps_[:, 0:T], wv[:, kc, :], h[:, kc, 0:T], kc == 0, kc == 15, [wk, "h"], [pk])
                        k0_, k1_, k2_ = (pt("ffn_dw_k", 86 * j + ch) for j in range(3))
                        bb = pt("ffn_dw_b", ch)
                        o = PV["t2"] if half == 0 else PV["t3"]
                        okey = "t2" if half == 0 else "t3"
                        if smp:
                            ze = uext[:, 0:96].rearrange("p (s t) -> p s t", t=6)
                            S.dma(Stmp[0:32, 0:128], st_ffn[:, :, 128 * ch:128 * ch + 128].rearrange("s t c -> (s t) c"), writes=["Stmp0"])
                            pq, pqk = PSL()
                            TRP(pq[:, 0:32], Stmp[0:32, 0:128], ["Stmp0"], [pqk])
                            CP(ze[:, :, 0:2], pq[:, 0:32].rearrange("p (s t) -> p s t", t=2), [pqk], ["uext"])
                            CP(ze[:, :, 2:6], ps_[:, 0:64].rearrange("p (s t) -> p s t", t=4), [pk], ["uext"], eng="act")
                            o3 = o[:, 0:64].rearrange("p (s t) -> p s t", t=4)
                            TSC(o3, ze[:, :, 0:4], k0_, bb, MULT, ADD, ["uext", "PT"], [okey])
                            STT(o3, ze[:, :, 1:5], k1_, o3, MULT, ADD, ["uext", "PT", okey], [okey])
                            STT(o3, ze[:, :, 2:6], k2_, o3, MULT, ADD, ["uext", "PT", okey], [okey])
                            pq, pqk = PSL()
                            CP(PV["t4"][:, 0:64].rearrange("p (t s) -> p t s", t=4), ze[:, :, 2:6].rearrange("p s t -> p t s"), ["uext"], ["t4"])
                            TRP(pq[0:64, 0:128], PV["t4"][:, 0:64], ["t4"], [pqk])
                            CP(Ytok[:, :], pq[0:64, 0:128], [pqk], ["Ytok"])
                            for t in range(2):
                                S.dma(ffn_s[:, t, 128 * ch:128 * ch + 128], Ytok[32 + 16 * t:48 + 16 * t, :], reads=["Ytok"])
                        else:
                            CP(uext[:, 0:2], ffnhalo[:, ch, :], ["state"], ["uext"])
                            CP(uext[:, 2:2 + T], ps_[:, 0:T], [pk], ["uext"], eng="act")
                            CP(ffnhalo[:, ch, :], uext[:, T:T + 2], ["uext"], ["state"])
                            TSC(o[:, 0:T], uext[:, 0:T], k0_, bb, MULT, ADD, ["uext", "PT"], [okey])
                            STT(o[:, 0:T], uext[:, 1:1 + T], k1_, o[:, 0:T], MULT, ADD, ["uext", "PT", okey], [okey])
                            STT(o[:, 0:T], uext[:, 2:2 + T], k2_, o[:, 0:T], MULT, ADD, ["uext", "PT", okey], [okey])
                            if last:
                                pq, pqk = PSL()
                                TRP(pq[0:2, 0:128], uext[:, T:T + 2], ["uext"], [pqk])
                                CP(Ytok[0:2, :], pq[0:2, 0:128], [pqk], ["Ytok"])
                                S.dma(ffn_p[:, 128 * ch:128 * ch + 128], Ytok[0:2, :], reads=["Ytok"])
                    ACT(PV["t2"][:, 0:T], PV["t2"][:, 0:T], AF.Silu, ["t2"], ["t2"])
                    TT(act[:, f, 0:T], PV["t2"][:, 0:T], PV["t3"][:, 0:T], MULT, ["t2", "t3"], ["mixact"])
                for m in range(16):
                    ps_, pk = PSL()
                    for g0 in range(0, NFC, 16):
                        n = min(16, NFC - g0)
                        wv, wk = WLOAD(w_down[128 * g0:128 * (g0 + n), 128 * m:128 * m + 128].rearrange("(kc ki) n -> ki kc n", ki=128),
                                       (128, n, 128))
                        for j in range(n):
                            MM(ps_[:, 0:T], wv[:, j, :], act[:, g0 + j, 0:T], g0 + j == 0, g0 + j == NFC - 1, [wk, "mixact"], [pk])
                    if smp:
                        t23 = PV["t2"][:, 0:T].rearrange("p (s t) -> p s t", t=4)
                        TT(t23, ps_[:, 0:T].rearrange("p (s t) -> p s t", t=4), TS[:, 5, m, :].unsqueeze(2).to_broadcast([128, 16, 4]),
                           MULT, [pk, "TS"], ["t2"])
                        TT(xT[:, m, 0:T], xT[:, m, 0:T], PV["t2"][:, 0:T], ADD, ["xT", "t2"], ["xT"])
                    else:
                        STT(xT[:, m, 0:T], ps_[:, 0:T], TP[:, 5, m:m + 1], xT[:, m, 0:T], MULT, ADD, [pk, "TP", "xT"], ["xT"])
                stage(17)
                rms(xT, "xT", rstd)
                TT(hf[:, :, 0:T], xT[:, :, 0:T], rstd[:, 0:T].unsqueeze(1).to_broadcast([128, 16, T]), MULT, ["xT", "rstd"], ["mixact"])
                TT(hf[:, :, 0:T], hf[:, :, 0:T], bc(ptr("normf_g", 0, 16)), MULT, ["mixact", "PT"], ["mixact"])
                R = 64 if smp else 128
                for ti_ in range(max(1, T // 128)):
                    for kc in range(16):
                        ps_, pk = PSL()
                        TRP(ps_[0:R, 0:128], hf[:, kc, 128 * ti_:128 * ti_ + R], ["mixact"], [pk])
                        CP(otile[0:R, 128 * kc:128 * kc + 128], ps_[0:R, 0:128], [pk], ["xtile"], eng="act" if kc % 2 else "dve")
                    S.dma(ys if smp else yp[TB * bi + 128 * ti_:TB * bi + 128 * ti_ + 128, :], otile[0:R, :], reads=["xtile"])

        try:
            for bi in range(n_pre):
                emit_block("q", bi)
            for bi in range(n_pblocks):
                emit_block("p", bi)
            if do_sample:
                S.dma(conv_s[:, 0:26, :], st_conv[:, 4:30, :])
                allset = [f"{n}{c}" for c in range(1, 4) for n in
                          ["TL", "TR", "TLz", "TRz", "Vtok", "nbtok", "ktok", "Ytok", "pz0_", "pz1_"] + CTN]
                S.op("pool", lambda e: e.memset(smpreg[:], 0.0), writes=allset + ["smpreg", "NBX", "KX", "KKX", "RX", "Ms", "selm", "sga1", "g1", "bonus", "rk", "kk", "t1", "kkn", "k"]
                     + [f"Stmp{i}" for i in range(1, 4)] + [f"Sout{i}" for i in range(1, 4)])
                S.op("pool", lambda e: e.memset(selm, 1.0), writes=["selm"])
                selm4 = selm.rearrange("p s (a b) -> p s a b", b=4)
                S.op("pool", lambda e: e.affine_select(out=selm4, in_=selm4, pattern=[[1, 16], [-1, 16], [0, 4]], compare_op=ALU.is_equal,
                                                       fill=0.0, base=0, channel_multiplier=0), reads=["selm"], writes=["selm"])
                emit_block("s", 0)
        except StopEmit:
            pass
        S.finish()
        print("instructions:", S.ninstr, "sbuf left", nc.sbuf_bytes_remaining)
    return nc


_NC_CACHE = {}
_CFG = {}


def kernel(**inputs):
    f32 = lambda a: np.ascontiguousarray(np.asarray(a, dtype=np.float32))
    I = {k: f32(v) for k, v in inputs.items()}
    ncores = 8
    npre, nmain = _CFG.get("npre", NPRE), _CFG.get("nmain", NMAIN)
    prm = np.zeros((NPROW, 128), np.float32)
    for name, cntc in _PSPEC:
        prm[POFF[name]:POFF[name] + cntc] = I[name].reshape(cntc, 128)
    if "nc" not in _NC_CACHE:
        _NC_CACHE["nc"] = build_nc()
    nc = _NC_CACHE["nc"]
    shared = {k: I[k] for k in ["w_ada", "w_in", "w1", "w2", "a1", "a2", "g1", "g2", "w_out", "w_up", "w_down"]}
    in_maps = []
    for c in range(ncores):
        m = dict(shared)
        m["params"] = prm
        seq, half = c // 2, c % 2
        x = I["x_prompt"][seq]
        if half:
            m["xq"] = x[0:npre * TB] if npre else np.zeros((TB, D), np.float32)
            m["xp"] = x[npre * TB:(npre + nmain) * TB]
        else:
            m["xq"] = np.zeros((max(npre, 1) * TB, D), np.float32)
            m["xp"] = np.concatenate([np.zeros((TB, D), np.float32), x[0:(nmain - 1) * TB]], 0)
        m["flag"] = np.full((128, 1), float(half), np.float32)
        sl = slice(NS * c, NS * c + NS)
        m["xs"] = I["x_sample"][sl].reshape(64, D)
        m["st_shift"] = I["state_shift"][sl]
        m["st_wkv"] = I["state_wkv"][sl]
        m["st_conv"] = I["state_conv"][sl]
        m["st_ffn"] = I["state_ffn"][sl]
        m["cvec"] = np.concatenate([I["c_prompt"][seq][None], I["c_sample"][sl]], 0)
        in_maps.append({k: np.ascontiguousarray(v) for k, v in m.items()})
    res = run_bass_kernel_spmd(nc, in_maps, core_ids=list(range(ncores)))
    R = res.results
    nv = (nmain - 1) * TB
    y_p = np.zeros((4, SEQ, D), np.float32)
    for c in range(ncores):
        seq, half = c // 2, c % 2
        t0 = (npre + 1) * TB if half else 0
        y_p[seq, t0:t0 + nv] = R[c]["yp"][TB:TB + nv]
    odd = [2 * s_ + 1 for s_ in range(4)]
    shift_p = np.stack([R[c]["shift_p"].reshape(D) for c in odd])
    wkv_p = np.stack([R[c]["wkv_p"] for c in odd])
    conv_p = np.stack([R[c]["conv_p"] for c in odd])
    ffn_p = np.stack([R[c]["ffn_p"] for c in odd])
    y_s = np.concatenate([R[c]["ys"].reshape(NS, 4, D) for c in range(8)])
    shift_s = np.concatenate([R[c]["shift_s"] for c in range(8)])
    wkv_s = np.concatenate([R[c]["wkv_s"] for c in range(8)])
    conv_s = np.concatenate([R[c]["conv_s"] for c in range(8)])
    ffn_s = np.concatenate([R[c]["ffn_s"] for c in range(8)])
    return (y_p, y_s, shift_p, wkv_p, conv_p, ffn_p, shift_s, wkv_s, conv_s, ffn_s)
```
